# Optimizing a Trainium2 kernel written in Bass

```python
import math
import jax
import jax.numpy as jnp
from jax import lax
import numpy as np

D_MODEL = 1024
BATCH = 16
SEQ = 2048
DEPTH = 4

GRID_W = 64
CTX_LEN = 256
N_MIXERS = 2
N_MOD = 6
EPS = 1e-6

GDN_HK = 8
GDN_HV = 16
GDN_DK = 128
GDN_DV = 128
GDN_QK_W = GDN_HK * GDN_DK
GDN_V_W = GDN_HV * GDN_DV
GDN_QKV_W = 2 * GDN_QK_W + GDN_V_W
GDN_IN_W = GDN_QKV_W + GDN_V_W + 4 * GDN_HV
CONV_W = 5
CHUNK = 64

ATT_HQ = 8
ATT_HKV = 2
ATT_G = ATT_HQ // ATT_HKV
ATT_DH = 128
ATT_IN_W = (ATT_HQ + 2 * ATT_HKV) * ATT_DH
Q_BLOCK = 128
ROPE_THETA = 10000.0

D_FF = ((8 * D_MODEL + 3 * 256 - 1) // (3 * 256)) * 256

ALPHA = (2 * DEPTH) ** 0.25
BETA_INIT = (8 * DEPTH) ** -0.25
N_GDN = (DEPTH + N_MIXERS - 1) // N_MIXERS
N_ATT = DEPTH // N_MIXERS

kernel_name = "hybrid_gdn_gqa_dit_trunk"


def layer_norm(x, g, b):
    xf = x.astype(jnp.float32)
    mu = jnp.mean(xf, axis=-1, keepdims=True)
    var = jnp.mean(jnp.square(xf - mu), axis=-1, keepdims=True)
    return ((xf - mu) * lax.rsqrt(var + EPS) * g + b).astype(x.dtype)


def rms_norm(x, g):
    xf = x.astype(jnp.float32)
    return xf * lax.rsqrt(jnp.mean(jnp.square(xf), axis=-1, keepdims=True) + EPS) * g


def l2_norm(x):
    return x * lax.rsqrt(jnp.sum(jnp.square(x), axis=-1, keepdims=True) + EPS)


def modulation(cond, w, b):
    m = jax.nn.silu(cond) @ w + b
    return m.reshape(m.shape[:-1] + (N_MOD, D_MODEL))


def modulate(x, shift, scale):
    return x * (1.0 + scale) + shift


def post_norm(x, delta, gate, g, b):
    return layer_norm(ALPHA * x + gate * delta, g, b)


def rope_tables(n_tokens):
    rows = n_tokens // GRID_W
    row = jnp.repeat(jnp.arange(rows), GRID_W).astype(jnp.float32)
    col = jnp.tile(jnp.arange(GRID_W), rows).astype(jnp.float32)
    n_freq = ATT_DH // 4
    freqs = ROPE_THETA ** (-jnp.arange(n_freq, dtype=jnp.float32) / n_freq)
    ang_r = row[:, None] * freqs
    ang_c = col[:, None] * freqs
    ang = jnp.concatenate([ang_r, ang_r, ang_c, ang_c], axis=-1)
    return jnp.cos(ang), jnp.sin(ang)


def rope_2d(x, cos, sin):
    xs = x.reshape(x.shape[:-1] + (2, 2, ATT_DH // 4))
    x1, x2 = xs[..., 0, :], xs[..., 1, :]
    rot = jnp.stack([-x2, x1], axis=-2).reshape(x.shape)
    return x * cos + rot * sin


def short_conv(x, w):
    pad = CONV_W // 2
    n_tok = x.shape[1]
    xp = jnp.pad(x, ((0, 0), (pad, pad), (0, 0)))
    y = xp[:, 0:n_tok] * w[0]
    for j in range(1, CONV_W):
        y = y + xp[:, j:j + n_tok] * w[j]
    return y


def chunk_gated_delta(q, k, v, g, beta, s0):
    f32 = jnp.float32
    B, T, H, DK = q.shape
    n = T // CHUNK

    def chunks(a):
        a = a.astype(f32).reshape((B, n, CHUNK, H) + a.shape[3:])
        return jnp.moveaxis(a, (1, 3), (0, 2))

    qc = chunks(q) * (DK ** -0.5)
    kc = chunks(k)
    vc = chunks(v)
    bc = chunks(beta)
    gcum = jnp.cumsum(chunks(g), axis=-1)
    incl = jnp.tril(jnp.ones((CHUNK, CHUNK), bool))
    strict = jnp.tril(jnp.ones((CHUNK, CHUNK), bool), -1)
    diff = gcum[..., :, None] - gcum[..., None, :]
    decay = jnp.where(incl, jnp.exp(jnp.where(incl, diff, 0.0)), 0.0)
    kb = kc * bc[..., None]
    a_mat = jnp.where(strict, jnp.einsum('nbhik,nbhjk->nbhij', kb, kc) * decay, 0.0)
    eye = jnp.eye(CHUNK, dtype=f32)
    t_inv = lax.linalg.triangular_solve(eye + a_mat, jnp.broadcast_to(eye, a_mat.shape),
                                        left_side=True, lower=True)
    u = jnp.einsum('nbhij,nbhjv->nbhiv', t_inv, vc * bc[..., None])
    w = jnp.einsum('nbhij,nbhjk->nbhik', t_inv, kb * jnp.exp(gcum)[..., None])
    qk = jnp.einsum('nbhik,nbhjk->nbhij', qc, kc) * decay

    def step(s, inp):
        q_i, k_i, u_i, w_i, qk_i, g_i = inp
        v_new = u_i - jnp.einsum('bhck,bhkv->bhcv', w_i, s)
        o_i = (jnp.einsum('bhck,bhkv->bhcv', q_i * jnp.exp(g_i)[..., None], s)
               + jnp.einsum('bhij,bhjv->bhiv', qk_i, v_new))
        g_last = g_i[..., -1:]
        s = (s * jnp.exp(g_last)[..., None]
             + jnp.einsum('bhck,bhcv->bhkv', k_i * jnp.exp(g_last - g_i)[..., None], v_new))
        return s, o_i

    s_fin, o = lax.scan(step, s0.astype(f32), (qc, kc, u, w, qk, gcum))
    o = jnp.moveaxis(o, (0, 2), (1, 3)).reshape(B, T, H, -1)
    return o, s_fin


def gdn_direction(q, k, v, g, beta, s0, reverse):
    if reverse:
        q, k, v, g, beta = (jnp.flip(a, axis=1) for a in (q, k, v, g, beta))
    o, s = chunk_gated_delta(q, k, v, g, beta, s0)
    if reverse:
        o = jnp.flip(o, axis=1)
    return o, s


def gdn_project(h, w_in, conv_w, a_log, dt_bias):
    B, T, _ = h.shape
    p = (h @ w_in).astype(jnp.float32)
    qkv = jax.nn.silu(short_conv(p[..., :GDN_QKV_W], conv_w.astype(jnp.float32)))
    z = p[..., GDN_QKV_W:GDN_QKV_W + GDN_V_W]
    off = GDN_QKV_W + GDN_V_W
    b_raw = p[..., off:off + 2 * GDN_HV].reshape(B, T, 2, GDN_HV)
    a_raw = p[..., off + 2 * GDN_HV:].reshape(B, T, 2, GDN_HV)
    rep = GDN_HV // GDN_HK
    q = jnp.repeat(l2_norm(qkv[..., :GDN_QK_W].reshape(B, T, GDN_HK, GDN_DK)), rep, axis=2)
    k = jnp.repeat(l2_norm(qkv[..., GDN_QK_W:2 * GDN_QK_W].reshape(B, T, GDN_HK, GDN_DK)), rep, axis=2)
    v = qkv[..., 2 * GDN_QK_W:].reshape(B, T, GDN_HV, GDN_DV)
    beta = jax.nn.sigmoid(b_raw)
    g = -jnp.exp(a_log.astype(jnp.float32)) * jax.nn.softplus(a_raw + dt_bias.astype(jnp.float32))
    return q, k, v, z, beta, g


def gdn_output(o, z, norm_g, w_out, dtype):
    B, T = o.shape[:2]
    y = rms_norm(o, norm_g) * jax.nn.silu(z.reshape(B, T, GDN_HV, GDN_DV))
    return y.reshape(B, T, GDN_V_W).astype(dtype) @ w_out


def gdn_mixer(h_lat, h_ctx, w_in, conv_w, a_log, dt_bias, norm_g, w_out, with_ctx_out):
    ql, kl, vl, zl, bl, gl = gdn_project(h_lat, w_in, conv_w, a_log, dt_bias)
    qc, kc, vc, zc, bc, gc = gdn_project(h_ctx, w_in, conv_w, a_log, dt_bias)
    s_zero = jnp.zeros((h_lat.shape[0], GDN_HV, GDN_DK, GDN_DV), jnp.float32)
    o_lat = []
    o_ctx = []
    for d, reverse in enumerate((False, True)):
        oc, s_ctx = gdn_direction(qc, kc, vc, gc[:, :, d], bc[:, :, d], s_zero, reverse)
        ol, _ = gdn_direction(ql, kl, vl, gl[:, :, d], bl[:, :, d], s_ctx, reverse)
        o_lat.append(ol)
        o_ctx.append(oc)
    out_lat = gdn_output(o_lat[0] + o_lat[1], zl, norm_g, w_out, h_lat.dtype)
    out_ctx = gdn_output(o_ctx[0] + o_ctx[1], zc, norm_g, w_out, h_ctx.dtype) if with_ctx_out else None
    return out_lat, out_ctx


def attn_project(h, w_qkv, q_norm, k_norm):
    B, T, _ = h.shape
    p = h @ w_qkv
    nq = ATT_HQ * ATT_DH
    nk = ATT_HKV * ATT_DH
    q = rms_norm(p[..., :nq].reshape(B, T, ATT_HQ, ATT_DH), q_norm)
    k = rms_norm(p[..., nq:nq + nk].reshape(B, T, ATT_HKV, ATT_DH), k_norm)
    v = p[..., nq + nk:].reshape(B, T, ATT_HKV, ATT_DH).astype(jnp.float32)
    q = q.reshape(B, T, ATT_HKV, ATT_G, ATT_DH).transpose(0, 2, 3, 1, 4)
    return q, k.transpose(0, 2, 1, 3), v.transpose(0, 2, 1, 3)


def softmax_attend(q, k, v):
    s = jnp.einsum('bhgqd,bhsd->bhgqs', q, k).astype(jnp.float32) * (ATT_DH ** -0.5)
    p = jax.nn.softmax(s, axis=-1)
    return jnp.einsum('bhgqs,bhsd->bhgqd', p, v)


def merge_heads(o):
    B, _, _, T, _ = o.shape
    return o.transpose(0, 3, 1, 2, 4).reshape(B, T, ATT_HQ * ATT_DH)


def attn_mixer(h_lat, h_ctx, w_qkv, q_norm, k_norm, w_out, cos, sin, with_ctx_out):
    ql, kl, vl = attn_project(h_lat, w_qkv, q_norm, k_norm)
    ql = rope_2d(ql, cos, sin)
    kl = rope_2d(kl, cos, sin)
    qc, kc, vc = attn_project(h_ctx, w_qkv, q_norm, k_norm)
    k_all = jnp.concatenate([kc, kl], axis=2)
    v_all = jnp.concatenate([vc, vl], axis=2)
    B, _, _, T, _ = ql.shape
    nblk = T // Q_BLOCK
    qb = jnp.moveaxis(ql.reshape(B, ATT_HKV, ATT_G, nblk, Q_BLOCK, ATT_DH), 3, 0)
    ob = lax.map(lambda q_blk: softmax_attend(q_blk, k_all, v_all), qb)
    ol = jnp.moveaxis(ob, 0, 3).reshape(B, ATT_HKV, ATT_G, T, ATT_DH)
    out_lat = merge_heads(ol).astype(h_lat.dtype) @ w_out
    out_ctx = merge_heads(softmax_attend(qc, kc, vc)).astype(h_ctx.dtype) @ w_out if with_ctx_out else None
    return out_lat, out_ctx


def swiglu(h, w_in, w_out):
    gu = h @ w_in
    return (jax.nn.silu(gu[..., :D_FF]) * gu[..., D_FF:]) @ w_out


def setup_inputs(seed: int = 0) -> dict:
    key = jax.random.key(seed)
    ks = jax.random.split(key, 24)
    f32 = jnp.float32
    D = D_MODEL

    def nrm(k, shape, scale):
        return jax.random.normal(k, shape, f32) * scale

    dt = jnp.exp(jax.random.uniform(ks[13], (N_GDN, 2, GDN_HV), f32, math.log(1e-3), math.log(1e-1)))
    return {
        "x": nrm(ks[0], (BATCH, SEQ, D), 1.0),
        "c": nrm(ks[1], (BATCH, D), 1.0),
        "ctx": nrm(ks[2], (BATCH, CTX_LEN, D), 1.0),
        "c_ctx": nrm(ks[3], (D,), 1.0),
        "w_mod": nrm(ks[4], (DEPTH, D, N_MOD * D), D ** -0.5),
        "b_mod": nrm(ks[5], (DEPTH, N_MOD * D), 0.02),
        "ln_g": 1.0 + nrm(ks[6], (DEPTH, 2, D), 0.05),
        "ln_b": nrm(ks[7], (DEPTH, 2, D), 0.02),
        "w_ffn_in": nrm(ks[8], (DEPTH, D, 2 * D_FF), D ** -0.5),
        "w_ffn_out": nrm(ks[9], (DEPTH, D_FF, D), D_FF ** -0.5 * BETA_INIT),
        "gdn_w_in": nrm(ks[10], (N_GDN, D, GDN_IN_W), D ** -0.5),
        "gdn_conv": nrm(ks[11], (N_GDN, CONV_W, GDN_QKV_W), CONV_W ** -0.5),
        "gdn_a_log": jnp.log(jax.random.uniform(ks[12], (N_GDN, 2, GDN_HV), f32, 1.0, 16.0)),
        "gdn_dt_bias": dt + jnp.log(-jnp.expm1(-dt)),
        "gdn_norm_g": 1.0 + nrm(ks[14], (N_GDN, GDN_DV), 0.05),
        "gdn_w_out": nrm(ks[15], (N_GDN, GDN_V_W, D), GDN_V_W ** -0.5 * BETA_INIT),
        "attn_w_qkv": nrm(ks[16], (N_ATT, D, ATT_IN_W), D ** -0.5),
        "attn_q_norm": 1.0 + nrm(ks[17], (N_ATT, ATT_DH), 0.05),
        "attn_k_norm": 1.0 + nrm(ks[18], (N_ATT, ATT_DH), 0.05),
        "attn_w_out": nrm(ks[19], (N_ATT, ATT_HQ * ATT_DH, D), (ATT_HQ * ATT_DH) ** -0.5 * BETA_INIT),
    }


def reference(x, c, ctx, c_ctx, w_mod, b_mod, ln_g, ln_b, w_ffn_in, w_ffn_out,
              gdn_w_in, gdn_conv, gdn_a_log, gdn_dt_bias, gdn_norm_g, gdn_w_out,
              attn_w_qkv, attn_q_norm, attn_k_norm, attn_w_out):
    cos, sin = rope_tables(x.shape[1])
    xl, xc = x, ctx
    for i in range(DEPTH):
        with_ctx_out = i < DEPTH - 1
        ml = modulation(c, w_mod[i], b_mod[i])
        mc = modulation(c_ctx, w_mod[i], b_mod[i])
        sh_l, sc_l, ga_l, shf_l, scf_l, gaf_l = (ml[:, j, None, :] for j in range(N_MOD))
        sh_c, sc_c, ga_c, shf_c, scf_c, gaf_c = (mc[j] for j in range(N_MOD))
        hl = modulate(xl, sh_l, sc_l)
        hc = modulate(xc, sh_c, sc_c)
        j = i // N_MIXERS
        if i % N_MIXERS == 0:
            dl, dc = gdn_mixer(hl, hc, gdn_w_in[j], gdn_conv[j], gdn_a_log[j], gdn_dt_bias[j],
                               gdn_norm_g[j], gdn_w_out[j], with_ctx_out)
        else:
            dl, dc = attn_mixer(hl, hc, attn_w_qkv[j], attn_q_norm[j], attn_k_norm[j], attn_w_out[j],
                                cos, sin, with_ctx_out)
        xl = post_norm(xl, dl, ga_l, ln_g[i, 0], ln_b[i, 0])
        xl = post_norm(xl, swiglu(modulate(xl, shf_l, scf_l), w_ffn_in[i], w_ffn_out[i]),
                       gaf_l, ln_g[i, 1], ln_b[i, 1])
        if with_ctx_out:
            xc = post_norm(xc, dc, ga_c, ln_g[i, 0], ln_b[i, 0])
            xc = post_norm(xc, swiglu(modulate(xc, shf_c, scf_c), w_ffn_in[i], w_ffn_out[i]),
                           gaf_c, ln_g[i, 1], ln_b[i, 1])
    return xl
```

```python
import numpy as np
from contextlib import ExitStack
import concourse.bass as bass
import concourse.mybir as mybir
from concourse.bass_utils import run_bass_kernel_spmd

F32 = mybir.dt.float32
BF16 = mybir.dt.bfloat16
U8 = mybir.dt.uint8
AF = mybir.ActivationFunctionType
ALU = mybir.AluOpType
AX = mybir.AxisListType

D = 1024
KC = 8
T = 2304
NCH = 18
DEPTH = 4
DFF = 2816
EPS = 1e-6
ALPHA = 8.0 ** 0.25
TILES = [(0, 256), (256, 512), (768, 512), (1280, 512), (1792, 512)]
FWD = list(range(18))
BWD = [1, 0] + list(range(17, 1, -1))
NCFA = 1156
DEBUG_ALLOC = False
GDN_LANES = 5
DBG = {}


def which_of(ti):
    return 0 if ti == 0 else 1


class B:
    def __init__(self, nc):
        self.nc = nc
        self.es = ExitStack()
        self.E = ['pe', 'act', 'dve', 'pool', 'sp']
        self.q = {e: [] for e in self.E}
        self.cnt = {e: 0 for e in self.E}
        self.sem = {e: self.es.enter_context(nc.semaphore("s_" + e)) for e in self.E if e != 'sp'}
        self.ND = 16
        self.dsem = [self.es.enter_context(nc.semaphore("d%d" % i)) for i in range(self.ND)]
        self.dcnt = [0] * self.ND
        self.drr = 0
        self.waited = {e: {} for e in self.E}
        self.track = {}
        self.ninstr = 0
        self.psb = [self.es.enter_context(nc.psum_tensor("ps%d" % i, [128, 512], F32)) for i in range(8)]
        self.psrr = 0

    def _semh(self, sk):
        return self.sem[sk] if isinstance(sk, str) else self.dsem[sk[1]]

    def _deps(self, eng, r, w):
        raw = {}
        oth = {}

        def need(dct, idv):
            if idv is None:
                return
            sk, v = idv
            if dct.get(sk, 0) < v:
                dct[sk] = v
        for key in r:
            t = self.track.get(key)
            if t:
                need(raw, t[0])
        for key in w:
            t = self.track.get(key)
            if t:
                need(oth, t[0])
                for sk, v in t[1].items():
                    need(oth, (sk, v))
        for sk, v in oth.items():
            if sk == eng and eng == 'pe':
                continue
            if raw.get(sk, 0) < v:
                raw[sk] = v
        for sk, v in raw.items():
            if self.waited[eng].get(sk, 0) >= v:
                continue
            self.waited[eng][sk] = v
            sh = self._semh(sk)
            self.q[eng].append(lambda e, sh=sh, v=v: e.wait_ge(sh, v))
            self.ninstr += 1

    def _mark(self, idv, r, w):
        for key in w:
            self.track[key] = [idv, {}]
        for key in r:
            t = self.track.setdefault(key, [None, {}])
            if t[1].get(idv[0], 0) < idv[1]:
                t[1][idv[0]] = idv[1]

    def op(self, eng, fn, r=(), w=()):
        self._deps(eng, r, w)
        self.cnt[eng] += 1
        sh = self.sem[eng]
        self.q[eng].append(lambda e, fn=fn, sh=sh: fn(e).then_inc(sh, 1))
        self.ninstr += 1
        self._mark((eng, self.cnt[eng]), r, w)

    def dma(self, out, in_, r=(), w=()):
        eng = 'sp'
        k = self.drr
        self.drr = (self.drr + 1) % self.ND
        self._deps(eng, r, w)
        prev = 16 * self.dcnt[k]
        if prev and self.waited[eng].get(('d', k), 0) < prev:
            self.waited[eng][('d', k)] = prev
            self.q[eng].append(lambda e, sh=self.dsem[k], v=prev: e.wait_ge(sh, v))
        self.dcnt[k] += 1
        val = 16 * self.dcnt[k]
        self.q[eng].append(lambda e, out=out, in_=in_, sh=self.dsem[k]: e.dma_start(out=out, in_=in_).then_inc(sh, 16))
        self.ninstr += 1
        idv = (('d', k), val)
        self._mark(idv, r, w)
        return idv

    def barrier(self):
        for e in self.E:
            for o in self.E:
                if o == 'sp' or o == e:
                    continue
                v = self.cnt[o]
                if v and self.waited[e].get(o, 0) < v:
                    self.waited[e][o] = v
                    self.q[e].append(lambda en, sh=self.sem[o], v=v: en.wait_ge(sh, v))
                    self.ninstr += 1
            for k in range(self.ND):
                v = 16 * self.dcnt[k]
                if v and self.waited[e].get(('d', k), 0) < v:
                    self.waited[e][('d', k)] = v
                    self.q[e].append(lambda en, sh=self.dsem[k], v=v: en.wait_ge(sh, v))
                    self.ninstr += 1

    def scope(self):
        return _Scope(self)

    def new_epoch(self):
        self.barrier()
        self.nep = getattr(self, 'nep', 0) + 1
        for e in self.E:
            if e == 'sp':
                continue
            self.sem[e] = self.es.enter_context(self.nc.semaphore("s_%s_%d" % (e, self.nep)))
            self.cnt[e] = 0
        for e in self.E:
            for o in list(self.waited[e].keys()):
                if isinstance(o, str):
                    del self.waited[e][o]
        self.track = {}

    def ps(self, hold=False):
        if not hasattr(self, 'held'):
            self.held = set()
        while True:
            k = self.psrr
            self.psrr = (self.psrr + 1) % 8
            if k not in self.held:
                break
        if hold:
            self.held.add(k)
        return self.psb[k], ('ps', k)

    def release(self, key):
        self.held.discard(key[1])

    def mm(self, out, lhsT, rhs, start=True, stop=True, r=(), w=()):
        self.op('pe', lambda e: e.matmul(out, lhsT, rhs, start=start, stop=stop), r, w)

    def tr(self, out, in_, ident, r=(), w=()):
        self.op('pe', lambda e: e.transpose(out, in_, ident), r, w)

    def act(self, out, in_, func, bias=None, scale=None, r=(), w=()):
        kw = {}
        if bias is not None:
            kw['bias'] = bias
        if scale is not None:
            kw['scale'] = scale
        self.op('act', lambda e: e.activation(out=out, in_=in_, func=func, **kw), r, w)

    def tt(self, eng, out, a, b, op, r=(), w=()):
        self.op(eng, lambda e: e.tensor_tensor(out=out, in0=a, in1=b, op=op), r, w)

    def ts(self, eng, out, a, s1, s2, op0, op1=None, r=(), w=()):
        if op1 is None:
            self.op(eng, lambda e: e.tensor_scalar(out=out, in0=a, scalar1=s1, scalar2=None, op0=op0), r, w)
        else:
            self.op(eng, lambda e: e.tensor_scalar(out=out, in0=a, scalar1=s1, scalar2=s2, op0=op0, op1=op1), r, w)

    def stt(self, eng, out, a, s, b, op0, op1, r=(), w=()):
        self.op(eng, lambda e: e.scalar_tensor_tensor(out=out, in0=a, scalar=s, in1=b, op0=op0, op1=op1), r, w)

    def cp(self, eng, out, in_, r=(), w=()):
        if eng == 'act':
            self.act(out, in_, AF.Copy, r=r, w=w)
        else:
            self.op(eng, lambda e: e.tensor_copy(out=out, in_=in_), r, w)

    def sb(self, es, name, shape, dt):
        if DEBUG_ALLOC:
            print("alloc", name, shape, dt, "remaining", self.nc.sbuf_bytes_remaining)
        self.uid = getattr(self, "uid", 0) + 1
        return es.enter_context(self.nc.sbuf_tensor("sb%d_%s" % (self.uid, name), shape, dt))

    def finish(self):
        nc = self.nc
        q = self.q
        with nc.Block() as block:
            @block.tensor
            def _(e):
                for f in q['pe']:
                    f(e)

            @block.scalar
            def _(e):
                for f in q['act']:
                    f(e)

            @block.vector
            def _(e):
                for f in q['dve']:
                    f(e)

            @block.gpsimd
            def _(e):
                for f in q['pool']:
                    f(e)

            @block.sync
            def _(e):
                for f in q['sp']:
                    f(e)
        self.es.close()


class _Scope:
    def __init__(self, b):
        self.b = b
        self.es = ExitStack()

    def __enter__(self):
        self.es.__enter__()
        return self.es

    def __exit__(self, *a):
        self.b.barrier()
        return self.es.__exit__(*a)


def xa_keys(c, tis):
    return [('xa', c, ti) for ti in tis]


def h_keys(c, tis):
    return [('h', c, ti) for ti in tis]


ALLT = list(range(5))


class Prog:
    def __init__(self, nc, layers, nseq):
        self.nc = nc
        self.b = B(nc)
        self.layers = layers
        self.nseq = nseq
        self.dram = {}

    def din(self, name, shape, dt=F32):
        if name not in self.dram:
            self.dram[name] = self.nc.dram_tensor(name, list(shape), dt, kind="ExternalInput").ap()
        return self.dram[name]

    def dout(self, name, shape, dt=F32):
        if name not in self.dram:
            self.dram[name] = self.nc.dram_tensor(name, list(shape), dt, kind="ExternalOutput").ap()
        return self.dram[name]

    def build(self):
        b = self.b
        nc = self.nc
        es = b.es
        self.xa = b.sb(es, "xa", [128, KC, T], F32)
        self.hb = b.sb(es, "hb", [128, KC, T], BF16)
        self.cfa = b.sb(es, "cfa", [128, NCFA], F32)
        self.identb = b.sb(es, "identb", [128, 128], BF16)
        self.onesb = b.sb(es, "onesb", [128, 128], BF16)
        self.mean1k = b.sb(es, "mean1k", [128, 128], F32)
        self.mean1kb = b.sb(es, "mean1kb", [128, 128], BF16)
        cfa_d = self.din("cfa", [128, NCFA])
        b.dma(self.cfa[:], cfa_d, w=['cfa'])
        self.ident = self.cfa[:, 0:128]
        self.trif = self.cfa[:, 128:256]
        self.trib = self.cfa[:, 256:384]
        self.ones = self.cfa[:, 384:512]
        self.maskq = self.cfa[:, 512:768]
        self.rot = self.cfa[:, 768:896]
        self.epsv = self.cfa[:, 1152:1155]
        self.nident = self.cfa[:, 896:1024]
        self.nones = self.cfa[:, 1024:1152]
        b.cp('act', self.identb[:], self.ident, r=['cfa'], w=['identb'])
        b.cp('act', self.onesb[:], self.ones, r=['cfa'], w=['onesb'])
        b.act(self.mean1k[:], self.ones, AF.Copy, scale=1.0 / 1024.0, r=['cfa'], w=['mean1k'])
        b.act(self.mean1kb[:], self.ones, AF.Copy, scale=1.0 / 1024.0, r=['cfa'], w=['mean1k'])
        self.mod_all(es)
        outs = []
        for s in range(self.nseq):
            xin = self.din("xin%d" % s, [D, T])
            xout = self.dout("xout%d" % s, [D, T])
            with b.scope() as les:
                stg = [b.sb(les, "ldst%d" % i, [128, T], F32) for i in range(2)]
                for c in range(KC):
                    st = stg[c % 2]
                    b.dma(st[:], xin[c * 128:(c + 1) * 128, :], w=[('ldst', c % 2)])
                    b.act(self.xa[:, c, :], st[:], AF.Copy, scale=ALPHA, r=[('ldst', c % 2)], w=xa_keys(c, ALLT))
            for n, li in enumerate(self.layers):
                last = (n == len(self.layers) - 1)
                b.new_epoch()
                self.layer(li, s, out_plain=last)
            for c in range(KC):
                outs.append(b.dma(xout[c * 128:(c + 1) * 128, :], self.xa[:, c, :], r=xa_keys(c, ALLT)))
        for (sk, v) in outs + getattr(self, 'dbg_outs', []):
            if b.waited['sp'].get(sk, 0) < v:
                b.waited['sp'][sk] = v
                b.q['sp'].append(lambda e, sh=b.dsem[sk[1]], v=v: e.wait_ge(sh, v))
        b.finish()

    def layer(self, li, s, out_plain):
        b = self.b
        with b.scope() as les:
            self.lv = {}
            stop = DBG.get('stop')
            self.modulation(li, s, les, out_plain)
            if DBG.get('dump'):
                self.dbg_outs = getattr(self, 'dbg_outs', [])
                self.dbg_outs.append(b.dma(self.dout("dbg_MOD", [128, 96]), self.P['MOD'][:].rearrange("p a b -> p (a b)"), r=['MOD']))
            if stop == 'mod':
                return
            self.modulate_in()
            if DBG.get('dump'):
                self.dbg_outs.append(b.dma(self.dout("dbg_h", [128, KC * T], BF16), self.hb[:].rearrange("p a b -> p (a b)"),
                                           r=[('h', c, ti) for c in range(KC) for ti in ALLT]))
            if stop == 'modin':
                return
            if li % 2 == 0:
                self.gdn(li)
            else:
                self.attn(li)
            if stop == 'mixer':
                return
            self.layernorm(0, want_h=True)
            if stop == 'ln0':
                return
            self.ffn(li)
            if stop == 'ffn':
                return
            self.layernorm(1, want_h=False)

    def mod_all(self, es):
        b = self.b
        ns = 1 + self.nseq
        self.MODall = {}
        for li in self.layers:
            self.MODall[li] = b.sb(es, "MODall%d" % li, [128, 48, ns], F32)
        with b.scope() as wes:
            sc = b.sb(wes, "ma_sc", [128, KC, ns], F32)
            bm = b.sb(wes, "ma_bm", [128, 48], F32)
            wst = [b.sb(wes, "ma_wst%d" % i, [128, KC, 512], F32) for i in range(3)]
            cT = self.din("cTall", [128, KC, ns])
            b.dma(sc[:], cT, w=['sc'])
            b.act(sc[:], sc[:], AF.Silu, r=['sc'], w=['sc'])
            nblk = 0
            for li in self.layers:
                w_mod = self.din("w_mod%d" % li, [D, 6 * D])
                bmodT = self.din("bmodT%d" % li, [128, 48])
                b.dma(bm[:], bmodT, w=['bmodT'])
                ps, pk = b.ps(hold=True)
                for nb in range(12):
                    st = wst[nblk % 3]
                    sk = ('wmst', nblk % 3)
                    nblk += 1
                    b.dma(st[:], w_mod[:, nb * 512:(nb + 1) * 512].rearrange("(k p) n -> p k n", p=128), w=[sk])
                    for f in range(4):
                        fc = nb * 4 + f
                        for kc in range(KC):
                            b.mm(ps[:, fc * ns:(fc + 1) * ns], st[:, kc, f * 128:(f + 1) * 128], sc[:, kc, :],
                                 start=(kc == 0), stop=(kc == KC - 1), r=[sk, 'sc'], w=[pk])
                b.tt('dve', self.MODall[li][:], ps[:, 0:48 * ns].rearrange("p (a b) -> p a b", b=ns),
                     bm[:].unsqueeze(2).to_broadcast([128, 48, ns]), ALU.add, r=[pk, 'bmodT'], w=[('MODall', li)])
                b.release(pk)

    def modulation(self, li, s, les, out_plain):
        b = self.b
        P = {}
        for nm, shape in [('MOD', [128, 48, 2]), ('s1', [128, 8, 2]), ('H1s', [128, 8, 2]), ('H1b', [128, 8, 2]),
                          ('A', [128, 2, 8]), ('Bv', [128, 2, 8]), ('lnT', [128, 4, 8]), ('bmodT', [128, 48]),
                          ('sc', [128, 8, 2]), ('tmp1', [128, 8, 2])]:
            P[nm] = b.sb(les, "m_" + nm, shape, F32)
        self.P = P
        lnT = self.din("lnT%d" % li, [128, 4, 8])
        b.dma(P['lnT'][:], lnT, w=['lnT'])
        MOD = P['MOD']
        MA = self.MODall[li]
        b.cp('dve', MOD[:, :, 0:1], MA[:, :, 0:1], r=[('MODall', li)], w=['MOD'])
        b.cp('dve', MOD[:, :, 1:2], MA[:, :, 1 + s:2 + s], r=[('MODall', li)], w=['MOD'])

        def mj(j):
            return MOD[:, j * 8:(j + 1) * 8, :]
        b.ts('dve', P['s1'][:], mj(1), 1.0, 1.0 / ALPHA, ALU.add, ALU.mult, r=['MOD'], w=['s1'])
        ln = P['lnT']
        g0 = ln[:, 0, :].unsqueeze(2).to_broadcast([128, 8, 2])
        b0 = ln[:, 1, :].unsqueeze(2).to_broadcast([128, 8, 2])
        b.ts('dve', P['tmp1'][:], mj(4), 1.0, None, ALU.add, r=['MOD'], w=['tmp1'])
        b.tt('dve', P['H1s'][:], P['tmp1'][:], g0, ALU.mult, r=['tmp1', 'lnT'], w=['H1s'])
        b.tt('dve', P['H1b'][:], P['tmp1'][:], b0, ALU.mult, r=['tmp1', 'lnT'], w=['H1b'])
        b.tt('dve', P['H1b'][:], P['H1b'][:], mj(3), ALU.add, r=['H1b', 'MOD'], w=['H1b'])
        b.ts('dve', P['A'][:, 0, :], ln[:, 0, :], ALPHA, None, ALU.mult, r=['lnT'], w=['A'])
        b.ts('dve', P['Bv'][:, 0, :], ln[:, 1, :], ALPHA, None, ALU.mult, r=['lnT'], w=['Bv'])
        a2 = 1.0 if out_plain else ALPHA
        b.ts('dve', P['A'][:, 1, :], ln[:, 2, :], a2, None, ALU.mult, r=['lnT'], w=['A'])
        b.ts('dve', P['Bv'][:, 1, :], ln[:, 3, :], a2, None, ALU.mult, r=['lnT'], w=['Bv'])
        self.ga = mj(2)
        self.gaf = mj(5)
        self.sh = mj(0)

    def modulate_in(self):
        b = self.b
        P = self.P
        for c in range(KC):
            for (wh, t0, n, tis) in [(0, 0, 256, [0]), (1, 256, 2048, [1, 2, 3, 4])]:
                b.act(self.hb[:, c, t0:t0 + n], self.xa[:, c, t0:t0 + n], AF.Identity,
                      bias=self.sh[:, c, wh:wh + 1], scale=P['s1'][:, c, wh:wh + 1],
                      r=xa_keys(c, tis) + ['MOD', 's1'], w=h_keys(c, tis))

    def layernorm(self, idx, want_h):
        b = self.b
        P = self.P
        with b.scope() as es:
            sq = [b.sb(es, "ln_sq%d" % i, [128, 512], BF16) for i in range(2)]
            msb = b.sb(es, "ln_msb", [128, 512], F32)
            m2 = b.sb(es, "ln_m2", [128, 512], F32)
            rstd = b.sb(es, "ln_rstd", [128, 512], F32)
            tt_ = [b.sb(es, "ln_t%d" % i, [128, 512], F32) for i in range(2)]
            for ti, (t0, n) in enumerate(TILES):
                wh = which_of(ti)
                pm, pmk = b.ps()
                pe2, pe2k = b.ps()
                for c in range(KC):
                    b.act(sq[c % 2][:, :n], self.xa[:, c, t0:t0 + n], AF.Square, r=[('xa', c, ti)], w=[('lnsq', c % 2)])
                    b.mm(pm[:, :n], self.mean1k[:], self.xa[:, c, t0:t0 + n], start=(c == 0), stop=(c == KC - 1),
                         r=[('xa', c, ti), 'mean1k'], w=[pmk])
                    b.mm(pe2[:, :n], self.mean1kb[:], sq[c % 2][:, :n], start=(c == 0), stop=(c == KC - 1),
                         r=[('lnsq', c % 2), 'mean1k'], w=[pe2k])
                b.cp('act', msb[:, :n], pm[:, :n], r=[pmk], w=['lnmsb'])
                b.tt('dve', m2[:, :n], msb[:, :n], msb[:, :n], ALU.mult, r=['lnmsb'], w=['lnm2'])
                b.tt('dve', m2[:, :n], pe2[:, :n], m2[:, :n], ALU.subtract, r=[pe2k, 'lnm2'], w=['lnm2'])
                b.act(m2[:, :n], m2[:, :n], AF.Ln, bias=self.epsv[:, 0:1], r=['lnm2', 'cfa'], w=['lnm2'])
                b.act(rstd[:, :n], m2[:, :n], AF.Exp, scale=-0.5, r=['lnm2'], w=['lnrstd'])
                for c in range(KC):
                    t = tt_[c % 2]
                    tk = ('lnt', c % 2)
                    e1 = 'dve'
                    e2 = 'dve'
                    b.tt(e1, t[:, :n], self.xa[:, c, t0:t0 + n], msb[:, :n], ALU.subtract, r=[('xa', c, ti), 'lnmsb'], w=[tk])
                    b.tt(e2, t[:, :n], t[:, :n], rstd[:, :n], ALU.mult, r=[tk, 'lnrstd'], w=[tk])
                    b.act(self.xa[:, c, t0:t0 + n], t[:, :n], AF.Identity, bias=P['Bv'][:, idx, c:c + 1],
                          scale=P['A'][:, idx, c:c + 1], r=[tk, 'A', 'Bv'], w=[('xa', c, ti)])
                    if want_h:
                        b.ts('pool', self.hb[:, c, t0:t0 + n], t[:, :n], P['H1s'][:, c, wh:wh + 1], P['H1b'][:, c, wh:wh + 1],
                             ALU.mult, ALU.add, r=[tk, 'H1s', 'H1b'], w=[('h', c, ti)])

    def ffn(self, li):
        b = self.b
        w_in = self.din("w_ffn_in%d" % li, [D, 2 * DFF])
        w_out = self.din("w_ffn_out%d" % li, [DFF, D])
        with b.scope() as es:
            wis = [b.sb(es, "f_wis%d" % i, [128, KC, 512], F32) for i in range(2)]
            wos = [b.sb(es, "f_wos%d" % i, [128, 2, D], F32) for i in range(2)]
            wib = [b.sb(es, "f_wib%d" % i, [128, KC, 512], BF16) for i in range(2)]
            wob = [b.sb(es, "f_wob%d" % i, [128, 2, D], BF16) for i in range(2)]
            actb = b.sb(es, "f_act", [128, 2, T], BF16)
            sg = [b.sb(es, "f_sg%d" % i, [128, 512], F32) for i in range(2)]
            nsg = 0
            for fb in range(11):
                p = fb % 2
                f0 = fb * 256
                b.dma(wis[p][:, :, 0:256], w_in[:, f0:f0 + 256].rearrange("(k p) n -> p k n", p=128), w=[('wis', p, 0)])
                b.dma(wis[p][:, :, 256:512], w_in[:, DFF + f0:DFF + f0 + 256].rearrange("(k p) n -> p k n", p=128), w=[('wis', p, 1)])
                b.dma(wos[p][:], w_out[f0:f0 + 256, :].rearrange("(k p) n -> p k n", p=128), w=[('wos', p)])
                b.cp('pool', wib[p][:], wis[p][:], r=[('wis', p, 0), ('wis', p, 1)], w=[('wib', p)])
                b.cp('pool', wob[p][:], wos[p][:], r=[('wos', p)], w=[('wob', p)])
                for ti, (t0, n) in enumerate(TILES):
                    for j in range(2):
                        pg, pgk = b.ps()
                        pu, puk = b.ps()
                        for kc in range(KC):
                            b.mm(pg[:, :n], wib[p][:, kc, j * 128:(j + 1) * 128], self.hb[:, kc, t0:t0 + n],
                                 start=(kc == 0), stop=(kc == KC - 1), r=[('wib', p), ('h', kc, ti)], w=[pgk])
                        for kc in range(KC):
                            b.mm(pu[:, :n], wib[p][:, kc, 256 + j * 128:256 + (j + 1) * 128], self.hb[:, kc, t0:t0 + n],
                                 start=(kc == 0), stop=(kc == KC - 1), r=[('wib', p), ('h', kc, ti)], w=[puk])
                        s_ = sg[nsg % 2]
                        sk = ('fsg', nsg % 2)
                        nsg += 1
                        b.act(s_[:, :n], pg[:, :n], AF.Silu, r=[pgk], w=[sk])
                        b.tt('dve', actb[:, j, t0:t0 + n], pu[:, :n], s_[:, :n], ALU.mult, r=[puk, sk], w=[('fact', j, ti)])
                for ti, (t0, n) in enumerate(TILES):
                    wh = which_of(ti)
                    for oc in range(KC):
                        po, pok = b.ps()
                        for j in range(2):
                            b.mm(po[:, :n], wob[p][:, j, oc * 128:(oc + 1) * 128], actb[:, j, t0:t0 + n],
                                 start=(j == 0), stop=(j == 1), r=[('wob', p), ('fact', j, ti)], w=[pok])
                        b.stt('dve', self.xa[:, oc, t0:t0 + n], po[:, :n], self.gaf[:, oc, wh:wh + 1], self.xa[:, oc, t0:t0 + n],
                              ALU.mult, ALU.add, r=[pok, 'MOD', ('xa', oc, ti)], w=[('xa', oc, ti)])

    def out_proj(self, wb, wkey, nk, yfn, ykeys):
        b = self.b
        for ti, (t0, n) in enumerate(TILES):
            wh = which_of(ti)
            for oc in range(KC):
                po, pok = b.ps()
                for k in range(nk):
                    b.mm(po[:, :n], wb[:, k, oc * 128:(oc + 1) * 128], yfn(k, t0, n), start=(k == 0), stop=(k == nk - 1),
                         r=[wkey] + ykeys(k, ti), w=[pok])
                b.stt('dve', self.xa[:, oc, t0:t0 + n], po[:, :n], self.ga[:, oc, wh:wh + 1], self.xa[:, oc, t0:t0 + n],
                      ALU.mult, ALU.add, r=[pok, 'MOD', ('xa', oc, ti)], w=[('xa', oc, ti)])

    def attn(self, li):
        b = self.b
        j = li // 2
        w_qkv = self.din("attn_w_qkv%d" % j, [D, 1536])
        w_o = self.din("attn_w_out%d" % j, [D, D])
        gains = self.din("attn_gain%d" % j, [128, 2])
        ropeD = self.din("rope", [128, 4096])
        with b.scope() as es:
            QR = b.sb(es, "a_QR", [128, 8, T], BF16)
            KR = b.sb(es, "a_KR", [128, 2, T], BF16)
            VT = b.sb(es, "a_VT", [128, NCH, 256], BF16)
            gn = b.sb(es, "a_gn", [128, 2], F32)
            b.dma(gn[:], gains, w=['gn'])
            b.ts('dve', gn[:], gn[:], float(np.sqrt(128.0)), None, ALU.mult, r=['gn'], w=['gn'])
            with b.scope() as es2:
                rope = b.sb(es2, "a_rope", [128, 4096], F32)
                b.dma(rope[:], ropeD, w=['rope'])
                cosT = rope[:, 0:2048]
                sinT = rope[:, 2048:4096]
                wst_ = b.sb(es2, "a_wst", [128, KC, 128], F32)
                wst = [wst_, wst_]
                wbf = [b.sb(es2, "a_wbf%d" % i, [128, KC, 128], BF16) for i in range(2)]
                sqb = b.sb(es2, "a_sq", [128, 512], BF16)
                rs = b.sb(es2, "a_rs", [128, 512], F32)
                qn = b.sb(es2, "a_qn", [128, 512], F32)
                t1 = b.sb(es2, "a_t1", [128, 512], F32)
                t2 = b.sb(es2, "a_t2", [128, 512], F32)
                for wbk in range(12):
                    p = wbk % 2
                    b.dma(wst[p][:], w_qkv[:, wbk * 128:(wbk + 1) * 128].rearrange("(k p) n -> p k n", p=128), w=[('awst', 0)])
                    b.cp('pool', wbf[p][:], wst[p][:], r=[('awst', 0)], w=[('awbf', p)])
                    if wbk >= 10:
                        kvh = wbk - 10
                        for c in range(NCH):
                            pv, pvk = b.ps()
                            for kc in range(KC):
                                b.mm(pv[:, 0:128], self.hb[:, kc, c * 128:(c + 1) * 128], wbf[p][:, kc, :],
                                     start=(kc == 0), stop=(kc == KC - 1), r=[('awbf', p)] + h_keys(kc, ALLT), w=[pvk])
                            b.cp('act', VT[:, c, kvh * 128:(kvh + 1) * 128], pv[:, 0:128], r=[pvk], w=['VT'])
                        continue
                    isq = wbk < 8
                    hidx = wbk if isq else wbk - 8
                    dst = QR if isq else KR
                    gcol = gn[:, 0:1] if isq else gn[:, 1:2]
                    for ti, (t0, n) in enumerate(TILES):
                        pp, ppk = b.ps()
                        for kc in range(KC):
                            b.mm(pp[:, :n], wbf[p][:, kc, :], self.hb[:, kc, t0:t0 + n],
                                 start=(kc == 0), stop=(kc == KC - 1), r=[('awbf', p), ('h', kc, ti)], w=[ppk])
                        b.act(sqb[:, :n], pp[:, :n], AF.Square, r=[ppk], w=['asq'])
                        p2, p2k = b.ps()
                        b.mm(p2[:, :n], self.onesb[:], sqb[:, :n], r=['asq', 'onesb'], w=[p2k])
                        b.act(rs[:, :n], p2[:, :n], AF.Ln, bias=self.epsv[:, 1:2], r=[p2k, 'cfa'], w=['ars'])
                        b.act(rs[:, :n], rs[:, :n], AF.Exp, scale=-0.5, r=['ars'], w=['ars'])
                        if ti == 0:
                            b.stt('dve', dst[:, hidx, t0:t0 + n], pp[:, :n], gcol, rs[:, :n], ALU.mult, ALU.mult,
                                  r=[ppk, 'gn', 'ars'], w=[('aqk', isq, hidx, ti)])
                        else:
                            b.stt('dve', qn[:, :n], pp[:, :n], gcol, rs[:, :n], ALU.mult, ALU.mult,
                                  r=[ppk, 'gn', 'ars'], w=['aqn'])
                            p3, p3k = b.ps()
                            b.mm(p3[:, :n], self.rot, qn[:, :n], r=['aqn', 'cfa'], w=[p3k])
                            l0 = t0 - 256
                            b.tt('dve', t1[:, :n], qn[:, :n], cosT[:, l0:l0 + n], ALU.mult, r=['aqn', 'rope'], w=['at1'])
                            b.tt('dve', t2[:, :n], p3[:, :n], sinT[:, l0:l0 + n], ALU.mult, r=[p3k, 'rope'], w=['at2'])
                            b.tt('pool', dst[:, hidx, t0:t0 + n], t1[:, :n], t2[:, :n], ALU.add, r=['at1', 'at2'],
                                 w=[('aqk', isq, hidx, ti)])
            with b.scope() as es3:
                pts = [b.sb(es3, "a_pt%d" % i, [128, 512], BF16) for i in range(3)]
                rden = b.sb(es3, "a_rden", [128, 512], F32)
                wos_ = b.sb(es3, "a_wos", [128, 2, D], F32)
                wos = [wos_, wos_]
                wob = b.sb(es3, "a_wob", [128, KC, D], BF16)
                for i in range(4):
                    b.dma(wos[i % 2][:], w_o[i * 256:(i + 1) * 256, :].rearrange("(k p) n -> p k n", p=128), w=[('aos', 0)])
                    b.cp('pool', wob[:, 2 * i:2 * i + 2, :], wos[i % 2][:], r=[('aos', 0)], w=['awob'])
                npt = 0
                scale = float(128.0 ** -0.5)
                for hq in range(8):
                    kv = hq // 4
                    for ti, (t0, n) in enumerate(TILES):
                        kts = [0, 1] if ti == 0 else list(range(NCH))
                        pden, pdk = b.ps(hold=True)
                        po, pok = b.ps(hold=True)
                        prev = None

                        def flush(pv_):
                            pt_, ptk_, ii_, kt_ = pv_
                            b.mm(pden[:, :n], self.onesb[:], pt_[:, :n], start=(ii_ == 0), stop=(ii_ == len(kts) - 1),
                                 r=[ptk_, 'onesb'], w=[pdk])
                            b.mm(po[:, :n], VT[:, kt_, kv * 128:(kv + 1) * 128], pt_[:, :n], start=(ii_ == 0), stop=(ii_ == len(kts) - 1),
                                 r=[ptk_, 'VT'], w=[pok])
                        for ii, kt in enumerate(kts):
                            psc, psk = b.ps()
                            b.mm(psc[:, :n], KR[:, kv, kt * 128:(kt + 1) * 128], QR[:, hq, t0:t0 + n],
                                 r=[('aqk', False, kv, tt_) for tt_ in ALLT] + [('aqk', True, hq, ti)], w=[psk])
                            pt = pts[npt % 3]
                            ptk = ('apt', npt % 3)
                            npt += 1
                            b.act(pt[:, :n], psc[:, :n], AF.Exp, scale=scale, r=[psk], w=[ptk])
                            if prev is not None:
                                flush(prev)
                            prev = (pt, ptk, ii, kt)
                        flush(prev)
                        b.op('dve', lambda e, o=rden[:, :n], i=pden[:, :n]: e.reciprocal(out=o, in_=i), r=[pdk], w=['arden'])
                        b.tt('dve', self.hb[:, hq, t0:t0 + n], po[:, :n], rden[:, :n], ALU.mult, r=[pok, 'arden'], w=[('h', hq, ti)])
                        b.release(pdk)
                        b.release(pok)
                self.out_proj(wob, 'awob', KC, lambda k, t0, n: self.hb[:, k, t0:t0 + n], lambda k, ti: [('h', k, ti)])

    def gdn(self, li):
        b = self.b
        j = li // 2
        w_in = self.din("gdn_w_in%d" % j, [D, 6208])
        w_out = self.din("gdn_w_out%d" % j, [2048, D])
        convD = self.din("gdn_convT%d" % j, [128, 32, 5])
        gparD = self.din("gdn_gpar%d" % j, [128, 64])
        normgD = self.din("gdn_normg%d" % j, [128, 128])
        lvlD = self.din("lvlmask", [128, 7 * 4 * 128], U8)
        ident = self.ident
        one_col = self.epsv[:, 2:3]
        eps_col = self.epsv[:, 0:1]

        def bc(ap2, n=128):
            return ap2.unsqueeze(2).to_broadcast([128, ap2.shape[1], n])

        with b.scope() as es:
            G = {}
            for nm in ['NBETA', 'BETA', 'GCUM', 'EG', 'KD', 'EGL']:
                G[nm] = b.sb(es, "g_" + nm, [128, NCH, 32], F32)
            convw = b.sb(es, "g_convw", [128, 32, 5], F32)
            normg = b.sb(es, "g_normg", [128, 128], F32)
            lvl = b.sb(es, "g_lvl", [128, 7, 4, 128], U8)
            b.dma(convw[:], convD, w=['convw'])
            b.dma(normg[:], normgD, w=['normg'])
            b.dma(lvl[:].rearrange("p a b c -> p (a b c)"), lvlD, w=['lvl'])
            with b.scope() as ges:
                gpar = b.sb(ges, "g_gpar", [128, 64], F32)
                wgs = b.sb(ges, "g_wgs", [128, KC, 64], F32)
                wgb = b.sb(ges, "g_wgb", [128, KC, 64], BF16)
                GRAW = b.sb(ges, "g_graw", [128, NCH, 64], F32)
                T1 = b.sb(ges, "g_t1", [128, NCH, 32], F32)
                T2 = b.sb(ges, "g_t2", [128, NCH, 32], F32)
                GG = b.sb(ges, "g_g", [128, NCH, 32], F32)
                GL = b.sb(ges, "g_gl", [128, NCH, 32], F32)
                NA = b.sb(ges, "g_na", [128, 32], F32)
                b.dma(gpar[:], gparD, w=['gpar'])
                b.dma(wgs[:], w_in[:, 6144:6208].rearrange("(k p) n -> p k n", p=128), w=['wgs'])
                b.cp('pool', wgb[:], wgs[:], r=['wgs'], w=['wgb'])
                for c0 in range(0, NCH, 8):
                    nc_ = min(8, NCH - c0)
                    pg, pgk = b.ps()
                    for cc in range(nc_):
                        c = c0 + cc
                        for kc in range(KC):
                            b.mm(pg[:, cc * 64:(cc + 1) * 64], self.hb[:, kc, c * 128:(c + 1) * 128], wgb[:, kc, :],
                                 start=(kc == 0), stop=(kc == KC - 1), r=['wgb'] + h_keys(kc, ALLT), w=[pgk])
                    b.cp('act', GRAW[:, c0:c0 + nc_, :], pg[:, 0:nc_ * 64].rearrange("p (a b) -> p a b", b=64), r=[pgk], w=['graw'])
                braw = GRAW[:, :, 0:32]
                araw = GRAW[:, :, 32:64]
                b.act(T1[:], braw, AF.Exp, scale=-1.0, r=['graw'], w=['gt1'])
                b.act(T1[:], T1[:], AF.Ln, bias=one_col, r=['gt1', 'cfa'], w=['gt1'])
                b.act(G['BETA'][:], T1[:], AF.Exp, scale=-1.0, r=['gt1'], w=['BETA'])
                b.ts('pool', G['NBETA'][:], G['BETA'][:], -1.0, None, ALU.mult, r=['BETA'], w=['NBETA'])
                b.tt('dve', T2[:], araw, gpar[:, 32:64].unsqueeze(1).to_broadcast([128, NCH, 32]), ALU.add, r=['graw', 'gpar'], w=['gt2'])
                b.act(T2[:], T2[:], AF.Exp, r=['gt2'], w=['gt2'])
                b.act(T2[:], T2[:], AF.Ln, bias=one_col, r=['gt2', 'cfa'], w=['gt2'])
                b.act(NA[:], gpar[:, 0:32], AF.Exp, r=['gpar'], w=['gna'])
                b.ts('pool', NA[:], NA[:], -1.0, None, ALU.mult, r=['gna'], w=['gna'])
                b.tt('dve', GG[:], T2[:], NA[:].unsqueeze(1).to_broadcast([128, NCH, 32]), ALU.mult, r=['gt2', 'gna'], w=['gg'])
                pc, pck = b.ps()
                b.mm(pc[:, 0:288], self.trif, GG[:, :, 0:16], r=['gg', 'cfa'], w=[pck])
                pc2, pc2k = b.ps()
                b.mm(pc2[:, 0:288], self.trib, GG[:, :, 16:32], r=['gg', 'cfa'], w=[pc2k])
                b.cp('act', G['GCUM'][:, :, 0:16], pc[:, 0:288].rearrange("p (a b) -> p a b", b=16), r=[pck], w=['GCUM'])
                b.cp('act', G['GCUM'][:, :, 16:32], pc2[:, 0:288].rearrange("p (a b) -> p a b", b=16), r=[pc2k], w=['GCUM'])
                pl, plk = b.ps()
                b.mm(pl[:, 0:288], self.ones, GG[:, 0:9, :], r=['gg', 'cfa'], w=[plk])
                pl2, pl2k = b.ps()
                b.mm(pl2[:, 0:288], self.ones, GG[:, 9:18, :], r=['gg', 'cfa'], w=[pl2k])
                b.cp('act', GL[:, 0:9, :], pl[:, 0:288].rearrange("p (a b) -> p a b", b=32), r=[plk], w=['ggl'])
                b.cp('act', GL[:, 9:18, :], pl2[:, 0:288].rearrange("p (a b) -> p a b", b=32), r=[pl2k], w=['ggl'])
                b.act(G['EG'][:], G['GCUM'][:], AF.Exp, r=['GCUM'], w=['EG'])
                b.tt('dve', T1[:], GL[:], G['GCUM'][:], ALU.subtract, r=['ggl', 'GCUM', 'gt1'], w=['gt1'])
                b.act(G['KD'][:], T1[:], AF.Exp, r=['gt1'], w=['KD'])
                b.act(G['EGL'][:], GL[:], AF.Exp, r=['ggl'], w=['EGL'])
            for g in range(8):
                self.gdn_group(li, g, es, G, convw, normg, lvl, w_in, w_out, bc)

    def gdn_group(self, li, g, es_unused, G, convw, normg, lvl, w_in, w_out, bc):
        b = self.b
        ident = self.ident
        eps_col = self.epsv[:, 0:1]
        ps_bf = lambda ps: ps[:].bitcast(BF16)
        gk = lambda nm: nm + "_%d" % g

        def gcols(d):
            return slice(d * 16 + 2 * g, d * 16 + 2 * g + 2)

        with b.scope() as ges:
            O = b.sb(ges, "g_O", [128, NCH, 2, 128], BF16)
            with b.scope() as aes:
                KQ = b.sb(aes, "g_KQ", [128, NCH, 2, 128], BF16)
                KTM = b.sb(aes, "g_KTM", [128, NCH, 128], BF16)
                VTM = b.sb(aes, "g_VTM", [128, NCH, 2, 128], BF16)
                with b.scope() as pes:
                    CB = b.sb(pes, "g_CB", [128, 2310], F32)
                    ACC = b.sb(pes, "g_ACC", [128, T], F32)
                    TB = b.sb(pes, "g_TB", [128, T], BF16)
                    wst = [b.sb(pes, "g_wst%d" % i, [128, KC, 128], F32) for i in range(2)]
                    wbf = [b.sb(pes, "g_wbf%d" % i, [128, KC, 128], BF16) for i in range(2)]
                    rs = b.sb(pes, "g_rs", [128, 512], F32)
                    b.op('pool', lambda e: e.memset(CB[:, 0:2], 0.0), w=['CBp0'])
                    b.op('pool', lambda e: e.memset(CB[:, 258:260], 0.0), w=['CBp1'])
                    b.op('pool', lambda e: e.memset(CB[:, 2308:2310], 0.0), w=['CBp2'])
                    fcs = [('q', g * 128, g), ('k', 1024 + g * 128, 8 + g),
                           ('v0', 2048 + (2 * g) * 128, 16 + 2 * g), ('v1', 2048 + (2 * g + 1) * 128, 16 + 2 * g + 1)]
                    def emit_proj(fi):
                        kind, col0, cq = fcs[fi]
                        held = []
                        p = fi % 2
                        b.dma(wst[p][:], w_in[:, col0:col0 + 128].rearrange("(k p) n -> p k n", p=128), w=[('gwst', p)])
                        b.cp('pool', wbf[p][:], wst[p][:], r=[('gwst', p)], w=[('gwbf', p)])
                        for ti, (t0, n) in enumerate(TILES):
                            pp, ppk = b.ps(hold=True)
                            for kc in range(KC):
                                b.mm(pp[:, :n], wbf[p][:, kc, :], self.hb[:, kc, t0:t0 + n], start=(kc == 0), stop=(kc == KC - 1),
                                     r=[('gwbf', p), ('h', kc, ti)], w=[ppk])
                            o0 = 2 if ti == 0 else t0 + 4
                            b.cp('act', CB[:, o0:o0 + n], pp[:, :n], r=[ppk], w=[('CB', ti)])
                            held.append(ppk)
                        return held

                    def emit_conv(fi):
                        kind, col0, cq = fcs[fi]
                        acck = [('ACC', 0), ('ACC', 256), ('ACC', 1280)]
                        for (d0, L, s0, tis) in [(0, 256, 2, [0]), (256, 1024, 260, [1, 2]), (1280, 1024, 1284, [3, 4])]:
                            ak = ('ACC', d0)
                            rk = [('CB', tj) for tj in {0: [0], 256: [1, 2, 3], 1280: [2, 3, 4]}[d0]] + ['CBp0', 'CBp1', 'CBp2', 'convw']
                            b.ts('dve', ACC[:, d0:d0 + L], CB[:, s0 - 2:s0 - 2 + L], convw[:, cq, 0:1], None, ALU.mult, r=rk, w=[ak])
                            for jj in range(1, 5):
                                b.stt('dve', ACC[:, d0:d0 + L], CB[:, s0 - 2 + jj:s0 - 2 + jj + L], convw[:, cq, jj:jj + 1],
                                      ACC[:, d0:d0 + L], ALU.mult, ALU.add, r=rk + [ak], w=[ak])
                            if kind in ('q', 'k'):
                                b.act(ACC[:, d0:d0 + L], ACC[:, d0:d0 + L], AF.Silu, r=[ak], w=[ak])
                            else:
                                b.act(TB[:, d0:d0 + L], ACC[:, d0:d0 + L], AF.Silu, r=[ak], w=['TB'])

                    def emit_tail(fi):
                        kind, col0, cq = fcs[fi]
                        acck = [('ACC', 0), ('ACC', 256), ('ACC', 1280)]
                        if kind in ('q', 'k'):
                            b.act(TB[:], ACC[:], AF.Square, r=acck, w=['TB'])
                            kq = 0 if kind == 'k' else 1
                            sc_ = 1.0 if kind == 'k' else float(128.0 ** -0.5)
                            for ti, (t0, n) in enumerate(TILES):
                                p2, p2k = b.ps()
                                b.mm(p2[:, :n], self.onesb[:], TB[:, t0:t0 + n], r=['TB', 'onesb'], w=[p2k])
                                b.act(rs[:, :n], p2[:, :n], AF.Ln, bias=eps_col, r=[p2k, 'cfa'], w=['grs'])
                                b.act(rs[:, :n], rs[:, :n], AF.Exp, scale=-0.5, r=['grs'], w=['grs'])
                                c0 = t0 // 128
                                nc_ = n // 128
                                b.stt('dve', KQ[:, c0:c0 + nc_, kq, :], ACC[:, t0:t0 + n].rearrange("p (a b) -> p a b", b=128), sc_,
                                      rs[:, :n].rearrange("p (a b) -> p a b", b=128), ALU.mult, ALU.mult,
                                      r=acck + ['grs'], w=[gk('KQ')])
                            if kind == 'k':
                                for c0 in range(0, NCH, 4):
                                    nc_ = min(4, NCH - c0)
                                    pt, ptk = b.ps()
                                    ptb = ps_bf(pt)
                                    for cc in range(nc_):
                                        b.tr(ptb[:, cc * 128:(cc + 1) * 128], KQ[:, c0 + cc, 0, :], self.identb[:], r=[gk('KQ'), 'identb'], w=[ptk])
                                    b.cp('act', KTM[:, c0:c0 + nc_, :], ptb[:, 0:nc_ * 128].rearrange("p (a b) -> p a b", b=128), r=[ptk], w=[gk('KTM')])
                        else:
                            a_ = 0 if kind == 'v0' else 1
                            for c0 in range(0, NCH, 4):
                                nc_ = min(4, NCH - c0)
                                pt, ptk = b.ps()
                                ptb = ps_bf(pt)
                                for cc in range(nc_):
                                    b.tr(ptb[:, cc * 128:(cc + 1) * 128], TB[:, (c0 + cc) * 128:(c0 + cc + 1) * 128], self.identb[:], r=['TB', 'identb'], w=[ptk])
                                b.cp('act', VTM[:, c0:c0 + nc_, a_, :], ptb[:, 0:nc_ * 128].rearrange("p (a b) -> p a b", b=128), r=[ptk], w=[gk('VTM')])

                    for k_ in emit_proj(0):
                        b.release(k_)
                    for fi in range(4):
                        emit_conv(fi)
                        hk = emit_proj(fi + 1) if fi + 1 < 4 else []
                        emit_tail(fi)
                        for k_ in hk:
                            b.release(k_)
                with b.scope() as ses:
                    NL = GDN_LANES

                    def t4(nm, dt):
                        return b.sb(ses, "g_" + nm, [128, 4, 128], dt)
                    DG = b.sb(ses, "g_DG", [128, 4, 128], F32)
                    BGEg = b.sb(ses, "g_BGEg", [128, NCH, 4], F32)
                    C1 = t4("C1", F32)
                    TO = C1
                    DT = t4("DT", BF16)
                    Dm = t4("Dm", BF16)
                    KKn = t4("KKn", BF16)
                    QKs = b.sb(ses, "g_QKs", [128, 2, 128], BF16)
                    VN = t4("VN", BF16)
                    S = t4("S", F32)
                    Sbf = t4("Sbf", BF16)
                    VB = t4("VB", BF16)
                    KBG = t4("KBG", BF16)
                    KDEC = t4("KDEC", BF16)
                    NWT = t4("NWT", BF16)
                    lanes = []
                    for k in range(NL):
                        lanes.append({nm: t4("%s_l%d" % (nm, k), BF16) for nm in ['Ap', 'QKD', 'Y', 'W', 'X']})
                        lanes[-1]['k'] = k
                    b.op('pool', lambda e: e.memset(S[:], 0.0), w=['S'])
                    b.op('pool', lambda e: e.memset(Sbf[:], 0.0), w=['Sbf'])
                    for d in range(2):
                        b.tt('pool', BGEg[:, :, 2 * d:2 * d + 2], G['BETA'][:, :, gcols(d)], G['EG'][:, :, gcols(d)], ALU.mult, r=['BETA', 'EG'], w=['BGEg'])
                    owritten = set()
                    f2 = lambda ap: ap.rearrange("p a b -> p (a b)")
                    idb = ident.unsqueeze(1).to_broadcast([128, 2, 128])
                    nidb = self.nident.unsqueeze(1).to_broadcast([128, 2, 128])

                    def step_gen(s, Ln):
                        lk = lambda nm: (nm, Ln['k'])
                        Ap, QKD, Y, W, X = (Ln[nm] for nm in ['Ap', 'QKD', 'Y', 'W', 'X'])
                        cds = [FWD[s], BWD[s]]

                        def scale_ops(which):
                            for u in range(4):
                                d, a_ = u // 2, u % 2
                                cd = cds[d]
                                col = d * 16 + 2 * g + a_
                                if which == 'VB':
                                    b.act(VB[:, u, :], VTM[:, cd, a_, :], AF.Copy, scale=G['BETA'][:, cd, col:col + 1], r=[gk('VTM'), 'BETA'], w=['VB'])
                                elif which == 'KBG':
                                    b.act(KBG[:, u, :], KTM[:, cd, :], AF.Copy, scale=BGEg[:, cd, u:u + 1], r=[gk('KTM'), 'BGEg'], w=['KBG'])
                                else:
                                    b.act(KDEC[:, u, :], KTM[:, cd, :], AF.Copy, scale=G['KD'][:, cd, col:col + 1], r=[gk('KTM'), 'KD'], w=['KDEC'])
                        pkq, pkqk = b.ps(hold=True)
                        for d in range(2):
                            cd = cds[d]
                            b.mm(pkq[:, d * 256:(d + 1) * 256], KQ[:, cd, 0, :], KQ[:, cd, :, :].rearrange("p a b -> p (a b)"),
                                 r=[gk('KQ')], w=[pkqk])
                        for d in range(2):
                            cd = cds[d]
                            b.tt('pool', DG[:, 2 * d:2 * d + 2, :], idb, bc(G['GCUM'][:, cd, gcols(d)]), ALU.mult, r=['GCUM', 'cfa'], w=['DG'])
                        pe_, pek = b.ps(hold=True)
                        for u in range(4):
                            b.mm(pe_[:, u * 128:(u + 1) * 128], self.ones, DG[:, u, :], start=True, stop=False, r=['DG', 'cfa'], w=[pek])
                            b.mm(pe_[:, u * 128:(u + 1) * 128], DG[:, u, :], self.nones, start=False, stop=True, r=['DG', 'cfa'], w=[pek])
                        b.ts('dve', f2(C1[:]), pe_[:], 0.0, None, ALU.min, r=[pek], w=['C1'])
                        b.act(f2(DT[:]), f2(C1[:]), AF.Exp, r=['C1'], w=['DT'])
                        b.ts('dve', f2(C1[:]), pe_[:], 0.0, None, ALU.max, r=[pek], w=['C1'])
                        b.release(pek)
                        b.act(f2(Dm[:]), f2(C1[:]), AF.Exp, scale=-1.0, r=['C1'], w=['Dm'])
                        for d in range(2):
                            cd = cds[d]
                            kk = pkq[:, d * 256:d * 256 + 128].unsqueeze(1).to_broadcast([128, 2, 128])
                            b.tt('dve', KKn[:, 2 * d:2 * d + 2, :], kk, bc(G['NBETA'][:, cd, gcols(d)]), ALU.mult, r=[pkqk, 'NBETA'], w=['KKn'])
                        qkv_ = pkq[:].rearrange("p (d x) -> p d x", d=2)[:, :, 128:256]
                        b.tt('dve', QKs[:], qkv_, self.maskq.rearrange("p (d x) -> p d x", d=2), ALU.mult, r=[pkqk, 'cfa'], w=['QKs'])
                        b.release(pkqk)
                        yield
                        b.tt('pool', f2(Ap[:]), f2(KKn[:]), f2(Dm[:]), ALU.mult, r=['KKn', 'Dm'], w=[lk('Ap')])
                        b.tt('pool', QKD[:].rearrange("p (d a) x -> p d a x", d=2), QKs[:].unsqueeze(2).to_broadcast([128, 2, 2, 128]),
                             DT[:].rearrange("p (d a) x -> p d a x", d=2), ALU.mult, r=['QKs', 'DT'], w=[lk('QKD')])
                        ptt, pttk = b.ps(hold=True)
                        pttb = ps_bf(ptt)
                        for u in range(4):
                            b.tr(pttb[:, u * 128:(u + 1) * 128], Ap[:, u, :], self.identb[:], r=[lk('Ap'), 'identb'], w=[pttk])
                        b.cp('pool', Y[:], self.identb[:].unsqueeze(1).to_broadcast([128, 4, 128]), r=['identb'], w=[lk('Y')])
                        b.op('dve', lambda e, o=f2(Y[:]), m=f2(lvl[:, 0, :, :]), dd=pttb[:, 0:512]: e.copy_predicated(o, m, dd),
                             r=[pttk, 'lvl', lk('Y')], w=[lk('Y')])
                        b.release(pttk)
                        yield
                        for l in range(1, 7):
                            pw, pwk = b.ps(hold=True)
                            for u in range(4):
                                b.mm(pw[:, u * 128:(u + 1) * 128], Ap[:, u, :], Y[:, u, :], r=[lk('Ap'), lk('Y')], w=[pwk])
                            px, pxk = b.ps(hold=True)
                            pxb = ps_bf(px)
                            for u in range(4):
                                b.tr(pxb[:, u * 128:(u + 1) * 128], Y[:, u, :], self.identb[:], r=[lk('Y'), 'identb'], w=[pxk])
                            b.cp('act', f2(W[:]), pw[:], r=[pwk], w=[lk('W')])
                            b.cp('dve', f2(X[:]), pxb[:, 0:512], r=[pxk], w=[lk('X')])
                            b.release(pwk)
                            b.release(pxk)
                            yield
                            pz, pzk = b.ps(hold=True)
                            for u in range(4):
                                b.mm(pz[:, u * 128:(u + 1) * 128], X[:, u, :], W[:, u, :], r=[lk('X'), lk('W')], w=[pzk])
                            b.op('dve', lambda e, o=f2(Y[:]), m=f2(lvl[:, l, :, :]), dd=pz[:]: e.copy_predicated(o, m, dd),
                                 r=[pzk, 'lvl', lk('Y')], w=[lk('Y')])
                            b.release(pzk)
                            if l == 6:
                                scale_ops('KBG')
                            yield
                        pwt, pwtk = b.ps(hold=True)
                        for u in range(4):
                            b.mm(pwt[:, u * 128:(u + 1) * 128], KBG[:, u, :], Y[:, u, :], r=['KBG', lk('Y')], w=[pwtk])
                        b.act(f2(NWT[:]), pwt[:], AF.Copy, scale=-1.0, r=[pwtk], w=['NWT'])
                        b.release(pwtk)
                        scale_ops('VB')
                        yield
                        pvn, pvnk = b.ps(hold=True)
                        for u in range(4):
                            b.mm(pvn[:, u * 128:(u + 1) * 128], Y[:, u, :], VB[:, u, :], start=True, stop=False, r=[lk('Y'), 'VB'], w=[pvnk])
                            b.mm(pvn[:, u * 128:(u + 1) * 128], NWT[:, u, :], Sbf[:, u, :], start=False, stop=True, r=['NWT', 'Sbf'], w=[pvnk])
                        b.cp('act', f2(VN[:]), pvn[:], r=[pvnk], w=['VN'])
                        b.release(pvnk)
                        scale_ops('KDEC')
                        yield
                        pds, pdsk = b.ps(hold=True)
                        po1, po1k = b.ps(hold=True)
                        po2, po2k = b.ps(hold=True)
                        for u in range(4):
                            b.mm(pds[:, u * 128:(u + 1) * 128], KDEC[:, u, :], VN[:, u, :], r=['KDEC', 'VN'], w=[pdsk])
                        for u in range(4):
                            cd = cds[u // 2]
                            b.mm(po1[:, u * 128:(u + 1) * 128], KQ[:, cd, 1, :], Sbf[:, u, :], r=[gk('KQ'), 'Sbf'], w=[po1k])
                        for u in range(4):
                            b.mm(po2[:, u * 128:(u + 1) * 128], QKD[:, u, :], VN[:, u, :], r=[lk('QKD'), 'VN'], w=[po2k])
                        for d in range(2):
                            cd = cds[d]
                            b.tt('pool', S[:, 2 * d:2 * d + 2, :], S[:, 2 * d:2 * d + 2, :], bc(G['EGL'][:, cd, gcols(d)]), ALU.mult,
                                 r=['S', 'EGL'], w=['S'])
                        b.tt('dve', f2(S[:]), f2(S[:]), pds[:], ALU.add, r=['S', pdsk], w=['S'])
                        b.release(pdsk)
                        b.cp('act', f2(Sbf[:]), f2(S[:]), r=['S'], w=['Sbf'])
                        for d in range(2):
                            cd = cds[d]
                            b.tt('dve', TO[:, 2 * d:2 * d + 2, :], po1[:, d * 256:(d + 1) * 256].rearrange("p (a x) -> p a x", a=2),
                                 bc(G['EG'][:, cd, gcols(d)]), ALU.mult, r=[po1k, 'EG'], w=['C1'])
                        b.tt('dve', f2(TO[:]), f2(TO[:]), po2[:], ALU.add, r=['C1', po2k], w=['C1'])
                        b.release(po1k)
                        b.release(po2k)
                        for d in range(2):
                            cd = cds[d]
                            if cd not in owritten:
                                owritten.add(cd)
                                b.cp('act', O[:, cd, :, :], TO[:, 2 * d:2 * d + 2, :], r=['C1'], w=[gk('O')])
                            else:
                                b.tt('pool', O[:, cd, :, :], O[:, cd, :, :], TO[:, 2 * d:2 * d + 2, :], ALU.add, r=['C1', gk('O')], w=[gk('O')])
                        yield

                    gens = []
                    next_s = 0
                    turn = 0
                    stagger = max(1, (17 + NL - 1) // NL)
                    while next_s < NCH or gens:
                        if next_s < NCH and turn % stagger == 0 and len(gens) < NL:
                            gens.append(step_gen(next_s, lanes[next_s % NL]))
                            next_s += 1
                        for g_ in list(gens):
                            try:
                                next(g_)
                            except StopIteration:
                                gens.remove(g_)
                        turn += 1
            with b.scope() as oes:
                ZS = b.sb(oes, "g_ZS", [128, NCH, 256], BF16)
                wzs = b.sb(oes, "g_wzs", [128, KC, 256], F32)
                wzb = b.sb(oes, "g_wzb", [128, KC, 256], BF16)
                wos = b.sb(oes, "g_wos", [128, 2, D], F32)
                wob = b.sb(oes, "g_wob", [128, 2, D], BF16)
                OSQ = b.sb(oes, "g_OSQ", [128, 6, 2, 128], F32)
                SS = b.sb(oes, "g_SS", [128, NCH, 2], F32)
                YT = b.sb(oes, "g_YT", [128, 6, 2, 128], F32)
                YTb = b.sb(oes, "g_YTb", [128, 6, 2, 128], BF16)
                YF = b.sb(oes, "g_YF", [128, 2, T], BF16)
                zc0 = 4096 + 2 * g * 128
                b.dma(wzs[:], w_in[:, zc0:zc0 + 256].rearrange("(k p) n -> p k n", p=128), w=['wzs'])
                b.cp('pool', wzb[:], wzs[:], r=['wzs'], w=['wzb'])
                b.dma(wos[:], w_out[2 * g * 128:(2 * g + 2) * 128, :].rearrange("(k p) n -> p k n", p=128), w=['gwos'])
                b.cp('pool', wob[:], wos[:], r=['gwos'], w=['gwob'])
                for c0 in range(0, NCH, 2):
                    pz_, pzk_ = b.ps()
                    for cc in range(2):
                        c = c0 + cc
                        for kc in range(KC):
                            b.mm(pz_[:, cc * 256:(cc + 1) * 256], self.hb[:, kc, c * 128:(c + 1) * 128], wzb[:, kc, :],
                                 start=(kc == 0), stop=(kc == KC - 1), r=['wzb'] + h_keys(kc, ALLT), w=[pzk_])
                    b.act(ZS[:, c0:c0 + 2, :], pz_[:].rearrange("p (a b) -> p a b", a=2), AF.Silu, r=[pzk_], w=['ZS'])
                for c0 in range(0, NCH, 6):
                    Oc = O[:, c0:c0 + 6, :, :]
                    b.tt('pool', OSQ[:], Oc, Oc, ALU.mult, r=[gk('O')], w=['OSQ'])
                    b.op('dve', lambda e, o=SS[:, c0:c0 + 6, :], i=OSQ[:]: e.tensor_reduce(out=o, in_=i, axis=AX.X, op=ALU.add), r=['OSQ'], w=['SS'])
                b.act(SS[:], SS[:], AF.Ln, bias=eps_col, scale=1.0 / 128.0, r=['SS', 'cfa'], w=['SS'])
                b.act(SS[:], SS[:], AF.Exp, scale=-0.5, r=['SS'], w=['SS'])
                for c0 in range(0, NCH, 6):
                    Oc = O[:, c0:c0 + 6, :, :]
                    b.tt('dve', YT[:], Oc, SS[:, c0:c0 + 6, :].unsqueeze(3).to_broadcast([128, 6, 2, 128]), ALU.mult, r=[gk('O'), 'SS'], w=['YT'])
                    b.tt('pool', YT[:].rearrange("p a b c -> p (a b) c"), YT[:].rearrange("p a b c -> p (a b) c"),
                         normg[:].unsqueeze(1).to_broadcast([128, 12, 128]), ALU.mult, r=['YT', 'normg'], w=['YT'])
                    b.tt('dve', YTb[:], YT[:], ZS[:, c0:c0 + 6, :].rearrange("p a (b c) -> p a b c", b=2), ALU.mult, r=['YT', 'ZS'], w=['YTb'])
                    for a_ in range(2):
                        for q4 in range(0, 6, 4):
                            nq = min(4, 6 - q4)
                            pt, ptk = b.ps()
                            ptb = ps_bf(pt)
                            for cc in range(nq):
                                b.tr(ptb[:, cc * 128:(cc + 1) * 128], YTb[:, q4 + cc, a_, :], self.identb[:], r=['YTb', 'identb'], w=[ptk])
                            t0_ = (c0 + q4) * 128
                            b.cp('act', YF[:, a_, t0_:t0_ + nq * 128], ptb[:, 0:nq * 128], r=[ptk], w=['YF'])
                self.out_proj(wob, 'gwob', 2, lambda k, t0, n: YF[:, k, t0:t0 + n], lambda k, ti: ['YF'])


def host_consts():
    cfa = np.zeros((128, NCFA), np.float32)
    idx = np.arange(128)
    cfa[:, 0:128] = np.eye(128, dtype=np.float32)
    cfa[:, 128:256] = (idx[:, None] <= idx[None, :]).astype(np.float32)
    cfa[:, 256:384] = (idx[:, None] >= idx[None, :]).astype(np.float32)
    cfa[:, 384:512] = 1.0
    cfa[:, 512:640] = (idx[None, :] >= idx[:, None]).astype(np.float32)
    cfa[:, 640:768] = (idx[None, :] <= idx[:, None]).astype(np.float32)
    rot = np.zeros((128, 128), np.float32)
    for m in range(128):
        half = (m % 64) // 32
        if half == 0:
            rot[m + 32, m] = -1.0
        else:
            rot[m - 32, m] = 1.0
    cfa[:, 768:896] = rot
    cfa[:, 1152] = EPS
    cfa[:, 1153] = 128.0 * EPS
    cfa[:, 1154] = 1.0
    cfa[:, 896:1024] = -np.eye(128, dtype=np.float32)
    cfa[:, 1024:1152] = -1.0
    return cfa


def rope_host():
    rows = 2048 // 64
    row = np.repeat(np.arange(rows), 64).astype(np.float32)
    col = np.tile(np.arange(64), rows).astype(np.float32)
    n_freq = 32
    freqs = (np.float32(10000.0) ** (-np.arange(n_freq, dtype=np.float32) / np.float32(n_freq))).astype(np.float32)
    ang_r = row[:, None] * freqs
    ang_c = col[:, None] * freqs
    ang = np.concatenate([ang_r, ang_r, ang_c, ang_c], axis=-1).astype(np.float32)
    out = np.zeros((128, 4096), np.float32)
    out[:, 0:2048] = np.cos(ang).T
    out[:, 2048:4096] = np.sin(ang).T
    return out


_CACHE = {}


def get_prog(layers, nseq):
    key = (tuple(layers), nseq)
    if key not in _CACHE:
        nc = bass.Bass("TRN2", target_bir_lowering=False)
        p = Prog(nc, list(layers), nseq)
        p.build()
        _CACHE[key] = (nc, p)
    return _CACHE[key]


def layer_inputs(inp, li):
    d = {}
    d["w_mod%d" % li] = np.ascontiguousarray(inp["w_mod"][li])
    d["bmodT%d" % li] = np.ascontiguousarray(inp["b_mod"][li].reshape(48, 128).T)
    ln = np.stack([inp["ln_g"][li, 0], inp["ln_b"][li, 0], inp["ln_g"][li, 1], inp["ln_b"][li, 1]], 0)
    d["lnT%d" % li] = np.ascontiguousarray(ln.reshape(4, 8, 128).transpose(2, 0, 1))
    d["w_ffn_in%d" % li] = np.ascontiguousarray(inp["w_ffn_in"][li])
    d["w_ffn_out%d" % li] = np.ascontiguousarray(inp["w_ffn_out"][li])
    j = li // 2
    if li % 2 == 1:
        d["attn_w_qkv%d" % j] = np.ascontiguousarray(inp["attn_w_qkv"][j])
        d["attn_w_out%d" % j] = np.ascontiguousarray(inp["attn_w_out"][j])
        d["attn_gain%d" % j] = np.ascontiguousarray(np.stack([inp["attn_q_norm"][j], inp["attn_k_norm"][j]], 1))
        d["rope"] = rope_host()
    else:
        d["gdn_w_in%d" % j] = np.ascontiguousarray(inp["gdn_w_in"][j])
        d["gdn_w_out%d" % j] = np.ascontiguousarray(inp["gdn_w_out"][j])
        d["gdn_convT%d" % j] = np.ascontiguousarray(inp["gdn_conv"][j].reshape(5, 32, 128).transpose(2, 1, 0))
        gp = np.concatenate([inp["gdn_a_log"][j].reshape(32), inp["gdn_dt_bias"][j].reshape(32)])
        d["gdn_gpar%d" % j] = np.ascontiguousarray(np.broadcast_to(gp[None, :], (128, 64)).astype(np.float32))
        d["gdn_normg%d" % j] = np.ascontiguousarray(np.broadcast_to(inp["gdn_norm_g"][j][None, :], (128, 128)).astype(np.float32))
        d["lvlmask"] = lvlmask_host()
    return d


def lvlmask_host():
    p = np.arange(128)[:, None]
    f = np.arange(128)[None, :]
    m = np.zeros((128, 7, 4, 128), np.uint8)
    for l in range(7):
        if l == 0:
            blk = (p >> 1) == (f >> 1)
        else:
            blk = ((p >> (l + 1)) == (f >> (l + 1))) & ((p >> l) != (f >> l))
        fw = (blk & (f > p)).astype(np.uint8)
        bw = (blk & (f < p)).astype(np.uint8)
        m[:, l, 0, :] = fw
        m[:, l, 1, :] = fw
        m[:, l, 2, :] = bw
        m[:, l, 3, :] = bw
    return np.ascontiguousarray(m.reshape(128, 7 * 4 * 128))


def seq_fm(inp, bidx):
    return np.ascontiguousarray(np.concatenate([inp["ctx"][bidx], inp["x"][bidx]], 0).T)


def cT_host(inp, bidxs):
    v = np.stack([inp["c_ctx"]] + [inp["c"][bi] for bi in bidxs], 1)
    return np.ascontiguousarray(v.reshape(8, 128, len(bidxs) + 1).transpose(1, 0, 2))


def run_layers(inp, layers, xs):
    nc, p = get_prog(layers, 1)
    outs = [None] * 16
    cfa = host_consts()
    for rnd in range(2):
        in_maps = []
        for core in range(8):
            bidx = rnd * 8 + core
            m = {"cfa": cfa, "xin0": xs[bidx], "cTall": cT_host(inp, [bidx])}
            for li in layers:
                m.update(layer_inputs(inp, li))
            in_maps.append({k: v for k, v in m.items() if k in p.dram})
        res = run_bass_kernel_spmd(nc, in_maps, core_ids=list(range(8)))
        for core in range(8):
            outs[rnd * 8 + core] = res.results[core]["xout0"]
    return outs


def kernel_unfused(**inp):
    inp = {k: np.asarray(v) for k, v in inp.items()}
    xs = [seq_fm(inp, bi) for bi in range(16)]
    for li in range(DEPTH):
        xs = run_layers(inp, [li], xs)
    out = np.stack([x.T[256:, :] for x in xs], 0)
    return np.ascontiguousarray(out.astype(np.float32))


def kernel(**inp):
    inp = {k: np.asarray(v) for k, v in inp.items()}
    layers = list(range(DEPTH))
    nc, p = get_prog(layers, 2)
    cfa = host_consts()
    shared = {"cfa": cfa}
    for li in layers:
        shared.update(layer_inputs(inp, li))
    shared = {k: v for k, v in shared.items() if k in p.dram}
    in_maps = []
    for core in range(8):
        m = dict(shared)
        for s in range(2):
            bidx = 2 * core + s
            m["xin%d" % s] = seq_fm(inp, bidx)
        m["cTall"] = cT_host(inp, [2 * core, 2 * core + 1])
        in_maps.append(m)
    res = run_bass_kernel_spmd(nc, in_maps, core_ids=list(range(8)))
    out = np.zeros((16, 2048, 1024), np.float32)
    for core in range(8):
        for s in range(2):
            out[2 * core + s] = res.results[core]["xout%d" % s].T[256:, :]
    return out
```

```python
import numpy as np
from contextlib import ExitStack
import concourse.bass as bass
import concourse.mybir as mybir
from concourse.bass_utils import run_bass_kernel_spmd

F32 = mybir.dt.float32
BF16 = mybir.dt.bfloat16
U8 = mybir.dt.uint8
AF = mybir.ActivationFunctionType
ALU = mybir.AluOpType
AX = mybir.AxisListType

D = 1024
KC = 8
T = 2304
NCH = 18
DEPTH = 4
DFF = 2816
EPS = 1e-6
ALPHA = 8.0 ** 0.25
TILES = [(0, 256), (256, 512), (768, 512), (1280, 512), (1792, 512)]
FWD = list(range(18))
BWD = [1, 0] + list(range(17, 1, -1))
NCFA = 1156
DEBUG_ALLOC = False
GDN_LANES = 5
DBG = {}


def which_of(ti):
    return 0 if ti == 0 else 1


class B:
    def __init__(self, nc):
        self.nc = nc
        self.es = ExitStack()
        self.E = ['pe', 'act', 'dve', 'pool', 'sp']
        self.q = {e: [] for e in self.E}
        self.cnt = {e: 0 for e in self.E}
        self.sem = {e: self.es.enter_context(nc.semaphore("s_" + e)) for e in self.E if e != 'sp'}
        self.ND = 16
        self.dsem = [self.es.enter_context(nc.semaphore("d%d" % i)) for i in range(self.ND)]
        self.dcnt = [0] * self.ND
        self.drr = 0
        self.waited = {e: {} for e in self.E}
        self.track = {}
        self.ninstr = 0
        self.psb = [self.es.enter_context(nc.psum_tensor("ps%d" % i, [128, 512], F32)) for i in range(8)]
        self.psrr = 0

    def _semh(self, sk):
        return self.sem[sk] if isinstance(sk, str) else self.dsem[sk[1]]

    def _deps(self, eng, r, w):
        raw = {}
        oth = {}

        def need(dct, idv):
            if idv is None:
                return
            sk, v = idv
            if dct.get(sk, 0) < v:
                dct[sk] = v
        for key in r:
            t = self.track.get(key)
            if t:
                need(raw, t[0])
        for key in w:
            t = self.track.get(key)
            if t:
                need(oth, t[0])
                for sk, v in t[1].items():
                    need(oth, (sk, v))
        for sk, v in oth.items():
            if sk == eng and eng == 'pe':
                continue
            if raw.get(sk, 0) < v:
                raw[sk] = v
        for sk, v in raw.items():
            if self.waited[eng].get(sk, 0) >= v:
                continue
            self.waited[eng][sk] = v
            sh = self._semh(sk)
            self.q[eng].append(lambda e, sh=sh, v=v: e.wait_ge(sh, v))
            self.ninstr += 1

    def _mark(self, idv, r, w):
        for key in w:
            self.track[key] = [idv, {}]
        for key in r:
            t = self.track.setdefault(key, [None, {}])
            if t[1].get(idv[0], 0) < idv[1]:
                t[1][idv[0]] = idv[1]

    def op(self, eng, fn, r=(), w=()):
        self._deps(eng, r, w)
        self.cnt[eng] += 1
        sh = self.sem[eng]
        self.q[eng].append(lambda e, fn=fn, sh=sh: fn(e).then_inc(sh, 1))
        self.ninstr += 1
        self._mark((eng, self.cnt[eng]), r, w)

    def dma(self, out, in_, r=(), w=()):
        eng = 'sp'
        k = self.drr
        self.drr = (self.drr + 1) % self.ND
        self._deps(eng, r, w)
        prev = 16 * self.dcnt[k]
        if prev and self.waited[eng].get(('d', k), 0) < prev:
            self.waited[eng][('d', k)] = prev
            self.q[eng].append(lambda e, sh=self.dsem[k], v=prev: e.wait_ge(sh, v))
        self.dcnt[k] += 1
        val = 16 * self.dcnt[k]
        self.q[eng].append(lambda e, out=out, in_=in_, sh=self.dsem[k]: e.dma_start(out=out, in_=in_).then_inc(sh, 16))
        self.ninstr += 1
        idv = (('d', k), val)
        self._mark(idv, r, w)
        return idv

    def barrier(self):
        for e in self.E:
            for o in self.E:
                if o == 'sp' or o == e:
                    continue
                v = self.cnt[o]
                if v and self.waited[e].get(o, 0) < v:
                    self.waited[e][o] = v
                    self.q[e].append(lambda en, sh=self.sem[o], v=v: en.wait_ge(sh, v))
                    self.ninstr += 1
            for k in range(self.ND):
                v = 16 * self.dcnt[k]
                if v and self.waited[e].get(('d', k), 0) < v:
                    self.waited[e][('d', k)] = v
                    self.q[e].append(lambda en, sh=self.dsem[k], v=v: en.wait_ge(sh, v))
                    self.ninstr += 1

    def scope(self):
        return _Scope(self)

    def new_epoch(self):
        self.barrier()
        self.nep = getattr(self, 'nep', 0) + 1
        for e in self.E:
            if e == 'sp':
                continue
            self.sem[e] = self.es.enter_context(self.nc.semaphore("s_%s_%d" % (e, self.nep)))
            self.cnt[e] = 0
        for e in self.E:
            for o in list(self.waited[e].keys()):
                if isinstance(o, str):
                    del self.waited[e][o]
        self.track = {}

    def ps(self, hold=False):
        if not hasattr(self, 'held'):
            self.held = set()
        while True:
            k = self.psrr
            self.psrr = (self.psrr + 1) % 8
            if k not in self.held:
                break
        if hold:
            self.held.add(k)
        return self.psb[k], ('ps', k)

    def release(self, key):
        self.held.discard(key[1])

    def mm(self, out, lhsT, rhs, start=True, stop=True, r=(), w=()):
        self.op('pe', lambda e: e.matmul(out, lhsT, rhs, start=start, stop=stop), r, w)

    def tr(self, out, in_, ident, r=(), w=()):
        self.op('pe', lambda e: e.transpose(out, in_, ident), r, w)

    def act(self, out, in_, func, bias=None, scale=None, r=(), w=()):
        kw = {}
        if bias is not None:
            kw['bias'] = bias
        if scale is not None:
            kw['scale'] = scale
        self.op('act', lambda e: e.activation(out=out, in_=in_, func=func, **kw), r, w)

    def tt(self, eng, out, a, b, op, r=(), w=()):
        self.op(eng, lambda e: e.tensor_tensor(out=out, in0=a, in1=b, op=op), r, w)

    def ts(self, eng, out, a, s1, s2, op0, op1=None, r=(), w=()):
        if op1 is None:
            self.op(eng, lambda e: e.tensor_scalar(out=out, in0=a, scalar1=s1, scalar2=None, op0=op0), r, w)
        else:
            self.op(eng, lambda e: e.tensor_scalar(out=out, in0=a, scalar1=s1, scalar2=s2, op0=op0, op1=op1), r, w)

    def stt(self, eng, out, a, s, b, op0, op1, r=(), w=()):
        self.op(eng, lambda e: e.scalar_tensor_tensor(out=out, in0=a, scalar=s, in1=b, op0=op0, op1=op1), r, w)

    def cp(self, eng, out, in_, r=(), w=()):
        if eng == 'act':
            self.act(out, in_, AF.Copy, r=r, w=w)
        else:
            self.op(eng, lambda e: e.tensor_copy(out=out, in_=in_), r, w)

    def sb(self, es, name, shape, dt):
        if DEBUG_ALLOC:
            print("alloc", name, shape, dt, "remaining", self.nc.sbuf_bytes_remaining)
        self.uid = getattr(self, "uid", 0) + 1
        return es.enter_context(self.nc.sbuf_tensor("sb%d_%s" % (self.uid, name), shape, dt))

    def finish(self):
        nc = self.nc
        q = self.q
        with nc.Block() as block:
            @block.tensor
            def _(e):
                for f in q['pe']:
                    f(e)

            @block.scalar
            def _(e):
                for f in q['act']:
                    f(e)

            @block.vector
            def _(e):
                for f in q['dve']:
                    f(e)

            @block.gpsimd
            def _(e):
                for f in q['pool']:
                    f(e)

            @block.sync
            def _(e):
                for f in q['sp']:
                    f(e)
        self.es.close()


class _Scope:
    def __init__(self, b):
        self.b = b
        self.es = ExitStack()

    def __enter__(self):
        self.es.__enter__()
        return self.es

    def __exit__(self, *a):
        self.b.barrier()
        return self.es.__exit__(*a)


def xa_keys(c, tis):
    return [('xa', c, ti) for ti in tis]


def h_keys(c, tis):
    return [('h', c, ti) for ti in tis]


ALLT = list(range(5))


class Prog:
    def __init__(self, nc, layers, nseq):
        self.nc = nc
        self.b = B(nc)
        self.layers = layers
        self.nseq = nseq
        self.dram = {}

    def din(self, name, shape, dt=F32):
        if name not in self.dram:
            self.dram[name] = self.nc.dram_tensor(name, list(shape), dt, kind="ExternalInput").ap()
        return self.dram[name]

    def dout(self, name, shape, dt=F32):
        if name not in self.dram:
            self.dram[name] = self.nc.dram_tensor(name, list(shape), dt, kind="ExternalOutput").ap()
        return self.dram[name]

    def build(self):
        b = self.b
        nc = self.nc
        es = b.es
        self.xa = b.sb(es, "xa", [128, KC, T], F32)
        self.hb = b.sb(es, "hb", [128, KC, T], BF16)
        self.cfa = b.sb(es, "cfa", [128, NCFA], F32)
        self.identb = b.sb(es, "identb", [128, 128], BF16)
        self.onesb = b.sb(es, "onesb", [128, 128], BF16)
        self.mean1k = b.sb(es, "mean1k", [128, 128], F32)
        self.mean1kb = b.sb(es, "mean1kb", [128, 128], BF16)
        cfa_d = self.din("cfa", [128, NCFA])
        b.dma(self.cfa[:], cfa_d, w=['cfa'])
        self.ident = self.cfa[:, 0:128]
        self.trif = self.cfa[:, 128:256]
        self.trib = self.cfa[:, 256:384]
        self.ones = self.cfa[:, 384:512]
        self.maskq = self.cfa[:, 512:768]
        self.rot = self.cfa[:, 768:896]
        self.epsv = self.cfa[:, 1152:1155]
        self.nident = self.cfa[:, 896:1024]
        self.nones = self.cfa[:, 1024:1152]
        b.cp('act', self.identb[:], self.ident, r=['cfa'], w=['identb'])
        b.cp('act', self.onesb[:], self.ones, r=['cfa'], w=['onesb'])
        b.act(self.mean1k[:], self.ones, AF.Copy, scale=1.0 / 1024.0, r=['cfa'], w=['mean1k'])
        b.act(self.mean1kb[:], self.ones, AF.Copy, scale=1.0 / 1024.0, r=['cfa'], w=['mean1k'])
        self.mod_all(es)
        outs = []
        for s in range(self.nseq):
            xin = self.din("xin%d" % s, [D, T])
            xout = self.dout("xout%d" % s, [D, T])
            with b.scope() as les:
                stg = [b.sb(les, "ldst%d" % i, [128, T], F32) for i in range(2)]
                for c in range(KC):
                    st = stg[c % 2]
                    b.dma(st[:], xin[c * 128:(c + 1) * 128, :], w=[('ldst', c % 2)])
                    b.act(self.xa[:, c, :], st[:], AF.Copy, scale=ALPHA, r=[('ldst', c % 2)], w=xa_keys(c, ALLT))
            for n, li in enumerate(self.layers):
                last = (n == len(self.layers) - 1)
                b.new_epoch()
                self.layer(li, s, out_plain=last)
            for c in range(KC):
                outs.append(b.dma(xout[c * 128:(c + 1) * 128, :], self.xa[:, c, :], r=xa_keys(c, ALLT)))
        for (sk, v) in outs + getattr(self, 'dbg_outs', []):
            if b.waited['sp'].get(sk, 0) < v:
                b.waited['sp'][sk] = v
                b.q['sp'].append(lambda e, sh=b.dsem[sk[1]], v=v: e.wait_ge(sh, v))
        b.finish()

    def layer(self, li, s, out_plain):
        b = self.b
        with b.scope() as les:
            self.lv = {}
            stop = DBG.get('stop')
            self.modulation(li, s, les, out_plain)
            if DBG.get('dump'):
                self.dbg_outs = getattr(self, 'dbg_outs', [])
                self.dbg_outs.append(b.dma(self.dout("dbg_MOD", [128, 96]), self.P['MOD'][:].rearrange("p a b -> p (a b)"), r=['MOD']))
            if stop == 'mod':
                return
            self.modulate_in()
            if DBG.get('dump'):
                self.dbg_outs.append(b.dma(self.dout("dbg_h", [128, KC * T], BF16), self.hb[:].rearrange("p a b -> p (a b)"),
                                           r=[('h', c, ti) for c in range(KC) for ti in ALLT]))
            if stop == 'modin':
                return
            if li % 2 == 0:
                self.gdn(li)
            else:
                self.attn(li)
            if stop == 'mixer':
                return
            self.layernorm(0, want_h=True)
            if stop == 'ln0':
                return
            self.ffn(li)
            if stop == 'ffn':
                return
            self.layernorm(1, want_h=False)

    def mod_all(self, es):
        b = self.b
        ns = 1 + self.nseq
        self.MODall = {}
        for li in self.layers:
            self.MODall[li] = b.sb(es, "MODall%d" % li, [128, 48, ns], F32)
        with b.scope() as wes:
            sc = b.sb(wes, "ma_sc", [128, KC, ns], F32)
            bm = b.sb(wes, "ma_bm", [128, 48], F32)
            wst = [b.sb(wes, "ma_wst%d" % i, [128, KC, 512], F32) for i in range(3)]
            cT = self.din("cTall", [128, KC, ns])
            b.dma(sc[:], cT, w=['sc'])
            b.act(sc[:], sc[:], AF.Silu, r=['sc'], w=['sc'])
            nblk = 0
            for li in self.layers:
                w_mod = self.din("w_mod%d" % li, [D, 6 * D])
                bmodT = self.din("bmodT%d" % li, [128, 48])
                b.dma(bm[:], bmodT, w=['bmodT'])
                ps, pk = b.ps(hold=True)
                for nb in range(12):
                    st = wst[nblk % 3]
                    sk = ('wmst', nblk % 3)
                    nblk += 1
                    b.dma(st[:], w_mod[:, nb * 512:(nb + 1) * 512].rearrange("(k p) n -> p k n", p=128), w=[sk])
                    for f in range(4):
                        fc = nb * 4 + f
                        for kc in range(KC):
                            b.mm(ps[:, fc * ns:(fc + 1) * ns], st[:, kc, f * 128:(f + 1) * 128], sc[:, kc, :],
                                 start=(kc == 0), stop=(kc == KC - 1), r=[sk, 'sc'], w=[pk])
                b.tt('dve', self.MODall[li][:], ps[:, 0:48 * ns].rearrange("p (a b) -> p a b", b=ns),
                     bm[:].unsqueeze(2).to_broadcast([128, 48, ns]), ALU.add, r=[pk, 'bmodT'], w=[('MODall', li)])
                b.release(pk)

    def modulation(self, li, s, les, out_plain):
        b = self.b
        P = {}
        for nm, shape in [('MOD', [128, 48, 2]), ('s1', [128, 8, 2]), ('H1s', [128, 8, 2]), ('H1b', [128, 8, 2]),
                          ('A', [128, 2, 8]), ('Bv', [128, 2, 8]), ('lnT', [128, 4, 8]), ('bmodT', [128, 48]),
                          ('sc', [128, 8, 2]), ('tmp1', [128, 8, 2])]:
            P[nm] = b.sb(les, "m_" + nm, shape, F32)
        self.P = P
        lnT = self.din("lnT%d" % li, [128, 4, 8])
        b.dma(P['lnT'][:], lnT, w=['lnT'])
        MOD = P['MOD']
        MA = self.MODall[li]
        b.cp('dve', MOD[:, :, 0:1], MA[:, :, 0:1], r=[('MODall', li)], w=['MOD'])
        b.cp('dve', MOD[:, :, 1:2], MA[:, :, 1 + s:2 + s], r=[('MODall', li)], w=['MOD'])

        def mj(j):
            return MOD[:, j * 8:(j + 1) * 8, :]
        b.ts('dve', P['s1'][:], mj(1), 1.0, 1.0 / ALPHA, ALU.add, ALU.mult, r=['MOD'], w=['s1'])
        ln = P['lnT']
        g0 = ln[:, 0, :].unsqueeze(2).to_broadcast([128, 8, 2])
        b0 = ln[:, 1, :].unsqueeze(2).to_broadcast([128, 8, 2])
        b.ts('dve', P['tmp1'][:], mj(4), 1.0, None, ALU.add, r=['MOD'], w=['tmp1'])
        b.tt('dve', P['H1s'][:], P['tmp1'][:], g0, ALU.mult, r=['tmp1', 'lnT'], w=['H1s'])
        b.tt('dve', P['H1b'][:], P['tmp1'][:], b0, ALU.mult, r=['tmp1', 'lnT'], w=['H1b'])
        b.tt('dve', P['H1b'][:], P['H1b'][:], mj(3), ALU.add, r=['H1b', 'MOD'], w=['H1b'])
        b.ts('dve', P['A'][:, 0, :], ln[:, 0, :], ALPHA, None, ALU.mult, r=['lnT'], w=['A'])
        b.ts('dve', P['Bv'][:, 0, :], ln[:, 1, :], ALPHA, None, ALU.mult, r=['lnT'], w=['Bv'])
        a2 = 1.0 if out_plain else ALPHA
        b.ts('dve', P['A'][:, 1, :], ln[:, 2, :], a2, None, ALU.mult, r=['lnT'], w=['A'])
        b.ts('dve', P['Bv'][:, 1, :], ln[:, 3, :], a2, None, ALU.mult, r=['lnT'], w=['Bv'])
        self.ga = mj(2)
        self.gaf = mj(5)
        self.sh = mj(0)

    def modulate_in(self):
        b = self.b
        P = self.P
        for c in range(KC):
            for (wh, t0, n, tis) in [(0, 0, 256, [0]), (1, 256, 2048, [1, 2, 3, 4])]:
                b.act(self.hb[:, c, t0:t0 + n], self.xa[:, c, t0:t0 + n], AF.Identity,
                      bias=self.sh[:, c, wh:wh + 1], scale=P['s1'][:, c, wh:wh + 1],
                      r=xa_keys(c, tis) + ['MOD', 's1'], w=h_keys(c, tis))

    def layernorm(self, idx, want_h):
        b = self.b
        P = self.P
        with b.scope() as es:
            sq = [b.sb(es, "ln_sq%d" % i, [128, 512], BF16) for i in range(2)]
            msb = b.sb(es, "ln_msb", [128, 512], F32)
            m2 = b.sb(es, "ln_m2", [128, 512], F32)
            rstd = b.sb(es, "ln_rstd", [128, 512], F32)
            tt_ = [b.sb(es, "ln_t%d" % i, [128, 512], F32) for i in range(2)]
            for ti, (t0, n) in enumerate(TILES):
                wh = which_of(ti)
                pm, pmk = b.ps()
                pe2, pe2k = b.ps()
                for c in range(KC):
                    b.act(sq[c % 2][:, :n], self.xa[:, c, t0:t0 + n], AF.Square, r=[('xa', c, ti)], w=[('lnsq', c % 2)])
                    b.mm(pm[:, :n], self.mean1k[:], self.xa[:, c, t0:t0 + n], start=(c == 0), stop=(c == KC - 1),
                         r=[('xa', c, ti), 'mean1k'], w=[pmk])
                    b.mm(pe2[:, :n], self.mean1kb[:], sq[c % 2][:, :n], start=(c == 0), stop=(c == KC - 1),
                         r=[('lnsq', c % 2), 'mean1k'], w=[pe2k])
                b.cp('act', msb[:, :n], pm[:, :n], r=[pmk], w=['lnmsb'])
                b.tt('dve', m2[:, :n], msb[:, :n], msb[:, :n], ALU.mult, r=['lnmsb'], w=['lnm2'])
                b.tt('dve', m2[:, :n], pe2[:, :n], m2[:, :n], ALU.subtract, r=[pe2k, 'lnm2'], w=['lnm2'])
                b.act(m2[:, :n], m2[:, :n], AF.Ln, bias=self.epsv[:, 0:1], r=['lnm2', 'cfa'], w=['lnm2'])
                b.act(rstd[:, :n], m2[:, :n], AF.Exp, scale=-0.5, r=['lnm2'], w=['lnrstd'])
                for c in range(KC):
                    t = tt_[c % 2]
                    tk = ('lnt', c % 2)
                    e1 = 'dve'
                    e2 = 'dve'
                    b.tt(e1, t[:, :n], self.xa[:, c, t0:t0 + n], msb[:, :n], ALU.subtract, r=[('xa', c, ti), 'lnmsb'], w=[tk])
                    b.tt(e2, t[:, :n], t[:, :n], rstd[:, :n], ALU.mult, r=[tk, 'lnrstd'], w=[tk])
                    b.act(self.xa[:, c, t0:t0 + n], t[:, :n], AF.Identity, bias=P['Bv'][:, idx, c:c + 1],
                          scale=P['A'][:, idx, c:c + 1], r=[tk, 'A', 'Bv'], w=[('xa', c, ti)])
                    if want_h:
                        b.ts('pool', self.hb[:, c, t0:t0 + n], t[:, :n], P['H1s'][:, c, wh:wh + 1], P['H1b'][:, c, wh:wh + 1],
                             ALU.mult, ALU.add, r=[tk, 'H1s', 'H1b'], w=[('h', c, ti)])

    def ffn(self, li):
        b = self.b
        w_in = self.din("w_ffn_in%d" % li, [D, 2 * DFF])
        w_out = self.din("w_ffn_out%d" % li, [DFF, D])
        with b.scope() as es:
            wis = [b.sb(es, "f_wis%d" % i, [128, KC, 512], F32) for i in range(2)]
            wos = [b.sb(es, "f_wos%d" % i, [128, 2, D], F32) for i in range(2)]
            wib = [b.sb(es, "f_wib%d" % i, [128, KC, 512], BF16) for i in range(2)]
            wob = [b.sb(es, "f_wob%d" % i, [128, 2, D], BF16) for i in range(2)]
            actb = b.sb(es, "f_act", [128, 2, T], BF16)
            sg = [b.sb(es, "f_sg%d" % i, [128, 512], F32) for i in range(2)]
            nsg = 0
            for fb in range(11):
                p = fb % 2
                f0 = fb * 256
                b.dma(wis[p][:, :, 0:256], w_in[:, f0:f0 + 256].rearrange("(k p) n -> p k n", p=128), w=[('wis', p, 0)])
                b.dma(wis[p][:, :, 256:512], w_in[:, DFF + f0:DFF + f0 + 256].rearrange("(k p) n -> p k n", p=128), w=[('wis', p, 1)])
                b.dma(wos[p][:], w_out[f0:f0 + 256, :].rearrange("(k p) n -> p k n", p=128), w=[('wos', p)])
                b.cp('pool', wib[p][:], wis[p][:], r=[('wis', p, 0), ('wis', p, 1)], w=[('wib', p)])
                b.cp('pool', wob[p][:], wos[p][:], r=[('wos', p)], w=[('wob', p)])
                for ti, (t0, n) in enumerate(TILES):
                    for j in range(2):
                        pg, pgk = b.ps()
                        pu, puk = b.ps()
                        for kc in range(KC):
                            b.mm(pg[:, :n], wib[p][:, kc, j * 128:(j + 1) * 128], self.hb[:, kc, t0:t0 + n],
                                 start=(kc == 0), stop=(kc == KC - 1), r=[('wib', p), ('h', kc, ti)], w=[pgk])
                        for kc in range(KC):
                            b.mm(pu[:, :n], wib[p][:, kc, 256 + j * 128:256 + (j + 1) * 128], self.hb[:, kc, t0:t0 + n],
                                 start=(kc == 0), stop=(kc == KC - 1), r=[('wib', p), ('h', kc, ti)], w=[puk])
                        s_ = sg[nsg % 2]
                        sk = ('fsg', nsg % 2)
                        nsg += 1
                        b.act(s_[:, :n], pg[:, :n], AF.Silu, r=[pgk], w=[sk])
                        b.tt('dve', actb[:, j, t0:t0 + n], pu[:, :n], s_[:, :n], ALU.mult, r=[puk, sk], w=[('fact', j, ti)])
                for ti, (t0, n) in enumerate(TILES):
                    wh = which_of(ti)
                    for oc in range(KC):
                        po, pok = b.ps()
                        for j in range(2):
                            b.mm(po[:, :n], wob[p][:, j, oc * 128:(oc + 1) * 128], actb[:, j, t0:t0 + n],
                                 start=(j == 0), stop=(j == 1), r=[('wob', p), ('fact', j, ti)], w=[pok])
                        b.stt('dve', self.xa[:, oc, t0:t0 + n], po[:, :n], self.gaf[:, oc, wh:wh + 1], self.xa[:, oc, t0:t0 + n],
                              ALU.mult, ALU.add, r=[pok, 'MOD', ('xa', oc, ti)], w=[('xa', oc, ti)])

    def out_proj(self, wb, wkey, nk, yfn, ykeys):
        b = self.b
        for ti, (t0, n) in enumerate(TILES):
            wh = which_of(ti)
            for oc in range(KC):
                po, pok = b.ps()
                for k in range(nk):
                    b.mm(po[:, :n], wb[:, k, oc * 128:(oc + 1) * 128], yfn(k, t0, n), start=(k == 0), stop=(k == nk - 1),
                         r=[wkey] + ykeys(k, ti), w=[pok])
                b.stt('dve', self.xa[:, oc, t0:t0 + n], po[:, :n], self.ga[:, oc, wh:wh + 1], self.xa[:, oc, t0:t0 + n],
                      ALU.mult, ALU.add, r=[pok, 'MOD', ('xa', oc, ti)], w=[('xa', oc, ti)])

    def attn(self, li):
        b = self.b
        j = li // 2
        w_qkv = self.din("attn_w_qkv%d" % j, [D, 1536])
        w_o = self.din("attn_w_out%d" % j, [D, D])
        gains = self.din("attn_gain%d" % j, [128, 2])
        ropeD = self.din("rope", [128, 4096])
        with b.scope() as es:
            QR = b.sb(es, "a_QR", [128, 8, T], BF16)
            KR = b.sb(es, "a_KR", [128, 2, T], BF16)
            VT = b.sb(es, "a_VT", [128, NCH, 256], BF16)
            gn = b.sb(es, "a_gn", [128, 2], F32)
            b.dma(gn[:], gains, w=['gn'])
            b.ts('dve', gn[:], gn[:], float(np.sqrt(128.0)), None, ALU.mult, r=['gn'], w=['gn'])
            with b.scope() as es2:
                rope = b.sb(es2, "a_rope", [128, 4096], F32)
                b.dma(rope[:], ropeD, w=['rope'])
                cosT = rope[:, 0:2048]
                sinT = rope[:, 2048:4096]
                wst_ = b.sb(es2, "a_wst", [128, KC, 128], F32)
                wst = [wst_, wst_]
                wbf = [b.sb(es2, "a_wbf%d" % i, [128, KC, 128], BF16) for i in range(2)]
                sqb = b.sb(es2, "a_sq", [128, 512], BF16)
                rs = b.sb(es2, "a_rs", [128, 512], F32)
                qn = b.sb(es2, "a_qn", [128, 512], F32)
                t1 = b.sb(es2, "a_t1", [128, 512], F32)
                t2 = b.sb(es2, "a_t2", [128, 512], F32)
                sqb2 = [sqb, b.sb(es2, "a_sq2", [128, 512], BF16)]
                qn2 = [qn, b.sb(es2, "a_qn2", [128, 512], F32)]

                def load_w(wbk):
                    p = wbk % 2
                    b.dma(wst[p][:], w_qkv[:, wbk * 128:(wbk + 1) * 128].rearrange("(k p) n -> p k n", p=128), w=[('awst', 0)])
                    b.cp('pool', wbf[p][:], wst[p][:], r=[('awst', 0)], w=[('awbf', p)])
                items = [(wbk, ti) for wbk in range(10) for ti in range(5)]
                st = {}

                def stA(i):
                    wbk, ti = items[i]
                    t0, n = TILES[ti]
                    p = wbk % 2
                    if ti == 0:
                        load_w(wbk)
                    pp, ppk = b.ps(hold=True)
                    for kc in range(KC):
                        b.mm(pp[:, :n], wbf[p][:, kc, :], self.hb[:, kc, t0:t0 + n],
                             start=(kc == 0), stop=(kc == KC - 1), r=[('awbf', p), ('h', kc, ti)], w=[ppk])
                    b.act(sqb2[i % 2][:, :n], pp[:, :n], AF.Square, r=[ppk], w=[('asq', i % 2)])
                    st[i] = (pp, ppk)

                def stB(i):
                    wbk, ti = items[i]
                    t0, n = TILES[ti]
                    pp, ppk = st.pop(i)
                    isq = wbk < 8
                    hidx = wbk if isq else wbk - 8
                    dst = QR if isq else KR
                    gcol = gn[:, 0:1] if isq else gn[:, 1:2]
                    p2, p2k = b.ps(hold=True)
                    b.mm(p2[:, :n], self.onesb[:], sqb2[i % 2][:, :n], r=[('asq', i % 2), 'onesb'], w=[p2k])
                    b.act(rs[:, :n], p2[:, :n], AF.Ln, bias=self.epsv[:, 1:2], r=[p2k, 'cfa'], w=['ars'])
                    b.release(p2k)
                    b.act(rs[:, :n], rs[:, :n], AF.Exp, scale=-0.5, r=['ars'], w=['ars'])
                    if ti == 0:
                        b.stt('dve', dst[:, hidx, t0:t0 + n], pp[:, :n], gcol, rs[:, :n], ALU.mult, ALU.mult,
                              r=[ppk, 'gn', 'ars'], w=[('aqk', isq, hidx, ti)])
                    else:
                        b.stt('dve', qn2[i % 2][:, :n], pp[:, :n], gcol, rs[:, :n], ALU.mult, ALU.mult,
                              r=[ppk, 'gn', 'ars'], w=[('aqn', i % 2)])
                    b.release(ppk)

                def stC(i):
                    wbk, ti = items[i]
                    if ti == 0:
                        return
                    t0, n = TILES[ti]
                    isq = wbk < 8
                    hidx = wbk if isq else wbk - 8
                    dst = QR if isq else KR
                    qn_ = qn2[i % 2]
                    p3, p3k = b.ps(hold=True)
                    b.mm(p3[:, :n], self.rot, qn_[:, :n], r=[('aqn', i % 2), 'cfa'], w=[p3k])
                    l0 = t0 - 256
                    b.tt('dve', t1[:, :n], qn_[:, :n], cosT[:, l0:l0 + n], ALU.mult, r=[('aqn', i % 2), 'rope'], w=['at1'])
                    b.tt('dve', t2[:, :n], p3[:, :n], sinT[:, l0:l0 + n], ALU.mult, r=[p3k, 'rope'], w=['at2'])
                    b.release(p3k)
                    b.tt('pool', dst[:, hidx, t0:t0 + n], t1[:, :n], t2[:, :n], ALU.add, r=['at1', 'at2'],
                         w=[('aqk', isq, hidx, ti)])
                nit = len(items)
                for i in range(nit + 2):
                    if i < nit:
                        stA(i)
                    if 0 <= i - 1 < nit:
                        stB(i - 1)
                    if 0 <= i - 2 < nit:
                        stC(i - 2)
                for wbk in (10, 11):
                    p = wbk % 2
                    load_w(wbk)
                    kvh = wbk - 10
                    for c in range(NCH):
                        pv, pvk = b.ps()
                        for kc in range(KC):
                            b.mm(pv[:, 0:128], self.hb[:, kc, c * 128:(c + 1) * 128], wbf[p][:, kc, :],
                                 start=(kc == 0), stop=(kc == KC - 1), r=[('awbf', p)] + h_keys(kc, ALLT), w=[pvk])
                        b.cp('act', VT[:, c, kvh * 128:(kvh + 1) * 128], pv[:, 0:128], r=[pvk], w=['VT'])
            with b.scope() as es3:
                pts = [b.sb(es3, "a_pt%d" % i, [128, 512], BF16) for i in range(3)]
                rden = b.sb(es3, "a_rden", [128, 512], F32)
                wos_ = b.sb(es3, "a_wos", [128, 2, D], F32)
                wos = [wos_, wos_]
                wob = b.sb(es3, "a_wob", [128, KC, D], BF16)
                for i in range(4):
                    b.dma(wos[i % 2][:], w_o[i * 256:(i + 1) * 256, :].rearrange("(k p) n -> p k n", p=128), w=[('aos', 0)])
                    b.cp('pool', wob[:, 2 * i:2 * i + 2, :], wos[i % 2][:], r=[('aos', 0)], w=['awob'])
                npt = 0
                scale = float(128.0 ** -0.5)
                for hq in range(8):
                    kv = hq // 4
                    for ti, (t0, n) in enumerate(TILES):
                        kts = [0, 1] if ti == 0 else list(range(NCH))
                        pden, pdk = b.ps(hold=True)
                        po, pok = b.ps(hold=True)
                        prev = None

                        def flush(pv_):
                            pt_, ptk_, ii_, kt_ = pv_
                            b.mm(pden[:, :n], self.onesb[:], pt_[:, :n], start=(ii_ == 0), stop=(ii_ == len(kts) - 1),
                                 r=[ptk_, 'onesb'], w=[pdk])
                            b.mm(po[:, :n], VT[:, kt_, kv * 128:(kv + 1) * 128], pt_[:, :n], start=(ii_ == 0), stop=(ii_ == len(kts) - 1),
                                 r=[ptk_, 'VT'], w=[pok])
                        for ii, kt in enumerate(kts):
                            psc, psk = b.ps()
                            b.mm(psc[:, :n], KR[:, kv, kt * 128:(kt + 1) * 128], QR[:, hq, t0:t0 + n],
                                 r=[('aqk', False, kv, tt_) for tt_ in ALLT] + [('aqk', True, hq, ti)], w=[psk])
                            pt = pts[npt % 3]
                            ptk = ('apt', npt % 3)
                            npt += 1
                            b.act(pt[:, :n], psc[:, :n], AF.Exp, scale=scale, r=[psk], w=[ptk])
                            if prev is not None:
                                flush(prev)
                            prev = (pt, ptk, ii, kt)
                        flush(prev)
                        b.op('dve', lambda e, o=rden[:, :n], i=pden[:, :n]: e.reciprocal(out=o, in_=i), r=[pdk], w=['arden'])
                        b.tt('dve', self.hb[:, hq, t0:t0 + n], po[:, :n], rden[:, :n], ALU.mult, r=[pok, 'arden'], w=[('h', hq, ti)])
                        b.release(pdk)
                        b.release(pok)
                self.out_proj(wob, 'awob', KC, lambda k, t0, n: self.hb[:, k, t0:t0 + n], lambda k, ti: [('h', k, ti)])

    def gdn(self, li):
        b = self.b
        j = li // 2
        w_in = self.din("gdn_w_in%d" % j, [D, 6208])
        w_out = self.din("gdn_w_out%d" % j, [2048, D])
        convD = self.din("gdn_convT%d" % j, [128, 32, 5])
        gparD = self.din("gdn_gpar%d" % j, [128, 64])
        normgD = self.din("gdn_normg%d" % j, [128, 128])
        lvlD = self.din("lvlmask", [128, 7 * 4 * 128], U8)
        ident = self.ident
        one_col = self.epsv[:, 2:3]
        eps_col = self.epsv[:, 0:1]

        def bc(ap2, n=128):
            return ap2.unsqueeze(2).to_broadcast([128, ap2.shape[1], n])

        with b.scope() as es:
            G = {}
            for nm in ['NBETA', 'BETA', 'GCUM', 'EG', 'KD', 'EGL']:
                G[nm] = b.sb(es, "g_" + nm, [128, NCH, 32], F32)
            convw = b.sb(es, "g_convw", [128, 32, 5], F32)
            normg = b.sb(es, "g_normg", [128, 128], F32)
            lvl = b.sb(es, "g_lvl", [128, 7, 4, 128], U8)
            b.dma(convw[:], convD, w=['convw'])
            b.dma(normg[:], normgD, w=['normg'])
            b.dma(lvl[:].rearrange("p a b c -> p (a b c)"), lvlD, w=['lvl'])
            with b.scope() as ges:
                gpar = b.sb(ges, "g_gpar", [128, 64], F32)
                wgs = b.sb(ges, "g_wgs", [128, KC, 64], F32)
                wgb = b.sb(ges, "g_wgb", [128, KC, 64], BF16)
                GRAW = b.sb(ges, "g_graw", [128, NCH, 64], F32)
                T1 = b.sb(ges, "g_t1", [128, NCH, 32], F32)
                T2 = b.sb(ges, "g_t2", [128, NCH, 32], F32)
                GG = b.sb(ges, "g_g", [128, NCH, 32], F32)
                GL = b.sb(ges, "g_gl", [128, NCH, 32], F32)
                NA = b.sb(ges, "g_na", [128, 32], F32)
                b.dma(gpar[:], gparD, w=['gpar'])
                b.dma(wgs[:], w_in[:, 6144:6208].rearrange("(k p) n -> p k n", p=128), w=['wgs'])
                b.cp('pool', wgb[:], wgs[:], r=['wgs'], w=['wgb'])
                for c0 in range(0, NCH, 8):
                    nc_ = min(8, NCH - c0)
                    pg, pgk = b.ps()
                    for cc in range(nc_):
                        c = c0 + cc
                        for kc in range(KC):
                            b.mm(pg[:, cc * 64:(cc + 1) * 64], self.hb[:, kc, c * 128:(c + 1) * 128], wgb[:, kc, :],
                                 start=(kc == 0), stop=(kc == KC - 1), r=['wgb'] + h_keys(kc, ALLT), w=[pgk])
                    b.cp('act', GRAW[:, c0:c0 + nc_, :], pg[:, 0:nc_ * 64].rearrange("p (a b) -> p a b", b=64), r=[pgk], w=['graw'])
                braw = GRAW[:, :, 0:32]
                araw = GRAW[:, :, 32:64]
                b.act(T1[:], braw, AF.Exp, scale=-1.0, r=['graw'], w=['gt1'])
                b.act(T1[:], T1[:], AF.Ln, bias=one_col, r=['gt1', 'cfa'], w=['gt1'])
                b.act(G['BETA'][:], T1[:], AF.Exp, scale=-1.0, r=['gt1'], w=['BETA'])
                b.ts('pool', G['NBETA'][:], G['BETA'][:], -1.0, None, ALU.mult, r=['BETA'], w=['NBETA'])
                b.tt('dve', T2[:], araw, gpar[:, 32:64].unsqueeze(1).to_broadcast([128, NCH, 32]), ALU.add, r=['graw', 'gpar'], w=['gt2'])
                b.act(T2[:], T2[:], AF.Exp, r=['gt2'], w=['gt2'])
                b.act(T2[:], T2[:], AF.Ln, bias=one_col, r=['gt2', 'cfa'], w=['gt2'])
                b.act(NA[:], gpar[:, 0:32], AF.Exp, r=['gpar'], w=['gna'])
                b.ts('pool', NA[:], NA[:], -1.0, None, ALU.mult, r=['gna'], w=['gna'])
                b.tt('dve', GG[:], T2[:], NA[:].unsqueeze(1).to_broadcast([128, NCH, 32]), ALU.mult, r=['gt2', 'gna'], w=['gg'])
                pc, pck = b.ps()
                b.mm(pc[:, 0:288], self.trif, GG[:, :, 0:16], r=['gg', 'cfa'], w=[pck])
                pc2, pc2k = b.ps()
                b.mm(pc2[:, 0:288], self.trib, GG[:, :, 16:32], r=['gg', 'cfa'], w=[pc2k])
                b.cp('act', G['GCUM'][:, :, 0:16], pc[:, 0:288].rearrange("p (a b) -> p a b", b=16), r=[pck], w=['GCUM'])
                b.cp('act', G['GCUM'][:, :, 16:32], pc2[:, 0:288].rearrange("p (a b) -> p a b", b=16), r=[pc2k], w=['GCUM'])
                pl, plk = b.ps()
                b.mm(pl[:, 0:288], self.ones, GG[:, 0:9, :], r=['gg', 'cfa'], w=[plk])
                pl2, pl2k = b.ps()
                b.mm(pl2[:, 0:288], self.ones, GG[:, 9:18, :], r=['gg', 'cfa'], w=[pl2k])
                b.cp('act', GL[:, 0:9, :], pl[:, 0:288].rearrange("p (a b) -> p a b", b=32), r=[plk], w=['ggl'])
                b.cp('act', GL[:, 9:18, :], pl2[:, 0:288].rearrange("p (a b) -> p a b", b=32), r=[pl2k], w=['ggl'])
                b.act(G['EG'][:], G['GCUM'][:], AF.Exp, r=['GCUM'], w=['EG'])
                b.tt('dve', T1[:], GL[:], G['GCUM'][:], ALU.subtract, r=['ggl', 'GCUM', 'gt1'], w=['gt1'])
                b.act(G['KD'][:], T1[:], AF.Exp, r=['gt1'], w=['KD'])
                b.act(G['EGL'][:], GL[:], AF.Exp, r=['ggl'], w=['EGL'])
            for g in range(8):
                self.gdn_group(li, g, es, G, convw, normg, lvl, w_in, w_out, bc)

    def gdn_group(self, li, g, es_unused, G, convw, normg, lvl, w_in, w_out, bc):
        b = self.b
        ident = self.ident
        eps_col = self.epsv[:, 0:1]
        ps_bf = lambda ps: ps[:].bitcast(BF16)
        gk = lambda nm: nm + "_%d" % g

        def gcols(d):
            return slice(d * 16 + 2 * g, d * 16 + 2 * g + 2)

        with b.scope() as ges:
            O = b.sb(ges, "g_O", [128, NCH, 2, 128], BF16)
            with b.scope() as aes:
                KQ = b.sb(aes, "g_KQ", [128, NCH, 2, 128], BF16)
                KTM = b.sb(aes, "g_KTM", [128, NCH, 128], BF16)
                VTM = b.sb(aes, "g_VTM", [128, NCH, 2, 128], BF16)
                with b.scope() as pes:
                    CB = b.sb(pes, "g_CB", [128, 2310], F32)
                    ACC = b.sb(pes, "g_ACC", [128, T], F32)
                    TB = b.sb(pes, "g_TB", [128, T], BF16)
                    wst = [b.sb(pes, "g_wst%d" % i, [128, KC, 128], F32) for i in range(2)]
                    wbf = [b.sb(pes, "g_wbf%d" % i, [128, KC, 128], BF16) for i in range(2)]
                    rs = b.sb(pes, "g_rs", [128, 512], F32)
                    b.op('pool', lambda e: e.memset(CB[:, 0:2], 0.0), w=['CBp0'])
                    b.op('pool', lambda e: e.memset(CB[:, 258:260], 0.0), w=['CBp1'])
                    b.op('pool', lambda e: e.memset(CB[:, 2308:2310], 0.0), w=['CBp2'])
                    fcs = [('q', g * 128, g), ('k', 1024 + g * 128, 8 + g),
                           ('v0', 2048 + (2 * g) * 128, 16 + 2 * g), ('v1', 2048 + (2 * g + 1) * 128, 16 + 2 * g + 1)]
                    def emit_proj(fi):
                        kind, col0, cq = fcs[fi]
                        held = []
                        p = fi % 2
                        b.dma(wst[p][:], w_in[:, col0:col0 + 128].rearrange("(k p) n -> p k n", p=128), w=[('gwst', p)])
                        b.cp('pool', wbf[p][:], wst[p][:], r=[('gwst', p)], w=[('gwbf', p)])
                        for ti, (t0, n) in enumerate(TILES):
                            pp, ppk = b.ps(hold=True)
                            for kc in range(KC):
                                b.mm(pp[:, :n], wbf[p][:, kc, :], self.hb[:, kc, t0:t0 + n], start=(kc == 0), stop=(kc == KC - 1),
                                     r=[('gwbf', p), ('h', kc, ti)], w=[ppk])
                            o0 = 2 if ti == 0 else t0 + 4
                            b.cp('act', CB[:, o0:o0 + n], pp[:, :n], r=[ppk], w=[('CB', ti)])
                            held.append(ppk)
                        return held

                    def emit_conv(fi):
                        kind, col0, cq = fcs[fi]
                        acck = [('ACC', 0), ('ACC', 256), ('ACC', 1280)]
                        for (d0, L, s0, tis) in [(0, 256, 2, [0]), (256, 1024, 260, [1, 2]), (1280, 1024, 1284, [3, 4])]:
                            ak = ('ACC', d0)
                            rk = [('CB', tj) for tj in {0: [0], 256: [1, 2, 3], 1280: [2, 3, 4]}[d0]] + ['CBp0', 'CBp1', 'CBp2', 'convw']
                            b.ts('dve', ACC[:, d0:d0 + L], CB[:, s0 - 2:s0 - 2 + L], convw[:, cq, 0:1], None, ALU.mult, r=rk, w=[ak])
                            for jj in range(1, 5):
                                b.stt('dve', ACC[:, d0:d0 + L], CB[:, s0 - 2 + jj:s0 - 2 + jj + L], convw[:, cq, jj:jj + 1],
                                      ACC[:, d0:d0 + L], ALU.mult, ALU.add, r=rk + [ak], w=[ak])
                            if kind in ('q', 'k'):
                                b.act(ACC[:, d0:d0 + L], ACC[:, d0:d0 + L], AF.Silu, r=[ak], w=[ak])
                            else:
                                b.act(TB[:, d0:d0 + L], ACC[:, d0:d0 + L], AF.Silu, r=[ak], w=['TB'])

                    def emit_tail(fi):
                        kind, col0, cq = fcs[fi]
                        acck = [('ACC', 0), ('ACC', 256), ('ACC', 1280)]
                        if kind in ('q', 'k'):
                            b.act(TB[:], ACC[:], AF.Square, r=acck, w=['TB'])
                            kq = 0 if kind == 'k' else 1
                            sc_ = 1.0 if kind == 'k' else float(128.0 ** -0.5)
                            for ti, (t0, n) in enumerate(TILES):
                                p2, p2k = b.ps()
                                b.mm(p2[:, :n], self.onesb[:], TB[:, t0:t0 + n], r=['TB', 'onesb'], w=[p2k])
                                b.act(rs[:, :n], p2[:, :n], AF.Ln, bias=eps_col, r=[p2k, 'cfa'], w=['grs'])
                                b.act(rs[:, :n], rs[:, :n], AF.Exp, scale=-0.5, r=['grs'], w=['grs'])
                                c0 = t0 // 128
                                nc_ = n // 128
                                b.stt('dve', KQ[:, c0:c0 + nc_, kq, :], ACC[:, t0:t0 + n].rearrange("p (a b) -> p a b", b=128), sc_,
                                      rs[:, :n].rearrange("p (a b) -> p a b", b=128), ALU.mult, ALU.mult,
                                      r=acck + ['grs'], w=[gk('KQ')])
                            if kind == 'k':
                                for c0 in range(0, NCH, 4):
                                    nc_ = min(4, NCH - c0)
                                    pt, ptk = b.ps()
                                    ptb = ps_bf(pt)
                                    for cc in range(nc_):
                                        b.tr(ptb[:, cc * 128:(cc + 1) * 128], KQ[:, c0 + cc, 0, :], self.identb[:], r=[gk('KQ'), 'identb'], w=[ptk])
                                    b.cp('act', KTM[:, c0:c0 + nc_, :], ptb[:, 0:nc_ * 128].rearrange("p (a b) -> p a b", b=128), r=[ptk], w=[gk('KTM')])
                        else:
                            a_ = 0 if kind == 'v0' else 1
                            for c0 in range(0, NCH, 4):
                                nc_ = min(4, NCH - c0)
                                pt, ptk = b.ps()
                                ptb = ps_bf(pt)
                                for cc in range(nc_):
                                    b.tr(ptb[:, cc * 128:(cc + 1) * 128], TB[:, (c0 + cc) * 128:(c0 + cc + 1) * 128], self.identb[:], r=['TB', 'identb'], w=[ptk])
                                b.cp('act', VTM[:, c0:c0 + nc_, a_, :], ptb[:, 0:nc_ * 128].rearrange("p (a b) -> p a b", b=128), r=[ptk], w=[gk('VTM')])

                    for k_ in emit_proj(0):
                        b.release(k_)
                    for fi in range(4):
                        emit_conv(fi)
                        hk = emit_proj(fi + 1) if fi + 1 < 4 else []
                        emit_tail(fi)
                        for k_ in hk:
                            b.release(k_)
                with b.scope() as ses:
                    NL = GDN_LANES

                    def t4(nm, dt):
                        return b.sb(ses, "g_" + nm, [128, 4, 128], dt)
                    DG = b.sb(ses, "g_DG", [128, 4, 128], F32)
                    BGEg = b.sb(ses, "g_BGEg", [128, NCH, 4], F32)
                    C1 = t4("C1", F32)
                    TO = C1
                    DT = t4("DT", BF16)
                    Dm = t4("Dm", BF16)
                    KKn = t4("KKn", BF16)
                    QKs = b.sb(ses, "g_QKs", [128, 2, 128], BF16)
                    VN = t4("VN", BF16)
                    S = t4("S", F32)
                    Sbf = t4("Sbf", BF16)
                    VB = t4("VB", BF16)
                    KBG = t4("KBG", BF16)
                    KDEC = t4("KDEC", BF16)
                    NWT = t4("NWT", BF16)
                    lanes = []
                    for k in range(NL):
                        lanes.append({nm: t4("%s_l%d" % (nm, k), BF16) for nm in ['Ap', 'QKD', 'Y', 'W', 'X']})
                        lanes[-1]['k'] = k
                    b.op('pool', lambda e: e.memset(S[:], 0.0), w=['S'])
                    b.op('pool', lambda e: e.memset(Sbf[:], 0.0), w=['Sbf'])
                    for d in range(2):
                        b.tt('pool', BGEg[:, :, 2 * d:2 * d + 2], G['BETA'][:, :, gcols(d)], G['EG'][:, :, gcols(d)], ALU.mult, r=['BETA', 'EG'], w=['BGEg'])
                    owritten = set()
                    f2 = lambda ap: ap.rearrange("p a b -> p (a b)")
                    idb = ident.unsqueeze(1).to_broadcast([128, 2, 128])
                    nidb = self.nident.unsqueeze(1).to_broadcast([128, 2, 128])

                    def step_gen(s, Ln):
                        lk = lambda nm: (nm, Ln['k'])
                        Ap, QKD, Y, W, X = (Ln[nm] for nm in ['Ap', 'QKD', 'Y', 'W', 'X'])
                        cds = [FWD[s], BWD[s]]

                        def scale_ops(which):
                            for u in range(4):
                                d, a_ = u // 2, u % 2
                                cd = cds[d]
                                col = d * 16 + 2 * g + a_
                                if which == 'VB':
                                    b.act(VB[:, u, :], VTM[:, cd, a_, :], AF.Copy, scale=G['BETA'][:, cd, col:col + 1], r=[gk('VTM'), 'BETA'], w=['VB'])
                                elif which == 'KBG':
                                    b.act(KBG[:, u, :], KTM[:, cd, :], AF.Copy, scale=BGEg[:, cd, u:u + 1], r=[gk('KTM'), 'BGEg'], w=['KBG'])
                                else:
                                    b.act(KDEC[:, u, :], KTM[:, cd, :], AF.Copy, scale=G['KD'][:, cd, col:col + 1], r=[gk('KTM'), 'KD'], w=['KDEC'])
                        pkq, pkqk = b.ps(hold=True)
                        for d in range(2):
                            cd = cds[d]
                            b.mm(pkq[:, d * 256:(d + 1) * 256], KQ[:, cd, 0, :], KQ[:, cd, :, :].rearrange("p a b -> p (a b)"),
                                 r=[gk('KQ')], w=[pkqk])
                        for d in range(2):
                            cd = cds[d]
                            b.tt('pool', DG[:, 2 * d:2 * d + 2, :], idb, bc(G['GCUM'][:, cd, gcols(d)]), ALU.mult, r=['GCUM', 'cfa'], w=['DG'])
                        pe_, pek = b.ps(hold=True)
                        for u in range(4):
                            b.mm(pe_[:, u * 128:(u + 1) * 128], self.ones, DG[:, u, :], start=True, stop=False, r=['DG', 'cfa'], w=[pek])
                            b.mm(pe_[:, u * 128:(u + 1) * 128], DG[:, u, :], self.nones, start=False, stop=True, r=['DG', 'cfa'], w=[pek])
                        b.ts('dve', f2(C1[:]), pe_[:], 0.0, None, ALU.min, r=[pek], w=['C1'])
                        b.act(f2(DT[:]), f2(C1[:]), AF.Exp, r=['C1'], w=['DT'])
                        b.ts('dve', f2(C1[:]), pe_[:], 0.0, None, ALU.max, r=[pek], w=['C1'])
                        b.release(pek)
                        b.act(f2(Dm[:]), f2(C1[:]), AF.Exp, scale=-1.0, r=['C1'], w=['Dm'])
                        for d in range(2):
                            cd = cds[d]
                            kk = pkq[:, d * 256:d * 256 + 128].unsqueeze(1).to_broadcast([128, 2, 128])
                            b.tt('dve', KKn[:, 2 * d:2 * d + 2, :], kk, bc(G['NBETA'][:, cd, gcols(d)]), ALU.mult, r=[pkqk, 'NBETA'], w=['KKn'])
                        qkv_ = pkq[:].rearrange("p (d x) -> p d x", d=2)[:, :, 128:256]
                        b.tt('dve', QKs[:], qkv_, self.maskq.rearrange("p (d x) -> p d x", d=2), ALU.mult, r=[pkqk, 'cfa'], w=['QKs'])
                        b.release(pkqk)
                        yield
                        b.tt('pool', f2(Ap[:]), f2(KKn[:]), f2(Dm[:]), ALU.mult, r=['KKn', 'Dm'], w=[lk('Ap')])
                        b.tt('pool', QKD[:].rearrange("p (d a) x -> p d a x", d=2), QKs[:].unsqueeze(2).to_broadcast([128, 2, 2, 128]),
                             DT[:].rearrange("p (d a) x -> p d a x", d=2), ALU.mult, r=['QKs', 'DT'], w=[lk('QKD')])
                        ptt, pttk = b.ps(hold=True)
                        pttb = ps_bf(ptt)
                        for u in range(4):
                            b.tr(pttb[:, u * 128:(u + 1) * 128], Ap[:, u, :], self.identb[:], r=[lk('Ap'), 'identb'], w=[pttk])
                        b.cp('pool', Y[:], self.identb[:].unsqueeze(1).to_broadcast([128, 4, 128]), r=['identb'], w=[lk('Y')])
                        b.op('dve', lambda e, o=f2(Y[:]), m=f2(lvl[:, 0, :, :]), dd=pttb[:, 0:512]: e.copy_predicated(o, m, dd),
                             r=[pttk, 'lvl', lk('Y')], w=[lk('Y')])
                        b.release(pttk)
                        yield
                        for l in range(1, 7):
                            pw, pwk = b.ps(hold=True)
                            for u in range(4):
                                b.mm(pw[:, u * 128:(u + 1) * 128], Ap[:, u, :], Y[:, u, :], r=[lk('Ap'), lk('Y')], w=[pwk])
                            px, pxk = b.ps(hold=True)
                            pxb = ps_bf(px)
                            for u in range(4):
                                b.tr(pxb[:, u * 128:(u + 1) * 128], Y[:, u, :], self.identb[:], r=[lk('Y'), 'identb'], w=[pxk])
                            b.cp('act', f2(W[:]), pw[:], r=[pwk], w=[lk('W')])
                            b.cp('dve', f2(X[:]), pxb[:, 0:512], r=[pxk], w=[lk('X')])
                            b.release(pwk)
                            b.release(pxk)
                            yield
                            pz, pzk = b.ps(hold=True)
                            for u in range(4):
                                b.mm(pz[:, u * 128:(u + 1) * 128], X[:, u, :], W[:, u, :], r=[lk('X'), lk('W')], w=[pzk])
                            b.op('dve', lambda e, o=f2(Y[:]), m=f2(lvl[:, l, :, :]), dd=pz[:]: e.copy_predicated(o, m, dd),
                                 r=[pzk, 'lvl', lk('Y')], w=[lk('Y')])
                            b.release(pzk)
                            if l == 6:
                                scale_ops('KBG')
                            yield
                        pwt, pwtk = b.ps(hold=True)
                        for u in range(4):
                            b.mm(pwt[:, u * 128:(u + 1) * 128], KBG[:, u, :], Y[:, u, :], r=['KBG', lk('Y')], w=[pwtk])
                        b.act(f2(NWT[:]), pwt[:], AF.Copy, scale=-1.0, r=[pwtk], w=['NWT'])
                        b.release(pwtk)
                        scale_ops('VB')
                        yield
                        pvn, pvnk = b.ps(hold=True)
                        for u in range(4):
                            b.mm(pvn[:, u * 128:(u + 1) * 128], Y[:, u, :], VB[:, u, :], start=True, stop=False, r=[lk('Y'), 'VB'], w=[pvnk])
                            b.mm(pvn[:, u * 128:(u + 1) * 128], NWT[:, u, :], Sbf[:, u, :], start=False, stop=True, r=['NWT', 'Sbf'], w=[pvnk])
                        b.cp('act', f2(VN[:]), pvn[:], r=[pvnk], w=['VN'])
                        b.release(pvnk)
                        scale_ops('KDEC')
                        yield
                        pds, pdsk = b.ps(hold=True)
                        po1, po1k = b.ps(hold=True)
                        po2, po2k = b.ps(hold=True)
                        for u in range(4):
                            b.mm(pds[:, u * 128:(u + 1) * 128], KDEC[:, u, :], VN[:, u, :], r=['KDEC', 'VN'], w=[pdsk])
                        for u in range(4):
                            cd = cds[u // 2]
                            b.mm(po1[:, u * 128:(u + 1) * 128], KQ[:, cd, 1, :], Sbf[:, u, :], r=[gk('KQ'), 'Sbf'], w=[po1k])
                        for u in range(4):
                            b.mm(po2[:, u * 128:(u + 1) * 128], QKD[:, u, :], VN[:, u, :], r=[lk('QKD'), 'VN'], w=[po2k])
                        for d in range(2):
                            cd = cds[d]
                            b.tt('pool', S[:, 2 * d:2 * d + 2, :], S[:, 2 * d:2 * d + 2, :], bc(G['EGL'][:, cd, gcols(d)]), ALU.mult,
                                 r=['S', 'EGL'], w=['S'])
                        b.tt('dve', f2(S[:]), f2(S[:]), pds[:], ALU.add, r=['S', pdsk], w=['S'])
                        b.release(pdsk)
                        b.cp('act', f2(Sbf[:]), f2(S[:]), r=['S'], w=['Sbf'])
                        for d in range(2):
                            cd = cds[d]
                            b.tt('dve', TO[:, 2 * d:2 * d + 2, :], po1[:, d * 256:(d + 1) * 256].rearrange("p (a x) -> p a x", a=2),
                                 bc(G['EG'][:, cd, gcols(d)]), ALU.mult, r=[po1k, 'EG'], w=['C1'])
                        b.tt('dve', f2(TO[:]), f2(TO[:]), po2[:], ALU.add, r=['C1', po2k], w=['C1'])
                        b.release(po1k)
                        b.release(po2k)
                        for d in range(2):
                            cd = cds[d]
                            if cd not in owritten:
                                owritten.add(cd)
                                b.cp('act', O[:, cd, :, :], TO[:, 2 * d:2 * d + 2, :], r=['C1'], w=[gk('O')])
                            else:
                                b.tt('pool', O[:, cd, :, :], O[:, cd, :, :], TO[:, 2 * d:2 * d + 2, :], ALU.add, r=['C1', gk('O')], w=[gk('O')])
                        yield

                    gens = []
                    next_s = 0
                    turn = 0
                    stagger = max(1, (17 + NL - 1) // NL)
                    while next_s < NCH or gens:
                        if next_s < NCH and turn % stagger == 0 and len(gens) < NL:
                            gens.append(step_gen(next_s, lanes[next_s % NL]))
                            next_s += 1
                        for g_ in list(gens):
                            try:
                                next(g_)
                            except StopIteration:
                                gens.remove(g_)
                        turn += 1
            with b.scope() as oes:
                ZS = b.sb(oes, "g_ZS", [128, NCH, 256], BF16)
                wzs = b.sb(oes, "g_wzs", [128, KC, 256], F32)
                wzb = b.sb(oes, "g_wzb", [128, KC, 256], BF16)
                wos = b.sb(oes, "g_wos", [128, 2, D], F32)
                wob = b.sb(oes, "g_wob", [128, 2, D], BF16)
                OSQ = b.sb(oes, "g_OSQ", [128, 6, 2, 128], F32)
                SS = b.sb(oes, "g_SS", [128, NCH, 2], F32)
                YT = b.sb(oes, "g_YT", [128, 6, 2, 128], F32)
                YTb = b.sb(oes, "g_YTb", [128, 6, 2, 128], BF16)
                YF = b.sb(oes, "g_YF", [128, 2, T], BF16)
                zc0 = 4096 + 2 * g * 128
                b.dma(wzs[:], w_in[:, zc0:zc0 + 256].rearrange("(k p) n -> p k n", p=128), w=['wzs'])
                b.cp('pool', wzb[:], wzs[:], r=['wzs'], w=['wzb'])
                b.dma(wos[:], w_out[2 * g * 128:(2 * g + 2) * 128, :].rearrange("(k p) n -> p k n", p=128), w=['gwos'])
                b.cp('pool', wob[:], wos[:], r=['gwos'], w=['gwob'])
                for c0 in range(0, NCH, 2):
                    pz_, pzk_ = b.ps()
                    for cc in range(2):
                        c = c0 + cc
                        for kc in range(KC):
                            b.mm(pz_[:, cc * 256:(cc + 1) * 256], self.hb[:, kc, c * 128:(c + 1) * 128], wzb[:, kc, :],
                                 start=(kc == 0), stop=(kc == KC - 1), r=['wzb'] + h_keys(kc, ALLT), w=[pzk_])
                    b.act(ZS[:, c0:c0 + 2, :], pz_[:].rearrange("p (a b) -> p a b", a=2), AF.Silu, r=[pzk_], w=['ZS'])
                for c0 in range(0, NCH, 6):
                    Oc = O[:, c0:c0 + 6, :, :]
                    b.tt('pool', OSQ[:], Oc, Oc, ALU.mult, r=[gk('O')], w=['OSQ'])
                    b.op('dve', lambda e, o=SS[:, c0:c0 + 6, :], i=OSQ[:]: e.tensor_reduce(out=o, in_=i, axis=AX.X, op=ALU.add), r=['OSQ'], w=['SS'])
                b.act(SS[:], SS[:], AF.Ln, bias=eps_col, scale=1.0 / 128.0, r=['SS', 'cfa'], w=['SS'])
                b.act(SS[:], SS[:], AF.Exp, scale=-0.5, r=['SS'], w=['SS'])
                for c0 in range(0, NCH, 6):
                    Oc = O[:, c0:c0 + 6, :, :]
                    b.tt('dve', YT[:], Oc, SS[:, c0:c0 + 6, :].unsqueeze(3).to_broadcast([128, 6, 2, 128]), ALU.mult, r=[gk('O'), 'SS'], w=['YT'])
                    b.tt('pool', YT[:].rearrange("p a b c -> p (a b) c"), YT[:].rearrange("p a b c -> p (a b) c"),
                         normg[:].unsqueeze(1).to_broadcast([128, 12, 128]), ALU.mult, r=['YT', 'normg'], w=['YT'])
                    b.tt('dve', YTb[:], YT[:], ZS[:, c0:c0 + 6, :].rearrange("p a (b c) -> p a b c", b=2), ALU.mult, r=['YT', 'ZS'], w=['YTb'])
                    for a_ in range(2):
                        for q4 in range(0, 6, 4):
                            nq = min(4, 6 - q4)
                            pt, ptk = b.ps()
                            ptb = ps_bf(pt)
                            for cc in range(nq):
                                b.tr(ptb[:, cc * 128:(cc + 1) * 128], YTb[:, q4 + cc, a_, :], self.identb[:], r=['YTb', 'identb'], w=[ptk])
                            t0_ = (c0 + q4) * 128
                            b.cp('act', YF[:, a_, t0_:t0_ + nq * 128], ptb[:, 0:nq * 128], r=[ptk], w=['YF'])
                self.out_proj(wob, 'gwob', 2, lambda k, t0, n: YF[:, k, t0:t0 + n], lambda k, ti: ['YF'])


def host_consts():
    cfa = np.zeros((128, NCFA), np.float32)
    idx = np.arange(128)
    cfa[:, 0:128] = np.eye(128, dtype=np.float32)
    cfa[:, 128:256] = (idx[:, None] <= idx[None, :]).astype(np.float32)
    cfa[:, 256:384] = (idx[:, None] >= idx[None, :]).astype(np.float32)
    cfa[:, 384:512] = 1.0
    cfa[:, 512:640] = (idx[None, :] >= idx[:, None]).astype(np.float32)
    cfa[:, 640:768] = (idx[None, :] <= idx[:, None]).astype(np.float32)
    rot = np.zeros((128, 128), np.float32)
    for m in range(128):
        half = (m % 64) // 32
        if half == 0:
            rot[m + 32, m] = -1.0
        else:
            rot[m - 32, m] = 1.0
    cfa[:, 768:896] = rot
    cfa[:, 1152] = EPS
    cfa[:, 1153] = 128.0 * EPS
    cfa[:, 1154] = 1.0
    cfa[:, 896:1024] = -np.eye(128, dtype=np.float32)
    cfa[:, 1024:1152] = -1.0
    return cfa


def rope_host():
    rows = 2048 // 64
    row = np.repeat(np.arange(rows), 64).astype(np.float32)
    col = np.tile(np.arange(64), rows).astype(np.float32)
    n_freq = 32
    freqs = (np.float32(10000.0) ** (-np.arange(n_freq, dtype=np.float32) / np.float32(n_freq))).astype(np.float32)
    ang_r = row[:, None] * freqs
    ang_c = col[:, None] * freqs
    ang = np.concatenate([ang_r, ang_r, ang_c, ang_c], axis=-1).astype(np.float32)
    out = np.zeros((128, 4096), np.float32)
    out[:, 0:2048] = np.cos(ang).T
    out[:, 2048:4096] = np.sin(ang).T
    return out


_CACHE = {}


def get_prog(layers, nseq):
    key = (tuple(layers), nseq)
    if key not in _CACHE:
        nc = bass.Bass("TRN2", target_bir_lowering=False)
        p = Prog(nc, list(layers), nseq)
        p.build()
        _CACHE[key] = (nc, p)
    return _CACHE[key]


def layer_inputs(inp, li):
    d = {}
    d["w_mod%d" % li] = np.ascontiguousarray(inp["w_mod"][li])
    d["bmodT%d" % li] = np.ascontiguousarray(inp["b_mod"][li].reshape(48, 128).T)
    ln = np.stack([inp["ln_g"][li, 0], inp["ln_b"][li, 0], inp["ln_g"][li, 1], inp["ln_b"][li, 1]], 0)
    d["lnT%d" % li] = np.ascontiguousarray(ln.reshape(4, 8, 128).transpose(2, 0, 1))
    d["w_ffn_in%d" % li] = np.ascontiguousarray(inp["w_ffn_in"][li])
    d["w_ffn_out%d" % li] = np.ascontiguousarray(inp["w_ffn_out"][li])
    j = li // 2
    if li % 2 == 1:
        d["attn_w_qkv%d" % j] = np.ascontiguousarray(inp["attn_w_qkv"][j])
        d["attn_w_out%d" % j] = np.ascontiguousarray(inp["attn_w_out"][j])
        d["attn_gain%d" % j] = np.ascontiguousarray(np.stack([inp["attn_q_norm"][j], inp["attn_k_norm"][j]], 1))
        d["rope"] = rope_host()
    else:
        d["gdn_w_in%d" % j] = np.ascontiguousarray(inp["gdn_w_in"][j])
        d["gdn_w_out%d" % j] = np.ascontiguousarray(inp["gdn_w_out"][j])
        d["gdn_convT%d" % j] = np.ascontiguousarray(inp["gdn_conv"][j].reshape(5, 32, 128).transpose(2, 1, 0))
        gp = np.concatenate([inp["gdn_a_log"][j].reshape(32), inp["gdn_dt_bias"][j].reshape(32)])
        d["gdn_gpar%d" % j] = np.ascontiguousarray(np.broadcast_to(gp[None, :], (128, 64)).astype(np.float32))
        d["gdn_normg%d" % j] = np.ascontiguousarray(np.broadcast_to(inp["gdn_norm_g"][j][None, :], (128, 128)).astype(np.float32))
        d["lvlmask"] = lvlmask_host()
    return d


def lvlmask_host():
    p = np.arange(128)[:, None]
    f = np.arange(128)[None, :]
    m = np.zeros((128, 7, 4, 128), np.uint8)
    for l in range(7):
        if l == 0:
            blk = (p >> 1) == (f >> 1)
        else:
            blk = ((p >> (l + 1)) == (f >> (l + 1))) & ((p >> l) != (f >> l))
        fw = (blk & (f > p)).astype(np.uint8)
        bw = (blk & (f < p)).astype(np.uint8)
        m[:, l, 0, :] = fw
        m[:, l, 1, :] = fw
        m[:, l, 2, :] = bw
        m[:, l, 3, :] = bw
    return np.ascontiguousarray(m.reshape(128, 7 * 4 * 128))


def seq_fm(inp, bidx):
    return np.ascontiguousarray(np.concatenate([inp["ctx"][bidx], inp["x"][bidx]], 0).T)


def cT_host(inp, bidxs):
    v = np.stack([inp["c_ctx"]] + [inp["c"][bi] for bi in bidxs], 1)
    return np.ascontiguousarray(v.reshape(8, 128, len(bidxs) + 1).transpose(1, 0, 2))


def run_layers(inp, layers, xs):
    nc, p = get_prog(layers, 1)
    outs = [None] * 16
    cfa = host_consts()
    for rnd in range(2):
        in_maps = []
        for core in range(8):
            bidx = rnd * 8 + core
            m = {"cfa": cfa, "xin0": xs[bidx], "cTall": cT_host(inp, [bidx])}
            for li in layers:
                m.update(layer_inputs(inp, li))
            in_maps.append({k: v for k, v in m.items() if k in p.dram})
        res = run_bass_kernel_spmd(nc, in_maps, core_ids=list(range(8)))
        for core in range(8):
            outs[rnd * 8 + core] = res.results[core]["xout0"]
    return outs


def kernel_unfused(**inp):
    inp = {k: np.asarray(v) for k, v in inp.items()}
    xs = [seq_fm(inp, bi) for bi in range(16)]
    for li in range(DEPTH):
        xs = run_layers(inp, [li], xs)
    out = np.stack([x.T[256:, :] for x in xs], 0)
    return np.ascontiguousarray(out.astype(np.float32))


def kernel(**inp):
    inp = {k: np.asarray(v) for k, v in inp.items()}
    layers = list(range(DEPTH))
    nc, p = get_prog(layers, 2)
    cfa = host_consts()
    shared = {"cfa": cfa}
    for li in layers:
        shared.update(layer_inputs(inp, li))
    shared = {k: v for k, v in shared.items() if k in p.dram}
    in_maps = []
    for core in range(8):
        m = dict(shared)
        for s in range(2):
            bidx = 2 * core + s
            m["xin%d" % s] = seq_fm(inp, bidx)
        m["cTall"] = cT_host(inp, [2 * core, 2 * core + 1])
        in_maps.append(m)
    res = run_bass_kernel_spmd(nc, in_maps, core_ids=list(range(8)))
    out = np.zeros((16, 2048, 1024), np.float32)
    for core in range(8):
        for s in range(2):
            out[2 * core + s] = res.results[core]["xout%d" % s].T[256:, :]
    return out
```

```python
import numpy as np
from contextlib import ExitStack
import concourse.bass as bass
import concourse.mybir as mybir
from concourse.bass_utils import run_bass_kernel_spmd

F32 = mybir.dt.float32
BF16 = mybir.dt.bfloat16
U8 = mybir.dt.uint8
AF = mybir.ActivationFunctionType
ALU = mybir.AluOpType
AX = mybir.AxisListType

D = 1024
KC = 8
T = 2304
NCH = 18
DEPTH = 4
DFF = 2816
EPS = 1e-6
ALPHA = 8.0 ** 0.25
TILES = [(0, 256), (256, 512), (768, 512), (1280, 512), (1792, 512)]
FWD = list(range(18))
BWD = [1, 0] + list(range(17, 1, -1))
NCFA = 1156
DEBUG_ALLOC = False
GDN_LANES = 5
DBG = {}


def which_of(ti):
    return 0 if ti == 0 else 1


class B:
    def __init__(self, nc):
        self.nc = nc
        self.es = ExitStack()
        self.E = ['pe', 'act', 'dve', 'pool', 'sp']
        self.q = {e: [] for e in self.E}
        self.cnt = {e: 0 for e in self.E}
        self.sem = {e: self.es.enter_context(nc.semaphore("s_" + e)) for e in self.E if e != 'sp'}
        self.ND = 16
        self.dsem = [self.es.enter_context(nc.semaphore("d%d" % i)) for i in range(self.ND)]
        self.dcnt = [0] * self.ND
        self.drr = 0
        self.waited = {e: {} for e in self.E}
        self.track = {}
        self.ninstr = 0
        self.psb = [self.es.enter_context(nc.psum_tensor("ps%d" % i, [128, 512], F32)) for i in range(8)]
        self.psrr = 0

    def _semh(self, sk):
        return self.sem[sk] if isinstance(sk, str) else self.dsem[sk[1]]

    def _deps(self, eng, r, w):
        raw = {}
        oth = {}

        def need(dct, idv):
            if idv is None:
                return
            sk, v = idv
            if dct.get(sk, 0) < v:
                dct[sk] = v
        for key in r:
            t = self.track.get(key)
            if t:
                need(raw, t[0])
        for key in w:
            t = self.track.get(key)
            if t:
                need(oth, t[0])
                for sk, v in t[1].items():
                    need(oth, (sk, v))
        for sk, v in oth.items():
            if sk == eng and eng == 'pe':
                continue
            if raw.get(sk, 0) < v:
                raw[sk] = v
        for sk, v in raw.items():
            if self.waited[eng].get(sk, 0) >= v:
                continue
            self.waited[eng][sk] = v
            sh = self._semh(sk)
            self.q[eng].append(lambda e, sh=sh, v=v: e.wait_ge(sh, v))
            self.ninstr += 1

    def _mark(self, idv, r, w):
        for key in w:
            self.track[key] = [idv, {}]
        for key in r:
            t = self.track.setdefault(key, [None, {}])
            if t[1].get(idv[0], 0) < idv[1]:
                t[1][idv[0]] = idv[1]

    def op(self, eng, fn, r=(), w=()):
        self._deps(eng, r, w)
        self.cnt[eng] += 1
        sh = self.sem[eng]
        self.q[eng].append(lambda e, fn=fn, sh=sh: fn(e).then_inc(sh, 1))
        self.ninstr += 1
        self._mark((eng, self.cnt[eng]), r, w)

    def dma(self, out, in_, r=(), w=()):
        eng = 'sp'
        k = self.drr
        self.drr = (self.drr + 1) % self.ND
        self._deps(eng, r, w)
        prev = 16 * self.dcnt[k]
        if prev and self.waited[eng].get(('d', k), 0) < prev:
            self.waited[eng][('d', k)] = prev
            self.q[eng].append(lambda e, sh=self.dsem[k], v=prev: e.wait_ge(sh, v))
        self.dcnt[k] += 1
        val = 16 * self.dcnt[k]
        self.q[eng].append(lambda e, out=out, in_=in_, sh=self.dsem[k]: e.dma_start(out=out, in_=in_).then_inc(sh, 16))
        self.ninstr += 1
        idv = (('d', k), val)
        self._mark(idv, r, w)
        return idv

    def barrier(self):
        for e in self.E:
            for o in self.E:
                if o == 'sp' or o == e:
                    continue
                v = self.cnt[o]
                if v and self.waited[e].get(o, 0) < v:
                    self.waited[e][o] = v
                    self.q[e].append(lambda en, sh=self.sem[o], v=v: en.wait_ge(sh, v))
                    self.ninstr += 1
            for k in range(self.ND):
                v = 16 * self.dcnt[k]
                if v and self.waited[e].get(('d', k), 0) < v:
                    self.waited[e][('d', k)] = v
                    self.q[e].append(lambda en, sh=self.dsem[k], v=v: en.wait_ge(sh, v))
                    self.ninstr += 1

    def scope(self):
        return _Scope(self)

    def new_epoch(self):
        self.barrier()
        self.nep = getattr(self, 'nep', 0) + 1
        for e in self.E:
            if e == 'sp':
                continue
            self.sem[e] = self.es.enter_context(self.nc.semaphore("s_%s_%d" % (e, self.nep)))
            self.cnt[e] = 0
        for e in self.E:
            for o in list(self.waited[e].keys()):
                if isinstance(o, str):
                    del self.waited[e][o]
        self.track = {}

    def ps(self, hold=False):
        if not hasattr(self, 'held'):
            self.held = set()
        while True:
            k = self.psrr
            self.psrr = (self.psrr + 1) % 8
            if k not in self.held:
                break
        if hold:
            self.held.add(k)
        return self.psb[k], ('ps', k)

    def release(self, key):
        self.held.discard(key[1])

    def mm(self, out, lhsT, rhs, start=True, stop=True, r=(), w=()):
        self.op('pe', lambda e: e.matmul(out, lhsT, rhs, start=start, stop=stop), r, w)

    def tr(self, out, in_, ident, r=(), w=()):
        self.op('pe', lambda e: e.transpose(out, in_, ident), r, w)

    def act(self, out, in_, func, bias=None, scale=None, r=(), w=()):
        kw = {}
        if bias is not None:
            kw['bias'] = bias
        if scale is not None:
            kw['scale'] = scale
        self.op('act', lambda e: e.activation(out=out, in_=in_, func=func, **kw), r, w)

    def tt(self, eng, out, a, b, op, r=(), w=()):
        self.op(eng, lambda e: e.tensor_tensor(out=out, in0=a, in1=b, op=op), r, w)

    def ts(self, eng, out, a, s1, s2, op0, op1=None, r=(), w=()):
        if op1 is None:
            self.op(eng, lambda e: e.tensor_scalar(out=out, in0=a, scalar1=s1, scalar2=None, op0=op0), r, w)
        else:
            self.op(eng, lambda e: e.tensor_scalar(out=out, in0=a, scalar1=s1, scalar2=s2, op0=op0, op1=op1), r, w)

    def stt(self, eng, out, a, s, b, op0, op1, r=(), w=()):
        self.op(eng, lambda e: e.scalar_tensor_tensor(out=out, in0=a, scalar=s, in1=b, op0=op0, op1=op1), r, w)

    def cp(self, eng, out, in_, r=(), w=()):
        if eng == 'act':
            self.act(out, in_, AF.Copy, r=r, w=w)
        else:
            self.op(eng, lambda e: e.tensor_copy(out=out, in_=in_), r, w)

    def sb(self, es, name, shape, dt):
        if DEBUG_ALLOC:
            print("alloc", name, shape, dt, "remaining", self.nc.sbuf_bytes_remaining)
        self.uid = getattr(self, "uid", 0) + 1
        return es.enter_context(self.nc.sbuf_tensor("sb%d_%s" % (self.uid, name), shape, dt))

    def finish(self):
        nc = self.nc
        q = self.q
        with nc.Block() as block:
            @block.tensor
            def _(e):
                for f in q['pe']:
                    f(e)

            @block.scalar
            def _(e):
                for f in q['act']:
                    f(e)

            @block.vector
            def _(e):
                for f in q['dve']:
                    f(e)

            @block.gpsimd
            def _(e):
                for f in q['pool']:
                    f(e)

            @block.sync
            def _(e):
                for f in q['sp']:
                    f(e)
        self.es.close()


class _Scope:
    def __init__(self, b):
        self.b = b
        self.es = ExitStack()

    def __enter__(self):
        self.es.__enter__()
        return self.es

    def __exit__(self, *a):
        self.b.barrier()
        return self.es.__exit__(*a)


def xa_keys(c, tis):
    return [('xa', c, ti) for ti in tis]


def h_keys(c, tis):
    return [('h', c, ti) for ti in tis]


ALLT = list(range(5))


class Prog:
    def __init__(self, nc, layers, nseq):
        self.nc = nc
        self.b = B(nc)
        self.layers = layers
        self.nseq = nseq
        self.dram = {}

    def din(self, name, shape, dt=F32):
        if name not in self.dram:
            self.dram[name] = self.nc.dram_tensor(name, list(shape), dt, kind="ExternalInput").ap()
        return self.dram[name]

    def dout(self, name, shape, dt=F32):
        if name not in self.dram:
            self.dram[name] = self.nc.dram_tensor(name, list(shape), dt, kind="ExternalOutput").ap()
        return self.dram[name]

    def build(self):
        b = self.b
        nc = self.nc
        es = b.es
        self.xa = b.sb(es, "xa", [128, KC, T], F32)
        self.hb = b.sb(es, "hb", [128, KC, T], BF16)
        self.cfa = b.sb(es, "cfa", [128, NCFA], F32)
        self.identb = b.sb(es, "identb", [128, 128], BF16)
        self.onesb = b.sb(es, "onesb", [128, 128], BF16)
        self.mean1k = b.sb(es, "mean1k", [128, 128], F32)
        self.mean1kb = b.sb(es, "mean1kb", [128, 128], BF16)
        cfa_d = self.din("cfa", [128, NCFA])
        b.dma(self.cfa[:], cfa_d, w=['cfa'])
        self.ident = self.cfa[:, 0:128]
        self.trif = self.cfa[:, 128:256]
        self.trib = self.cfa[:, 256:384]
        self.ones = self.cfa[:, 384:512]
        self.maskq = self.cfa[:, 512:768]
        self.rot = self.cfa[:, 768:896]
        self.epsv = self.cfa[:, 1152:1155]
        self.nident = self.cfa[:, 896:1024]
        self.nones = self.cfa[:, 1024:1152]
        b.cp('act', self.identb[:], self.ident, r=['cfa'], w=['identb'])
        b.cp('act', self.onesb[:], self.ones, r=['cfa'], w=['onesb'])
        b.act(self.mean1k[:], self.ones, AF.Copy, scale=1.0 / 1024.0, r=['cfa'], w=['mean1k'])
        b.act(self.mean1kb[:], self.ones, AF.Copy, scale=1.0 / 1024.0, r=['cfa'], w=['mean1k'])
        self.mod_all(es)
        outs = []
        for s in range(self.nseq):
            xin = self.din("xin%d" % s, [D, T])
            xout = self.dout("xout%d" % s, [D, T])
            with b.scope() as les:
                stg = [b.sb(les, "ldst%d" % i, [128, T], F32) for i in range(2)]
                for c in range(KC):
                    st = stg[c % 2]
                    b.dma(st[:], xin[c * 128:(c + 1) * 128, :], w=[('ldst', c % 2)])
                    b.act(self.xa[:, c, :], st[:], AF.Copy, scale=ALPHA, r=[('ldst', c % 2)], w=xa_keys(c, ALLT))
            for n, li in enumerate(self.layers):
                last = (n == len(self.layers) - 1)
                b.new_epoch()
                self.layer(li, s, out_plain=last)
            for c in range(KC):
                outs.append(b.dma(xout[c * 128:(c + 1) * 128, :], self.xa[:, c, :], r=xa_keys(c, ALLT)))
        for (sk, v) in outs + getattr(self, 'dbg_outs', []):
            if b.waited['sp'].get(sk, 0) < v:
                b.waited['sp'][sk] = v
                b.q['sp'].append(lambda e, sh=b.dsem[sk[1]], v=v: e.wait_ge(sh, v))
        b.finish()

    def layer(self, li, s, out_plain):
        b = self.b
        with b.scope() as les:
            self.lv = {}
            stop = DBG.get('stop')
            self.modulation(li, s, les, out_plain)
            if DBG.get('dump'):
                self.dbg_outs = getattr(self, 'dbg_outs', [])
                self.dbg_outs.append(b.dma(self.dout("dbg_MOD", [128, 96]), self.P['MOD'][:].rearrange("p a b -> p (a b)"), r=['MOD']))
            if stop == 'mod':
                return
            self.modulate_in()
            if DBG.get('dump'):
                self.dbg_outs.append(b.dma(self.dout("dbg_h", [128, KC * T], BF16), self.hb[:].rearrange("p a b -> p (a b)"),
                                           r=[('h', c, ti) for c in range(KC) for ti in ALLT]))
            if stop == 'modin':
                return
            if li % 2 == 0:
                self.gdn(li)
            else:
                self.attn(li)
            if stop == 'mixer':
                return
            self.layernorm(0, want_h=True)
            if stop == 'ln0':
                return
            self.ffn(li)
            if stop == 'ffn':
                return
            self.layernorm(1, want_h=False)

    def mod_all(self, es):
        b = self.b
        ns = 1 + self.nseq
        self.MODall = {}
        for li in self.layers:
            self.MODall[li] = b.sb(es, "MODall%d" % li, [128, 48, ns], F32)
        with b.scope() as wes:
            sc = b.sb(wes, "ma_sc", [128, KC, ns], F32)
            bm = b.sb(wes, "ma_bm", [128, 48], F32)
            wst = [b.sb(wes, "ma_wst%d" % i, [128, KC, 512], F32) for i in range(3)]
            cT = self.din("cTall", [128, KC, ns])
            b.dma(sc[:], cT, w=['sc'])
            b.act(sc[:], sc[:], AF.Silu, r=['sc'], w=['sc'])
            nblk = 0
            for li in self.layers:
                w_mod = self.din("w_mod%d" % li, [D, 6 * D])
                bmodT = self.din("bmodT%d" % li, [128, 48])
                b.dma(bm[:], bmodT, w=['bmodT'])
                ps, pk = b.ps(hold=True)
                for nb in range(12):
                    st = wst[nblk % 3]
                    sk = ('wmst', nblk % 3)
                    nblk += 1
                    b.dma(st[:], w_mod[:, nb * 512:(nb + 1) * 512].rearrange("(k p) n -> p k n", p=128), w=[sk])
                    for f in range(4):
                        fc = nb * 4 + f
                        for kc in range(KC):
                            b.mm(ps[:, fc * ns:(fc + 1) * ns], st[:, kc, f * 128:(f + 1) * 128], sc[:, kc, :],
                                 start=(kc == 0), stop=(kc == KC - 1), r=[sk, 'sc'], w=[pk])
                b.tt('dve', self.MODall[li][:], ps[:, 0:48 * ns].rearrange("p (a b) -> p a b", b=ns),
                     bm[:].unsqueeze(2).to_broadcast([128, 48, ns]), ALU.add, r=[pk, 'bmodT'], w=[('MODall', li)])
                b.release(pk)

    def modulation(self, li, s, les, out_plain):
        b = self.b
        P = {}
        for nm, shape in [('MOD', [128, 48, 2]), ('s1', [128, 8, 2]), ('H1s', [128, 8, 2]), ('H1b', [128, 8, 2]),
                          ('A', [128, 2, 8]), ('Bv', [128, 2, 8]), ('lnT', [128, 4, 8]), ('bmodT', [128, 48]),
                          ('sc', [128, 8, 2]), ('tmp1', [128, 8, 2])]:
            P[nm] = b.sb(les, "m_" + nm, shape, F32)
        self.P = P
        lnT = self.din("lnT%d" % li, [128, 4, 8])
        b.dma(P['lnT'][:], lnT, w=['lnT'])
        MOD = P['MOD']
        MA = self.MODall[li]
        b.cp('dve', MOD[:, :, 0:1], MA[:, :, 0:1], r=[('MODall', li)], w=['MOD'])
        b.cp('dve', MOD[:, :, 1:2], MA[:, :, 1 + s:2 + s], r=[('MODall', li)], w=['MOD'])

        def mj(j):
            return MOD[:, j * 8:(j + 1) * 8, :]
        b.ts('dve', P['s1'][:], mj(1), 1.0, 1.0 / ALPHA, ALU.add, ALU.mult, r=['MOD'], w=['s1'])
        ln = P['lnT']
        g0 = ln[:, 0, :].unsqueeze(2).to_broadcast([128, 8, 2])
        b0 = ln[:, 1, :].unsqueeze(2).to_broadcast([128, 8, 2])
        b.ts('dve', P['tmp1'][:], mj(4), 1.0, None, ALU.add, r=['MOD'], w=['tmp1'])
        b.tt('dve', P['H1s'][:], P['tmp1'][:], g0, ALU.mult, r=['tmp1', 'lnT'], w=['H1s'])
        b.tt('dve', P['H1b'][:], P['tmp1'][:], b0, ALU.mult, r=['tmp1', 'lnT'], w=['H1b'])
        b.tt('dve', P['H1b'][:], P['H1b'][:], mj(3), ALU.add, r=['H1b', 'MOD'], w=['H1b'])
        b.ts('dve', P['A'][:, 0, :], ln[:, 0, :], ALPHA, None, ALU.mult, r=['lnT'], w=['A'])
        b.ts('dve', P['Bv'][:, 0, :], ln[:, 1, :], ALPHA, None, ALU.mult, r=['lnT'], w=['Bv'])
        a2 = 1.0 if out_plain else ALPHA
        b.ts('dve', P['A'][:, 1, :], ln[:, 2, :], a2, None, ALU.mult, r=['lnT'], w=['A'])
        b.ts('dve', P['Bv'][:, 1, :], ln[:, 3, :], a2, None, ALU.mult, r=['lnT'], w=['Bv'])
        self.ga = mj(2)
        self.gaf = mj(5)
        self.sh = mj(0)

    def modulate_in(self):
        b = self.b
        P = self.P
        for c in range(KC):
            for (wh, t0, n, tis) in [(0, 0, 256, [0]), (1, 256, 2048, [1, 2, 3, 4])]:
                b.act(self.hb[:, c, t0:t0 + n], self.xa[:, c, t0:t0 + n], AF.Identity,
                      bias=self.sh[:, c, wh:wh + 1], scale=P['s1'][:, c, wh:wh + 1],
                      r=xa_keys(c, tis) + ['MOD', 's1'], w=h_keys(c, tis))

    def layernorm(self, idx, want_h):
        b = self.b
        P = self.P
        with b.scope() as es:
            sq = [b.sb(es, "ln_sq%d" % i, [128, 512], BF16) for i in range(2)]
            msb2 = [b.sb(es, "ln_msb%d" % i, [128, 512], F32) for i in range(2)]
            m22 = [b.sb(es, "ln_m2%d" % i, [128, 512], F32) for i in range(2)]
            rstd2 = [b.sb(es, "ln_rstd%d" % i, [128, 512], F32) for i in range(2)]
            tt_ = [b.sb(es, "ln_t%d" % i, [128, 512], F32) for i in range(2)]
            nsq = [0]

            def stats(ti):
                t0, n = TILES[ti]
                q_ = ti % 2
                msb, m2, rstd = msb2[q_], m22[q_], rstd2[q_]
                pm, pmk = b.ps(hold=True)
                pe2, pe2k = b.ps(hold=True)
                for c in range(KC):
                    k_ = nsq[0] % 2
                    nsq[0] += 1
                    b.act(sq[k_][:, :n], self.xa[:, c, t0:t0 + n], AF.Square, r=[('xa', c, ti)], w=[('lnsq', k_)])
                    b.mm(pm[:, :n], self.mean1k[:], self.xa[:, c, t0:t0 + n], start=(c == 0), stop=(c == KC - 1),
                         r=[('xa', c, ti), 'mean1k'], w=[pmk])
                    b.mm(pe2[:, :n], self.mean1kb[:], sq[k_][:, :n], start=(c == 0), stop=(c == KC - 1),
                         r=[('lnsq', k_), 'mean1k'], w=[pe2k])
                b.cp('act', msb[:, :n], pm[:, :n], r=[pmk], w=[('lnmsb', q_)])
                b.release(pmk)
                b.tt('dve', m2[:, :n], msb[:, :n], msb[:, :n], ALU.mult, r=[('lnmsb', q_)], w=[('lnm2', q_)])
                b.tt('dve', m2[:, :n], pe2[:, :n], m2[:, :n], ALU.subtract, r=[pe2k, ('lnm2', q_)], w=[('lnm2', q_)])
                b.release(pe2k)
                b.act(m2[:, :n], m2[:, :n], AF.Ln, bias=self.epsv[:, 0:1], r=[('lnm2', q_), 'cfa'], w=[('lnm2', q_)])
                b.act(rstd[:, :n], m2[:, :n], AF.Exp, scale=-0.5, r=[('lnm2', q_)], w=[('lnrstd', q_)])

            def norm(ti):
                t0, n = TILES[ti]
                wh = which_of(ti)
                q_ = ti % 2
                msb, rstd = msb2[q_], rstd2[q_]
                for c in range(KC):
                    t = tt_[c % 2]
                    tk = ('lnt', c % 2)
                    b.tt('dve', t[:, :n], self.xa[:, c, t0:t0 + n], msb[:, :n], ALU.subtract, r=[('xa', c, ti), ('lnmsb', q_)], w=[tk])
                    b.tt('dve', t[:, :n], t[:, :n], rstd[:, :n], ALU.mult, r=[tk, ('lnrstd', q_)], w=[tk])
                    b.act(self.xa[:, c, t0:t0 + n], t[:, :n], AF.Identity, bias=P['Bv'][:, idx, c:c + 1],
                          scale=P['A'][:, idx, c:c + 1], r=[tk, 'A', 'Bv'], w=[('xa', c, ti)])
                    if want_h:
                        b.ts('pool', self.hb[:, c, t0:t0 + n], t[:, :n], P['H1s'][:, c, wh:wh + 1], P['H1b'][:, c, wh:wh + 1],
                             ALU.mult, ALU.add, r=[tk, 'H1s', 'H1b'], w=[('h', c, ti)])
            stats(0)
            for ti in range(len(TILES)):
                if ti + 1 < len(TILES):
                    stats(ti + 1)
                norm(ti)

    def ffn(self, li):
        b = self.b
        w_in = self.din("w_ffn_in%d" % li, [D, 2 * DFF])
        w_out = self.din("w_ffn_out%d" % li, [DFF, D])
        with b.scope() as es:
            wis = [b.sb(es, "f_wis%d" % i, [128, KC, 512], F32) for i in range(2)]
            wos = [b.sb(es, "f_wos%d" % i, [128, 2, D], F32) for i in range(2)]
            wib = [b.sb(es, "f_wib%d" % i, [128, KC, 512], BF16) for i in range(2)]
            wob = [b.sb(es, "f_wob%d" % i, [128, 2, D], BF16) for i in range(2)]
            actb = b.sb(es, "f_act", [128, 2, T], BF16)
            sg = [b.sb(es, "f_sg%d" % i, [128, 512], F32) for i in range(2)]
            nsg = 0
            for fb in range(11):
                p = fb % 2
                f0 = fb * 256
                b.dma(wis[p][:, :, 0:256], w_in[:, f0:f0 + 256].rearrange("(k p) n -> p k n", p=128), w=[('wis', p, 0)])
                b.dma(wis[p][:, :, 256:512], w_in[:, DFF + f0:DFF + f0 + 256].rearrange("(k p) n -> p k n", p=128), w=[('wis', p, 1)])
                b.dma(wos[p][:], w_out[f0:f0 + 256, :].rearrange("(k p) n -> p k n", p=128), w=[('wos', p)])
                b.cp('pool', wib[p][:], wis[p][:], r=[('wis', p, 0), ('wis', p, 1)], w=[('wib', p)])
                b.cp('pool', wob[p][:], wos[p][:], r=[('wos', p)], w=[('wob', p)])
                for ti, (t0, n) in enumerate(TILES):
                    for j in range(2):
                        pg, pgk = b.ps()
                        pu, puk = b.ps()
                        for kc in range(KC):
                            b.mm(pg[:, :n], wib[p][:, kc, j * 128:(j + 1) * 128], self.hb[:, kc, t0:t0 + n],
                                 start=(kc == 0), stop=(kc == KC - 1), r=[('wib', p), ('h', kc, ti)], w=[pgk])
                        for kc in range(KC):
                            b.mm(pu[:, :n], wib[p][:, kc, 256 + j * 128:256 + (j + 1) * 128], self.hb[:, kc, t0:t0 + n],
                                 start=(kc == 0), stop=(kc == KC - 1), r=[('wib', p), ('h', kc, ti)], w=[puk])
                        s_ = sg[nsg % 2]
                        sk = ('fsg', nsg % 2)
                        nsg += 1
                        b.act(s_[:, :n], pg[:, :n], AF.Silu, r=[pgk], w=[sk])
                        b.tt('dve', actb[:, j, t0:t0 + n], pu[:, :n], s_[:, :n], ALU.mult, r=[puk, sk], w=[('fact', j, ti)])
                for ti, (t0, n) in enumerate(TILES):
                    wh = which_of(ti)
                    for oc in range(KC):
                        po, pok = b.ps()
                        for j in range(2):
                            b.mm(po[:, :n], wob[p][:, j, oc * 128:(oc + 1) * 128], actb[:, j, t0:t0 + n],
                                 start=(j == 0), stop=(j == 1), r=[('wob', p), ('fact', j, ti)], w=[pok])
                        b.stt('dve', self.xa[:, oc, t0:t0 + n], po[:, :n], self.gaf[:, oc, wh:wh + 1], self.xa[:, oc, t0:t0 + n],
                              ALU.mult, ALU.add, r=[pok, 'MOD', ('xa', oc, ti)], w=[('xa', oc, ti)])

    def out_proj(self, wb, wkey, nk, yfn, ykeys):
        b = self.b
        for ti, (t0, n) in enumerate(TILES):
            wh = which_of(ti)
            for oc in range(KC):
                po, pok = b.ps()
                for k in range(nk):
                    b.mm(po[:, :n], wb[:, k, oc * 128:(oc + 1) * 128], yfn(k, t0, n), start=(k == 0), stop=(k == nk - 1),
                         r=[wkey] + ykeys(k, ti), w=[pok])
                b.stt('dve', self.xa[:, oc, t0:t0 + n], po[:, :n], self.ga[:, oc, wh:wh + 1], self.xa[:, oc, t0:t0 + n],
                      ALU.mult, ALU.add, r=[pok, 'MOD', ('xa', oc, ti)], w=[('xa', oc, ti)])

    def attn(self, li):
        b = self.b
        j = li // 2
        w_qkv = self.din("attn_w_qkv%d" % j, [D, 1536])
        w_o = self.din("attn_w_out%d" % j, [D, D])
        gains = self.din("attn_gain%d" % j, [128, 2])
        ropeD = self.din("rope", [128, 4096])
        with b.scope() as es:
            QR = b.sb(es, "a_QR", [128, 8, T], BF16)
            KR = b.sb(es, "a_KR", [128, 2, T], BF16)
            VT = b.sb(es, "a_VT", [128, NCH, 256], BF16)
            gn = b.sb(es, "a_gn", [128, 2], F32)
            b.dma(gn[:], gains, w=['gn'])
            b.ts('dve', gn[:], gn[:], float(np.sqrt(128.0)), None, ALU.mult, r=['gn'], w=['gn'])
            with b.scope() as es2:
                rope = b.sb(es2, "a_rope", [128, 4096], F32)
                b.dma(rope[:], ropeD, w=['rope'])
                cosT = rope[:, 0:2048]
                sinT = rope[:, 2048:4096]
                wst_ = b.sb(es2, "a_wst", [128, KC, 128], F32)
                wst = [wst_, wst_]
                wbf = [b.sb(es2, "a_wbf%d" % i, [128, KC, 128], BF16) for i in range(2)]
                sqb = b.sb(es2, "a_sq", [128, 512], BF16)
                rs = b.sb(es2, "a_rs", [128, 512], F32)
                qn = b.sb(es2, "a_qn", [128, 512], F32)
                t1 = b.sb(es2, "a_t1", [128, 512], F32)
                t2 = b.sb(es2, "a_t2", [128, 512], F32)
                sqb2 = [sqb, b.sb(es2, "a_sq2", [128, 512], BF16)]
                qn2 = [qn, b.sb(es2, "a_qn2", [128, 512], F32)]

                def load_w(wbk):
                    p = wbk % 2
                    b.dma(wst[p][:], w_qkv[:, wbk * 128:(wbk + 1) * 128].rearrange("(k p) n -> p k n", p=128), w=[('awst', 0)])
                    b.cp('pool', wbf[p][:], wst[p][:], r=[('awst', 0)], w=[('awbf', p)])
                items = [(wbk, ti) for wbk in range(10) for ti in range(5)]
                st = {}

                def stA(i):
                    wbk, ti = items[i]
                    t0, n = TILES[ti]
                    p = wbk % 2
                    if ti == 0:
                        load_w(wbk)
                    pp, ppk = b.ps(hold=True)
                    for kc in range(KC):
                        b.mm(pp[:, :n], wbf[p][:, kc, :], self.hb[:, kc, t0:t0 + n],
                             start=(kc == 0), stop=(kc == KC - 1), r=[('awbf', p), ('h', kc, ti)], w=[ppk])
                    b.act(sqb2[i % 2][:, :n], pp[:, :n], AF.Square, r=[ppk], w=[('asq', i % 2)])
                    st[i] = (pp, ppk)

                def stB(i):
                    wbk, ti = items[i]
                    t0, n = TILES[ti]
                    pp, ppk = st.pop(i)
                    isq = wbk < 8
                    hidx = wbk if isq else wbk - 8
                    dst = QR if isq else KR
                    gcol = gn[:, 0:1] if isq else gn[:, 1:2]
                    p2, p2k = b.ps(hold=True)
                    b.mm(p2[:, :n], self.onesb[:], sqb2[i % 2][:, :n], r=[('asq', i % 2), 'onesb'], w=[p2k])
                    b.act(rs[:, :n], p2[:, :n], AF.Ln, bias=self.epsv[:, 1:2], r=[p2k, 'cfa'], w=['ars'])
                    b.release(p2k)
                    b.act(rs[:, :n], rs[:, :n], AF.Exp, scale=-0.5, r=['ars'], w=['ars'])
                    if ti == 0:
                        b.stt('dve', dst[:, hidx, t0:t0 + n], pp[:, :n], gcol, rs[:, :n], ALU.mult, ALU.mult,
                              r=[ppk, 'gn', 'ars'], w=[('aqk', isq, hidx, ti)])
                    else:
                        b.stt('dve', qn2[i % 2][:, :n], pp[:, :n], gcol, rs[:, :n], ALU.mult, ALU.mult,
                              r=[ppk, 'gn', 'ars'], w=[('aqn', i % 2)])
                    b.release(ppk)

                def stC(i):
                    wbk, ti = items[i]
                    if ti == 0:
                        return
                    t0, n = TILES[ti]
                    isq = wbk < 8
                    hidx = wbk if isq else wbk - 8
                    dst = QR if isq else KR
                    qn_ = qn2[i % 2]
                    p3, p3k = b.ps(hold=True)
                    b.mm(p3[:, :n], self.rot, qn_[:, :n], r=[('aqn', i % 2), 'cfa'], w=[p3k])
                    l0 = t0 - 256
                    b.tt('dve', t1[:, :n], qn_[:, :n], cosT[:, l0:l0 + n], ALU.mult, r=[('aqn', i % 2), 'rope'], w=['at1'])
                    b.tt('dve', t2[:, :n], p3[:, :n], sinT[:, l0:l0 + n], ALU.mult, r=[p3k, 'rope'], w=['at2'])
                    b.release(p3k)
                    b.tt('pool', dst[:, hidx, t0:t0 + n], t1[:, :n], t2[:, :n], ALU.add, r=['at1', 'at2'],
                         w=[('aqk', isq, hidx, ti)])
                nit = len(items)
                for i in range(nit + 2):
                    if i < nit:
                        stA(i)
                    if 0 <= i - 1 < nit:
                        stB(i - 1)
                    if 0 <= i - 2 < nit:
                        stC(i - 2)
                for wbk in (10, 11):
                    p = wbk % 2
                    load_w(wbk)
                    kvh = wbk - 10
                    for c in range(NCH):
                        pv, pvk = b.ps()
                        for kc in range(KC):
                            b.mm(pv[:, 0:128], self.hb[:, kc, c * 128:(c + 1) * 128], wbf[p][:, kc, :],
                                 start=(kc == 0), stop=(kc == KC - 1), r=[('awbf', p)] + h_keys(kc, ALLT), w=[pvk])
                        b.cp('act', VT[:, c, kvh * 128:(kvh + 1) * 128], pv[:, 0:128], r=[pvk], w=['VT'])
            with b.scope() as es3:
                pts = [b.sb(es3, "a_pt%d" % i, [128, 512], BF16) for i in range(3)]
                rden = b.sb(es3, "a_rden", [128, 512], F32)
                wos_ = b.sb(es3, "a_wos", [128, 2, D], F32)
                wos = [wos_, wos_]
                wob = b.sb(es3, "a_wob", [128, KC, D], BF16)
                for i in range(4):
                    b.dma(wos[i % 2][:], w_o[i * 256:(i + 1) * 256, :].rearrange("(k p) n -> p k n", p=128), w=[('aos', 0)])
                    b.cp('pool', wob[:, 2 * i:2 * i + 2, :], wos[i % 2][:], r=[('aos', 0)], w=['awob'])
                npt = 0
                scale = float(128.0 ** -0.5)
                for hq in range(8):
                    kv = hq // 4
                    for ti, (t0, n) in enumerate(TILES):
                        kts = [0, 1] if ti == 0 else list(range(NCH))
                        pden, pdk = b.ps(hold=True)
                        po, pok = b.ps(hold=True)
                        prev = None

                        def flush(pv_):
                            pt_, ptk_, ii_, kt_ = pv_
                            b.mm(pden[:, :n], self.onesb[:], pt_[:, :n], start=(ii_ == 0), stop=(ii_ == len(kts) - 1),
                                 r=[ptk_, 'onesb'], w=[pdk])
                            b.mm(po[:, :n], VT[:, kt_, kv * 128:(kv + 1) * 128], pt_[:, :n], start=(ii_ == 0), stop=(ii_ == len(kts) - 1),
                                 r=[ptk_, 'VT'], w=[pok])
                        for ii, kt in enumerate(kts):
                            psc, psk = b.ps()
                            b.mm(psc[:, :n], KR[:, kv, kt * 128:(kt + 1) * 128], QR[:, hq, t0:t0 + n],
                                 r=[('aqk', False, kv, tt_) for tt_ in ALLT] + [('aqk', True, hq, ti)], w=[psk])
                            pt = pts[npt % 3]
                            ptk = ('apt', npt % 3)
                            npt += 1
                            b.act(pt[:, :n], psc[:, :n], AF.Exp, scale=scale, r=[psk], w=[ptk])
                            if prev is not None:
                                flush(prev)
                            prev = (pt, ptk, ii, kt)
                        flush(prev)
                        b.op('dve', lambda e, o=rden[:, :n], i=pden[:, :n]: e.reciprocal(out=o, in_=i), r=[pdk], w=['arden'])
                        b.tt('dve', self.hb[:, hq, t0:t0 + n], po[:, :n], rden[:, :n], ALU.mult, r=[pok, 'arden'], w=[('h', hq, ti)])
                        b.release(pdk)
                        b.release(pok)
                self.out_proj(wob, 'awob', KC, lambda k, t0, n: self.hb[:, k, t0:t0 + n], lambda k, ti: [('h', k, ti)])

    def gdn(self, li):
        b = self.b
        j = li // 2
        w_in = self.din("gdn_w_in%d" % j, [D, 6208])
        w_out = self.din("gdn_w_out%d" % j, [2048, D])
        convD = self.din("gdn_convT%d" % j, [128, 32, 5])
        gparD = self.din("gdn_gpar%d" % j, [128, 64])
        normgD = self.din("gdn_normg%d" % j, [128, 128])
        lvlD = self.din("lvlmask", [128, 7 * 4 * 128], U8)
        ident = self.ident
        one_col = self.epsv[:, 2:3]
        eps_col = self.epsv[:, 0:1]

        def bc(ap2, n=128):
            return ap2.unsqueeze(2).to_broadcast([128, ap2.shape[1], n])

        with b.scope() as es:
            G = {}
            for nm in ['NBETA', 'BETA', 'GCUM', 'EG', 'KD', 'EGL']:
                G[nm] = b.sb(es, "g_" + nm, [128, NCH, 32], F32)
            convw = b.sb(es, "g_convw", [128, 32, 5], F32)
            normg = b.sb(es, "g_normg", [128, 128], F32)
            lvl = b.sb(es, "g_lvl", [128, 7, 4, 128], U8)
            b.dma(convw[:], convD, w=['convw'])
            b.dma(normg[:], normgD, w=['normg'])
            b.dma(lvl[:].rearrange("p a b c -> p (a b c)"), lvlD, w=['lvl'])
            with b.scope() as ges:
                gpar = b.sb(ges, "g_gpar", [128, 64], F32)
                wgs = b.sb(ges, "g_wgs", [128, KC, 64], F32)
                wgb = b.sb(ges, "g_wgb", [128, KC, 64], BF16)
                GRAW = b.sb(ges, "g_graw", [128, NCH, 64], F32)
                T1 = b.sb(ges, "g_t1", [128, NCH, 32], F32)
                T2 = b.sb(ges, "g_t2", [128, NCH, 32], F32)
                GG = b.sb(ges, "g_g", [128, NCH, 32], F32)
                GL = b.sb(ges, "g_gl", [128, NCH, 32], F32)
                NA = b.sb(ges, "g_na", [128, 32], F32)
                b.dma(gpar[:], gparD, w=['gpar'])
                b.dma(wgs[:], w_in[:, 6144:6208].rearrange("(k p) n -> p k n", p=128), w=['wgs'])
                b.cp('pool', wgb[:], wgs[:], r=['wgs'], w=['wgb'])
                for c0 in range(0, NCH, 8):
                    nc_ = min(8, NCH - c0)
                    pg, pgk = b.ps()
                    for cc in range(nc_):
                        c = c0 + cc
                        for kc in range(KC):
                            b.mm(pg[:, cc * 64:(cc + 1) * 64], self.hb[:, kc, c * 128:(c + 1) * 128], wgb[:, kc, :],
                                 start=(kc == 0), stop=(kc == KC - 1), r=['wgb'] + h_keys(kc, ALLT), w=[pgk])
                    b.cp('act', GRAW[:, c0:c0 + nc_, :], pg[:, 0:nc_ * 64].rearrange("p (a b) -> p a b", b=64), r=[pgk], w=['graw'])
                braw = GRAW[:, :, 0:32]
                araw = GRAW[:, :, 32:64]
                b.act(T1[:], braw, AF.Exp, scale=-1.0, r=['graw'], w=['gt1'])
                b.act(T1[:], T1[:], AF.Ln, bias=one_col, r=['gt1', 'cfa'], w=['gt1'])
                b.act(G['BETA'][:], T1[:], AF.Exp, scale=-1.0, r=['gt1'], w=['BETA'])
                b.ts('pool', G['NBETA'][:], G['BETA'][:], -1.0, None, ALU.mult, r=['BETA'], w=['NBETA'])
                b.tt('dve', T2[:], araw, gpar[:, 32:64].unsqueeze(1).to_broadcast([128, NCH, 32]), ALU.add, r=['graw', 'gpar'], w=['gt2'])
                b.act(T2[:], T2[:], AF.Exp, r=['gt2'], w=['gt2'])
                b.act(T2[:], T2[:], AF.Ln, bias=one_col, r=['gt2', 'cfa'], w=['gt2'])
                b.act(NA[:], gpar[:, 0:32], AF.Exp, r=['gpar'], w=['gna'])
                b.ts('pool', NA[:], NA[:], -1.0, None, ALU.mult, r=['gna'], w=['gna'])
                b.tt('dve', GG[:], T2[:], NA[:].unsqueeze(1).to_broadcast([128, NCH, 32]), ALU.mult, r=['gt2', 'gna'], w=['gg'])
                pc, pck = b.ps()
                b.mm(pc[:, 0:288], self.trif, GG[:, :, 0:16], r=['gg', 'cfa'], w=[pck])
                pc2, pc2k = b.ps()
                b.mm(pc2[:, 0:288], self.trib, GG[:, :, 16:32], r=['gg', 'cfa'], w=[pc2k])
                b.cp('act', G['GCUM'][:, :, 0:16], pc[:, 0:288].rearrange("p (a b) -> p a b", b=16), r=[pck], w=['GCUM'])
                b.cp('act', G['GCUM'][:, :, 16:32], pc2[:, 0:288].rearrange("p (a b) -> p a b", b=16), r=[pc2k], w=['GCUM'])
                pl, plk = b.ps()
                b.mm(pl[:, 0:288], self.ones, GG[:, 0:9, :], r=['gg', 'cfa'], w=[plk])
                pl2, pl2k = b.ps()
                b.mm(pl2[:, 0:288], self.ones, GG[:, 9:18, :], r=['gg', 'cfa'], w=[pl2k])
                b.cp('act', GL[:, 0:9, :], pl[:, 0:288].rearrange("p (a b) -> p a b", b=32), r=[plk], w=['ggl'])
                b.cp('act', GL[:, 9:18, :], pl2[:, 0:288].rearrange("p (a b) -> p a b", b=32), r=[pl2k], w=['ggl'])
                b.act(G['EG'][:], G['GCUM'][:], AF.Exp, r=['GCUM'], w=['EG'])
                b.tt('dve', T1[:], GL[:], G['GCUM'][:], ALU.subtract, r=['ggl', 'GCUM', 'gt1'], w=['gt1'])
                b.act(G['KD'][:], T1[:], AF.Exp, r=['gt1'], w=['KD'])
                b.act(G['EGL'][:], GL[:], AF.Exp, r=['ggl'], w=['EGL'])
            for g in range(8):
                self.gdn_group(li, g, es, G, convw, normg, lvl, w_in, w_out, bc)

    def gdn_group(self, li, g, es_unused, G, convw, normg, lvl, w_in, w_out, bc):
        b = self.b
        ident = self.ident
        eps_col = self.epsv[:, 0:1]
        ps_bf = lambda ps: ps[:].bitcast(BF16)
        gk = lambda nm: nm + "_%d" % g

        def gcols(d):
            return slice(d * 16 + 2 * g, d * 16 + 2 * g + 2)

        with b.scope() as ges:
            O = b.sb(ges, "g_O", [128, NCH, 2, 128], BF16)
            with b.scope() as aes:
                KQ = b.sb(aes, "g_KQ", [128, NCH, 2, 128], BF16)
                KTM = b.sb(aes, "g_KTM", [128, NCH, 128], BF16)
                VTM = b.sb(aes, "g_VTM", [128, NCH, 2, 128], BF16)
                with b.scope() as pes:
                    CB = b.sb(pes, "g_CB", [128, 2310], F32)
                    ACC = b.sb(pes, "g_ACC", [128, T], F32)
                    TB = b.sb(pes, "g_TB", [128, T], BF16)
                    wst = [b.sb(pes, "g_wst%d" % i, [128, KC, 128], F32) for i in range(2)]
                    wbf = [b.sb(pes, "g_wbf%d" % i, [128, KC, 128], BF16) for i in range(2)]
                    rs = b.sb(pes, "g_rs", [128, 512], F32)
                    b.op('pool', lambda e: e.memset(CB[:, 0:2], 0.0), w=['CBp0'])
                    b.op('pool', lambda e: e.memset(CB[:, 258:260], 0.0), w=['CBp1'])
                    b.op('pool', lambda e: e.memset(CB[:, 2308:2310], 0.0), w=['CBp2'])
                    fcs = [('q', g * 128, g), ('k', 1024 + g * 128, 8 + g),
                           ('v0', 2048 + (2 * g) * 128, 16 + 2 * g), ('v1', 2048 + (2 * g + 1) * 128, 16 + 2 * g + 1)]
                    def emit_proj(fi):
                        kind, col0, cq = fcs[fi]
                        held = []
                        p = fi % 2
                        b.dma(wst[p][:], w_in[:, col0:col0 + 128].rearrange("(k p) n -> p k n", p=128), w=[('gwst', p)])
                        b.cp('pool', wbf[p][:], wst[p][:], r=[('gwst', p)], w=[('gwbf', p)])
                        for ti, (t0, n) in enumerate(TILES):
                            pp, ppk = b.ps(hold=True)
                            for kc in range(KC):
                                b.mm(pp[:, :n], wbf[p][:, kc, :], self.hb[:, kc, t0:t0 + n], start=(kc == 0), stop=(kc == KC - 1),
                                     r=[('gwbf', p), ('h', kc, ti)], w=[ppk])
                            o0 = 2 if ti == 0 else t0 + 4
                            b.cp('act', CB[:, o0:o0 + n], pp[:, :n], r=[ppk], w=[('CB', ti)])
                            held.append(ppk)
                        return held

                    def emit_conv(fi):
                        kind, col0, cq = fcs[fi]
                        acck = [('ACC', 0), ('ACC', 256), ('ACC', 1280)]
                        for (d0, L, s0, tis) in [(0, 256, 2, [0]), (256, 1024, 260, [1, 2]), (1280, 1024, 1284, [3, 4])]:
                            ak = ('ACC', d0)
                            rk = [('CB', tj) for tj in {0: [0], 256: [1, 2, 3], 1280: [2, 3, 4]}[d0]] + ['CBp0', 'CBp1', 'CBp2', 'convw']
                            b.ts('dve', ACC[:, d0:d0 + L], CB[:, s0 - 2:s0 - 2 + L], convw[:, cq, 0:1], None, ALU.mult, r=rk, w=[ak])
                            for jj in range(1, 5):
                                b.stt('dve', ACC[:, d0:d0 + L], CB[:, s0 - 2 + jj:s0 - 2 + jj + L], convw[:, cq, jj:jj + 1],
                                      ACC[:, d0:d0 + L], ALU.mult, ALU.add, r=rk + [ak], w=[ak])
                            if kind in ('q', 'k'):
                                b.act(ACC[:, d0:d0 + L], ACC[:, d0:d0 + L], AF.Silu, r=[ak], w=[ak])
                            else:
                                b.act(TB[:, d0:d0 + L], ACC[:, d0:d0 + L], AF.Silu, r=[ak], w=['TB'])

                    def emit_tail(fi):
                        kind, col0, cq = fcs[fi]
                        acck = [('ACC', 0), ('ACC', 256), ('ACC', 1280)]
                        if kind in ('q', 'k'):
                            b.act(TB[:], ACC[:], AF.Square, r=acck, w=['TB'])
                            kq = 0 if kind == 'k' else 1
                            sc_ = 1.0 if kind == 'k' else float(128.0 ** -0.5)
                            for ti, (t0, n) in enumerate(TILES):
                                p2, p2k = b.ps()
                                b.mm(p2[:, :n], self.onesb[:], TB[:, t0:t0 + n], r=['TB', 'onesb'], w=[p2k])
                                b.act(rs[:, :n], p2[:, :n], AF.Ln, bias=eps_col, r=[p2k, 'cfa'], w=['grs'])
                                b.act(rs[:, :n], rs[:, :n], AF.Exp, scale=-0.5, r=['grs'], w=['grs'])
                                c0 = t0 // 128
                                nc_ = n // 128
                                b.stt('dve', KQ[:, c0:c0 + nc_, kq, :], ACC[:, t0:t0 + n].rearrange("p (a b) -> p a b", b=128), sc_,
                                      rs[:, :n].rearrange("p (a b) -> p a b", b=128), ALU.mult, ALU.mult,
                                      r=acck + ['grs'], w=[gk('KQ')])
                            if kind == 'k':
                                for c0 in range(0, NCH, 4):
                                    nc_ = min(4, NCH - c0)
                                    pt, ptk = b.ps()
                                    ptb = ps_bf(pt)
                                    for cc in range(nc_):
                                        b.tr(ptb[:, cc * 128:(cc + 1) * 128], KQ[:, c0 + cc, 0, :], self.identb[:], r=[gk('KQ'), 'identb'], w=[ptk])
                                    b.cp('act', KTM[:, c0:c0 + nc_, :], ptb[:, 0:nc_ * 128].rearrange("p (a b) -> p a b", b=128), r=[ptk], w=[gk('KTM')])
                        else:
                            a_ = 0 if kind == 'v0' else 1
                            for c0 in range(0, NCH, 4):
                                nc_ = min(4, NCH - c0)
                                pt, ptk = b.ps()
                                ptb = ps_bf(pt)
                                for cc in range(nc_):
                                    b.tr(ptb[:, cc * 128:(cc + 1) * 128], TB[:, (c0 + cc) * 128:(c0 + cc + 1) * 128], self.identb[:], r=['TB', 'identb'], w=[ptk])
                                b.cp('act', VTM[:, c0:c0 + nc_, a_, :], ptb[:, 0:nc_ * 128].rearrange("p (a b) -> p a b", b=128), r=[ptk], w=[gk('VTM')])

                    for k_ in emit_proj(0):
                        b.release(k_)
                    for fi in range(4):
                        emit_conv(fi)
                        hk = emit_proj(fi + 1) if fi + 1 < 4 else []
                        emit_tail(fi)
                        for k_ in hk:
                            b.release(k_)
                with b.scope() as ses:
                    NL = GDN_LANES

                    def t4(nm, dt):
                        return b.sb(ses, "g_" + nm, [128, 4, 128], dt)
                    DG = b.sb(ses, "g_DG", [128, 4, 128], F32)
                    BGEg = b.sb(ses, "g_BGEg", [128, NCH, 4], F32)
                    C1 = t4("C1", F32)
                    TO = C1
                    DT = t4("DT", BF16)
                    Dm = t4("Dm", BF16)
                    KKn = t4("KKn", BF16)
                    QKs = b.sb(ses, "g_QKs", [128, 2, 128], BF16)
                    VN = t4("VN", BF16)
                    S = t4("S", F32)
                    Sbf = t4("Sbf", BF16)
                    VB = t4("VB", BF16)
                    KBG = t4("KBG", BF16)
                    KDEC = t4("KDEC", BF16)
                    NWT = t4("NWT", BF16)
                    lanes = []
                    for k in range(NL):
                        lanes.append({nm: t4("%s_l%d" % (nm, k), BF16) for nm in ['Ap', 'QKD', 'Y', 'W', 'X']})
                        lanes[-1]['k'] = k
                    b.op('pool', lambda e: e.memset(S[:], 0.0), w=['S'])
                    b.op('pool', lambda e: e.memset(Sbf[:], 0.0), w=['Sbf'])
                    for d in range(2):
                        b.tt('pool', BGEg[:, :, 2 * d:2 * d + 2], G['BETA'][:, :, gcols(d)], G['EG'][:, :, gcols(d)], ALU.mult, r=['BETA', 'EG'], w=['BGEg'])
                    owritten = set()
                    f2 = lambda ap: ap.rearrange("p a b -> p (a b)")
                    idb = ident.unsqueeze(1).to_broadcast([128, 2, 128])
                    nidb = self.nident.unsqueeze(1).to_broadcast([128, 2, 128])

                    def step_gen(s, Ln):
                        lk = lambda nm: (nm, Ln['k'])
                        Ap, QKD, Y, W, X = (Ln[nm] for nm in ['Ap', 'QKD', 'Y', 'W', 'X'])
                        cds = [FWD[s], BWD[s]]

                        def scale_ops(which):
                            for u in range(4):
                                d, a_ = u // 2, u % 2
                                cd = cds[d]
                                col = d * 16 + 2 * g + a_
                                if which == 'VB':
                                    b.act(VB[:, u, :], VTM[:, cd, a_, :], AF.Copy, scale=G['BETA'][:, cd, col:col + 1], r=[gk('VTM'), 'BETA'], w=['VB'])
                                elif which == 'KBG':
                                    b.act(KBG[:, u, :], KTM[:, cd, :], AF.Copy, scale=BGEg[:, cd, u:u + 1], r=[gk('KTM'), 'BGEg'], w=['KBG'])
                                else:
                                    b.act(KDEC[:, u, :], KTM[:, cd, :], AF.Copy, scale=G['KD'][:, cd, col:col + 1], r=[gk('KTM'), 'KD'], w=['KDEC'])
                        pkq, pkqk = b.ps(hold=True)
                        for d in range(2):
                            cd = cds[d]
                            b.mm(pkq[:, d * 256:(d + 1) * 256], KQ[:, cd, 0, :], KQ[:, cd, :, :].rearrange("p a b -> p (a b)"),
                                 r=[gk('KQ')], w=[pkqk])
                        for d in range(2):
                            cd = cds[d]
                            b.tt('pool', DG[:, 2 * d:2 * d + 2, :], idb, bc(G['GCUM'][:, cd, gcols(d)]), ALU.mult, r=['GCUM', 'cfa'], w=['DG'])
                        pe_, pek = b.ps(hold=True)
                        for u in range(4):
                            b.mm(pe_[:, u * 128:(u + 1) * 128], self.ones, DG[:, u, :], start=True, stop=False, r=['DG', 'cfa'], w=[pek])
                            b.mm(pe_[:, u * 128:(u + 1) * 128], DG[:, u, :], self.nones, start=False, stop=True, r=['DG', 'cfa'], w=[pek])
                        b.ts('dve', f2(C1[:]), pe_[:], 0.0, None, ALU.min, r=[pek], w=['C1'])
                        b.act(f2(DT[:]), f2(C1[:]), AF.Exp, r=['C1'], w=['DT'])
                        b.ts('dve', f2(C1[:]), pe_[:], 0.0, None, ALU.max, r=[pek], w=['C1'])
                        b.release(pek)
                        b.act(f2(Dm[:]), f2(C1[:]), AF.Exp, scale=-1.0, r=['C1'], w=['Dm'])
                        for d in range(2):
                            cd = cds[d]
                            kk = pkq[:, d * 256:d * 256 + 128].unsqueeze(1).to_broadcast([128, 2, 128])
                            b.tt('dve', KKn[:, 2 * d:2 * d + 2, :], kk, bc(G['NBETA'][:, cd, gcols(d)]), ALU.mult, r=[pkqk, 'NBETA'], w=['KKn'])
                        qkv_ = pkq[:].rearrange("p (d x) -> p d x", d=2)[:, :, 128:256]
                        b.tt('dve', QKs[:], qkv_, self.maskq.rearrange("p (d x) -> p d x", d=2), ALU.mult, r=[pkqk, 'cfa'], w=['QKs'])
                        b.release(pkqk)
                        yield
                        b.tt('pool', f2(Ap[:]), f2(KKn[:]), f2(Dm[:]), ALU.mult, r=['KKn', 'Dm'], w=[lk('Ap')])
                        b.tt('pool', QKD[:].rearrange("p (d a) x -> p d a x", d=2), QKs[:].unsqueeze(2).to_broadcast([128, 2, 2, 128]),
                             DT[:].rearrange("p (d a) x -> p d a x", d=2), ALU.mult, r=['QKs', 'DT'], w=[lk('QKD')])
                        ptt, pttk = b.ps(hold=True)
                        pttb = ps_bf(ptt)
                        for u in range(4):
                            b.tr(pttb[:, u * 128:(u + 1) * 128], Ap[:, u, :], self.identb[:], r=[lk('Ap'), 'identb'], w=[pttk])
                        b.cp('pool', Y[:], self.identb[:].unsqueeze(1).to_broadcast([128, 4, 128]), r=['identb'], w=[lk('Y')])
                        b.op('dve', lambda e, o=f2(Y[:]), m=f2(lvl[:, 0, :, :]), dd=pttb[:, 0:512]: e.copy_predicated(o, m, dd),
                             r=[pttk, 'lvl', lk('Y')], w=[lk('Y')])
                        b.release(pttk)
                        yield
                        for l in range(1, 7):
                            pw, pwk = b.ps(hold=True)
                            for u in range(4):
                                b.mm(pw[:, u * 128:(u + 1) * 128], Ap[:, u, :], Y[:, u, :], r=[lk('Ap'), lk('Y')], w=[pwk])
                            px, pxk = b.ps(hold=True)
                            pxb = ps_bf(px)
                            for u in range(4):
                                b.tr(pxb[:, u * 128:(u + 1) * 128], Y[:, u, :], self.identb[:], r=[lk('Y'), 'identb'], w=[pxk])
                            b.cp('act', f2(W[:]), pw[:], r=[pwk], w=[lk('W')])
                            b.cp('dve', f2(X[:]), pxb[:, 0:512], r=[pxk], w=[lk('X')])
                            b.release(pwk)
                            b.release(pxk)
                            yield
                            pz, pzk = b.ps(hold=True)
                            for u in range(4):
                                b.mm(pz[:, u * 128:(u + 1) * 128], X[:, u, :], W[:, u, :], r=[lk('X'), lk('W')], w=[pzk])
                            b.op('dve', lambda e, o=f2(Y[:]), m=f2(lvl[:, l, :, :]), dd=pz[:]: e.copy_predicated(o, m, dd),
                                 r=[pzk, 'lvl', lk('Y')], w=[lk('Y')])
                            b.release(pzk)
                            if l == 6:
                                scale_ops('KBG')
                            yield
                        pwt, pwtk = b.ps(hold=True)
                        for u in range(4):
                            b.mm(pwt[:, u * 128:(u + 1) * 128], KBG[:, u, :], Y[:, u, :], r=['KBG', lk('Y')], w=[pwtk])
                        b.act(f2(NWT[:]), pwt[:], AF.Copy, scale=-1.0, r=[pwtk], w=['NWT'])
                        b.release(pwtk)
                        scale_ops('VB')
                        yield
                        pvn, pvnk = b.ps(hold=True)
                        for u in range(4):
                            b.mm(pvn[:, u * 128:(u + 1) * 128], Y[:, u, :], VB[:, u, :], start=True, stop=False, r=[lk('Y'), 'VB'], w=[pvnk])
                            b.mm(pvn[:, u * 128:(u + 1) * 128], NWT[:, u, :], Sbf[:, u, :], start=False, stop=True, r=['NWT', 'Sbf'], w=[pvnk])
                        b.cp('act', f2(VN[:]), pvn[:], r=[pvnk], w=['VN'])
                        b.release(pvnk)
                        scale_ops('KDEC')
                        yield
                        pds, pdsk = b.ps(hold=True)
                        po1, po1k = b.ps(hold=True)
                        po2, po2k = b.ps(hold=True)
                        for u in range(4):
                            b.mm(pds[:, u * 128:(u + 1) * 128], KDEC[:, u, :], VN[:, u, :], r=['KDEC', 'VN'], w=[pdsk])
                        for u in range(4):
                            cd = cds[u // 2]
                            b.mm(po1[:, u * 128:(u + 1) * 128], KQ[:, cd, 1, :], Sbf[:, u, :], r=[gk('KQ'), 'Sbf'], w=[po1k])
                        for u in range(4):
                            b.mm(po2[:, u * 128:(u + 1) * 128], QKD[:, u, :], VN[:, u, :], r=[lk('QKD'), 'VN'], w=[po2k])
                        for d in range(2):
                            cd = cds[d]
                            b.tt('pool', S[:, 2 * d:2 * d + 2, :], S[:, 2 * d:2 * d + 2, :], bc(G['EGL'][:, cd, gcols(d)]), ALU.mult,
                                 r=['S', 'EGL'], w=['S'])
                        b.tt('dve', f2(S[:]), f2(S[:]), pds[:], ALU.add, r=['S', pdsk], w=['S'])
                        b.release(pdsk)
                        b.cp('act', f2(Sbf[:]), f2(S[:]), r=['S'], w=['Sbf'])
                        for d in range(2):
                            cd = cds[d]
                            b.tt('dve', TO[:, 2 * d:2 * d + 2, :], po1[:, d * 256:(d + 1) * 256].rearrange("p (a x) -> p a x", a=2),
                                 bc(G['EG'][:, cd, gcols(d)]), ALU.mult, r=[po1k, 'EG'], w=['C1'])
                        b.tt('dve', f2(TO[:]), f2(TO[:]), po2[:], ALU.add, r=['C1', po2k], w=['C1'])
                        b.release(po1k)
                        b.release(po2k)
                        for d in range(2):
                            cd = cds[d]
                            if cd not in owritten:
                                owritten.add(cd)
                                b.cp('act', O[:, cd, :, :], TO[:, 2 * d:2 * d + 2, :], r=['C1'], w=[gk('O')])
                            else:
                                b.tt('pool', O[:, cd, :, :], O[:, cd, :, :], TO[:, 2 * d:2 * d + 2, :], ALU.add, r=['C1', gk('O')], w=[gk('O')])
                        yield

                    gens = []
                    next_s = 0
                    turn = 0
                    stagger = max(1, (17 + NL - 1) // NL)
                    while next_s < NCH or gens:
                        if next_s < NCH and turn % stagger == 0 and len(gens) < NL:
                            gens.append(step_gen(next_s, lanes[next_s % NL]))
                            next_s += 1
                        for g_ in list(gens):
                            try:
                                next(g_)
                            except StopIteration:
                                gens.remove(g_)
                        turn += 1
            with b.scope() as oes:
                ZS = b.sb(oes, "g_ZS", [128, NCH, 256], BF16)
                wzs = b.sb(oes, "g_wzs", [128, KC, 256], F32)
                wzb = b.sb(oes, "g_wzb", [128, KC, 256], BF16)
                wos = b.sb(oes, "g_wos", [128, 2, D], F32)
                wob = b.sb(oes, "g_wob", [128, 2, D], BF16)
                OSQ = b.sb(oes, "g_OSQ", [128, 6, 2, 128], F32)
                SS = b.sb(oes, "g_SS", [128, NCH, 2], F32)
                YT = b.sb(oes, "g_YT", [128, 6, 2, 128], F32)
                YTb = b.sb(oes, "g_YTb", [128, 6, 2, 128], BF16)
                YF = b.sb(oes, "g_YF", [128, 2, T], BF16)
                zc0 = 4096 + 2 * g * 128
                b.dma(wzs[:], w_in[:, zc0:zc0 + 256].rearrange("(k p) n -> p k n", p=128), w=['wzs'])
                b.cp('pool', wzb[:], wzs[:], r=['wzs'], w=['wzb'])
                b.dma(wos[:], w_out[2 * g * 128:(2 * g + 2) * 128, :].rearrange("(k p) n -> p k n", p=128), w=['gwos'])
                b.cp('pool', wob[:], wos[:], r=['gwos'], w=['gwob'])
                for c0 in range(0, NCH, 2):
                    pz_, pzk_ = b.ps()
                    for cc in range(2):
                        c = c0 + cc
                        for kc in range(KC):
                            b.mm(pz_[:, cc * 256:(cc + 1) * 256], self.hb[:, kc, c * 128:(c + 1) * 128], wzb[:, kc, :],
                                 start=(kc == 0), stop=(kc == KC - 1), r=['wzb'] + h_keys(kc, ALLT), w=[pzk_])
                    b.act(ZS[:, c0:c0 + 2, :], pz_[:].rearrange("p (a b) -> p a b", a=2), AF.Silu, r=[pzk_], w=['ZS'])
                for c0 in range(0, NCH, 6):
                    Oc = O[:, c0:c0 + 6, :, :]
                    b.tt('pool', OSQ[:], Oc, Oc, ALU.mult, r=[gk('O')], w=['OSQ'])
                    b.op('dve', lambda e, o=SS[:, c0:c0 + 6, :], i=OSQ[:]: e.tensor_reduce(out=o, in_=i, axis=AX.X, op=ALU.add), r=['OSQ'], w=['SS'])
                b.act(SS[:], SS[:], AF.Ln, bias=eps_col, scale=1.0 / 128.0, r=['SS', 'cfa'], w=['SS'])
                b.act(SS[:], SS[:], AF.Exp, scale=-0.5, r=['SS'], w=['SS'])
                for c0 in range(0, NCH, 6):
                    Oc = O[:, c0:c0 + 6, :, :]
                    b.tt('dve', YT[:], Oc, SS[:, c0:c0 + 6, :].unsqueeze(3).to_broadcast([128, 6, 2, 128]), ALU.mult, r=[gk('O'), 'SS'], w=['YT'])
                    b.tt('pool', YT[:].rearrange("p a b c -> p (a b) c"), YT[:].rearrange("p a b c -> p (a b) c"),
                         normg[:].unsqueeze(1).to_broadcast([128, 12, 128]), ALU.mult, r=['YT', 'normg'], w=['YT'])
                    b.tt('dve', YTb[:], YT[:], ZS[:, c0:c0 + 6, :].rearrange("p a (b c) -> p a b c", b=2), ALU.mult, r=['YT', 'ZS'], w=['YTb'])
                    for a_ in range(2):
                        for q4 in range(0, 6, 4):
                            nq = min(4, 6 - q4)
                            pt, ptk = b.ps()
                            ptb = ps_bf(pt)
                            for cc in range(nq):
                                b.tr(ptb[:, cc * 128:(cc + 1) * 128], YTb[:, q4 + cc, a_, :], self.identb[:], r=['YTb', 'identb'], w=[ptk])
                            t0_ = (c0 + q4) * 128
                            b.cp('act', YF[:, a_, t0_:t0_ + nq * 128], ptb[:, 0:nq * 128], r=[ptk], w=['YF'])
                self.out_proj(wob, 'gwob', 2, lambda k, t0, n: YF[:, k, t0:t0 + n], lambda k, ti: ['YF'])


def host_consts():
    cfa = np.zeros((128, NCFA), np.float32)
    idx = np.arange(128)
    cfa[:, 0:128] = np.eye(128, dtype=np.float32)
    cfa[:, 128:256] = (idx[:, None] <= idx[None, :]).astype(np.float32)
    cfa[:, 256:384] = (idx[:, None] >= idx[None, :]).astype(np.float32)
    cfa[:, 384:512] = 1.0
    cfa[:, 512:640] = (idx[None, :] >= idx[:, None]).astype(np.float32)
    cfa[:, 640:768] = (idx[None, :] <= idx[:, None]).astype(np.float32)
    rot = np.zeros((128, 128), np.float32)
    for m in range(128):
        half = (m % 64) // 32
        if half == 0:
            rot[m + 32, m] = -1.0
        else:
            rot[m - 32, m] = 1.0
    cfa[:, 768:896] = rot
    cfa[:, 1152] = EPS
    cfa[:, 1153] = 128.0 * EPS
    cfa[:, 1154] = 1.0
    cfa[:, 896:1024] = -np.eye(128, dtype=np.float32)
    cfa[:, 1024:1152] = -1.0
    return cfa


def rope_host():
    rows = 2048 // 64
    row = np.repeat(np.arange(rows), 64).astype(np.float32)
    col = np.tile(np.arange(64), rows).astype(np.float32)
    n_freq = 32
    freqs = (np.float32(10000.0) ** (-np.arange(n_freq, dtype=np.float32) / np.float32(n_freq))).astype(np.float32)
    ang_r = row[:, None] * freqs
    ang_c = col[:, None] * freqs
    ang = np.concatenate([ang_r, ang_r, ang_c, ang_c], axis=-1).astype(np.float32)
    out = np.zeros((128, 4096), np.float32)
    out[:, 0:2048] = np.cos(ang).T
    out[:, 2048:4096] = np.sin(ang).T
    return out


_CACHE = {}


def get_prog(layers, nseq):
    key = (tuple(layers), nseq)
    if key not in _CACHE:
        nc = bass.Bass("TRN2", target_bir_lowering=False)
        p = Prog(nc, list(layers), nseq)
        p.build()
        _CACHE[key] = (nc, p)
    return _CACHE[key]


def layer_inputs(inp, li):
    d = {}
    d["w_mod%d" % li] = np.ascontiguousarray(inp["w_mod"][li])
    d["bmodT%d" % li] = np.ascontiguousarray(inp["b_mod"][li].reshape(48, 128).T)
    ln = np.stack([inp["ln_g"][li, 0], inp["ln_b"][li, 0], inp["ln_g"][li, 1], inp["ln_b"][li, 1]], 0)
    d["lnT%d" % li] = np.ascontiguousarray(ln.reshape(4, 8, 128).transpose(2, 0, 1))
    d["w_ffn_in%d" % li] = np.ascontiguousarray(inp["w_ffn_in"][li])
    d["w_ffn_out%d" % li] = np.ascontiguousarray(inp["w_ffn_out"][li])
    j = li // 2
    if li % 2 == 1:
        d["attn_w_qkv%d" % j] = np.ascontiguousarray(inp["attn_w_qkv"][j])
        d["attn_w_out%d" % j] = np.ascontiguousarray(inp["attn_w_out"][j])
        d["attn_gain%d" % j] = np.ascontiguousarray(np.stack([inp["attn_q_norm"][j], inp["attn_k_norm"][j]], 1))
        d["rope"] = rope_host()
    else:
        d["gdn_w_in%d" % j] = np.ascontiguousarray(inp["gdn_w_in"][j])
        d["gdn_w_out%d" % j] = np.ascontiguousarray(inp["gdn_w_out"][j])
        d["gdn_convT%d" % j] = np.ascontiguousarray(inp["gdn_conv"][j].reshape(5, 32, 128).transpose(2, 1, 0))
        gp = np.concatenate([inp["gdn_a_log"][j].reshape(32), inp["gdn_dt_bias"][j].reshape(32)])
        d["gdn_gpar%d" % j] = np.ascontiguousarray(np.broadcast_to(gp[None, :], (128, 64)).astype(np.float32))
        d["gdn_normg%d" % j] = np.ascontiguousarray(np.broadcast_to(inp["gdn_norm_g"][j][None, :], (128, 128)).astype(np.float32))
        d["lvlmask"] = lvlmask_host()
    return d


def lvlmask_host():
    p = np.arange(128)[:, None]
    f = np.arange(128)[None, :]
    m = np.zeros((128, 7, 4, 128), np.uint8)
    for l in range(7):
        if l == 0:
            blk = (p >> 1) == (f >> 1)
        else:
            blk = ((p >> (l + 1)) == (f >> (l + 1))) & ((p >> l) != (f >> l))
        fw = (blk & (f > p)).astype(np.uint8)
        bw = (blk & (f < p)).astype(np.uint8)
        m[:, l, 0, :] = fw
        m[:, l, 1, :] = fw
        m[:, l, 2, :] = bw
        m[:, l, 3, :] = bw
    return np.ascontiguousarray(m.reshape(128, 7 * 4 * 128))


def seq_fm(inp, bidx):
    return np.ascontiguousarray(np.concatenate([inp["ctx"][bidx], inp["x"][bidx]], 0).T)


def cT_host(inp, bidxs):
    v = np.stack([inp["c_ctx"]] + [inp["c"][bi] for bi in bidxs], 1)
    return np.ascontiguousarray(v.reshape(8, 128, len(bidxs) + 1).transpose(1, 0, 2))


def run_layers(inp, layers, xs):
    nc, p = get_prog(layers, 1)
    outs = [None] * 16
    cfa = host_consts()
    for rnd in range(2):
        in_maps = []
        for core in range(8):
            bidx = rnd * 8 + core
            m = {"cfa": cfa, "xin0": xs[bidx], "cTall": cT_host(inp, [bidx])}
            for li in layers:
                m.update(layer_inputs(inp, li))
            in_maps.append({k: v for k, v in m.items() if k in p.dram})
        res = run_bass_kernel_spmd(nc, in_maps, core_ids=list(range(8)))
        for core in range(8):
            outs[rnd * 8 + core] = res.results[core]["xout0"]
    return outs


def kernel_unfused(**inp):
    inp = {k: np.asarray(v) for k, v in inp.items()}
    xs = [seq_fm(inp, bi) for bi in range(16)]
    for li in range(DEPTH):
        xs = run_layers(inp, [li], xs)
    out = np.stack([x.T[256:, :] for x in xs], 0)
    return np.ascontiguousarray(out.astype(np.float32))


def kernel(**inp):
    inp = {k: np.asarray(v) for k, v in inp.items()}
    layers = list(range(DEPTH))
    nc, p = get_prog(layers, 2)
    cfa = host_consts()
    shared = {"cfa": cfa}
    for li in layers:
        shared.update(layer_inputs(inp, li))
    shared = {k: v for k, v in shared.items() if k in p.dram}
    in_maps = []
    for core in range(8):
        m = dict(shared)
        for s in range(2):
            bidx = 2 * core + s
            m["xin%d" % s] = seq_fm(inp, bidx)
        m["cTall"] = cT_host(inp, [2 * core, 2 * core + 1])
        in_maps.append(m)
    res = run_bass_kernel_spmd(nc, in_maps, core_ids=list(range(8)))
    out = np.zeros((16, 2048, 1024), np.float32)
    for core in range(8):
        for s in range(2):
            out[2 * core + s] = res.results[core]["xout%d" % s].T[256:, :]
    return out
```

```python
import numpy as np
from contextlib import ExitStack
import concourse.bass as bass
import concourse.mybir as mybir
from concourse.bass_utils import run_bass_kernel_spmd

F32 = mybir.dt.float32
BF16 = mybir.dt.bfloat16
U8 = mybir.dt.uint8
AF = mybir.ActivationFunctionType
ALU = mybir.AluOpType
AX = mybir.AxisListType

D = 1024
KC = 8
T = 2304
NCH = 18
DEPTH = 4
DFF = 2816
EPS = 1e-6
ALPHA = 8.0 ** 0.25
TILES = [(0, 256), (256, 512), (768, 512), (1280, 512), (1792, 512)]
FWD = list(range(18))
BWD = [1, 0] + list(range(17, 1, -1))
NCFA = 1156
DEBUG_ALLOC = False
GDN_LANES = 5
GDN_STAGGER = 3
DBG = {}


def which_of(ti):
    return 0 if ti == 0 else 1


class B:
    def __init__(self, nc):
        self.nc = nc
        self.es = ExitStack()
        self.E = ['pe', 'act', 'dve', 'pool', 'sp']
        self.q = {e: [] for e in self.E}
        self.cnt = {e: 0 for e in self.E}
        self.sem = {e: self.es.enter_context(nc.semaphore("s_" + e)) for e in self.E if e != 'sp'}
        self.ND = 16
        self.dsem = [self.es.enter_context(nc.semaphore("d%d" % i)) for i in range(self.ND)]
        self.dcnt = [0] * self.ND
        self.drr = 0
        self.waited = {e: {} for e in self.E}
        self.track = {}
        self.ninstr = 0
        self.psb = [self.es.enter_context(nc.psum_tensor("ps%d" % i, [128, 512], F32)) for i in range(8)]
        self.psrr = 0

    def _semh(self, sk):
        return self.sem[sk] if isinstance(sk, str) else self.dsem[sk[1]]

    def _deps(self, eng, r, w):
        raw = {}
        oth = {}

        def need(dct, idv):
            if idv is None:
                return
            sk, v = idv
            if dct.get(sk, 0) < v:
                dct[sk] = v
        for key in r:
            t = self.track.get(key)
            if t:
                need(raw, t[0])
        for key in w:
            t = self.track.get(key)
            if t:
                need(oth, t[0])
                for sk, v in t[1].items():
                    need(oth, (sk, v))
        for sk, v in oth.items():
            if sk == eng and eng == 'pe':
                continue
            if raw.get(sk, 0) < v:
                raw[sk] = v
        for sk, v in raw.items():
            if self.waited[eng].get(sk, 0) >= v:
                continue
            self.waited[eng][sk] = v
            sh = self._semh(sk)
            self.q[eng].append(lambda e, sh=sh, v=v: e.wait_ge(sh, v))
            self.ninstr += 1

    def _mark(self, idv, r, w):
        for key in w:
            self.track[key] = [idv, {}]
        for key in r:
            t = self.track.setdefault(key, [None, {}])
            if t[1].get(idv[0], 0) < idv[1]:
                t[1][idv[0]] = idv[1]

    def op(self, eng, fn, r=(), w=()):
        self._deps(eng, r, w)
        self.cnt[eng] += 1
        sh = self.sem[eng]
        self.q[eng].append(lambda e, fn=fn, sh=sh: fn(e).then_inc(sh, 1))
        self.ninstr += 1
        self._mark((eng, self.cnt[eng]), r, w)

    def dma(self, out, in_, r=(), w=()):
        eng = 'sp'
        k = self.drr
        self.drr = (self.drr + 1) % self.ND
        self._deps(eng, r, w)
        prev = 16 * self.dcnt[k]
        if prev and self.waited[eng].get(('d', k), 0) < prev:
            self.waited[eng][('d', k)] = prev
            self.q[eng].append(lambda e, sh=self.dsem[k], v=prev: e.wait_ge(sh, v))
        self.dcnt[k] += 1
        val = 16 * self.dcnt[k]
        self.q[eng].append(lambda e, out=out, in_=in_, sh=self.dsem[k]: e.dma_start(out=out, in_=in_).then_inc(sh, 16))
        self.ninstr += 1
        idv = (('d', k), val)
        self._mark(idv, r, w)
        return idv

    def barrier(self):
        for e in self.E:
            for o in self.E:
                if o == 'sp' or o == e:
                    continue
                v = self.cnt[o]
                if v and self.waited[e].get(o, 0) < v:
                    self.waited[e][o] = v
                    self.q[e].append(lambda en, sh=self.sem[o], v=v: en.wait_ge(sh, v))
                    self.ninstr += 1
            for k in range(self.ND):
                v = 16 * self.dcnt[k]
                if v and self.waited[e].get(('d', k), 0) < v:
                    self.waited[e][('d', k)] = v
                    self.q[e].append(lambda en, sh=self.dsem[k], v=v: en.wait_ge(sh, v))
                    self.ninstr += 1

    def scope(self):
        return _Scope(self)

    def new_epoch(self):
        self.barrier()
        self.nep = getattr(self, 'nep', 0) + 1
        for e in self.E:
            if e == 'sp':
                continue
            self.sem[e] = self.es.enter_context(self.nc.semaphore("s_%s_%d" % (e, self.nep)))
            self.cnt[e] = 0
        for e in self.E:
            for o in list(self.waited[e].keys()):
                if isinstance(o, str):
                    del self.waited[e][o]
        self.track = {}

    def ps(self, hold=False):
        if not hasattr(self, 'held'):
            self.held = set()
        while True:
            k = self.psrr
            self.psrr = (self.psrr + 1) % 8
            if k not in self.held:
                break
        if hold:
            self.held.add(k)
        return self.psb[k], ('ps', k)

    def release(self, key):
        self.held.discard(key[1])

    def mm(self, out, lhsT, rhs, start=True, stop=True, r=(), w=()):
        self.op('pe', lambda e: e.matmul(out, lhsT, rhs, start=start, stop=stop), r, w)

    def tr(self, out, in_, ident, r=(), w=()):
        self.op('pe', lambda e: e.transpose(out, in_, ident), r, w)

    def act(self, out, in_, func, bias=None, scale=None, r=(), w=()):
        kw = {}
        if bias is not None:
            kw['bias'] = bias
        if scale is not None:
            kw['scale'] = scale
        self.op('act', lambda e: e.activation(out=out, in_=in_, func=func, **kw), r, w)

    def tt(self, eng, out, a, b, op, r=(), w=()):
        self.op(eng, lambda e: e.tensor_tensor(out=out, in0=a, in1=b, op=op), r, w)

    def ts(self, eng, out, a, s1, s2, op0, op1=None, r=(), w=()):
        if op1 is None:
            self.op(eng, lambda e: e.tensor_scalar(out=out, in0=a, scalar1=s1, scalar2=None, op0=op0), r, w)
        else:
            self.op(eng, lambda e: e.tensor_scalar(out=out, in0=a, scalar1=s1, scalar2=s2, op0=op0, op1=op1), r, w)

    def stt(self, eng, out, a, s, b, op0, op1, r=(), w=()):
        self.op(eng, lambda e: e.scalar_tensor_tensor(out=out, in0=a, scalar=s, in1=b, op0=op0, op1=op1), r, w)

    def cp(self, eng, out, in_, r=(), w=()):
        if eng == 'act':
            self.act(out, in_, AF.Copy, r=r, w=w)
        else:
            self.op(eng, lambda e: e.tensor_copy(out=out, in_=in_), r, w)

    def sb(self, es, name, shape, dt):
        if DEBUG_ALLOC:
            print("alloc", name, shape, dt, "remaining", self.nc.sbuf_bytes_remaining)
        self.uid = getattr(self, "uid", 0) + 1
        return es.enter_context(self.nc.sbuf_tensor("sb%d_%s" % (self.uid, name), shape, dt))

    def finish(self):
        nc = self.nc
        q = self.q
        with nc.Block() as block:
            @block.tensor
            def _(e):
                for f in q['pe']:
                    f(e)

            @block.scalar
            def _(e):
                for f in q['act']:
                    f(e)

            @block.vector
            def _(e):
                for f in q['dve']:
                    f(e)

            @block.gpsimd
            def _(e):
                for f in q['pool']:
                    f(e)

            @block.sync
            def _(e):
                for f in q['sp']:
                    f(e)
        self.es.close()


class _Scope:
    def __init__(self, b):
        self.b = b
        self.es = ExitStack()

    def __enter__(self):
        self.es.__enter__()
        return self.es

    def __exit__(self, *a):
        self.b.barrier()
        return self.es.__exit__(*a)


def xa_keys(c, tis):
    return [('xa', c, ti) for ti in tis]


def h_keys(c, tis):
    return [('h', c, ti) for ti in tis]


ALLT = list(range(5))


class Prog:
    def __init__(self, nc, layers, nseq):
        self.nc = nc
        self.b = B(nc)
        self.layers = layers
        self.nseq = nseq
        self.dram = {}

    def din(self, name, shape, dt=F32):
        if name not in self.dram:
            self.dram[name] = self.nc.dram_tensor(name, list(shape), dt, kind="ExternalInput").ap()
        return self.dram[name]

    def dout(self, name, shape, dt=F32):
        if name not in self.dram:
            self.dram[name] = self.nc.dram_tensor(name, list(shape), dt, kind="ExternalOutput").ap()
        return self.dram[name]

    def build(self):
        b = self.b
        nc = self.nc
        es = b.es
        self.xa = b.sb(es, "xa", [128, KC, T], F32)
        self.hb = b.sb(es, "hb", [128, KC, T], BF16)
        self.cfa = b.sb(es, "cfa", [128, NCFA], F32)
        self.identb = b.sb(es, "identb", [128, 128], BF16)
        self.onesb = b.sb(es, "onesb", [128, 128], BF16)
        self.mean1k = b.sb(es, "mean1k", [128, 128], F32)
        self.mean1kb = b.sb(es, "mean1kb", [128, 128], BF16)
        cfa_d = self.din("cfa", [128, NCFA])
        b.dma(self.cfa[:], cfa_d, w=['cfa'])
        self.ident = self.cfa[:, 0:128]
        self.trif = self.cfa[:, 128:256]
        self.trib = self.cfa[:, 256:384]
        self.ones = self.cfa[:, 384:512]
        self.maskq = self.cfa[:, 512:768]
        self.rot = self.cfa[:, 768:896]
        self.epsv = self.cfa[:, 1152:1155]
        self.nident = self.cfa[:, 896:1024]
        self.nones = self.cfa[:, 1024:1152]
        b.cp('act', self.identb[:], self.ident, r=['cfa'], w=['identb'])
        b.cp('act', self.onesb[:], self.ones, r=['cfa'], w=['onesb'])
        b.act(self.mean1k[:], self.ones, AF.Copy, scale=1.0 / 1024.0, r=['cfa'], w=['mean1k'])
        b.act(self.mean1kb[:], self.ones, AF.Copy, scale=1.0 / 1024.0, r=['cfa'], w=['mean1k'])
        self.mod_all(es)
        outs = []
        for s in range(self.nseq):
            xin = self.din("xin%d" % s, [D, T])
            xout = self.dout("xout%d" % s, [D, T])
            with b.scope() as les:
                stg = [b.sb(les, "ldst%d" % i, [128, T], F32) for i in range(2)]
                for c in range(KC):
                    st = stg[c % 2]
                    b.dma(st[:], xin[c * 128:(c + 1) * 128, :], w=[('ldst', c % 2)])
                    b.act(self.xa[:, c, :], st[:], AF.Copy, scale=ALPHA, r=[('ldst', c % 2)], w=xa_keys(c, ALLT))
            for n, li in enumerate(self.layers):
                last = (n == len(self.layers) - 1)
                b.new_epoch()
                self.layer(li, s, out_plain=last)
            for c in range(KC):
                outs.append(b.dma(xout[c * 128:(c + 1) * 128, :], self.xa[:, c, :], r=xa_keys(c, ALLT)))
        for (sk, v) in outs + getattr(self, 'dbg_outs', []):
            if b.waited['sp'].get(sk, 0) < v:
                b.waited['sp'][sk] = v
                b.q['sp'].append(lambda e, sh=b.dsem[sk[1]], v=v: e.wait_ge(sh, v))
        b.finish()

    def layer(self, li, s, out_plain):
        b = self.b
        with b.scope() as les:
            self.lv = {}
            stop = DBG.get('stop')
            self.modulation(li, s, les, out_plain)
            if DBG.get('dump'):
                self.dbg_outs = getattr(self, 'dbg_outs', [])
                self.dbg_outs.append(b.dma(self.dout("dbg_MOD", [128, 96]), self.P['MOD'][:].rearrange("p a b -> p (a b)"), r=['MOD']))
            if stop == 'mod':
                return
            self.modulate_in()
            if DBG.get('dump'):
                self.dbg_outs.append(b.dma(self.dout("dbg_h", [128, KC * T], BF16), self.hb[:].rearrange("p a b -> p (a b)"),
                                           r=[('h', c, ti) for c in range(KC) for ti in ALLT]))
            if stop == 'modin':
                return
            if li % 2 == 0:
                self.gdn(li)
            else:
                self.attn(li)
            if stop == 'mixer':
                return
            self.layernorm(0, want_h=True)
            if stop == 'ln0':
                return
            self.ffn(li)
            if stop == 'ffn':
                return
            self.layernorm(1, want_h=False)

    def mod_all(self, es):
        b = self.b
        ns = 1 + self.nseq
        self.MODall = {}
        for li in self.layers:
            self.MODall[li] = b.sb(es, "MODall%d" % li, [128, 48, ns], F32)
        with b.scope() as wes:
            sc = b.sb(wes, "ma_sc", [128, KC, ns], F32)
            bm = b.sb(wes, "ma_bm", [128, 48], F32)
            wst = [b.sb(wes, "ma_wst%d" % i, [128, KC, 512], F32) for i in range(3)]
            cT = self.din("cTall", [128, KC, ns])
            b.dma(sc[:], cT, w=['sc'])
            b.act(sc[:], sc[:], AF.Silu, r=['sc'], w=['sc'])
            nblk = 0
            for li in self.layers:
                w_mod = self.din("w_mod%d" % li, [D, 6 * D])
                bmodT = self.din("bmodT%d" % li, [128, 48])
                b.dma(bm[:], bmodT, w=['bmodT'])
                ps, pk = b.ps(hold=True)
                for nb in range(12):
                    st = wst[nblk % 3]
                    sk = ('wmst', nblk % 3)
                    nblk += 1
                    b.dma(st[:], w_mod[:, nb * 512:(nb + 1) * 512].rearrange("(k p) n -> p k n", p=128), w=[sk])
                    for f in range(4):
                        fc = nb * 4 + f
                        for kc in range(KC):
                            b.mm(ps[:, fc * ns:(fc + 1) * ns], st[:, kc, f * 128:(f + 1) * 128], sc[:, kc, :],
                                 start=(kc == 0), stop=(kc == KC - 1), r=[sk, 'sc'], w=[pk])
                b.tt('dve', self.MODall[li][:], ps[:, 0:48 * ns].rearrange("p (a b) -> p a b", b=ns),
                     bm[:].unsqueeze(2).to_broadcast([128, 48, ns]), ALU.add, r=[pk, 'bmodT'], w=[('MODall', li)])
                b.release(pk)

    def modulation(self, li, s, les, out_plain):
        b = self.b
        P = {}
        for nm, shape in [('MOD', [128, 48, 2]), ('s1', [128, 8, 2]), ('H1s', [128, 8, 2]), ('H1b', [128, 8, 2]),
                          ('A', [128, 2, 8]), ('Bv', [128, 2, 8]), ('lnT', [128, 4, 8]), ('bmodT', [128, 48]),
                          ('sc', [128, 8, 2]), ('tmp1', [128, 8, 2])]:
            P[nm] = b.sb(les, "m_" + nm, shape, F32)
        self.P = P
        lnT = self.din("lnT%d" % li, [128, 4, 8])
        b.dma(P['lnT'][:], lnT, w=['lnT'])
        MOD = P['MOD']
        MA = self.MODall[li]
        b.cp('dve', MOD[:, :, 0:1], MA[:, :, 0:1], r=[('MODall', li)], w=['MOD'])
        b.cp('dve', MOD[:, :, 1:2], MA[:, :, 1 + s:2 + s], r=[('MODall', li)], w=['MOD'])

        def mj(j):
            return MOD[:, j * 8:(j + 1) * 8, :]
        b.ts('dve', P['s1'][:], mj(1), 1.0, 1.0 / ALPHA, ALU.add, ALU.mult, r=['MOD'], w=['s1'])
        ln = P['lnT']
        g0 = ln[:, 0, :].unsqueeze(2).to_broadcast([128, 8, 2])
        b0 = ln[:, 1, :].unsqueeze(2).to_broadcast([128, 8, 2])
        b.ts('dve', P['tmp1'][:], mj(4), 1.0, None, ALU.add, r=['MOD'], w=['tmp1'])
        b.tt('dve', P['H1s'][:], P['tmp1'][:], g0, ALU.mult, r=['tmp1', 'lnT'], w=['H1s'])
        b.tt('dve', P['H1b'][:], P['tmp1'][:], b0, ALU.mult, r=['tmp1', 'lnT'], w=['H1b'])
        b.tt('dve', P['H1b'][:], P['H1b'][:], mj(3), ALU.add, r=['H1b', 'MOD'], w=['H1b'])
        b.ts('dve', P['A'][:, 0, :], ln[:, 0, :], ALPHA, None, ALU.mult, r=['lnT'], w=['A'])
        b.ts('dve', P['Bv'][:, 0, :], ln[:, 1, :], ALPHA, None, ALU.mult, r=['lnT'], w=['Bv'])
        a2 = 1.0 if out_plain else ALPHA
        b.ts('dve', P['A'][:, 1, :], ln[:, 2, :], a2, None, ALU.mult, r=['lnT'], w=['A'])
        b.ts('dve', P['Bv'][:, 1, :], ln[:, 3, :], a2, None, ALU.mult, r=['lnT'], w=['Bv'])
        self.ga = mj(2)
        self.gaf = mj(5)
        self.sh = mj(0)

    def modulate_in(self):
        b = self.b
        P = self.P
        for c in range(KC):
            for (wh, t0, n, tis) in [(0, 0, 256, [0]), (1, 256, 2048, [1, 2, 3, 4])]:
                b.act(self.hb[:, c, t0:t0 + n], self.xa[:, c, t0:t0 + n], AF.Identity,
                      bias=self.sh[:, c, wh:wh + 1], scale=P['s1'][:, c, wh:wh + 1],
                      r=xa_keys(c, tis) + ['MOD', 's1'], w=h_keys(c, tis))

    def layernorm(self, idx, want_h):
        b = self.b
        P = self.P
        with b.scope() as es:
            sq = [b.sb(es, "ln_sq%d" % i, [128, 512], BF16) for i in range(4)]
            msb2 = [b.sb(es, "ln_msb%d" % i, [128, 512], F32) for i in range(2)]
            m22 = [b.sb(es, "ln_m2%d" % i, [128, 512], F32) for i in range(2)]
            rstd2 = [b.sb(es, "ln_rstd%d" % i, [128, 512], F32) for i in range(2)]
            tt_ = [b.sb(es, "ln_t%d" % i, [128, 512], F32) for i in range(4)]
            nsq = [0]

            def stats(ti):
                t0, n = TILES[ti]
                q_ = ti % 2
                msb, m2, rstd = msb2[q_], m22[q_], rstd2[q_]
                pm, pmk = b.ps(hold=True)
                pe2, pe2k = b.ps(hold=True)
                for c in range(KC):
                    k_ = nsq[0] % 4
                    nsq[0] += 1
                    b.act(sq[k_][:, :n], self.xa[:, c, t0:t0 + n], AF.Square, r=[('xa', c, ti)], w=[('lnsq', k_)])
                    b.mm(pm[:, :n], self.mean1k[:], self.xa[:, c, t0:t0 + n], start=(c == 0), stop=(c == KC - 1),
                         r=[('xa', c, ti), 'mean1k'], w=[pmk])
                    b.mm(pe2[:, :n], self.mean1kb[:], sq[k_][:, :n], start=(c == 0), stop=(c == KC - 1),
                         r=[('lnsq', k_), 'mean1k'], w=[pe2k])
                b.cp('act', msb[:, :n], pm[:, :n], r=[pmk], w=[('lnmsb', q_)])
                b.release(pmk)
                b.tt('dve', m2[:, :n], msb[:, :n], msb[:, :n], ALU.mult, r=[('lnmsb', q_)], w=[('lnm2', q_)])
                b.tt('dve', m2[:, :n], pe2[:, :n], m2[:, :n], ALU.subtract, r=[pe2k, ('lnm2', q_)], w=[('lnm2', q_)])
                b.release(pe2k)
                b.act(m2[:, :n], m2[:, :n], AF.Ln, bias=self.epsv[:, 0:1], r=[('lnm2', q_), 'cfa'], w=[('lnm2', q_)])
                b.act(rstd[:, :n], m2[:, :n], AF.Exp, scale=-0.5, r=[('lnm2', q_)], w=[('lnrstd', q_)])

            def norm(ti):
                t0, n = TILES[ti]
                wh = which_of(ti)
                q_ = ti % 2
                msb, rstd = msb2[q_], rstd2[q_]
                for c in range(KC):
                    t = tt_[c % 4]
                    tk = ('lnt', c % 4)
                    b.tt('dve', t[:, :n], self.xa[:, c, t0:t0 + n], msb[:, :n], ALU.subtract, r=[('xa', c, ti), ('lnmsb', q_)], w=[tk])
                    b.tt('dve', t[:, :n], t[:, :n], rstd[:, :n], ALU.mult, r=[tk, ('lnrstd', q_)], w=[tk])
                    b.act(self.xa[:, c, t0:t0 + n], t[:, :n], AF.Identity, bias=P['Bv'][:, idx, c:c + 1],
                          scale=P['A'][:, idx, c:c + 1], r=[tk, 'A', 'Bv'], w=[('xa', c, ti)])
                    if want_h:
                        b.ts('pool', self.hb[:, c, t0:t0 + n], t[:, :n], P['H1s'][:, c, wh:wh + 1], P['H1b'][:, c, wh:wh + 1],
                             ALU.mult, ALU.add, r=[tk, 'H1s', 'H1b'], w=[('h', c, ti)])
            stats(0)
            for ti in range(len(TILES)):
                if ti + 1 < len(TILES):
                    stats(ti + 1)
                norm(ti)

    def ffn(self, li):
        b = self.b
        w_in = self.din("w_ffn_in%d" % li, [D, 2 * DFF])
        w_out = self.din("w_ffn_out%d" % li, [DFF, D])
        with b.scope() as es:
            wis = [b.sb(es, "f_wis%d" % i, [128, KC, 512], F32) for i in range(2)]
            wos = [b.sb(es, "f_wos%d" % i, [128, 2, D], F32) for i in range(2)]
            wib = [b.sb(es, "f_wib%d" % i, [128, KC, 512], BF16) for i in range(2)]
            wob = [b.sb(es, "f_wob%d" % i, [128, 2, D], BF16) for i in range(2)]
            actb = b.sb(es, "f_act", [128, 2, T], BF16)
            sg = [b.sb(es, "f_sg%d" % i, [128, 512], F32) for i in range(2)]
            nsg = 0
            for fb in range(11):
                p = fb % 2
                f0 = fb * 256
                b.dma(wis[p][:, :, 0:256], w_in[:, f0:f0 + 256].rearrange("(k p) n -> p k n", p=128), w=[('wis', p, 0)])
                b.dma(wis[p][:, :, 256:512], w_in[:, DFF + f0:DFF + f0 + 256].rearrange("(k p) n -> p k n", p=128), w=[('wis', p, 1)])
                b.dma(wos[p][:], w_out[f0:f0 + 256, :].rearrange("(k p) n -> p k n", p=128), w=[('wos', p)])
                b.cp('pool', wib[p][:], wis[p][:], r=[('wis', p, 0), ('wis', p, 1)], w=[('wib', p)])
                b.cp('pool', wob[p][:], wos[p][:], r=[('wos', p)], w=[('wob', p)])
                for ti, (t0, n) in enumerate(TILES):
                    for j in range(2):
                        pg, pgk = b.ps()
                        pu, puk = b.ps()
                        for kc in range(KC):
                            b.mm(pg[:, :n], wib[p][:, kc, j * 128:(j + 1) * 128], self.hb[:, kc, t0:t0 + n],
                                 start=(kc == 0), stop=(kc == KC - 1), r=[('wib', p), ('h', kc, ti)], w=[pgk])
                        for kc in range(KC):
                            b.mm(pu[:, :n], wib[p][:, kc, 256 + j * 128:256 + (j + 1) * 128], self.hb[:, kc, t0:t0 + n],
                                 start=(kc == 0), stop=(kc == KC - 1), r=[('wib', p), ('h', kc, ti)], w=[puk])
                        s_ = sg[nsg % 2]
                        sk = ('fsg', nsg % 2)
                        nsg += 1
                        b.act(s_[:, :n], pg[:, :n], AF.Silu, r=[pgk], w=[sk])
                        b.tt('dve', actb[:, j, t0:t0 + n], pu[:, :n], s_[:, :n], ALU.mult, r=[puk, sk], w=[('fact', j, ti)])
                for ti, (t0, n) in enumerate(TILES):
                    wh = which_of(ti)
                    for oc in range(KC):
                        po, pok = b.ps()
                        for j in range(2):
                            b.mm(po[:, :n], wob[p][:, j, oc * 128:(oc + 1) * 128], actb[:, j, t0:t0 + n],
                                 start=(j == 0), stop=(j == 1), r=[('wob', p), ('fact', j, ti)], w=[pok])
                        b.stt('dve', self.xa[:, oc, t0:t0 + n], po[:, :n], self.gaf[:, oc, wh:wh + 1], self.xa[:, oc, t0:t0 + n],
                              ALU.mult, ALU.add, r=[pok, 'MOD', ('xa', oc, ti)], w=[('xa', oc, ti)])

    def out_proj(self, wb, wkey, nk, yfn, ykeys):
        b = self.b
        for ti, (t0, n) in enumerate(TILES):
            wh = which_of(ti)
            for oc in range(KC):
                po, pok = b.ps()
                for k in range(nk):
                    b.mm(po[:, :n], wb[:, k, oc * 128:(oc + 1) * 128], yfn(k, t0, n), start=(k == 0), stop=(k == nk - 1),
                         r=[wkey] + ykeys(k, ti), w=[pok])
                b.stt('dve', self.xa[:, oc, t0:t0 + n], po[:, :n], self.ga[:, oc, wh:wh + 1], self.xa[:, oc, t0:t0 + n],
                      ALU.mult, ALU.add, r=[pok, 'MOD', ('xa', oc, ti)], w=[('xa', oc, ti)])

    def attn(self, li):
        b = self.b
        j = li // 2
        w_qkv = self.din("attn_w_qkv%d" % j, [D, 1536])
        w_o = self.din("attn_w_out%d" % j, [D, D])
        gains = self.din("attn_gain%d" % j, [128, 2])
        ropeD = self.din("rope", [128, 4096])
        with b.scope() as es:
            QR = b.sb(es, "a_QR", [128, 8, T], BF16)
            KR = b.sb(es, "a_KR", [128, 2, T], BF16)
            VT = b.sb(es, "a_VT", [128, NCH, 256], BF16)
            gn = b.sb(es, "a_gn", [128, 2], F32)
            b.dma(gn[:], gains, w=['gn'])
            b.ts('dve', gn[:], gn[:], float(np.sqrt(128.0)), None, ALU.mult, r=['gn'], w=['gn'])
            with b.scope() as es2:
                rope = b.sb(es2, "a_rope", [128, 4096], F32)
                b.dma(rope[:], ropeD, w=['rope'])
                cosT = rope[:, 0:2048]
                sinT = rope[:, 2048:4096]
                wst_ = b.sb(es2, "a_wst", [128, KC, 128], F32)
                wst = [wst_, wst_]
                wbf = [b.sb(es2, "a_wbf%d" % i, [128, KC, 128], BF16) for i in range(2)]
                sqb = b.sb(es2, "a_sq", [128, 512], BF16)
                rs = b.sb(es2, "a_rs", [128, 512], F32)
                qn = b.sb(es2, "a_qn", [128, 512], F32)
                t1 = b.sb(es2, "a_t1", [128, 512], F32)
                t2 = b.sb(es2, "a_t2", [128, 512], F32)
                sqb2 = [sqb, b.sb(es2, "a_sq2", [128, 512], BF16)]
                qn2 = [qn, b.sb(es2, "a_qn2", [128, 512], F32)]

                def load_w(wbk):
                    p = wbk % 2
                    b.dma(wst[p][:], w_qkv[:, wbk * 128:(wbk + 1) * 128].rearrange("(k p) n -> p k n", p=128), w=[('awst', 0)])
                    b.cp('pool', wbf[p][:], wst[p][:], r=[('awst', 0)], w=[('awbf', p)])
                items = [(wbk, ti) for wbk in range(10) for ti in range(5)]
                st = {}

                def stA(i):
                    wbk, ti = items[i]
                    t0, n = TILES[ti]
                    p = wbk % 2
                    if ti == 0:
                        load_w(wbk)
                    pp, ppk = b.ps(hold=True)
                    for kc in range(KC):
                        b.mm(pp[:, :n], wbf[p][:, kc, :], self.hb[:, kc, t0:t0 + n],
                             start=(kc == 0), stop=(kc == KC - 1), r=[('awbf', p), ('h', kc, ti)], w=[ppk])
                    b.act(sqb2[i % 2][:, :n], pp[:, :n], AF.Square, r=[ppk], w=[('asq', i % 2)])
                    st[i] = (pp, ppk)

                def stB(i):
                    wbk, ti = items[i]
                    t0, n = TILES[ti]
                    pp, ppk = st.pop(i)
                    isq = wbk < 8
                    hidx = wbk if isq else wbk - 8
                    dst = QR if isq else KR
                    gcol = gn[:, 0:1] if isq else gn[:, 1:2]
                    p2, p2k = b.ps(hold=True)
                    b.mm(p2[:, :n], self.onesb[:], sqb2[i % 2][:, :n], r=[('asq', i % 2), 'onesb'], w=[p2k])
                    b.act(rs[:, :n], p2[:, :n], AF.Ln, bias=self.epsv[:, 1:2], r=[p2k, 'cfa'], w=['ars'])
                    b.release(p2k)
                    b.act(rs[:, :n], rs[:, :n], AF.Exp, scale=-0.5, r=['ars'], w=['ars'])
                    if ti == 0:
                        b.stt('dve', dst[:, hidx, t0:t0 + n], pp[:, :n], gcol, rs[:, :n], ALU.mult, ALU.mult,
                              r=[ppk, 'gn', 'ars'], w=[('aqk', isq, hidx, ti)])
                    else:
                        b.stt('dve', qn2[i % 2][:, :n], pp[:, :n], gcol, rs[:, :n], ALU.mult, ALU.mult,
                              r=[ppk, 'gn', 'ars'], w=[('aqn', i % 2)])
                    b.release(ppk)

                def stC(i):
                    wbk, ti = items[i]
                    if ti == 0:
                        return
                    t0, n = TILES[ti]
                    isq = wbk < 8
                    hidx = wbk if isq else wbk - 8
                    dst = QR if isq else KR
                    qn_ = qn2[i % 2]
                    p3, p3k = b.ps(hold=True)
                    b.mm(p3[:, :n], self.rot, qn_[:, :n], r=[('aqn', i % 2), 'cfa'], w=[p3k])
                    l0 = t0 - 256
                    b.tt('dve', t1[:, :n], qn_[:, :n], cosT[:, l0:l0 + n], ALU.mult, r=[('aqn', i % 2), 'rope'], w=['at1'])
                    b.tt('dve', t2[:, :n], p3[:, :n], sinT[:, l0:l0 + n], ALU.mult, r=[p3k, 'rope'], w=['at2'])
                    b.release(p3k)
                    b.tt('pool', dst[:, hidx, t0:t0 + n], t1[:, :n], t2[:, :n], ALU.add, r=['at1', 'at2'],
                         w=[('aqk', isq, hidx, ti)])
                nit = len(items)
                for i in range(nit + 2):
                    if i < nit:
                        stA(i)
                    if 0 <= i - 1 < nit:
                        stB(i - 1)
                    if 0 <= i - 2 < nit:
                        stC(i - 2)
                for wbk in (10, 11):
                    p = wbk % 2
                    load_w(wbk)
                    kvh = wbk - 10
                    for c in range(NCH):
                        pv, pvk = b.ps()
                        for kc in range(KC):
                            b.mm(pv[:, 0:128], self.hb[:, kc, c * 128:(c + 1) * 128], wbf[p][:, kc, :],
                                 start=(kc == 0), stop=(kc == KC - 1), r=[('awbf', p)] + h_keys(kc, ALLT), w=[pvk])
                        b.cp('act', VT[:, c, kvh * 128:(kvh + 1) * 128], pv[:, 0:128], r=[pvk], w=['VT'])
            with b.scope() as es3:
                pts = [b.sb(es3, "a_pt%d" % i, [128, 512], BF16) for i in range(3)]
                rden = b.sb(es3, "a_rden", [128, 512], F32)
                wos_ = b.sb(es3, "a_wos", [128, 2, D], F32)
                wos = [wos_, wos_]
                wob = b.sb(es3, "a_wob", [128, KC, D], BF16)
                for i in range(4):
                    b.dma(wos[i % 2][:], w_o[i * 256:(i + 1) * 256, :].rearrange("(k p) n -> p k n", p=128), w=[('aos', 0)])
                    b.cp('pool', wob[:, 2 * i:2 * i + 2, :], wos[i % 2][:], r=[('aos', 0)], w=['awob'])
                npt = 0
                scale = float(128.0 ** -0.5)
                for hq in range(8):
                    kv = hq // 4
                    for ti, (t0, n) in enumerate(TILES):
                        kts = [0, 1] if ti == 0 else list(range(NCH))
                        pden, pdk = b.ps(hold=True)
                        po, pok = b.ps(hold=True)
                        prev = None

                        def flush(pv_):
                            pt_, ptk_, ii_, kt_ = pv_
                            b.mm(pden[:, :n], self.onesb[:], pt_[:, :n], start=(ii_ == 0), stop=(ii_ == len(kts) - 1),
                                 r=[ptk_, 'onesb'], w=[pdk])
                            b.mm(po[:, :n], VT[:, kt_, kv * 128:(kv + 1) * 128], pt_[:, :n], start=(ii_ == 0), stop=(ii_ == len(kts) - 1),
                                 r=[ptk_, 'VT'], w=[pok])
                        for ii, kt in enumerate(kts):
                            psc, psk = b.ps()
                            b.mm(psc[:, :n], KR[:, kv, kt * 128:(kt + 1) * 128], QR[:, hq, t0:t0 + n],
                                 r=[('aqk', False, kv, tt_) for tt_ in ALLT] + [('aqk', True, hq, ti)], w=[psk])
                            pt = pts[npt % 3]
                            ptk = ('apt', npt % 3)
                            npt += 1
                            b.act(pt[:, :n], psc[:, :n], AF.Exp, scale=scale, r=[psk], w=[ptk])
                            if prev is not None:
                                flush(prev)
                            prev = (pt, ptk, ii, kt)
                        flush(prev)
                        b.op('dve', lambda e, o=rden[:, :n], i=pden[:, :n]: e.reciprocal(out=o, in_=i), r=[pdk], w=['arden'])
                        b.tt('dve', self.hb[:, hq, t0:t0 + n], po[:, :n], rden[:, :n], ALU.mult, r=[pok, 'arden'], w=[('h', hq, ti)])
                        b.release(pdk)
                        b.release(pok)
                self.out_proj(wob, 'awob', KC, lambda k, t0, n: self.hb[:, k, t0:t0 + n], lambda k, ti: [('h', k, ti)])

    def gdn(self, li):
        b = self.b
        j = li // 2
        w_in = self.din("gdn_w_in%d" % j, [D, 6208])
        w_out = self.din("gdn_w_out%d" % j, [2048, D])
        convD = self.din("gdn_convT%d" % j, [128, 32, 5])
        gparD = self.din("gdn_gpar%d" % j, [128, 64])
        normgD = self.din("gdn_normg%d" % j, [128, 128])
        lvlD = self.din("lvlmask", [128, 7 * 4 * 128], U8)
        ident = self.ident
        one_col = self.epsv[:, 2:3]
        eps_col = self.epsv[:, 0:1]

        def bc(ap2, n=128):
            return ap2.unsqueeze(2).to_broadcast([128, ap2.shape[1], n])

        with b.scope() as es:
            G = {}
            for nm in ['NBETA', 'BETA', 'GCUM', 'EG', 'KD', 'EGL']:
                G[nm] = b.sb(es, "g_" + nm, [128, NCH, 32], F32)
            convw = b.sb(es, "g_convw", [128, 32, 5], F32)
            normg = b.sb(es, "g_normg", [128, 128], F32)
            lvl = b.sb(es, "g_lvl", [128, 7, 4, 128], U8)
            b.dma(convw[:], convD, w=['convw'])
            b.dma(normg[:], normgD, w=['normg'])
            b.dma(lvl[:].rearrange("p a b c -> p (a b c)"), lvlD, w=['lvl'])
            with b.scope() as ges:
                gpar = b.sb(ges, "g_gpar", [128, 64], F32)
                wgs = b.sb(ges, "g_wgs", [128, KC, 64], F32)
                wgb = b.sb(ges, "g_wgb", [128, KC, 64], BF16)
                GRAW = b.sb(ges, "g_graw", [128, NCH, 64], F32)
                T1 = b.sb(ges, "g_t1", [128, NCH, 32], F32)
                T2 = b.sb(ges, "g_t2", [128, NCH, 32], F32)
                GG = b.sb(ges, "g_g", [128, NCH, 32], F32)
                GL = b.sb(ges, "g_gl", [128, NCH, 32], F32)
                NA = b.sb(ges, "g_na", [128, 32], F32)
                b.dma(gpar[:], gparD, w=['gpar'])
                b.dma(wgs[:], w_in[:, 6144:6208].rearrange("(k p) n -> p k n", p=128), w=['wgs'])
                b.cp('pool', wgb[:], wgs[:], r=['wgs'], w=['wgb'])
                for c0 in range(0, NCH, 8):
                    nc_ = min(8, NCH - c0)
                    pg, pgk = b.ps()
                    for cc in range(nc_):
                        c = c0 + cc
                        for kc in range(KC):
                            b.mm(pg[:, cc * 64:(cc + 1) * 64], self.hb[:, kc, c * 128:(c + 1) * 128], wgb[:, kc, :],
                                 start=(kc == 0), stop=(kc == KC - 1), r=['wgb'] + h_keys(kc, ALLT), w=[pgk])
                    b.cp('act', GRAW[:, c0:c0 + nc_, :], pg[:, 0:nc_ * 64].rearrange("p (a b) -> p a b", b=64), r=[pgk], w=['graw'])
                braw = GRAW[:, :, 0:32]
                araw = GRAW[:, :, 32:64]
                b.act(T1[:], braw, AF.Exp, scale=-1.0, r=['graw'], w=['gt1'])
                b.act(T1[:], T1[:], AF.Ln, bias=one_col, r=['gt1', 'cfa'], w=['gt1'])
                b.act(G['BETA'][:], T1[:], AF.Exp, scale=-1.0, r=['gt1'], w=['BETA'])
                b.ts('pool', G['NBETA'][:], G['BETA'][:], -1.0, None, ALU.mult, r=['BETA'], w=['NBETA'])
                b.tt('dve', T2[:], araw, gpar[:, 32:64].unsqueeze(1).to_broadcast([128, NCH, 32]), ALU.add, r=['graw', 'gpar'], w=['gt2'])
                b.act(T2[:], T2[:], AF.Exp, r=['gt2'], w=['gt2'])
                b.act(T2[:], T2[:], AF.Ln, bias=one_col, r=['gt2', 'cfa'], w=['gt2'])
                b.act(NA[:], gpar[:, 0:32], AF.Exp, r=['gpar'], w=['gna'])
                b.ts('pool', NA[:], NA[:], -1.0, None, ALU.mult, r=['gna'], w=['gna'])
                b.tt('dve', GG[:], T2[:], NA[:].unsqueeze(1).to_broadcast([128, NCH, 32]), ALU.mult, r=['gt2', 'gna'], w=['gg'])
                pc, pck = b.ps()
                b.mm(pc[:, 0:288], self.trif, GG[:, :, 0:16], r=['gg', 'cfa'], w=[pck])
                pc2, pc2k = b.ps()
                b.mm(pc2[:, 0:288], self.trib, GG[:, :, 16:32], r=['gg', 'cfa'], w=[pc2k])
                b.cp('act', G['GCUM'][:, :, 0:16], pc[:, 0:288].rearrange("p (a b) -> p a b", b=16), r=[pck], w=['GCUM'])
                b.cp('act', G['GCUM'][:, :, 16:32], pc2[:, 0:288].rearrange("p (a b) -> p a b", b=16), r=[pc2k], w=['GCUM'])
                pl, plk = b.ps()
                b.mm(pl[:, 0:288], self.ones, GG[:, 0:9, :], r=['gg', 'cfa'], w=[plk])
                pl2, pl2k = b.ps()
                b.mm(pl2[:, 0:288], self.ones, GG[:, 9:18, :], r=['gg', 'cfa'], w=[pl2k])
                b.cp('act', GL[:, 0:9, :], pl[:, 0:288].rearrange("p (a b) -> p a b", b=32), r=[plk], w=['ggl'])
                b.cp('act', GL[:, 9:18, :], pl2[:, 0:288].rearrange("p (a b) -> p a b", b=32), r=[pl2k], w=['ggl'])
                b.act(G['EG'][:], G['GCUM'][:], AF.Exp, r=['GCUM'], w=['EG'])
                b.tt('dve', T1[:], GL[:], G['GCUM'][:], ALU.subtract, r=['ggl', 'GCUM', 'gt1'], w=['gt1'])
                b.act(G['KD'][:], T1[:], AF.Exp, r=['gt1'], w=['KD'])
                b.act(G['EGL'][:], GL[:], AF.Exp, r=['ggl'], w=['EGL'])
            for g in range(8):
                self.gdn_group(li, g, es, G, convw, normg, lvl, w_in, w_out, bc)

    def gdn_group(self, li, g, es_unused, G, convw, normg, lvl, w_in, w_out, bc):
        b = self.b
        ident = self.ident
        eps_col = self.epsv[:, 0:1]
        ps_bf = lambda ps: ps[:].bitcast(BF16)
        gk = lambda nm: nm + "_%d" % g

        def gcols(d):
            return slice(d * 16 + 2 * g, d * 16 + 2 * g + 2)

        with b.scope() as ges:
            O = b.sb(ges, "g_O", [128, NCH, 2, 128], BF16)
            with b.scope() as aes:
                KQ = b.sb(aes, "g_KQ", [128, NCH, 2, 128], BF16)
                KTM = b.sb(aes, "g_KTM", [128, NCH, 128], BF16)
                VTM = b.sb(aes, "g_VTM", [128, NCH, 2, 128], BF16)
                with b.scope() as pes:
                    CB = b.sb(pes, "g_CB", [128, 2310], F32)
                    ACC = b.sb(pes, "g_ACC", [128, T], F32)
                    TB = b.sb(pes, "g_TB", [128, T], BF16)
                    wst = [b.sb(pes, "g_wst%d" % i, [128, KC, 128], F32) for i in range(2)]
                    wbf = [b.sb(pes, "g_wbf%d" % i, [128, KC, 128], BF16) for i in range(2)]
                    rs = b.sb(pes, "g_rs", [128, 512], F32)
                    b.op('pool', lambda e: e.memset(CB[:, 0:2], 0.0), w=['CBp0'])
                    b.op('pool', lambda e: e.memset(CB[:, 258:260], 0.0), w=['CBp1'])
                    b.op('pool', lambda e: e.memset(CB[:, 2308:2310], 0.0), w=['CBp2'])
                    fcs = [('q', g * 128, g), ('k', 1024 + g * 128, 8 + g),
                           ('v0', 2048 + (2 * g) * 128, 16 + 2 * g), ('v1', 2048 + (2 * g + 1) * 128, 16 + 2 * g + 1)]
                    def emit_proj(fi):
                        kind, col0, cq = fcs[fi]
                        held = []
                        p = fi % 2
                        b.dma(wst[p][:], w_in[:, col0:col0 + 128].rearrange("(k p) n -> p k n", p=128), w=[('gwst', p)])
                        b.cp('pool', wbf[p][:], wst[p][:], r=[('gwst', p)], w=[('gwbf', p)])
                        for ti, (t0, n) in enumerate(TILES):
                            pp, ppk = b.ps(hold=True)
                            for kc in range(KC):
                                b.mm(pp[:, :n], wbf[p][:, kc, :], self.hb[:, kc, t0:t0 + n], start=(kc == 0), stop=(kc == KC - 1),
                                     r=[('gwbf', p), ('h', kc, ti)], w=[ppk])
                            o0 = 2 if ti == 0 else t0 + 4
                            b.cp('act', CB[:, o0:o0 + n], pp[:, :n], r=[ppk], w=[('CB', ti)])
                            held.append(ppk)
                        return held

                    def emit_conv(fi):
                        kind, col0, cq = fcs[fi]
                        acck = [('ACC', 0), ('ACC', 256), ('ACC', 1280)]
                        for (d0, L, s0, tis) in [(0, 256, 2, [0]), (256, 1024, 260, [1, 2]), (1280, 1024, 1284, [3, 4])]:
                            ak = ('ACC', d0)
                            rk = [('CB', tj) for tj in {0: [0], 256: [1, 2, 3], 1280: [2, 3, 4]}[d0]] + ['CBp0', 'CBp1', 'CBp2', 'convw']
                            b.ts('dve', ACC[:, d0:d0 + L], CB[:, s0 - 2:s0 - 2 + L], convw[:, cq, 0:1], None, ALU.mult, r=rk, w=[ak])
                            for jj in range(1, 5):
                                b.stt('dve', ACC[:, d0:d0 + L], CB[:, s0 - 2 + jj:s0 - 2 + jj + L], convw[:, cq, jj:jj + 1],
                                      ACC[:, d0:d0 + L], ALU.mult, ALU.add, r=rk + [ak], w=[ak])
                            if kind in ('q', 'k'):
                                b.act(ACC[:, d0:d0 + L], ACC[:, d0:d0 + L], AF.Silu, r=[ak], w=[ak])
                            else:
                                b.act(TB[:, d0:d0 + L], ACC[:, d0:d0 + L], AF.Silu, r=[ak], w=['TB'])

                    def emit_tail(fi):
                        kind, col0, cq = fcs[fi]
                        acck = [('ACC', 0), ('ACC', 256), ('ACC', 1280)]
                        if kind in ('q', 'k'):
                            b.act(TB[:], ACC[:], AF.Square, r=acck, w=['TB'])
                            kq = 0 if kind == 'k' else 1
                            sc_ = 1.0 if kind == 'k' else float(128.0 ** -0.5)
                            for ti, (t0, n) in enumerate(TILES):
                                p2, p2k = b.ps()
                                b.mm(p2[:, :n], self.onesb[:], TB[:, t0:t0 + n], r=['TB', 'onesb'], w=[p2k])
                                b.act(rs[:, :n], p2[:, :n], AF.Ln, bias=eps_col, r=[p2k, 'cfa'], w=['grs'])
                                b.act(rs[:, :n], rs[:, :n], AF.Exp, scale=-0.5, r=['grs'], w=['grs'])
                                c0 = t0 // 128
                                nc_ = n // 128
                                b.stt('dve', KQ[:, c0:c0 + nc_, kq, :], ACC[:, t0:t0 + n].rearrange("p (a b) -> p a b", b=128), sc_,
                                      rs[:, :n].rearrange("p (a b) -> p a b", b=128), ALU.mult, ALU.mult,
                                      r=acck + ['grs'], w=[gk('KQ')])
                            if kind == 'k':
                                for c0 in range(0, NCH, 4):
                                    nc_ = min(4, NCH - c0)
                                    pt, ptk = b.ps()
                                    ptb = ps_bf(pt)
                                    for cc in range(nc_):
                                        b.tr(ptb[:, cc * 128:(cc + 1) * 128], KQ[:, c0 + cc, 0, :], self.identb[:], r=[gk('KQ'), 'identb'], w=[ptk])
                                    b.cp('act', KTM[:, c0:c0 + nc_, :], ptb[:, 0:nc_ * 128].rearrange("p (a b) -> p a b", b=128), r=[ptk], w=[gk('KTM')])
                        else:
                            a_ = 0 if kind == 'v0' else 1
                            for c0 in range(0, NCH, 4):
                                nc_ = min(4, NCH - c0)
                                pt, ptk = b.ps()
                                ptb = ps_bf(pt)
                                for cc in range(nc_):
                                    b.tr(ptb[:, cc * 128:(cc + 1) * 128], TB[:, (c0 + cc) * 128:(c0 + cc + 1) * 128], self.identb[:], r=['TB', 'identb'], w=[ptk])
                                b.cp('act', VTM[:, c0:c0 + nc_, a_, :], ptb[:, 0:nc_ * 128].rearrange("p (a b) -> p a b", b=128), r=[ptk], w=[gk('VTM')])

                    for k_ in emit_proj(0):
                        b.release(k_)
                    for fi in range(4):
                        emit_conv(fi)
                        hk = emit_proj(fi + 1) if fi + 1 < 4 else []
                        emit_tail(fi)
                        for k_ in hk:
                            b.release(k_)
                with b.scope() as ses:
                    NL = GDN_LANES

                    def t4(nm, dt):
                        return b.sb(ses, "g_" + nm, [128, 4, 128], dt)
                    DG = b.sb(ses, "g_DG", [128, 4, 128], F32)
                    BGEg = b.sb(ses, "g_BGEg", [128, NCH, 4], F32)
                    C1 = t4("C1", F32)
                    TO = C1
                    DT = t4("DT", BF16)
                    Dm = t4("Dm", BF16)
                    KKn = t4("KKn", BF16)
                    QKs = b.sb(ses, "g_QKs", [128, 2, 128], BF16)
                    VN = t4("VN", BF16)
                    S = t4("S", F32)
                    Sbf = t4("Sbf", BF16)
                    VB = t4("VB", BF16)
                    KBG = t4("KBG", BF16)
                    KDEC = t4("KDEC", BF16)
                    NWT = t4("NWT", BF16)
                    lanes = []
                    for k in range(NL):
                        lanes.append({nm: t4("%s_l%d" % (nm, k), BF16) for nm in ['Ap', 'QKD', 'Y', 'W', 'X']})
                        lanes[-1]['k'] = k
                    b.op('pool', lambda e: e.memset(S[:], 0.0), w=['S'])
                    b.op('pool', lambda e: e.memset(Sbf[:], 0.0), w=['Sbf'])
                    for d in range(2):
                        b.tt('pool', BGEg[:, :, 2 * d:2 * d + 2], G['BETA'][:, :, gcols(d)], G['EG'][:, :, gcols(d)], ALU.mult, r=['BETA', 'EG'], w=['BGEg'])
                    owritten = set()
                    f2 = lambda ap: ap.rearrange("p a b -> p (a b)")
                    idb = ident.unsqueeze(1).to_broadcast([128, 2, 128])
                    nidb = self.nident.unsqueeze(1).to_broadcast([128, 2, 128])

                    def step_gen(s, Ln):
                        lk = lambda nm: (nm, Ln['k'])
                        Ap, QKD, Y, W, X = (Ln[nm] for nm in ['Ap', 'QKD', 'Y', 'W', 'X'])
                        cds = [FWD[s], BWD[s]]

                        def scale_ops(which):
                            for u in range(4):
                                d, a_ = u // 2, u % 2
                                cd = cds[d]
                                col = d * 16 + 2 * g + a_
                                if which == 'VB':
                                    b.act(VB[:, u, :], VTM[:, cd, a_, :], AF.Copy, scale=G['BETA'][:, cd, col:col + 1], r=[gk('VTM'), 'BETA'], w=['VB'])
                                elif which == 'KBG':
                                    b.act(KBG[:, u, :], KTM[:, cd, :], AF.Copy, scale=BGEg[:, cd, u:u + 1], r=[gk('KTM'), 'BGEg'], w=['KBG'])
                                else:
                                    b.act(KDEC[:, u, :], KTM[:, cd, :], AF.Copy, scale=G['KD'][:, cd, col:col + 1], r=[gk('KTM'), 'KD'], w=['KDEC'])
                        pkq, pkqk = b.ps(hold=True)
                        for d in range(2):
                            cd = cds[d]
                            b.mm(pkq[:, d * 256:(d + 1) * 256], KQ[:, cd, 0, :], KQ[:, cd, :, :].rearrange("p a b -> p (a b)"),
                                 r=[gk('KQ')], w=[pkqk])
                        for d in range(2):
                            cd = cds[d]
                            b.tt('pool', DG[:, 2 * d:2 * d + 2, :], idb, bc(G['GCUM'][:, cd, gcols(d)]), ALU.mult, r=['GCUM', 'cfa'], w=['DG'])
                        pe_, pek = b.ps(hold=True)
                        for u in range(4):
                            b.mm(pe_[:, u * 128:(u + 1) * 128], self.ones, DG[:, u, :], start=True, stop=False, r=['DG', 'cfa'], w=[pek])
                            b.mm(pe_[:, u * 128:(u + 1) * 128], DG[:, u, :], self.nones, start=False, stop=True, r=['DG', 'cfa'], w=[pek])
                        b.ts('dve', f2(C1[:]), pe_[:], 0.0, None, ALU.min, r=[pek], w=['C1'])
                        b.act(f2(DT[:]), f2(C1[:]), AF.Exp, r=['C1'], w=['DT'])
                        b.ts('dve', f2(C1[:]), pe_[:], 0.0, None, ALU.max, r=[pek], w=['C1'])
                        b.release(pek)
                        b.act(f2(Dm[:]), f2(C1[:]), AF.Exp, scale=-1.0, r=['C1'], w=['Dm'])
                        for d in range(2):
                            cd = cds[d]
                            kk = pkq[:, d * 256:d * 256 + 128].unsqueeze(1).to_broadcast([128, 2, 128])
                            b.tt('dve', KKn[:, 2 * d:2 * d + 2, :], kk, bc(G['NBETA'][:, cd, gcols(d)]), ALU.mult, r=[pkqk, 'NBETA'], w=['KKn'])
                        qkv_ = pkq[:].rearrange("p (d x) -> p d x", d=2)[:, :, 128:256]
                        b.tt('dve', QKs[:], qkv_, self.maskq.rearrange("p (d x) -> p d x", d=2), ALU.mult, r=[pkqk, 'cfa'], w=['QKs'])
                        b.release(pkqk)
                        yield
                        b.tt('pool', f2(Ap[:]), f2(KKn[:]), f2(Dm[:]), ALU.mult, r=['KKn', 'Dm'], w=[lk('Ap')])
                        b.tt('pool', QKD[:].rearrange("p (d a) x -> p d a x", d=2), QKs[:].unsqueeze(2).to_broadcast([128, 2, 2, 128]),
                             DT[:].rearrange("p (d a) x -> p d a x", d=2), ALU.mult, r=['QKs', 'DT'], w=[lk('QKD')])
                        ptt, pttk = b.ps(hold=True)
                        pttb = ps_bf(ptt)
                        for u in range(4):
                            b.tr(pttb[:, u * 128:(u + 1) * 128], Ap[:, u, :], self.identb[:], r=[lk('Ap'), 'identb'], w=[pttk])
                        b.cp('pool', Y[:], self.identb[:].unsqueeze(1).to_broadcast([128, 4, 128]), r=['identb'], w=[lk('Y')])
                        b.op('dve', lambda e, o=f2(Y[:]), m=f2(lvl[:, 0, :, :]), dd=pttb[:, 0:512]: e.copy_predicated(o, m, dd),
                             r=[pttk, 'lvl', lk('Y')], w=[lk('Y')])
                        b.release(pttk)
                        yield
                        for l in range(1, 7):
                            pw, pwk = b.ps(hold=True)
                            for u in range(4):
                                b.mm(pw[:, u * 128:(u + 1) * 128], Ap[:, u, :], Y[:, u, :], r=[lk('Ap'), lk('Y')], w=[pwk])
                            px, pxk = b.ps(hold=True)
                            pxb = ps_bf(px)
                            for u in range(4):
                                b.tr(pxb[:, u * 128:(u + 1) * 128], Y[:, u, :], self.identb[:], r=[lk('Y'), 'identb'], w=[pxk])
                            b.cp('act', f2(W[:]), pw[:], r=[pwk], w=[lk('W')])
                            b.cp('dve', f2(X[:]), pxb[:, 0:512], r=[pxk], w=[lk('X')])
                            b.release(pwk)
                            b.release(pxk)
                            yield
                            pz, pzk = b.ps(hold=True)
                            for u in range(4):
                                b.mm(pz[:, u * 128:(u + 1) * 128], X[:, u, :], W[:, u, :], r=[lk('X'), lk('W')], w=[pzk])
                            b.op('dve', lambda e, o=f2(Y[:]), m=f2(lvl[:, l, :, :]), dd=pz[:]: e.copy_predicated(o, m, dd),
                                 r=[pzk, 'lvl', lk('Y')], w=[lk('Y')])
                            b.release(pzk)
                            if l == 6:
                                scale_ops('KBG')
                            yield
                        pwt, pwtk = b.ps(hold=True)
                        for u in range(4):
                            b.mm(pwt[:, u * 128:(u + 1) * 128], KBG[:, u, :], Y[:, u, :], r=['KBG', lk('Y')], w=[pwtk])
                        b.act(f2(NWT[:]), pwt[:], AF.Copy, scale=-1.0, r=[pwtk], w=['NWT'])
                        b.release(pwtk)
                        scale_ops('VB')
                        yield
                        pvn, pvnk = b.ps(hold=True)
                        for u in range(4):
                            b.mm(pvn[:, u * 128:(u + 1) * 128], Y[:, u, :], VB[:, u, :], start=True, stop=False, r=[lk('Y'), 'VB'], w=[pvnk])
                            b.mm(pvn[:, u * 128:(u + 1) * 128], NWT[:, u, :], Sbf[:, u, :], start=False, stop=True, r=['NWT', 'Sbf'], w=[pvnk])
                        b.cp('act', f2(VN[:]), pvn[:], r=[pvnk], w=['VN'])
                        b.release(pvnk)
                        scale_ops('KDEC')
                        yield
                        pds, pdsk = b.ps(hold=True)
                        po1, po1k = b.ps(hold=True)
                        po2, po2k = b.ps(hold=True)
                        for u in range(4):
                            b.mm(pds[:, u * 128:(u + 1) * 128], KDEC[:, u, :], VN[:, u, :], r=['KDEC', 'VN'], w=[pdsk])
                        for u in range(4):
                            cd = cds[u // 2]
                            b.mm(po1[:, u * 128:(u + 1) * 128], KQ[:, cd, 1, :], Sbf[:, u, :], r=[gk('KQ'), 'Sbf'], w=[po1k])
                        for u in range(4):
                            b.mm(po2[:, u * 128:(u + 1) * 128], QKD[:, u, :], VN[:, u, :], r=[lk('QKD'), 'VN'], w=[po2k])
                        for d in range(2):
                            cd = cds[d]
                            b.tt('pool', S[:, 2 * d:2 * d + 2, :], S[:, 2 * d:2 * d + 2, :], bc(G['EGL'][:, cd, gcols(d)]), ALU.mult,
                                 r=['S', 'EGL'], w=['S'])
                        b.tt('dve', f2(S[:]), f2(S[:]), pds[:], ALU.add, r=['S', pdsk], w=['S'])
                        b.release(pdsk)
                        b.cp('act', f2(Sbf[:]), f2(S[:]), r=['S'], w=['Sbf'])
                        for d in range(2):
                            cd = cds[d]
                            b.tt('dve', TO[:, 2 * d:2 * d + 2, :], po1[:, d * 256:(d + 1) * 256].rearrange("p (a x) -> p a x", a=2),
                                 bc(G['EG'][:, cd, gcols(d)]), ALU.mult, r=[po1k, 'EG'], w=['C1'])
                        b.tt('dve', f2(TO[:]), f2(TO[:]), po2[:], ALU.add, r=['C1', po2k], w=['C1'])
                        b.release(po1k)
                        b.release(po2k)
                        for d in range(2):
                            cd = cds[d]
                            if cd not in owritten:
                                owritten.add(cd)
                                b.cp('act', O[:, cd, :, :], TO[:, 2 * d:2 * d + 2, :], r=['C1'], w=[gk('O')])
                            else:
                                b.tt('pool', O[:, cd, :, :], O[:, cd, :, :], TO[:, 2 * d:2 * d + 2, :], ALU.add, r=['C1', gk('O')], w=[gk('O')])
                        yield

                    gens = []
                    next_s = 0
                    turn = 0
                    stagger = GDN_STAGGER
                    while next_s < NCH or gens:
                        if next_s < NCH and turn % stagger == 0 and len(gens) < NL:
                            gens.append(step_gen(next_s, lanes[next_s % NL]))
                            next_s += 1
                        for g_ in list(gens):
                            try:
                                next(g_)
                            except StopIteration:
                                gens.remove(g_)
                        turn += 1
            with b.scope() as oes:
                ZS = b.sb(oes, "g_ZS", [128, NCH, 256], BF16)
                wzs = b.sb(oes, "g_wzs", [128, KC, 256], F32)
                wzb = b.sb(oes, "g_wzb", [128, KC, 256], BF16)
                wos = b.sb(oes, "g_wos", [128, 2, D], F32)
                wob = b.sb(oes, "g_wob", [128, 2, D], BF16)
                OSQ = b.sb(oes, "g_OSQ", [128, 6, 2, 128], F32)
                SS = b.sb(oes, "g_SS", [128, NCH, 2], F32)
                YT = b.sb(oes, "g_YT", [128, 6, 2, 128], F32)
                YTb = b.sb(oes, "g_YTb", [128, 6, 2, 128], BF16)
                YF = b.sb(oes, "g_YF", [128, 2, T], BF16)
                zc0 = 4096 + 2 * g * 128
                b.dma(wzs[:], w_in[:, zc0:zc0 + 256].rearrange("(k p) n -> p k n", p=128), w=['wzs'])
                b.cp('pool', wzb[:], wzs[:], r=['wzs'], w=['wzb'])
                b.dma(wos[:], w_out[2 * g * 128:(2 * g + 2) * 128, :].rearrange("(k p) n -> p k n", p=128), w=['gwos'])
                b.cp('pool', wob[:], wos[:], r=['gwos'], w=['gwob'])
                for c0 in range(0, NCH, 2):
                    pz_, pzk_ = b.ps()
                    for cc in range(2):
                        c = c0 + cc
                        for kc in range(KC):
                            b.mm(pz_[:, cc * 256:(cc + 1) * 256], self.hb[:, kc, c * 128:(c + 1) * 128], wzb[:, kc, :],
                                 start=(kc == 0), stop=(kc == KC - 1), r=['wzb'] + h_keys(kc, ALLT), w=[pzk_])
                    b.act(ZS[:, c0:c0 + 2, :], pz_[:].rearrange("p (a b) -> p a b", a=2), AF.Silu, r=[pzk_], w=['ZS'])
                for c0 in range(0, NCH, 6):
                    Oc = O[:, c0:c0 + 6, :, :]
                    b.tt('pool', OSQ[:], Oc, Oc, ALU.mult, r=[gk('O')], w=['OSQ'])
                    b.op('dve', lambda e, o=SS[:, c0:c0 + 6, :], i=OSQ[:]: e.tensor_reduce(out=o, in_=i, axis=AX.X, op=ALU.add), r=['OSQ'], w=['SS'])
                b.act(SS[:], SS[:], AF.Ln, bias=eps_col, scale=1.0 / 128.0, r=['SS', 'cfa'], w=['SS'])
                b.act(SS[:], SS[:], AF.Exp, scale=-0.5, r=['SS'], w=['SS'])
                for c0 in range(0, NCH, 6):
                    Oc = O[:, c0:c0 + 6, :, :]
                    b.tt('dve', YT[:], Oc, SS[:, c0:c0 + 6, :].unsqueeze(3).to_broadcast([128, 6, 2, 128]), ALU.mult, r=[gk('O'), 'SS'], w=['YT'])
                    b.tt('pool', YT[:].rearrange("p a b c -> p (a b) c"), YT[:].rearrange("p a b c -> p (a b) c"),
                         normg[:].unsqueeze(1).to_broadcast([128, 12, 128]), ALU.mult, r=['YT', 'normg'], w=['YT'])
                    b.tt('dve', YTb[:], YT[:], ZS[:, c0:c0 + 6, :].rearrange("p a (b c) -> p a b c", b=2), ALU.mult, r=['YT', 'ZS'], w=['YTb'])
                    for a_ in range(2):
                        for q4 in range(0, 6, 4):
                            nq = min(4, 6 - q4)
                            pt, ptk = b.ps()
                            ptb = ps_bf(pt)
                            for cc in range(nq):
                                b.tr(ptb[:, cc * 128:(cc + 1) * 128], YTb[:, q4 + cc, a_, :], self.identb[:], r=['YTb', 'identb'], w=[ptk])
                            t0_ = (c0 + q4) * 128
                            b.cp('act', YF[:, a_, t0_:t0_ + nq * 128], ptb[:, 0:nq * 128], r=[ptk], w=['YF'])
                self.out_proj(wob, 'gwob', 2, lambda k, t0, n: YF[:, k, t0:t0 + n], lambda k, ti: ['YF'])


def host_consts():
    cfa = np.zeros((128, NCFA), np.float32)
    idx = np.arange(128)
    cfa[:, 0:128] = np.eye(128, dtype=np.float32)
    cfa[:, 128:256] = (idx[:, None] <= idx[None, :]).astype(np.float32)
    cfa[:, 256:384] = (idx[:, None] >= idx[None, :]).astype(np.float32)
    cfa[:, 384:512] = 1.0
    cfa[:, 512:640] = (idx[None, :] >= idx[:, None]).astype(np.float32)
    cfa[:, 640:768] = (idx[None, :] <= idx[:, None]).astype(np.float32)
    rot = np.zeros((128, 128), np.float32)
    for m in range(128):
        half = (m % 64) // 32
        if half == 0:
            rot[m + 32, m] = -1.0
        else:
            rot[m - 32, m] = 1.0
    cfa[:, 768:896] = rot
    cfa[:, 1152] = EPS
    cfa[:, 1153] = 128.0 * EPS
    cfa[:, 1154] = 1.0
    cfa[:, 896:1024] = -np.eye(128, dtype=np.float32)
    cfa[:, 1024:1152] = -1.0
    return cfa


def rope_host():
    rows = 2048 // 64
    row = np.repeat(np.arange(rows), 64).astype(np.float32)
    col = np.tile(np.arange(64), rows).astype(np.float32)
    n_freq = 32
    freqs = (np.float32(10000.0) ** (-np.arange(n_freq, dtype=np.float32) / np.float32(n_freq))).astype(np.float32)
    ang_r = row[:, None] * freqs
    ang_c = col[:, None] * freqs
    ang = np.concatenate([ang_r, ang_r, ang_c, ang_c], axis=-1).astype(np.float32)
    out = np.zeros((128, 4096), np.float32)
    out[:, 0:2048] = np.cos(ang).T
    out[:, 2048:4096] = np.sin(ang).T
    return out


_CACHE = {}


def get_prog(layers, nseq):
    key = (tuple(layers), nseq)
    if key not in _CACHE:
        nc = bass.Bass("TRN2", target_bir_lowering=False)
        p = Prog(nc, list(layers), nseq)
        p.build()
        _CACHE[key] = (nc, p)
    return _CACHE[key]


def layer_inputs(inp, li):
    d = {}
    d["w_mod%d" % li] = np.ascontiguousarray(inp["w_mod"][li])
    d["bmodT%d" % li] = np.ascontiguousarray(inp["b_mod"][li].reshape(48, 128).T)
    ln = np.stack([inp["ln_g"][li, 0], inp["ln_b"][li, 0], inp["ln_g"][li, 1], inp["ln_b"][li, 1]], 0)
    d["lnT%d" % li] = np.ascontiguousarray(ln.reshape(4, 8, 128).transpose(2, 0, 1))
    d["w_ffn_in%d" % li] = np.ascontiguousarray(inp["w_ffn_in"][li])
    d["w_ffn_out%d" % li] = np.ascontiguousarray(inp["w_ffn_out"][li])
    j = li // 2
    if li % 2 == 1:
        d["attn_w_qkv%d" % j] = np.ascontiguousarray(inp["attn_w_qkv"][j])
        d["attn_w_out%d" % j] = np.ascontiguousarray(inp["attn_w_out"][j])
        d["attn_gain%d" % j] = np.ascontiguousarray(np.stack([inp["attn_q_norm"][j], inp["attn_k_norm"][j]], 1))
        d["rope"] = rope_host()
    else:
        d["gdn_w_in%d" % j] = np.ascontiguousarray(inp["gdn_w_in"][j])
        d["gdn_w_out%d" % j] = np.ascontiguousarray(inp["gdn_w_out"][j])
        d["gdn_convT%d" % j] = np.ascontiguousarray(inp["gdn_conv"][j].reshape(5, 32, 128).transpose(2, 1, 0))
        gp = np.concatenate([inp["gdn_a_log"][j].reshape(32), inp["gdn_dt_bias"][j].reshape(32)])
        d["gdn_gpar%d" % j] = np.ascontiguousarray(np.broadcast_to(gp[None, :], (128, 64)).astype(np.float32))
        d["gdn_normg%d" % j] = np.ascontiguousarray(np.broadcast_to(inp["gdn_norm_g"][j][None, :], (128, 128)).astype(np.float32))
        d["lvlmask"] = lvlmask_host()
    return d


def lvlmask_host():
    p = np.arange(128)[:, None]
    f = np.arange(128)[None, :]
    m = np.zeros((128, 7, 4, 128), np.uint8)
    for l in range(7):
        if l == 0:
            blk = (p >> 1) == (f >> 1)
        else:
            blk = ((p >> (l + 1)) == (f >> (l + 1))) & ((p >> l) != (f >> l))
        fw = (blk & (f > p)).astype(np.uint8)
        bw = (blk & (f < p)).astype(np.uint8)
        m[:, l, 0, :] = fw
        m[:, l, 1, :] = fw
        m[:, l, 2, :] = bw
        m[:, l, 3, :] = bw
    return np.ascontiguousarray(m.reshape(128, 7 * 4 * 128))


def seq_fm(inp, bidx):
    return np.ascontiguousarray(np.concatenate([inp["ctx"][bidx], inp["x"][bidx]], 0).T)


def cT_host(inp, bidxs):
    v = np.stack([inp["c_ctx"]] + [inp["c"][bi] for bi in bidxs], 1)
    return np.ascontiguousarray(v.reshape(8, 128, len(bidxs) + 1).transpose(1, 0, 2))


def run_layers(inp, layers, xs):
    nc, p = get_prog(layers, 1)
    outs = [None] * 16
    cfa = host_consts()
    for rnd in range(2):
        in_maps = []
        for core in range(8):
            bidx = rnd * 8 + core
            m = {"cfa": cfa, "xin0": xs[bidx], "cTall": cT_host(inp, [bidx])}
            for li in layers:
                m.update(layer_inputs(inp, li))
            in_maps.append({k: v for k, v in m.items() if k in p.dram})
        res = run_bass_kernel_spmd(nc, in_maps, core_ids=list(range(8)))
        for core in range(8):
            outs[rnd * 8 + core] = res.results[core]["xout0"]
    return outs


def kernel_unfused(**inp):
    inp = {k: np.asarray(v) for k, v in inp.items()}
    xs = [seq_fm(inp, bi) for bi in range(16)]
    for li in range(DEPTH):
        xs = run_layers(inp, [li], xs)
    out = np.stack([x.T[256:, :] for x in xs], 0)
    return np.ascontiguousarray(out.astype(np.float32))


def kernel(**inp):
    inp = {k: np.asarray(v) for k, v in inp.items()}
    layers = list(range(DEPTH))
    nc, p = get_prog(layers, 2)
    cfa = host_consts()
    shared = {"cfa": cfa}
    for li in layers:
        shared.update(layer_inputs(inp, li))
    shared = {k: v for k, v in shared.items() if k in p.dram}
    in_maps = []
    for core in range(8):
        m = dict(shared)
        for s in range(2):
            bidx = 2 * core + s
            m["xin%d" % s] = seq_fm(inp, bidx)
        m["cTall"] = cT_host(inp, [2 * core, 2 * core + 1])
        in_maps.append(m)
    res = run_bass_kernel_spmd(nc, in_maps, core_ids=list(range(8)))
    out = np.zeros((16, 2048, 1024), np.float32)
    for core in range(8):
        for s in range(2):
            out[2 * core + s] = res.results[core]["xout%d" % s].T[256:, :]
    return out
```

```python
import numpy as np
from contextlib import ExitStack
import concourse.bass as bass
import concourse.mybir as mybir
from concourse.bass_utils import run_bass_kernel_spmd

F32 = mybir.dt.float32
BF16 = mybir.dt.bfloat16
U8 = mybir.dt.uint8
AF = mybir.ActivationFunctionType
ALU = mybir.AluOpType
AX = mybir.AxisListType

D = 1024
KC = 8
T = 2304
NCH = 18
DEPTH = 4
DFF = 2816
EPS = 1e-6
ALPHA = 8.0 ** 0.25
TILES = [(0, 256), (256, 512), (768, 512), (1280, 512), (1792, 512)]
FWD = list(range(18))
BWD = [1, 0] + list(range(17, 1, -1))
NCFA = 1156
DEBUG_ALLOC = False
GDN_LANES = 5
GDN_STAGGER = 3
DBG = {}


def which_of(ti):
    return 0 if ti == 0 else 1


class B:
    def __init__(self, nc):
        self.nc = nc
        self.es = ExitStack()
        self.E = ['pe', 'act', 'dve', 'pool', 'sp']
        self.q = {e: [] for e in self.E}
        self.cnt = {e: 0 for e in self.E}
        self.sem = {e: self.es.enter_context(nc.semaphore("s_" + e)) for e in self.E if e != 'sp'}
        self.ND = 16
        self.dsem = [self.es.enter_context(nc.semaphore("d%d" % i)) for i in range(self.ND)]
        self.dcnt = [0] * self.ND
        self.drr = 0
        self.waited = {e: {} for e in self.E}
        self.track = {}
        self.ninstr = 0
        self.psb = [self.es.enter_context(nc.psum_tensor("ps%d" % i, [128, 512], F32)) for i in range(8)]
        self.psrr = 0

    def _semh(self, sk):
        return self.sem[sk] if isinstance(sk, str) else self.dsem[sk[1]]

    def _deps(self, eng, r, w):
        raw = {}
        oth = {}

        def need(dct, idv):
            if idv is None:
                return
            sk, v = idv
            if dct.get(sk, 0) < v:
                dct[sk] = v
        for key in r:
            t = self.track.get(key)
            if t:
                need(raw, t[0])
        for key in w:
            t = self.track.get(key)
            if t:
                need(oth, t[0])
                for sk, v in t[1].items():
                    need(oth, (sk, v))
        for sk, v in oth.items():
            if sk == eng and eng == 'pe':
                continue
            if raw.get(sk, 0) < v:
                raw[sk] = v
        for sk, v in raw.items():
            if self.waited[eng].get(sk, 0) >= v:
                continue
            self.waited[eng][sk] = v
            sh = self._semh(sk)
            self.q[eng].append(lambda e, sh=sh, v=v: e.wait_ge(sh, v))
            self.ninstr += 1

    def _mark(self, idv, r, w):
        for key in w:
            self.track[key] = [idv, {}]
        for key in r:
            t = self.track.setdefault(key, [None, {}])
            if t[1].get(idv[0], 0) < idv[1]:
                t[1][idv[0]] = idv[1]

    def op(self, eng, fn, r=(), w=()):
        self._deps(eng, r, w)
        self.cnt[eng] += 1
        sh = self.sem[eng]
        self.q[eng].append(lambda e, fn=fn, sh=sh: fn(e).then_inc(sh, 1))
        self.ninstr += 1
        self._mark((eng, self.cnt[eng]), r, w)

    def dma(self, out, in_, r=(), w=()):
        eng = 'sp'
        k = self.drr
        self.drr = (self.drr + 1) % self.ND
        self._deps(eng, r, w)
        prev = 16 * self.dcnt[k]
        if prev and self.waited[eng].get(('d', k), 0) < prev:
            self.waited[eng][('d', k)] = prev
            self.q[eng].append(lambda e, sh=self.dsem[k], v=prev: e.wait_ge(sh, v))
        self.dcnt[k] += 1
        val = 16 * self.dcnt[k]
        self.q[eng].append(lambda e, out=out, in_=in_, sh=self.dsem[k]: e.dma_start(out=out, in_=in_).then_inc(sh, 16))
        self.ninstr += 1
        idv = (('d', k), val)
        self._mark(idv, r, w)
        return idv

    def barrier(self):
        for e in self.E:
            for o in self.E:
                if o == 'sp' or o == e:
                    continue
                v = self.cnt[o]
                if v and self.waited[e].get(o, 0) < v:
                    self.waited[e][o] = v
                    self.q[e].append(lambda en, sh=self.sem[o], v=v: en.wait_ge(sh, v))
                    self.ninstr += 1
            for k in range(self.ND):
                v = 16 * self.dcnt[k]
                if v and self.waited[e].get(('d', k), 0) < v:
                    self.waited[e][('d', k)] = v
                    self.q[e].append(lambda en, sh=self.dsem[k], v=v: en.wait_ge(sh, v))
                    self.ninstr += 1

    def scope(self):
        return _Scope(self)

    def new_epoch(self):
        self.barrier()
        self.nep = getattr(self, 'nep', 0) + 1
        for e in self.E:
            if e == 'sp':
                continue
            self.sem[e] = self.es.enter_context(self.nc.semaphore("s_%s_%d" % (e, self.nep)))
            self.cnt[e] = 0
        for e in self.E:
            for o in list(self.waited[e].keys()):
                if isinstance(o, str):
                    del self.waited[e][o]
        self.track = {}

    def ps(self, hold=False):
        if not hasattr(self, 'held'):
            self.held = set()
        while True:
            k = self.psrr
            self.psrr = (self.psrr + 1) % 8
            if k not in self.held:
                break
        if hold:
            self.held.add(k)
        return self.psb[k], ('ps', k)

    def release(self, key):
        self.held.discard(key[1])

    def mm(self, out, lhsT, rhs, start=True, stop=True, r=(), w=()):
        self.op('pe', lambda e: e.matmul(out, lhsT, rhs, start=start, stop=stop), r, w)

    def tr(self, out, in_, ident, r=(), w=()):
        self.op('pe', lambda e: e.transpose(out, in_, ident), r, w)

    def act(self, out, in_, func, bias=None, scale=None, r=(), w=()):
        kw = {}
        if bias is not None:
            kw['bias'] = bias
        if scale is not None:
            kw['scale'] = scale
        self.op('act', lambda e: e.activation(out=out, in_=in_, func=func, **kw), r, w)

    def tt(self, eng, out, a, b, op, r=(), w=()):
        self.op(eng, lambda e: e.tensor_tensor(out=out, in0=a, in1=b, op=op), r, w)

    def ts(self, eng, out, a, s1, s2, op0, op1=None, r=(), w=()):
        if op1 is None:
            self.op(eng, lambda e: e.tensor_scalar(out=out, in0=a, scalar1=s1, scalar2=None, op0=op0), r, w)
        else:
            self.op(eng, lambda e: e.tensor_scalar(out=out, in0=a, scalar1=s1, scalar2=s2, op0=op0, op1=op1), r, w)

    def stt(self, eng, out, a, s, b, op0, op1, r=(), w=()):
        self.op(eng, lambda e: e.scalar_tensor_tensor(out=out, in0=a, scalar=s, in1=b, op0=op0, op1=op1), r, w)

    def cp(self, eng, out, in_, r=(), w=()):
        if eng == 'act':
            self.act(out, in_, AF.Copy, r=r, w=w)
        else:
            self.op(eng, lambda e: e.tensor_copy(out=out, in_=in_), r, w)

    def sb(self, es, name, shape, dt):
        if DEBUG_ALLOC:
            print("alloc", name, shape, dt, "remaining", self.nc.sbuf_bytes_remaining)
        self.uid = getattr(self, "uid", 0) + 1
        return es.enter_context(self.nc.sbuf_tensor("sb%d_%s" % (self.uid, name), shape, dt))

    def finish(self):
        nc = self.nc
        q = self.q
        with nc.Block() as block:
            @block.tensor
            def _(e):
                for f in q['pe']:
                    f(e)

            @block.scalar
            def _(e):
                for f in q['act']:
                    f(e)

            @block.vector
            def _(e):
                for f in q['dve']:
                    f(e)

            @block.gpsimd
            def _(e):
                for f in q['pool']:
                    f(e)

            @block.sync
            def _(e):
                for f in q['sp']:
                    f(e)
        self.es.close()


class _Scope:
    def __init__(self, b):
        self.b = b
        self.es = ExitStack()

    def __enter__(self):
        self.es.__enter__()
        return self.es

    def __exit__(self, *a):
        self.b.barrier()
        return self.es.__exit__(*a)


def xa_keys(c, tis):
    return [('xa', c, ti) for ti in tis]


def h_keys(c, tis):
    return [('h', c, ti) for ti in tis]


ALLT = list(range(5))


class Prog:
    def __init__(self, nc, layers, nseq):
        self.nc = nc
        self.b = B(nc)
        self.layers = layers
        self.nseq = nseq
        self.dram = {}

    def din(self, name, shape, dt=F32):
        if name not in self.dram:
            self.dram[name] = self.nc.dram_tensor(name, list(shape), dt, kind="ExternalInput").ap()
        return self.dram[name]

    def dout(self, name, shape, dt=F32):
        if name not in self.dram:
            self.dram[name] = self.nc.dram_tensor(name, list(shape), dt, kind="ExternalOutput").ap()
        return self.dram[name]

    def build(self):
        b = self.b
        nc = self.nc
        es = b.es
        self.xa = b.sb(es, "xa", [128, KC, T], F32)
        self.hb = b.sb(es, "hb", [128, KC, T], BF16)
        self.cfa = b.sb(es, "cfa", [128, NCFA], F32)
        self.identb = b.sb(es, "identb", [128, 128], BF16)
        self.onesb = b.sb(es, "onesb", [128, 128], BF16)
        self.mean1k = b.sb(es, "mean1k", [128, 128], F32)
        self.mean1kb = b.sb(es, "mean1kb", [128, 128], BF16)
        cfa_d = self.din("cfa", [128, NCFA])
        b.dma(self.cfa[:], cfa_d, w=['cfa'])
        self.ident = self.cfa[:, 0:128]
        self.trif = self.cfa[:, 128:256]
        self.trib = self.cfa[:, 256:384]
        self.ones = self.cfa[:, 384:512]
        self.maskq = self.cfa[:, 512:768]
        self.rot = self.cfa[:, 768:896]
        self.epsv = self.cfa[:, 1152:1155]
        self.nident = self.cfa[:, 896:1024]
        self.nones = self.cfa[:, 1024:1152]
        b.cp('act', self.identb[:], self.ident, r=['cfa'], w=['identb'])
        b.cp('act', self.onesb[:], self.ones, r=['cfa'], w=['onesb'])
        b.act(self.mean1k[:], self.ones, AF.Copy, scale=1.0 / 1024.0, r=['cfa'], w=['mean1k'])
        b.act(self.mean1kb[:], self.ones, AF.Copy, scale=1.0 / 1024.0, r=['cfa'], w=['mean1k'])
        self.mod_all(es)
        outs = []
        for s in range(self.nseq):
            xin = self.din("xin%d" % s, [D, T])
            xout = self.dout("xout%d" % s, [D, T])
            with b.scope() as les:
                stg = [b.sb(les, "ldst%d" % i, [128, T], F32) for i in range(2)]
                for c in range(KC):
                    st = stg[c % 2]
                    b.dma(st[:], xin[c * 128:(c + 1) * 128, :], w=[('ldst', c % 2)])
                    b.act(self.xa[:, c, :], st[:], AF.Copy, scale=ALPHA, r=[('ldst', c % 2)], w=xa_keys(c, ALLT))
            for n, li in enumerate(self.layers):
                last = (n == len(self.layers) - 1)
                b.new_epoch()
                self.layer(li, s, out_plain=last)
            for c in range(KC):
                outs.append(b.dma(xout[c * 128:(c + 1) * 128, :], self.xa[:, c, :], r=xa_keys(c, ALLT)))
        for (sk, v) in outs + getattr(self, 'dbg_outs', []):
            if b.waited['sp'].get(sk, 0) < v:
                b.waited['sp'][sk] = v
                b.q['sp'].append(lambda e, sh=b.dsem[sk[1]], v=v: e.wait_ge(sh, v))
        b.finish()

    def layer(self, li, s, out_plain):
        b = self.b
        with b.scope() as les:
            self.lv = {}
            stop = DBG.get('stop')
            self.modulation(li, s, les, out_plain)
            if DBG.get('dump'):
                self.dbg_outs = getattr(self, 'dbg_outs', [])
                self.dbg_outs.append(b.dma(self.dout("dbg_MOD", [128, 96]), self.P['MOD'][:].rearrange("p a b -> p (a b)"), r=['MOD']))
            if stop == 'mod':
                return
            self.modulate_in()
            if DBG.get('dump'):
                self.dbg_outs.append(b.dma(self.dout("dbg_h", [128, KC * T], BF16), self.hb[:].rearrange("p a b -> p (a b)"),
                                           r=[('h', c, ti) for c in range(KC) for ti in ALLT]))
            if stop == 'modin':
                return
            if li % 2 == 0:
                self.gdn(li)
            else:
                self.attn(li)
            if stop == 'mixer':
                return
            self.layernorm(0, want_h=True)
            if stop == 'ln0':
                return
            self.ffn(li)
            if stop == 'ffn':
                return
            self.layernorm(1, want_h=False)

    def mod_all(self, es):
        b = self.b
        ns = 1 + self.nseq
        self.MODall = {}
        for li in self.layers:
            self.MODall[li] = b.sb(es, "MODall%d" % li, [128, 48, ns], F32)
        with b.scope() as wes:
            sc = b.sb(wes, "ma_sc", [128, KC, ns], F32)
            bm = b.sb(wes, "ma_bm", [128, 48], F32)
            wst = [b.sb(wes, "ma_wst%d" % i, [128, KC, 512], F32) for i in range(3)]
            cT = self.din("cTall", [128, KC, ns])
            b.dma(sc[:], cT, w=['sc'])
            b.act(sc[:], sc[:], AF.Silu, r=['sc'], w=['sc'])
            nblk = 0
            for li in self.layers:
                w_mod = self.din("w_mod%d" % li, [D, 6 * D])
                bmodT = self.din("bmodT%d" % li, [128, 48])
                b.dma(bm[:], bmodT, w=['bmodT'])
                ps, pk = b.ps(hold=True)
                for nb in range(12):
                    st = wst[nblk % 3]
                    sk = ('wmst', nblk % 3)
                    nblk += 1
                    b.dma(st[:], w_mod[:, nb * 512:(nb + 1) * 512].rearrange("(k p) n -> p k n", p=128), w=[sk])
                    for f in range(4):
                        fc = nb * 4 + f
                        for kc in range(KC):
                            b.mm(ps[:, fc * ns:(fc + 1) * ns], st[:, kc, f * 128:(f + 1) * 128], sc[:, kc, :],
                                 start=(kc == 0), stop=(kc == KC - 1), r=[sk, 'sc'], w=[pk])
                b.tt('dve', self.MODall[li][:], ps[:, 0:48 * ns].rearrange("p (a b) -> p a b", b=ns),
                     bm[:].unsqueeze(2).to_broadcast([128, 48, ns]), ALU.add, r=[pk, 'bmodT'], w=[('MODall', li)])
                b.release(pk)

    def modulation(self, li, s, les, out_plain):
        b = self.b
        P = {}
        for nm, shape in [('MOD', [128, 48, 2]), ('s1', [128, 8, 2]), ('H1s', [128, 8, 2]), ('H1b', [128, 8, 2]),
                          ('A', [128, 2, 8]), ('Bv', [128, 2, 8]), ('lnT', [128, 4, 8]), ('bmodT', [128, 48]),
                          ('sc', [128, 8, 2]), ('tmp1', [128, 8, 2])]:
            P[nm] = b.sb(les, "m_" + nm, shape, F32)
        self.P = P
        lnT = self.din("lnT%d" % li, [128, 4, 8])
        b.dma(P['lnT'][:], lnT, w=['lnT'])
        MOD = P['MOD']
        MA = self.MODall[li]
        b.cp('dve', MOD[:, :, 0:1], MA[:, :, 0:1], r=[('MODall', li)], w=['MOD'])
        b.cp('dve', MOD[:, :, 1:2], MA[:, :, 1 + s:2 + s], r=[('MODall', li)], w=['MOD'])

        def mj(j):
            return MOD[:, j * 8:(j + 1) * 8, :]
        b.ts('dve', P['s1'][:], mj(1), 1.0, 1.0 / ALPHA, ALU.add, ALU.mult, r=['MOD'], w=['s1'])
        ln = P['lnT']
        g0 = ln[:, 0, :].unsqueeze(2).to_broadcast([128, 8, 2])
        b0 = ln[:, 1, :].unsqueeze(2).to_broadcast([128, 8, 2])
        b.ts('dve', P['tmp1'][:], mj(4), 1.0, None, ALU.add, r=['MOD'], w=['tmp1'])
        b.tt('dve', P['H1s'][:], P['tmp1'][:], g0, ALU.mult, r=['tmp1', 'lnT'], w=['H1s'])
        b.tt('dve', P['H1b'][:], P['tmp1'][:], b0, ALU.mult, r=['tmp1', 'lnT'], w=['H1b'])
        b.tt('dve', P['H1b'][:], P['H1b'][:], mj(3), ALU.add, r=['H1b', 'MOD'], w=['H1b'])
        b.ts('dve', P['A'][:, 0, :], ln[:, 0, :], ALPHA, None, ALU.mult, r=['lnT'], w=['A'])
        b.ts('dve', P['Bv'][:, 0, :], ln[:, 1, :], ALPHA, None, ALU.mult, r=['lnT'], w=['Bv'])
        a2 = 1.0 if out_plain else ALPHA
        b.ts('dve', P['A'][:, 1, :], ln[:, 2, :], a2, None, ALU.mult, r=['lnT'], w=['A'])
        b.ts('dve', P['Bv'][:, 1, :], ln[:, 3, :], a2, None, ALU.mult, r=['lnT'], w=['Bv'])
        self.ga = mj(2)
        self.gaf = mj(5)
        self.sh = mj(0)

    def modulate_in(self):
        b = self.b
        P = self.P
        for c in range(KC):
            for (wh, t0, n, tis) in [(0, 0, 256, [0]), (1, 256, 2048, [1, 2, 3, 4])]:
                b.act(self.hb[:, c, t0:t0 + n], self.xa[:, c, t0:t0 + n], AF.Identity,
                      bias=self.sh[:, c, wh:wh + 1], scale=P['s1'][:, c, wh:wh + 1],
                      r=xa_keys(c, tis) + ['MOD', 's1'], w=h_keys(c, tis))

    def layernorm(self, idx, want_h):
        b = self.b
        P = self.P
        with b.scope() as es:
            sq = [b.sb(es, "ln_sq%d" % i, [128, 512], BF16) for i in range(4)]
            msb2 = [b.sb(es, "ln_msb%d" % i, [128, 512], F32) for i in range(2)]
            m22 = [b.sb(es, "ln_m2%d" % i, [128, 512], F32) for i in range(2)]
            rstd2 = [b.sb(es, "ln_rstd%d" % i, [128, 512], F32) for i in range(2)]
            tt_ = [b.sb(es, "ln_t%d" % i, [128, 512], F32) for i in range(4)]
            nsq = [0]

            def stats(ti):
                t0, n = TILES[ti]
                q_ = ti % 2
                msb, m2, rstd = msb2[q_], m22[q_], rstd2[q_]
                pm, pmk = b.ps(hold=True)
                pe2, pe2k = b.ps(hold=True)
                for c in range(KC):
                    k_ = nsq[0] % 4
                    nsq[0] += 1
                    b.act(sq[k_][:, :n], self.xa[:, c, t0:t0 + n], AF.Square, r=[('xa', c, ti)], w=[('lnsq', k_)])
                    b.mm(pm[:, :n], self.mean1k[:], self.xa[:, c, t0:t0 + n], start=(c == 0), stop=(c == KC - 1),
                         r=[('xa', c, ti), 'mean1k'], w=[pmk])
                    b.mm(pe2[:, :n], self.mean1kb[:], sq[k_][:, :n], start=(c == 0), stop=(c == KC - 1),
                         r=[('lnsq', k_), 'mean1k'], w=[pe2k])
                b.cp('act', msb[:, :n], pm[:, :n], r=[pmk], w=[('lnmsb', q_)])
                b.release(pmk)
                b.tt('dve', m2[:, :n], msb[:, :n], msb[:, :n], ALU.mult, r=[('lnmsb', q_)], w=[('lnm2', q_)])
                b.tt('dve', m2[:, :n], pe2[:, :n], m2[:, :n], ALU.subtract, r=[pe2k, ('lnm2', q_)], w=[('lnm2', q_)])
                b.release(pe2k)
                b.act(m2[:, :n], m2[:, :n], AF.Ln, bias=self.epsv[:, 0:1], r=[('lnm2', q_), 'cfa'], w=[('lnm2', q_)])
                b.act(rstd[:, :n], m2[:, :n], AF.Exp, scale=-0.5, r=[('lnm2', q_)], w=[('lnrstd', q_)])

            def norm(ti):
                t0, n = TILES[ti]
                wh = which_of(ti)
                q_ = ti % 2
                msb, rstd = msb2[q_], rstd2[q_]
                for c in range(KC):
                    t = tt_[c % 4]
                    tk = ('lnt', c % 4)
                    b.tt('dve', t[:, :n], self.xa[:, c, t0:t0 + n], msb[:, :n], ALU.subtract, r=[('xa', c, ti), ('lnmsb', q_)], w=[tk])
                    b.tt('dve', t[:, :n], t[:, :n], rstd[:, :n], ALU.mult, r=[tk, ('lnrstd', q_)], w=[tk])
                    b.act(self.xa[:, c, t0:t0 + n], t[:, :n], AF.Identity, bias=P['Bv'][:, idx, c:c + 1],
                          scale=P['A'][:, idx, c:c + 1], r=[tk, 'A', 'Bv'], w=[('xa', c, ti)])
                    if want_h:
                        b.ts('pool', self.hb[:, c, t0:t0 + n], t[:, :n], P['H1s'][:, c, wh:wh + 1], P['H1b'][:, c, wh:wh + 1],
                             ALU.mult, ALU.add, r=[tk, 'H1s', 'H1b'], w=[('h', c, ti)])
            stats(0)
            for ti in range(len(TILES)):
                if ti + 1 < len(TILES):
                    stats(ti + 1)
                norm(ti)

    def ffn(self, li):
        b = self.b
        w_in = self.din("w_ffn_in%d" % li, [D, 2 * DFF])
        w_out = self.din("w_ffn_out%d" % li, [DFF, D])
        with b.scope() as es:
            wis = [b.sb(es, "f_wis%d" % i, [128, KC, 512], F32) for i in range(2)]
            wos = [b.sb(es, "f_wos%d" % i, [128, 2, D], F32) for i in range(2)]
            wib = [b.sb(es, "f_wib%d" % i, [128, KC, 512], BF16) for i in range(2)]
            wob = [b.sb(es, "f_wob%d" % i, [128, 2, D], BF16) for i in range(2)]
            actb = b.sb(es, "f_act", [128, 2, T], BF16)
            sg = [b.sb(es, "f_sg%d" % i, [128, 512], F32) for i in range(2)]
            nsg = 0
            for fb in range(11):
                p = fb % 2
                f0 = fb * 256
                b.dma(wis[p][:, :, 0:256], w_in[:, f0:f0 + 256].rearrange("(k p) n -> p k n", p=128), w=[('wis', p, 0)])
                b.dma(wis[p][:, :, 256:512], w_in[:, DFF + f0:DFF + f0 + 256].rearrange("(k p) n -> p k n", p=128), w=[('wis', p, 1)])
                b.dma(wos[p][:], w_out[f0:f0 + 256, :].rearrange("(k p) n -> p k n", p=128), w=[('wos', p)])
                b.cp('pool', wib[p][:], wis[p][:], r=[('wis', p, 0), ('wis', p, 1)], w=[('wib', p)])
                b.cp('pool', wob[p][:], wos[p][:], r=[('wos', p)], w=[('wob', p)])
                for ti, (t0, n) in enumerate(TILES):
                    for j in range(2):
                        pg, pgk = b.ps()
                        pu, puk = b.ps()
                        for kc in range(KC):
                            b.mm(pg[:, :n], wib[p][:, kc, j * 128:(j + 1) * 128], self.hb[:, kc, t0:t0 + n],
                                 start=(kc == 0), stop=(kc == KC - 1), r=[('wib', p), ('h', kc, ti)], w=[pgk])
                        for kc in range(KC):
                            b.mm(pu[:, :n], wib[p][:, kc, 256 + j * 128:256 + (j + 1) * 128], self.hb[:, kc, t0:t0 + n],
                                 start=(kc == 0), stop=(kc == KC - 1), r=[('wib', p), ('h', kc, ti)], w=[puk])
                        s_ = sg[nsg % 2]
                        sk = ('fsg', nsg % 2)
                        nsg += 1
                        b.act(s_[:, :n], pg[:, :n], AF.Silu, r=[pgk], w=[sk])
                        b.tt('dve', actb[:, j, t0:t0 + n], pu[:, :n], s_[:, :n], ALU.mult, r=[puk, sk], w=[('fact', j, ti)])
                for ti, (t0, n) in enumerate(TILES):
                    wh = which_of(ti)
                    for oc in range(KC):
                        po, pok = b.ps()
                        for j in range(2):
                            b.mm(po[:, :n], wob[p][:, j, oc * 128:(oc + 1) * 128], actb[:, j, t0:t0 + n],
                                 start=(j == 0), stop=(j == 1), r=[('wob', p), ('fact', j, ti)], w=[pok])
                        b.stt('dve', self.xa[:, oc, t0:t0 + n], po[:, :n], self.gaf[:, oc, wh:wh + 1], self.xa[:, oc, t0:t0 + n],
                              ALU.mult, ALU.add, r=[pok, 'MOD', ('xa', oc, ti)], w=[('xa', oc, ti)])

    def out_proj(self, wb, wkey, nk, yfn, ykeys):
        b = self.b
        for ti, (t0, n) in enumerate(TILES):
            wh = which_of(ti)
            for oc in range(KC):
                po, pok = b.ps()
                for k in range(nk):
                    b.mm(po[:, :n], wb[:, k, oc * 128:(oc + 1) * 128], yfn(k, t0, n), start=(k == 0), stop=(k == nk - 1),
                         r=[wkey] + ykeys(k, ti), w=[pok])
                b.stt('dve', self.xa[:, oc, t0:t0 + n], po[:, :n], self.ga[:, oc, wh:wh + 1], self.xa[:, oc, t0:t0 + n],
                      ALU.mult, ALU.add, r=[pok, 'MOD', ('xa', oc, ti)], w=[('xa', oc, ti)])

    def attn(self, li):
        b = self.b
        j = li // 2
        w_qkv = self.din("attn_w_qkv%d" % j, [D, 1536])
        w_o = self.din("attn_w_out%d" % j, [D, D])
        gains = self.din("attn_gain%d" % j, [128, 2])
        ropeD = self.din("rope", [128, 4096])
        with b.scope() as es:
            QR = b.sb(es, "a_QR", [128, 8, T], BF16)
            KR = b.sb(es, "a_KR", [128, 2, T], BF16)
            VT = b.sb(es, "a_VT", [128, NCH, 256], BF16)
            gn = b.sb(es, "a_gn", [128, 2], F32)
            b.dma(gn[:], gains, w=['gn'])
            b.ts('dve', gn[:], gn[:], float(np.sqrt(128.0)), None, ALU.mult, r=['gn'], w=['gn'])
            with b.scope() as es2:
                rope = b.sb(es2, "a_rope", [128, 4096], F32)
                b.dma(rope[:], ropeD, w=['rope'])
                cosT = rope[:, 0:2048]
                sinT = rope[:, 2048:4096]
                wst_ = b.sb(es2, "a_wst", [128, KC, 128], F32)
                wst = [wst_, wst_]
                wbf = [b.sb(es2, "a_wbf%d" % i, [128, KC, 128], BF16) for i in range(2)]
                sqb = b.sb(es2, "a_sq", [128, 512], BF16)
                rs = b.sb(es2, "a_rs", [128, 512], F32)
                qn = b.sb(es2, "a_qn", [128, 512], F32)
                t1 = b.sb(es2, "a_t1", [128, 512], F32)
                t2 = b.sb(es2, "a_t2", [128, 512], F32)
                sqb2 = [sqb, b.sb(es2, "a_sq2", [128, 512], BF16)]
                qn2 = [qn, b.sb(es2, "a_qn2", [128, 512], F32)]

                def load_w(wbk):
                    p = wbk % 2
                    b.dma(wst[p][:], w_qkv[:, wbk * 128:(wbk + 1) * 128].rearrange("(k p) n -> p k n", p=128), w=[('awst', 0)])
                    b.cp('pool', wbf[p][:], wst[p][:], r=[('awst', 0)], w=[('awbf', p)])
                items = [(wbk, ti) for wbk in range(10) for ti in range(5)]
                st = {}

                def stA(i):
                    wbk, ti = items[i]
                    t0, n = TILES[ti]
                    p = wbk % 2
                    if ti == 0:
                        load_w(wbk)
                    pp, ppk = b.ps(hold=True)
                    for kc in range(KC):
                        b.mm(pp[:, :n], wbf[p][:, kc, :], self.hb[:, kc, t0:t0 + n],
                             start=(kc == 0), stop=(kc == KC - 1), r=[('awbf', p), ('h', kc, ti)], w=[ppk])
                    b.act(sqb2[i % 2][:, :n], pp[:, :n], AF.Square, r=[ppk], w=[('asq', i % 2)])
                    st[i] = (pp, ppk)

                def stB(i):
                    wbk, ti = items[i]
                    t0, n = TILES[ti]
                    pp, ppk = st.pop(i)
                    isq = wbk < 8
                    hidx = wbk if isq else wbk - 8
                    dst = QR if isq else KR
                    gcol = gn[:, 0:1] if isq else gn[:, 1:2]
                    p2, p2k = b.ps(hold=True)
                    b.mm(p2[:, :n], self.onesb[:], sqb2[i % 2][:, :n], r=[('asq', i % 2), 'onesb'], w=[p2k])
                    b.act(rs[:, :n], p2[:, :n], AF.Ln, bias=self.epsv[:, 1:2], r=[p2k, 'cfa'], w=['ars'])
                    b.release(p2k)
                    b.act(rs[:, :n], rs[:, :n], AF.Exp, scale=-0.5, r=['ars'], w=['ars'])
                    if ti == 0:
                        b.stt('dve', dst[:, hidx, t0:t0 + n], pp[:, :n], gcol, rs[:, :n], ALU.mult, ALU.mult,
                              r=[ppk, 'gn', 'ars'], w=[('aqk', isq, hidx, ti)])
                    else:
                        b.stt('dve', qn2[i % 2][:, :n], pp[:, :n], gcol, rs[:, :n], ALU.mult, ALU.mult,
                              r=[ppk, 'gn', 'ars'], w=[('aqn', i % 2)])
                    b.release(ppk)

                def stC(i):
                    wbk, ti = items[i]
                    if ti == 0:
                        return
                    t0, n = TILES[ti]
                    isq = wbk < 8
                    hidx = wbk if isq else wbk - 8
                    dst = QR if isq else KR
                    qn_ = qn2[i % 2]
                    p3, p3k = b.ps(hold=True)
                    b.mm(p3[:, :n], self.rot, qn_[:, :n], r=[('aqn', i % 2), 'cfa'], w=[p3k])
                    l0 = t0 - 256
                    b.tt('dve', t1[:, :n], qn_[:, :n], cosT[:, l0:l0 + n], ALU.mult, r=[('aqn', i % 2), 'rope'], w=['at1'])
                    b.tt('dve', t2[:, :n], p3[:, :n], sinT[:, l0:l0 + n], ALU.mult, r=[p3k, 'rope'], w=['at2'])
                    b.release(p3k)
                    b.tt('pool', dst[:, hidx, t0:t0 + n], t1[:, :n], t2[:, :n], ALU.add, r=['at1', 'at2'],
                         w=[('aqk', isq, hidx, ti)])
                nit = len(items)
                for i in range(nit + 2):
                    if i < nit:
                        stA(i)
                    if 0 <= i - 1 < nit:
                        stB(i - 1)
                    if 0 <= i - 2 < nit:
                        stC(i - 2)
                for wbk in (10, 11):
                    p = wbk % 2
                    load_w(wbk)
                    kvh = wbk - 10
                    for c in range(NCH):
                        pv, pvk = b.ps()
                        for kc in range(KC):
                            b.mm(pv[:, 0:128], self.hb[:, kc, c * 128:(c + 1) * 128], wbf[p][:, kc, :],
                                 start=(kc == 0), stop=(kc == KC - 1), r=[('awbf', p)] + h_keys(kc, ALLT), w=[pvk])
                        b.cp('act', VT[:, c, kvh * 128:(kvh + 1) * 128], pv[:, 0:128], r=[pvk], w=['VT'])
            with b.scope() as es3:
                pts = [b.sb(es3, "a_pt%d" % i, [128, 512], BF16) for i in range(4)]
                rden = b.sb(es3, "a_rden", [128, 512], F32)
                wos_ = b.sb(es3, "a_wos", [128, 2, D], F32)
                wos = [wos_, wos_]
                wob = b.sb(es3, "a_wob", [128, KC, D], BF16)
                for i in range(4):
                    b.dma(wos[i % 2][:], w_o[i * 256:(i + 1) * 256, :].rearrange("(k p) n -> p k n", p=128), w=[('aos', 0)])
                    b.cp('pool', wob[:, 2 * i:2 * i + 2, :], wos[i % 2][:], r=[('aos', 0)], w=['awob'])
                npt = 0
                scale = float(128.0 ** -0.5)
                for hq in range(8):
                    kv = hq // 4
                    for ti, (t0, n) in enumerate(TILES):
                        kts = [0, 1] if ti == 0 else list(range(NCH))
                        pden, pdk = b.ps(hold=True)
                        po, pok = b.ps(hold=True)
                        pend = []

                        def flush(pv_):
                            pt_, ptk_, ii_, kt_ = pv_
                            b.mm(pden[:, :n], self.onesb[:], pt_[:, :n], start=(ii_ == 0), stop=(ii_ == len(kts) - 1),
                                 r=[ptk_, 'onesb'], w=[pdk])
                            b.mm(po[:, :n], VT[:, kt_, kv * 128:(kv + 1) * 128], pt_[:, :n], start=(ii_ == 0), stop=(ii_ == len(kts) - 1),
                                 r=[ptk_, 'VT'], w=[pok])
                        for ii, kt in enumerate(kts):
                            psc, psk = b.ps()
                            b.mm(psc[:, :n], KR[:, kv, kt * 128:(kt + 1) * 128], QR[:, hq, t0:t0 + n],
                                 r=[('aqk', False, kv, tt_) for tt_ in ALLT] + [('aqk', True, hq, ti)], w=[psk])
                            pt = pts[npt % 4]
                            ptk = ('apt', npt % 4)
                            npt += 1
                            b.act(pt[:, :n], psc[:, :n], AF.Exp, scale=scale, r=[psk], w=[ptk])
                            pend.append((pt, ptk, ii, kt))
                            if len(pend) > 2:
                                flush(pend.pop(0))
                        while pend:
                            flush(pend.pop(0))
                        b.op('dve', lambda e, o=rden[:, :n], i=pden[:, :n]: e.reciprocal(out=o, in_=i), r=[pdk], w=['arden'])
                        b.tt('dve', self.hb[:, hq, t0:t0 + n], po[:, :n], rden[:, :n], ALU.mult, r=[pok, 'arden'], w=[('h', hq, ti)])
                        b.release(pdk)
                        b.release(pok)
                self.out_proj(wob, 'awob', KC, lambda k, t0, n: self.hb[:, k, t0:t0 + n], lambda k, ti: [('h', k, ti)])

    def gdn(self, li):
        b = self.b
        j = li // 2
        w_in = self.din("gdn_w_in%d" % j, [D, 6208])
        w_out = self.din("gdn_w_out%d" % j, [2048, D])
        convD = self.din("gdn_convT%d" % j, [128, 32, 5])
        gparD = self.din("gdn_gpar%d" % j, [128, 64])
        normgD = self.din("gdn_normg%d" % j, [128, 128])
        lvlD = self.din("lvlmask", [128, 7 * 4 * 128], U8)
        ident = self.ident
        one_col = self.epsv[:, 2:3]
        eps_col = self.epsv[:, 0:1]

        def bc(ap2, n=128):
            return ap2.unsqueeze(2).to_broadcast([128, ap2.shape[1], n])

        with b.scope() as es:
            G = {}
            for nm in ['NBETA', 'BETA', 'GCUM', 'EG', 'KD', 'EGL']:
                G[nm] = b.sb(es, "g_" + nm, [128, NCH, 32], F32)
            convw = b.sb(es, "g_convw", [128, 32, 5], F32)
            normg = b.sb(es, "g_normg", [128, 128], F32)
            lvl = b.sb(es, "g_lvl", [128, 7, 4, 128], U8)
            b.dma(convw[:], convD, w=['convw'])
            b.dma(normg[:], normgD, w=['normg'])
            b.dma(lvl[:].rearrange("p a b c -> p (a b c)"), lvlD, w=['lvl'])
            with b.scope() as ges:
                gpar = b.sb(ges, "g_gpar", [128, 64], F32)
                wgs = b.sb(ges, "g_wgs", [128, KC, 64], F32)
                wgb = b.sb(ges, "g_wgb", [128, KC, 64], BF16)
                GRAW = b.sb(ges, "g_graw", [128, NCH, 64], F32)
                T1 = b.sb(ges, "g_t1", [128, NCH, 32], F32)
                T2 = b.sb(ges, "g_t2", [128, NCH, 32], F32)
                GG = b.sb(ges, "g_g", [128, NCH, 32], F32)
                GL = b.sb(ges, "g_gl", [128, NCH, 32], F32)
                NA = b.sb(ges, "g_na", [128, 32], F32)
                b.dma(gpar[:], gparD, w=['gpar'])
                b.dma(wgs[:], w_in[:, 6144:6208].rearrange("(k p) n -> p k n", p=128), w=['wgs'])
                b.cp('pool', wgb[:], wgs[:], r=['wgs'], w=['wgb'])
                for c0 in range(0, NCH, 8):
                    nc_ = min(8, NCH - c0)
                    pg, pgk = b.ps()
                    for cc in range(nc_):
                        c = c0 + cc
                        for kc in range(KC):
                            b.mm(pg[:, cc * 64:(cc + 1) * 64], self.hb[:, kc, c * 128:(c + 1) * 128], wgb[:, kc, :],
                                 start=(kc == 0), stop=(kc == KC - 1), r=['wgb'] + h_keys(kc, ALLT), w=[pgk])
                    b.cp('act', GRAW[:, c0:c0 + nc_, :], pg[:, 0:nc_ * 64].rearrange("p (a b) -> p a b", b=64), r=[pgk], w=['graw'])
                braw = GRAW[:, :, 0:32]
                araw = GRAW[:, :, 32:64]
                b.act(T1[:], braw, AF.Exp, scale=-1.0, r=['graw'], w=['gt1'])
                b.act(T1[:], T1[:], AF.Ln, bias=one_col, r=['gt1', 'cfa'], w=['gt1'])
                b.act(G['BETA'][:], T1[:], AF.Exp, scale=-1.0, r=['gt1'], w=['BETA'])
                b.ts('pool', G['NBETA'][:], G['BETA'][:], -1.0, None, ALU.mult, r=['BETA'], w=['NBETA'])
                b.tt('dve', T2[:], araw, gpar[:, 32:64].unsqueeze(1).to_broadcast([128, NCH, 32]), ALU.add, r=['graw', 'gpar'], w=['gt2'])
                b.act(T2[:], T2[:], AF.Exp, r=['gt2'], w=['gt2'])
                b.act(T2[:], T2[:], AF.Ln, bias=one_col, r=['gt2', 'cfa'], w=['gt2'])
                b.act(NA[:], gpar[:, 0:32], AF.Exp, r=['gpar'], w=['gna'])
                b.ts('pool', NA[:], NA[:], -1.0, None, ALU.mult, r=['gna'], w=['gna'])
                b.tt('dve', GG[:], T2[:], NA[:].unsqueeze(1).to_broadcast([128, NCH, 32]), ALU.mult, r=['gt2', 'gna'], w=['gg'])
                pc, pck = b.ps()
                b.mm(pc[:, 0:288], self.trif, GG[:, :, 0:16], r=['gg', 'cfa'], w=[pck])
                pc2, pc2k = b.ps()
                b.mm(pc2[:, 0:288], self.trib, GG[:, :, 16:32], r=['gg', 'cfa'], w=[pc2k])
                b.cp('act', G['GCUM'][:, :, 0:16], pc[:, 0:288].rearrange("p (a b) -> p a b", b=16), r=[pck], w=['GCUM'])
                b.cp('act', G['GCUM'][:, :, 16:32], pc2[:, 0:288].rearrange("p (a b) -> p a b", b=16), r=[pc2k], w=['GCUM'])
                pl, plk = b.ps()
                b.mm(pl[:, 0:288], self.ones, GG[:, 0:9, :], r=['gg', 'cfa'], w=[plk])
                pl2, pl2k = b.ps()
                b.mm(pl2[:, 0:288], self.ones, GG[:, 9:18, :], r=['gg', 'cfa'], w=[pl2k])
                b.cp('act', GL[:, 0:9, :], pl[:, 0:288].rearrange("p (a b) -> p a b", b=32), r=[plk], w=['ggl'])
                b.cp('act', GL[:, 9:18, :], pl2[:, 0:288].rearrange("p (a b) -> p a b", b=32), r=[pl2k], w=['ggl'])
                b.act(G['EG'][:], G['GCUM'][:], AF.Exp, r=['GCUM'], w=['EG'])
                b.tt('dve', T1[:], GL[:], G['GCUM'][:], ALU.subtract, r=['ggl', 'GCUM', 'gt1'], w=['gt1'])
                b.act(G['KD'][:], T1[:], AF.Exp, r=['gt1'], w=['KD'])
                b.act(G['EGL'][:], GL[:], AF.Exp, r=['ggl'], w=['EGL'])
            for g in range(8):
                self.gdn_group(li, g, es, G, convw, normg, lvl, w_in, w_out, bc)

    def gdn_group(self, li, g, es_unused, G, convw, normg, lvl, w_in, w_out, bc):
        b = self.b
        ident = self.ident
        eps_col = self.epsv[:, 0:1]
        ps_bf = lambda ps: ps[:].bitcast(BF16)
        gk = lambda nm: nm + "_%d" % g

        def gcols(d):
            return slice(d * 16 + 2 * g, d * 16 + 2 * g + 2)

        with b.scope() as ges:
            O = b.sb(ges, "g_O", [128, NCH, 2, 128], BF16)
            with b.scope() as aes:
                KQ = b.sb(aes, "g_KQ", [128, NCH, 2, 128], BF16)
                KTM = b.sb(aes, "g_KTM", [128, NCH, 128], BF16)
                VTM = b.sb(aes, "g_VTM", [128, NCH, 2, 128], BF16)
                with b.scope() as pes:
                    CB = b.sb(pes, "g_CB", [128, 2310], F32)
                    ACC = b.sb(pes, "g_ACC", [128, T], F32)
                    TB = b.sb(pes, "g_TB", [128, T], BF16)
                    wst = [b.sb(pes, "g_wst%d" % i, [128, KC, 128], F32) for i in range(2)]
                    wbf = [b.sb(pes, "g_wbf%d" % i, [128, KC, 128], BF16) for i in range(2)]
                    rs = b.sb(pes, "g_rs", [128, 512], F32)
                    b.op('pool', lambda e: e.memset(CB[:, 0:2], 0.0), w=['CBp0'])
                    b.op('pool', lambda e: e.memset(CB[:, 258:260], 0.0), w=['CBp1'])
                    b.op('pool', lambda e: e.memset(CB[:, 2308:2310], 0.0), w=['CBp2'])
                    fcs = [('q', g * 128, g), ('k', 1024 + g * 128, 8 + g),
                           ('v0', 2048 + (2 * g) * 128, 16 + 2 * g), ('v1', 2048 + (2 * g + 1) * 128, 16 + 2 * g + 1)]
                    def emit_proj(fi):
                        kind, col0, cq = fcs[fi]
                        held = []
                        p = fi % 2
                        b.dma(wst[p][:], w_in[:, col0:col0 + 128].rearrange("(k p) n -> p k n", p=128), w=[('gwst', p)])
                        b.cp('pool', wbf[p][:], wst[p][:], r=[('gwst', p)], w=[('gwbf', p)])
                        for ti, (t0, n) in enumerate(TILES):
                            pp, ppk = b.ps(hold=True)
                            for kc in range(KC):
                                b.mm(pp[:, :n], wbf[p][:, kc, :], self.hb[:, kc, t0:t0 + n], start=(kc == 0), stop=(kc == KC - 1),
                                     r=[('gwbf', p), ('h', kc, ti)], w=[ppk])
                            o0 = 2 if ti == 0 else t0 + 4
                            b.cp('act', CB[:, o0:o0 + n], pp[:, :n], r=[ppk], w=[('CB', ti)])
                            held.append(ppk)
                        return held

                    def emit_conv(fi):
                        kind, col0, cq = fcs[fi]
                        acck = [('ACC', 0), ('ACC', 256), ('ACC', 1280)]
                        for (d0, L, s0, tis) in [(0, 256, 2, [0]), (256, 1024, 260, [1, 2]), (1280, 1024, 1284, [3, 4])]:
                            ak = ('ACC', d0)
                            rk = [('CB', tj) for tj in {0: [0], 256: [1, 2, 3], 1280: [2, 3, 4]}[d0]] + ['CBp0', 'CBp1', 'CBp2', 'convw']
                            b.ts('dve', ACC[:, d0:d0 + L], CB[:, s0 - 2:s0 - 2 + L], convw[:, cq, 0:1], None, ALU.mult, r=rk, w=[ak])
                            for jj in range(1, 5):
                                b.stt('dve', ACC[:, d0:d0 + L], CB[:, s0 - 2 + jj:s0 - 2 + jj + L], convw[:, cq, jj:jj + 1],
                                      ACC[:, d0:d0 + L], ALU.mult, ALU.add, r=rk + [ak], w=[ak])
                            if kind in ('q', 'k'):
                                b.act(ACC[:, d0:d0 + L], ACC[:, d0:d0 + L], AF.Silu, r=[ak], w=[ak])
                            else:
                                b.act(TB[:, d0:d0 + L], ACC[:, d0:d0 + L], AF.Silu, r=[ak], w=['TB'])

                    def emit_tail(fi):
                        kind, col0, cq = fcs[fi]
                        acck = [('ACC', 0), ('ACC', 256), ('ACC', 1280)]
                        if kind in ('q', 'k'):
                            b.act(TB[:], ACC[:], AF.Square, r=acck, w=['TB'])
                            kq = 0 if kind == 'k' else 1
                            sc_ = 1.0 if kind == 'k' else float(128.0 ** -0.5)
                            for ti, (t0, n) in enumerate(TILES):
                                p2, p2k = b.ps()
                                b.mm(p2[:, :n], self.onesb[:], TB[:, t0:t0 + n], r=['TB', 'onesb'], w=[p2k])
                                b.act(rs[:, :n], p2[:, :n], AF.Ln, bias=eps_col, r=[p2k, 'cfa'], w=['grs'])
                                b.act(rs[:, :n], rs[:, :n], AF.Exp, scale=-0.5, r=['grs'], w=['grs'])
                                c0 = t0 // 128
                                nc_ = n // 128
                                b.stt('dve', KQ[:, c0:c0 + nc_, kq, :], ACC[:, t0:t0 + n].rearrange("p (a b) -> p a b", b=128), sc_,
                                      rs[:, :n].rearrange("p (a b) -> p a b", b=128), ALU.mult, ALU.mult,
                                      r=acck + ['grs'], w=[gk('KQ')])
                            if kind == 'k':
                                for c0 in range(0, NCH, 4):
                                    nc_ = min(4, NCH - c0)
                                    pt, ptk = b.ps()
                                    ptb = ps_bf(pt)
                                    for cc in range(nc_):
                                        b.tr(ptb[:, cc * 128:(cc + 1) * 128], KQ[:, c0 + cc, 0, :], self.identb[:], r=[gk('KQ'), 'identb'], w=[ptk])
                                    b.cp('act', KTM[:, c0:c0 + nc_, :], ptb[:, 0:nc_ * 128].rearrange("p (a b) -> p a b", b=128), r=[ptk], w=[gk('KTM')])
                        else:
                            a_ = 0 if kind == 'v0' else 1
                            for c0 in range(0, NCH, 4):
                                nc_ = min(4, NCH - c0)
                                pt, ptk = b.ps()
                                ptb = ps_bf(pt)
                                for cc in range(nc_):
                                    b.tr(ptb[:, cc * 128:(cc + 1) * 128], TB[:, (c0 + cc) * 128:(c0 + cc + 1) * 128], self.identb[:], r=['TB', 'identb'], w=[ptk])
                                b.cp('act', VTM[:, c0:c0 + nc_, a_, :], ptb[:, 0:nc_ * 128].rearrange("p (a b) -> p a b", b=128), r=[ptk], w=[gk('VTM')])

                    for k_ in emit_proj(0):
                        b.release(k_)
                    for fi in range(4):
                        emit_conv(fi)
                        hk = emit_proj(fi + 1) if fi + 1 < 4 else []
                        emit_tail(fi)
                        for k_ in hk:
                            b.release(k_)
                with b.scope() as ses:
                    NL = GDN_LANES

                    def t4(nm, dt):
                        return b.sb(ses, "g_" + nm, [128, 4, 128], dt)
                    DG = b.sb(ses, "g_DG", [128, 4, 128], F32)
                    BGEg = b.sb(ses, "g_BGEg", [128, NCH, 4], F32)
                    C1 = t4("C1", F32)
                    TO = C1
                    DT = t4("DT", BF16)
                    Dm = t4("Dm", BF16)
                    KKn = t4("KKn", BF16)
                    QKs = b.sb(ses, "g_QKs", [128, 2, 128], BF16)
                    VN = t4("VN", BF16)
                    S = t4("S", F32)
                    Sbf = t4("Sbf", BF16)
                    VB = t4("VB", BF16)
                    KBG = t4("KBG", BF16)
                    KDEC = t4("KDEC", BF16)
                    NWT = t4("NWT", BF16)
                    lanes = []
                    for k in range(NL):
                        lanes.append({nm: t4("%s_l%d" % (nm, k), BF16) for nm in ['Ap', 'QKD', 'Y', 'W', 'X']})
                        lanes[-1]['k'] = k
                    b.op('pool', lambda e: e.memset(S[:], 0.0), w=['S'])
                    b.op('pool', lambda e: e.memset(Sbf[:], 0.0), w=['Sbf'])
                    for d in range(2):
                        b.tt('pool', BGEg[:, :, 2 * d:2 * d + 2], G['BETA'][:, :, gcols(d)], G['EG'][:, :, gcols(d)], ALU.mult, r=['BETA', 'EG'], w=['BGEg'])
                    owritten = set()
                    f2 = lambda ap: ap.rearrange("p a b -> p (a b)")
                    idb = ident.unsqueeze(1).to_broadcast([128, 2, 128])
                    nidb = self.nident.unsqueeze(1).to_broadcast([128, 2, 128])

                    def step_gen(s, Ln):
                        lk = lambda nm: (nm, Ln['k'])
                        Ap, QKD, Y, W, X = (Ln[nm] for nm in ['Ap', 'QKD', 'Y', 'W', 'X'])
                        cds = [FWD[s], BWD[s]]

                        def scale_ops(which):
                            for u in range(4):
                                d, a_ = u // 2, u % 2
                                cd = cds[d]
                                col = d * 16 + 2 * g + a_
                                if which == 'VB':
                                    b.act(VB[:, u, :], VTM[:, cd, a_, :], AF.Copy, scale=G['BETA'][:, cd, col:col + 1], r=[gk('VTM'), 'BETA'], w=['VB'])
                                elif which == 'KBG':
                                    b.act(KBG[:, u, :], KTM[:, cd, :], AF.Copy, scale=BGEg[:, cd, u:u + 1], r=[gk('KTM'), 'BGEg'], w=['KBG'])
                                else:
                                    b.act(KDEC[:, u, :], KTM[:, cd, :], AF.Copy, scale=G['KD'][:, cd, col:col + 1], r=[gk('KTM'), 'KD'], w=['KDEC'])
                        pkq, pkqk = b.ps(hold=True)
                        for d in range(2):
                            cd = cds[d]
                            b.mm(pkq[:, d * 256:(d + 1) * 256], KQ[:, cd, 0, :], KQ[:, cd, :, :].rearrange("p a b -> p (a b)"),
                                 r=[gk('KQ')], w=[pkqk])
                        for d in range(2):
                            cd = cds[d]
                            b.tt('pool', DG[:, 2 * d:2 * d + 2, :], idb, bc(G['GCUM'][:, cd, gcols(d)]), ALU.mult, r=['GCUM', 'cfa'], w=['DG'])
                        pe_, pek = b.ps(hold=True)
                        for u in range(4):
                            b.mm(pe_[:, u * 128:(u + 1) * 128], self.ones, DG[:, u, :], start=True, stop=False, r=['DG', 'cfa'], w=[pek])
                            b.mm(pe_[:, u * 128:(u + 1) * 128], DG[:, u, :], self.nones, start=False, stop=True, r=['DG', 'cfa'], w=[pek])
                        b.ts('dve', f2(C1[:]), pe_[:], 0.0, None, ALU.min, r=[pek], w=['C1'])
                        b.act(f2(DT[:]), f2(C1[:]), AF.Exp, r=['C1'], w=['DT'])
                        b.ts('dve', f2(C1[:]), pe_[:], 0.0, None, ALU.max, r=[pek], w=['C1'])
                        b.release(pek)
                        b.act(f2(Dm[:]), f2(C1[:]), AF.Exp, scale=-1.0, r=['C1'], w=['Dm'])
                        for d in range(2):
                            cd = cds[d]
                            kk = pkq[:, d * 256:d * 256 + 128].unsqueeze(1).to_broadcast([128, 2, 128])
                            b.tt('dve', KKn[:, 2 * d:2 * d + 2, :], kk, bc(G['NBETA'][:, cd, gcols(d)]), ALU.mult, r=[pkqk, 'NBETA'], w=['KKn'])
                        qkv_ = pkq[:].rearrange("p (d x) -> p d x", d=2)[:, :, 128:256]
                        b.tt('dve', QKs[:], qkv_, self.maskq.rearrange("p (d x) -> p d x", d=2), ALU.mult, r=[pkqk, 'cfa'], w=['QKs'])
                        b.release(pkqk)
                        yield
                        b.tt('pool', f2(Ap[:]), f2(KKn[:]), f2(Dm[:]), ALU.mult, r=['KKn', 'Dm'], w=[lk('Ap')])
                        b.tt('pool', QKD[:].rearrange("p (d a) x -> p d a x", d=2), QKs[:].unsqueeze(2).to_broadcast([128, 2, 2, 128]),
                             DT[:].rearrange("p (d a) x -> p d a x", d=2), ALU.mult, r=['QKs', 'DT'], w=[lk('QKD')])
                        ptt, pttk = b.ps(hold=True)
                        pttb = ps_bf(ptt)
                        for u in range(4):
                            b.tr(pttb[:, u * 128:(u + 1) * 128], Ap[:, u, :], self.identb[:], r=[lk('Ap'), 'identb'], w=[pttk])
                        b.cp('pool', Y[:], self.identb[:].unsqueeze(1).to_broadcast([128, 4, 128]), r=['identb'], w=[lk('Y')])
                        b.op('dve', lambda e, o=f2(Y[:]), m=f2(lvl[:, 0, :, :]), dd=pttb[:, 0:512]: e.copy_predicated(o, m, dd),
                             r=[pttk, 'lvl', lk('Y')], w=[lk('Y')])
                        b.release(pttk)
                        yield
                        for l in range(1, 7):
                            pw, pwk = b.ps(hold=True)
                            for u in range(4):
                                b.mm(pw[:, u * 128:(u + 1) * 128], Ap[:, u, :], Y[:, u, :], r=[lk('Ap'), lk('Y')], w=[pwk])
                            px, pxk = b.ps(hold=True)
                            pxb = ps_bf(px)
                            for u in range(4):
                                b.tr(pxb[:, u * 128:(u + 1) * 128], Y[:, u, :], self.identb[:], r=[lk('Y'), 'identb'], w=[pxk])
                            b.cp('act', f2(W[:]), pw[:], r=[pwk], w=[lk('W')])
                            b.cp('dve', f2(X[:]), pxb[:, 0:512], r=[pxk], w=[lk('X')])
                            b.release(pwk)
                            b.release(pxk)
                            yield
                            pz, pzk = b.ps(hold=True)
                            for u in range(4):
                                b.mm(pz[:, u * 128:(u + 1) * 128], X[:, u, :], W[:, u, :], r=[lk('X'), lk('W')], w=[pzk])
                            b.op('dve', lambda e, o=f2(Y[:]), m=f2(lvl[:, l, :, :]), dd=pz[:]: e.copy_predicated(o, m, dd),
                                 r=[pzk, 'lvl', lk('Y')], w=[lk('Y')])
                            b.release(pzk)
                            if l == 6:
                                scale_ops('KBG')
                            yield
                        pwt, pwtk = b.ps(hold=True)
                        for u in range(4):
                            b.mm(pwt[:, u * 128:(u + 1) * 128], KBG[:, u, :], Y[:, u, :], r=['KBG', lk('Y')], w=[pwtk])
                        b.act(f2(NWT[:]), pwt[:], AF.Copy, scale=-1.0, r=[pwtk], w=['NWT'])
                        b.release(pwtk)
                        scale_ops('VB')
                        yield
                        pvn, pvnk = b.ps(hold=True)
                        for u in range(4):
                            b.mm(pvn[:, u * 128:(u + 1) * 128], Y[:, u, :], VB[:, u, :], start=True, stop=False, r=[lk('Y'), 'VB'], w=[pvnk])
                            b.mm(pvn[:, u * 128:(u + 1) * 128], NWT[:, u, :], Sbf[:, u, :], start=False, stop=True, r=['NWT', 'Sbf'], w=[pvnk])
                        b.cp('act', f2(VN[:]), pvn[:], r=[pvnk], w=['VN'])
                        b.release(pvnk)
                        scale_ops('KDEC')
                        yield
                        pds, pdsk = b.ps(hold=True)
                        po1, po1k = b.ps(hold=True)
                        po2, po2k = b.ps(hold=True)
                        for u in range(4):
                            b.mm(pds[:, u * 128:(u + 1) * 128], KDEC[:, u, :], VN[:, u, :], r=['KDEC', 'VN'], w=[pdsk])
                        for d in range(2):
                            cd = cds[d]
                            b.mm(po1[:, d * 256:(d + 1) * 256], KQ[:, cd, 1, :], Sbf[:, 2 * d:2 * d + 2, :].rearrange("p a b -> p (a b)"),
                                 r=[gk('KQ'), 'Sbf'], w=[po1k])
                        for u in range(4):
                            b.mm(po2[:, u * 128:(u + 1) * 128], QKD[:, u, :], VN[:, u, :], r=[lk('QKD'), 'VN'], w=[po2k])
                        for d in range(2):
                            cd = cds[d]
                            b.tt('pool', S[:, 2 * d:2 * d + 2, :], S[:, 2 * d:2 * d + 2, :], bc(G['EGL'][:, cd, gcols(d)]), ALU.mult,
                                 r=['S', 'EGL'], w=['S'])
                        b.tt('dve', f2(S[:]), f2(S[:]), pds[:], ALU.add, r=['S', pdsk], w=['S'])
                        b.release(pdsk)
                        b.cp('act', f2(Sbf[:]), f2(S[:]), r=['S'], w=['Sbf'])
                        for d in range(2):
                            cd = cds[d]
                            b.tt('dve', TO[:, 2 * d:2 * d + 2, :], po1[:, d * 256:(d + 1) * 256].rearrange("p (a x) -> p a x", a=2),
                                 bc(G['EG'][:, cd, gcols(d)]), ALU.mult, r=[po1k, 'EG'], w=['C1'])
                        b.tt('dve', f2(TO[:]), f2(TO[:]), po2[:], ALU.add, r=['C1', po2k], w=['C1'])
                        b.release(po1k)
                        b.release(po2k)
                        for d in range(2):
                            cd = cds[d]
                            if cd not in owritten:
                                owritten.add(cd)
                                b.cp('act', O[:, cd, :, :], TO[:, 2 * d:2 * d + 2, :], r=['C1'], w=[gk('O')])
                            else:
                                b.tt('pool', O[:, cd, :, :], O[:, cd, :, :], TO[:, 2 * d:2 * d + 2, :], ALU.add, r=['C1', gk('O')], w=[gk('O')])
                        yield

                    gens = []
                    next_s = 0
                    turn = 0
                    stagger = GDN_STAGGER
                    while next_s < NCH or gens:
                        if next_s < NCH and turn % stagger == 0 and len(gens) < NL:
                            gens.append(step_gen(next_s, lanes[next_s % NL]))
                            next_s += 1
                        for g_ in list(gens):
                            try:
                                next(g_)
                            except StopIteration:
                                gens.remove(g_)
                        turn += 1
            with b.scope() as oes:
                ZS = b.sb(oes, "g_ZS", [128, NCH, 256], BF16)
                wzs = b.sb(oes, "g_wzs", [128, KC, 256], F32)
                wzb = b.sb(oes, "g_wzb", [128, KC, 256], BF16)
                wos = b.sb(oes, "g_wos", [128, 2, D], F32)
                wob = b.sb(oes, "g_wob", [128, 2, D], BF16)
                OSQ = b.sb(oes, "g_OSQ", [128, 6, 2, 128], F32)
                SS = b.sb(oes, "g_SS", [128, NCH, 2], F32)
                YT = b.sb(oes, "g_YT", [128, 6, 2, 128], F32)
                YTb = b.sb(oes, "g_YTb", [128, 6, 2, 128], BF16)
                YF = b.sb(oes, "g_YF", [128, 2, T], BF16)
                zc0 = 4096 + 2 * g * 128
                b.dma(wzs[:], w_in[:, zc0:zc0 + 256].rearrange("(k p) n -> p k n", p=128), w=['wzs'])
                b.cp('pool', wzb[:], wzs[:], r=['wzs'], w=['wzb'])
                b.dma(wos[:], w_out[2 * g * 128:(2 * g + 2) * 128, :].rearrange("(k p) n -> p k n", p=128), w=['gwos'])
                b.cp('pool', wob[:], wos[:], r=['gwos'], w=['gwob'])
                for c0 in range(0, NCH, 2):
                    pz_, pzk_ = b.ps()
                    for cc in range(2):
                        c = c0 + cc
                        for kc in range(KC):
                            b.mm(pz_[:, cc * 256:(cc + 1) * 256], self.hb[:, kc, c * 128:(c + 1) * 128], wzb[:, kc, :],
                                 start=(kc == 0), stop=(kc == KC - 1), r=['wzb'] + h_keys(kc, ALLT), w=[pzk_])
                    b.act(ZS[:, c0:c0 + 2, :], pz_[:].rearrange("p (a b) -> p a b", a=2), AF.Silu, r=[pzk_], w=['ZS'])
                for c0 in range(0, NCH, 6):
                    Oc = O[:, c0:c0 + 6, :, :]
                    b.tt('pool', OSQ[:], Oc, Oc, ALU.mult, r=[gk('O')], w=['OSQ'])
                    b.op('dve', lambda e, o=SS[:, c0:c0 + 6, :], i=OSQ[:]: e.tensor_reduce(out=o, in_=i, axis=AX.X, op=ALU.add), r=['OSQ'], w=['SS'])
                b.act(SS[:], SS[:], AF.Ln, bias=eps_col, scale=1.0 / 128.0, r=['SS', 'cfa'], w=['SS'])
                b.act(SS[:], SS[:], AF.Exp, scale=-0.5, r=['SS'], w=['SS'])
                for c0 in range(0, NCH, 6):
                    Oc = O[:, c0:c0 + 6, :, :]
                    b.tt('dve', YT[:], Oc, SS[:, c0:c0 + 6, :].unsqueeze(3).to_broadcast([128, 6, 2, 128]), ALU.mult, r=[gk('O'), 'SS'], w=['YT'])
                    b.tt('pool', YT[:].rearrange("p a b c -> p (a b) c"), YT[:].rearrange("p a b c -> p (a b) c"),
                         normg[:].unsqueeze(1).to_broadcast([128, 12, 128]), ALU.mult, r=['YT', 'normg'], w=['YT'])
                    b.tt('dve', YTb[:], YT[:], ZS[:, c0:c0 + 6, :].rearrange("p a (b c) -> p a b c", b=2), ALU.mult, r=['YT', 'ZS'], w=['YTb'])
                    for a_ in range(2):
                        for q4 in range(0, 6, 4):
                            nq = min(4, 6 - q4)
                            pt, ptk = b.ps()
                            ptb = ps_bf(pt)
                            for cc in range(nq):
                                b.tr(ptb[:, cc * 128:(cc + 1) * 128], YTb[:, q4 + cc, a_, :], self.identb[:], r=['YTb', 'identb'], w=[ptk])
                            t0_ = (c0 + q4) * 128
                            b.cp('act', YF[:, a_, t0_:t0_ + nq * 128], ptb[:, 0:nq * 128], r=[ptk], w=['YF'])
                self.out_proj(wob, 'gwob', 2, lambda k, t0, n: YF[:, k, t0:t0 + n], lambda k, ti: ['YF'])


def host_consts():
    cfa = np.zeros((128, NCFA), np.float32)
    idx = np.arange(128)
    cfa[:, 0:128] = np.eye(128, dtype=np.float32)
    cfa[:, 128:256] = (idx[:, None] <= idx[None, :]).astype(np.float32)
    cfa[:, 256:384] = (idx[:, None] >= idx[None, :]).astype(np.float32)
    cfa[:, 384:512] = 1.0
    cfa[:, 512:640] = (idx[None, :] >= idx[:, None]).astype(np.float32)
    cfa[:, 640:768] = (idx[None, :] <= idx[:, None]).astype(np.float32)
    rot = np.zeros((128, 128), np.float32)
    for m in range(128):
        half = (m % 64) // 32
        if half == 0:
            rot[m + 32, m] = -1.0
        else:
            rot[m - 32, m] = 1.0
    cfa[:, 768:896] = rot
    cfa[:, 1152] = EPS
    cfa[:, 1153] = 128.0 * EPS
    cfa[:, 1154] = 1.0
    cfa[:, 896:1024] = -np.eye(128, dtype=np.float32)
    cfa[:, 1024:1152] = -1.0
    return cfa


def rope_host():
    rows = 2048 // 64
    row = np.repeat(np.arange(rows), 64).astype(np.float32)
    col = np.tile(np.arange(64), rows).astype(np.float32)
    n_freq = 32
    freqs = (np.float32(10000.0) ** (-np.arange(n_freq, dtype=np.float32) / np.float32(n_freq))).astype(np.float32)
    ang_r = row[:, None] * freqs
    ang_c = col[:, None] * freqs
    ang = np.concatenate([ang_r, ang_r, ang_c, ang_c], axis=-1).astype(np.float32)
    out = np.zeros((128, 4096), np.float32)
    out[:, 0:2048] = np.cos(ang).T
    out[:, 2048:4096] = np.sin(ang).T
    return out


_CACHE = {}


def get_prog(layers, nseq):
    key = (tuple(layers), nseq)
    if key not in _CACHE:
        nc = bass.Bass("TRN2", target_bir_lowering=False)
        p = Prog(nc, list(layers), nseq)
        p.build()
        _CACHE[key] = (nc, p)
    return _CACHE[key]


def layer_inputs(inp, li):
    d = {}
    d["w_mod%d" % li] = np.ascontiguousarray(inp["w_mod"][li])
    d["bmodT%d" % li] = np.ascontiguousarray(inp["b_mod"][li].reshape(48, 128).T)
    ln = np.stack([inp["ln_g"][li, 0], inp["ln_b"][li, 0], inp["ln_g"][li, 1], inp["ln_b"][li, 1]], 0)
    d["lnT%d" % li] = np.ascontiguousarray(ln.reshape(4, 8, 128).transpose(2, 0, 1))
    d["w_ffn_in%d" % li] = np.ascontiguousarray(inp["w_ffn_in"][li])
    d["w_ffn_out%d" % li] = np.ascontiguousarray(inp["w_ffn_out"][li])
    j = li // 2
    if li % 2 == 1:
        d["attn_w_qkv%d" % j] = np.ascontiguousarray(inp["attn_w_qkv"][j])
        d["attn_w_out%d" % j] = np.ascontiguousarray(inp["attn_w_out"][j])
        d["attn_gain%d" % j] = np.ascontiguousarray(np.stack([inp["attn_q_norm"][j], inp["attn_k_norm"][j]], 1))
        d["rope"] = rope_host()
    else:
        d["gdn_w_in%d" % j] = np.ascontiguousarray(inp["gdn_w_in"][j])
        d["gdn_w_out%d" % j] = np.ascontiguousarray(inp["gdn_w_out"][j])
        d["gdn_convT%d" % j] = np.ascontiguousarray(inp["gdn_conv"][j].reshape(5, 32, 128).transpose(2, 1, 0))
        gp = np.concatenate([inp["gdn_a_log"][j].reshape(32), inp["gdn_dt_bias"][j].reshape(32)])
        d["gdn_gpar%d" % j] = np.ascontiguousarray(np.broadcast_to(gp[None, :], (128, 64)).astype(np.float32))
        d["gdn_normg%d" % j] = np.ascontiguousarray(np.broadcast_to(inp["gdn_norm_g"][j][None, :], (128, 128)).astype(np.float32))
        d["lvlmask"] = lvlmask_host()
    return d


def lvlmask_host():
    p = np.arange(128)[:, None]
    f = np.arange(128)[None, :]
    m = np.zeros((128, 7, 4, 128), np.uint8)
    for l in range(7):
        if l == 0:
            blk = (p >> 1) == (f >> 1)
        else:
            blk = ((p >> (l + 1)) == (f >> (l + 1))) & ((p >> l) != (f >> l))
        fw = (blk & (f > p)).astype(np.uint8)
        bw = (blk & (f < p)).astype(np.uint8)
        m[:, l, 0, :] = fw
        m[:, l, 1, :] = fw
        m[:, l, 2, :] = bw
        m[:, l, 3, :] = bw
    return np.ascontiguousarray(m.reshape(128, 7 * 4 * 128))


def seq_fm(inp, bidx):
    return np.ascontiguousarray(np.concatenate([inp["ctx"][bidx], inp["x"][bidx]], 0).T)


def cT_host(inp, bidxs):
    v = np.stack([inp["c_ctx"]] + [inp["c"][bi] for bi in bidxs], 1)
    return np.ascontiguousarray(v.reshape(8, 128, len(bidxs) + 1).transpose(1, 0, 2))


def run_layers(inp, layers, xs):
    nc, p = get_prog(layers, 1)
    outs = [None] * 16
    cfa = host_consts()
    for rnd in range(2):
        in_maps = []
        for core in range(8):
            bidx = rnd * 8 + core
            m = {"cfa": cfa, "xin0": xs[bidx], "cTall": cT_host(inp, [bidx])}
            for li in layers:
                m.update(layer_inputs(inp, li))
            in_maps.append({k: v for k, v in m.items() if k in p.dram})
        res = run_bass_kernel_spmd(nc, in_maps, core_ids=list(range(8)))
        for core in range(8):
            outs[rnd * 8 + core] = res.results[core]["xout0"]
    return outs


def kernel_unfused(**inp):
    inp = {k: np.asarray(v) for k, v in inp.items()}
    xs = [seq_fm(inp, bi) for bi in range(16)]
    for li in range(DEPTH):
        xs = run_layers(inp, [li], xs)
    out = np.stack([x.T[256:, :] for x in xs], 0)
    return np.ascontiguousarray(out.astype(np.float32))


def kernel(**inp):
    inp = {k: np.asarray(v) for k, v in inp.items()}
    layers = list(range(DEPTH))
    nc, p = get_prog(layers, 2)
    cfa = host_consts()
    shared = {"cfa": cfa}
    for li in layers:
        shared.update(layer_inputs(inp, li))
    shared = {k: v for k, v in shared.items() if k in p.dram}
    in_maps = []
    for core in range(8):
        m = dict(shared)
        for s in range(2):
            bidx = 2 * core + s
            m["xin%d" % s] = seq_fm(inp, bidx)
        m["cTall"] = cT_host(inp, [2 * core, 2 * core + 1])
        in_maps.append(m)
    res = run_bass_kernel_spmd(nc, in_maps, core_ids=list(range(8)))
    out = np.zeros((16, 2048, 1024), np.float32)
    for core in range(8):
        for s in range(2):
            out[2 * core + s] = res.results[core]["xout%d" % s].T[256:, :]
    return out
```

```python
import numpy as np
from contextlib import ExitStack
import concourse.bass as bass
import concourse.mybir as mybir
from concourse.bass_utils import run_bass_kernel_spmd

F32 = mybir.dt.float32
BF16 = mybir.dt.bfloat16
U8 = mybir.dt.uint8
AF = mybir.ActivationFunctionType
ALU = mybir.AluOpType
AX = mybir.AxisListType

D = 1024
KC = 8
T = 2304
NCH = 18
DEPTH = 4
DFF = 2816
EPS = 1e-6
ALPHA = 8.0 ** 0.25
TILES = [(0, 256), (256, 512), (768, 512), (1280, 512), (1792, 512)]
FWD = list(range(18))
BWD = [1, 0] + list(range(17, 1, -1))
NCFA = 1156
DEBUG_ALLOC = False
GDN_LANES = 5
GDN_STAGGER = 3
DBG = {}


def which_of(ti):
    return 0 if ti == 0 else 1


class B:
    def __init__(self, nc):
        self.nc = nc
        self.es = ExitStack()
        self.E = ['pe', 'act', 'dve', 'pool', 'sp']
        self.q = {e: [] for e in self.E}
        self.cnt = {e: 0 for e in self.E}
        self.sem = {e: self.es.enter_context(nc.semaphore("s_" + e)) for e in self.E if e != 'sp'}
        self.ND = 16
        self.dsem = [self.es.enter_context(nc.semaphore("d%d" % i)) for i in range(self.ND)]
        self.dcnt = [0] * self.ND
        self.drr = 0
        self.waited = {e: {} for e in self.E}
        self.track = {}
        self.ninstr = 0
        self.psb = [self.es.enter_context(nc.psum_tensor("ps%d" % i, [128, 512], F32)) for i in range(8)]
        self.psrr = 0

    def _semh(self, sk):
        return self.sem[sk] if isinstance(sk, str) else self.dsem[sk[1]]

    def _deps(self, eng, r, w):
        raw = {}
        oth = {}

        def need(dct, idv):
            if idv is None:
                return
            sk, v = idv
            if dct.get(sk, 0) < v:
                dct[sk] = v
        for key in r:
            t = self.track.get(key)
            if t:
                need(raw, t[0])
        for key in w:
            t = self.track.get(key)
            if t:
                need(oth, t[0])
                for sk, v in t[1].items():
                    need(oth, (sk, v))
        for sk, v in oth.items():
            if sk == eng and eng == 'pe':
                continue
            if raw.get(sk, 0) < v:
                raw[sk] = v
        for sk, v in raw.items():
            if self.waited[eng].get(sk, 0) >= v:
                continue
            self.waited[eng][sk] = v
            sh = self._semh(sk)
            self.q[eng].append(lambda e, sh=sh, v=v: e.wait_ge(sh, v))
            self.ninstr += 1

    def _mark(self, idv, r, w):
        for key in w:
            self.track[key] = [idv, {}]
        for key in r:
            t = self.track.setdefault(key, [None, {}])
            if t[1].get(idv[0], 0) < idv[1]:
                t[1][idv[0]] = idv[1]

    def op(self, eng, fn, r=(), w=()):
        self._deps(eng, r, w)
        self.cnt[eng] += 1
        sh = self.sem[eng]
        self.q[eng].append(lambda e, fn=fn, sh=sh: fn(e).then_inc(sh, 1))
        self.ninstr += 1
        self._mark((eng, self.cnt[eng]), r, w)

    def dma(self, out, in_, r=(), w=()):
        eng = 'sp'
        k = self.drr
        self.drr = (self.drr + 1) % self.ND
        self._deps(eng, r, w)
        prev = 16 * self.dcnt[k]
        if prev and self.waited[eng].get(('d', k), 0) < prev:
            self.waited[eng][('d', k)] = prev
            self.q[eng].append(lambda e, sh=self.dsem[k], v=prev: e.wait_ge(sh, v))
        self.dcnt[k] += 1
        val = 16 * self.dcnt[k]
        self.q[eng].append(lambda e, out=out, in_=in_, sh=self.dsem[k]: e.dma_start(out=out, in_=in_).then_inc(sh, 16))
        self.ninstr += 1
        idv = (('d', k), val)
        self._mark(idv, r, w)
        return idv

    def barrier(self):
        for e in self.E:
            for o in self.E:
                if o == 'sp' or o == e:
                    continue
                v = self.cnt[o]
                if v and self.waited[e].get(o, 0) < v:
                    self.waited[e][o] = v
                    self.q[e].append(lambda en, sh=self.sem[o], v=v: en.wait_ge(sh, v))
                    self.ninstr += 1
            for k in range(self.ND):
                v = 16 * self.dcnt[k]
                if v and self.waited[e].get(('d', k), 0) < v:
                    self.waited[e][('d', k)] = v
                    self.q[e].append(lambda en, sh=self.dsem[k], v=v: en.wait_ge(sh, v))
                    self.ninstr += 1

    def scope(self):
        return _Scope(self)

    def new_epoch(self):
        self.barrier()
        self.nep = getattr(self, 'nep', 0) + 1
        for e in self.E:
            if e == 'sp':
                continue
            self.sem[e] = self.es.enter_context(self.nc.semaphore("s_%s_%d" % (e, self.nep)))
            self.cnt[e] = 0
        for e in self.E:
            for o in list(self.waited[e].keys()):
                if isinstance(o, str):
                    del self.waited[e][o]
        self.track = {}

    def ps(self, hold=False):
        if not hasattr(self, 'held'):
            self.held = set()
        while True:
            k = self.psrr
            self.psrr = (self.psrr + 1) % 8
            if k not in self.held:
                break
        if hold:
            self.held.add(k)
        return self.psb[k], ('ps', k)

    def release(self, key):
        self.held.discard(key[1])

    def mm(self, out, lhsT, rhs, start=True, stop=True, r=(), w=()):
        self.op('pe', lambda e: e.matmul(out, lhsT, rhs, start=start, stop=stop), r, w)

    def tr(self, out, in_, ident, r=(), w=()):
        self.op('pe', lambda e: e.transpose(out, in_, ident), r, w)

    def act(self, out, in_, func, bias=None, scale=None, r=(), w=()):
        kw = {}
        if bias is not None:
            kw['bias'] = bias
        if scale is not None:
            kw['scale'] = scale
        self.op('act', lambda e: e.activation(out=out, in_=in_, func=func, **kw), r, w)

    def tt(self, eng, out, a, b, op, r=(), w=()):
        self.op(eng, lambda e: e.tensor_tensor(out=out, in0=a, in1=b, op=op), r, w)

    def ts(self, eng, out, a, s1, s2, op0, op1=None, r=(), w=()):
        if op1 is None:
            self.op(eng, lambda e: e.tensor_scalar(out=out, in0=a, scalar1=s1, scalar2=None, op0=op0), r, w)
        else:
            self.op(eng, lambda e: e.tensor_scalar(out=out, in0=a, scalar1=s1, scalar2=s2, op0=op0, op1=op1), r, w)

    def stt(self, eng, out, a, s, b, op0, op1, r=(), w=()):
        self.op(eng, lambda e: e.scalar_tensor_tensor(out=out, in0=a, scalar=s, in1=b, op0=op0, op1=op1), r, w)

    def cp(self, eng, out, in_, r=(), w=()):
        if eng == 'act':
            self.act(out, in_, AF.Copy, r=r, w=w)
        else:
            self.op(eng, lambda e: e.tensor_copy(out=out, in_=in_), r, w)

    def sb(self, es, name, shape, dt):
        if DEBUG_ALLOC:
            print("alloc", name, shape, dt, "remaining", self.nc.sbuf_bytes_remaining)
        self.uid = getattr(self, "uid", 0) + 1
        return es.enter_context(self.nc.sbuf_tensor("sb%d_%s" % (self.uid, name), shape, dt))

    def finish(self):
        nc = self.nc
        q = self.q
        with nc.Block() as block:
            @block.tensor
            def _(e):
                for f in q['pe']:
                    f(e)

            @block.scalar
            def _(e):
                for f in q['act']:
                    f(e)

            @block.vector
            def _(e):
                for f in q['dve']:
                    f(e)

            @block.gpsimd
            def _(e):
                for f in q['pool']:
                    f(e)

            @block.sync
            def _(e):
                for f in q['sp']:
                    f(e)
        self.es.close()


class _Scope:
    def __init__(self, b):
        self.b = b
        self.es = ExitStack()

    def __enter__(self):
        self.es.__enter__()
        return self.es

    def __exit__(self, *a):
        self.b.barrier()
        return self.es.__exit__(*a)


def xa_keys(c, tis):
    return [('xa', c, ti) for ti in tis]


def h_keys(c, tis):
    return [('h', c, ti) for ti in tis]


ALLT = list(range(5))


class Prog:
    def __init__(self, nc, layers, nseq):
        self.nc = nc
        self.b = B(nc)
        self.layers = layers
        self.nseq = nseq
        self.dram = {}

    def din(self, name, shape, dt=F32):
        if name not in self.dram:
            self.dram[name] = self.nc.dram_tensor(name, list(shape), dt, kind="ExternalInput").ap()
        return self.dram[name]

    def dout(self, name, shape, dt=F32):
        if name not in self.dram:
            self.dram[name] = self.nc.dram_tensor(name, list(shape), dt, kind="ExternalOutput").ap()
        return self.dram[name]

    def build(self):
        b = self.b
        nc = self.nc
        es = b.es
        self.xa = b.sb(es, "xa", [128, KC, T], F32)
        self.hb = b.sb(es, "hb", [128, KC, T], BF16)
        self.cfa = b.sb(es, "cfa", [128, NCFA], F32)
        self.identb = b.sb(es, "identb", [128, 128], BF16)
        self.onesb = b.sb(es, "onesb", [128, 128], BF16)
        self.mean1k = b.sb(es, "mean1k", [128, 128], F32)
        self.mean1kb = b.sb(es, "mean1kb", [128, 128], BF16)
        cfa_d = self.din("cfa", [128, NCFA])
        b.dma(self.cfa[:], cfa_d, w=['cfa'])
        self.ident = self.cfa[:, 0:128]
        self.trif = self.cfa[:, 128:256]
        self.trib = self.cfa[:, 256:384]
        self.ones = self.cfa[:, 384:512]
        self.maskq = self.cfa[:, 512:768]
        self.rot = self.cfa[:, 768:896]
        self.epsv = self.cfa[:, 1152:1155]
        self.nident = self.cfa[:, 896:1024]
        self.nones = self.cfa[:, 1024:1152]
        b.cp('act', self.identb[:], self.ident, r=['cfa'], w=['identb'])
        b.cp('act', self.onesb[:], self.ones, r=['cfa'], w=['onesb'])
        b.act(self.mean1k[:], self.ones, AF.Copy, scale=1.0 / 1024.0, r=['cfa'], w=['mean1k'])
        b.act(self.mean1kb[:], self.ones, AF.Copy, scale=1.0 / 1024.0, r=['cfa'], w=['mean1k'])
        self.mod_all(es)
        outs = []
        for s in range(self.nseq):
            xin = self.din("xin%d" % s, [D, T])
            xout = self.dout("xout%d" % s, [D, T])
            with b.scope() as les:
                stg = [b.sb(les, "ldst%d" % i, [128, T], F32) for i in range(2)]
                for c in range(KC):
                    st = stg[c % 2]
                    b.dma(st[:], xin[c * 128:(c + 1) * 128, :], w=[('ldst', c % 2)])
                    b.act(self.xa[:, c, :], st[:], AF.Copy, scale=ALPHA, r=[('ldst', c % 2)], w=xa_keys(c, ALLT))
            for n, li in enumerate(self.layers):
                last = (n == len(self.layers) - 1)
                b.new_epoch()
                self.layer(li, s, out_plain=last)
            for c in range(KC):
                outs.append(b.dma(xout[c * 128:(c + 1) * 128, :], self.xa[:, c, :], r=xa_keys(c, ALLT)))
        for (sk, v) in outs + getattr(self, 'dbg_outs', []):
            if b.waited['sp'].get(sk, 0) < v:
                b.waited['sp'][sk] = v
                b.q['sp'].append(lambda e, sh=b.dsem[sk[1]], v=v: e.wait_ge(sh, v))
        b.finish()

    def layer(self, li, s, out_plain):
        b = self.b
        with b.scope() as les:
            self.lv = {}
            stop = DBG.get('stop')
            self.modulation(li, s, les, out_plain)
            if DBG.get('dump'):
                self.dbg_outs = getattr(self, 'dbg_outs', [])
                self.dbg_outs.append(b.dma(self.dout("dbg_MOD", [128, 96]), self.P['MOD'][:].rearrange("p a b -> p (a b)"), r=['MOD']))
            if stop == 'mod':
                return
            self.modulate_in()
            if DBG.get('dump'):
                self.dbg_outs.append(b.dma(self.dout("dbg_h", [128, KC * T], BF16), self.hb[:].rearrange("p a b -> p (a b)"),
                                           r=[('h', c, ti) for c in range(KC) for ti in ALLT]))
            if stop == 'modin':
                return
            if li % 2 == 0:
                self.gdn(li)
            else:
                self.attn(li)
            if stop == 'mixer':
                return
            self.layernorm(0, want_h=True)
            if stop == 'ln0':
                return
            self.ffn(li)
            if stop == 'ffn':
                return
            self.layernorm(1, want_h=False)

    def mod_all(self, es):
        b = self.b
        ns = 1 + self.nseq
        self.MODall = {}
        for li in self.layers:
            self.MODall[li] = b.sb(es, "MODall%d" % li, [128, 48, ns], F32)
        with b.scope() as wes:
            sc = b.sb(wes, "ma_sc", [128, KC, ns], F32)
            bm = b.sb(wes, "ma_bm", [128, 48], F32)
            wst = [b.sb(wes, "ma_wst%d" % i, [128, KC, 512], F32) for i in range(3)]
            cT = self.din("cTall", [128, KC, ns])
            b.dma(sc[:], cT, w=['sc'])
            b.act(sc[:], sc[:], AF.Silu, r=['sc'], w=['sc'])
            nblk = 0
            for li in self.layers:
                w_mod = self.din("w_mod%d" % li, [D, 6 * D])
                bmodT = self.din("bmodT%d" % li, [128, 48])
                b.dma(bm[:], bmodT, w=['bmodT'])
                ps, pk = b.ps(hold=True)
                for nb in range(12):
                    st = wst[nblk % 3]
                    sk = ('wmst', nblk % 3)
                    nblk += 1
                    b.dma(st[:], w_mod[:, nb * 512:(nb + 1) * 512].rearrange("(k p) n -> p k n", p=128), w=[sk])
                    for f in range(4):
                        fc = nb * 4 + f
                        for kc in range(KC):
                            b.mm(ps[:, fc * ns:(fc + 1) * ns], st[:, kc, f * 128:(f + 1) * 128], sc[:, kc, :],
                                 start=(kc == 0), stop=(kc == KC - 1), r=[sk, 'sc'], w=[pk])
                b.tt('dve', self.MODall[li][:], ps[:, 0:48 * ns].rearrange("p (a b) -> p a b", b=ns),
                     bm[:].unsqueeze(2).to_broadcast([128, 48, ns]), ALU.add, r=[pk, 'bmodT'], w=[('MODall', li)])
                b.release(pk)

    def modulation(self, li, s, les, out_plain):
        b = self.b
        P = {}
        for nm, shape in [('MOD', [128, 48, 2]), ('s1', [128, 8, 2]), ('H1s', [128, 8, 2]), ('H1b', [128, 8, 2]),
                          ('A', [128, 2, 8]), ('Bv', [128, 2, 8]), ('lnT', [128, 4, 8]), ('bmodT', [128, 48]),
                          ('sc', [128, 8, 2]), ('tmp1', [128, 8, 2])]:
            P[nm] = b.sb(les, "m_" + nm, shape, F32)
        self.P = P
        lnT = self.din("lnT%d" % li, [128, 4, 8])
        b.dma(P['lnT'][:], lnT, w=['lnT'])
        MOD = P['MOD']
        MA = self.MODall[li]
        b.cp('dve', MOD[:, :, 0:1], MA[:, :, 0:1], r=[('MODall', li)], w=['MOD'])
        b.cp('dve', MOD[:, :, 1:2], MA[:, :, 1 + s:2 + s], r=[('MODall', li)], w=['MOD'])

        def mj(j):
            return MOD[:, j * 8:(j + 1) * 8, :]
        b.ts('dve', P['s1'][:], mj(1), 1.0, 1.0 / ALPHA, ALU.add, ALU.mult, r=['MOD'], w=['s1'])
        ln = P['lnT']
        g0 = ln[:, 0, :].unsqueeze(2).to_broadcast([128, 8, 2])
        b0 = ln[:, 1, :].unsqueeze(2).to_broadcast([128, 8, 2])
        b.ts('dve', P['tmp1'][:], mj(4), 1.0, None, ALU.add, r=['MOD'], w=['tmp1'])
        b.tt('dve', P['H1s'][:], P['tmp1'][:], g0, ALU.mult, r=['tmp1', 'lnT'], w=['H1s'])
        b.tt('dve', P['H1b'][:], P['tmp1'][:], b0, ALU.mult, r=['tmp1', 'lnT'], w=['H1b'])
        b.tt('dve', P['H1b'][:], P['H1b'][:], mj(3), ALU.add, r=['H1b', 'MOD'], w=['H1b'])
        b.ts('dve', P['A'][:, 0, :], ln[:, 0, :], ALPHA, None, ALU.mult, r=['lnT'], w=['A'])
        b.ts('dve', P['Bv'][:, 0, :], ln[:, 1, :], ALPHA, None, ALU.mult, r=['lnT'], w=['Bv'])
        a2 = 1.0 if out_plain else ALPHA
        b.ts('dve', P['A'][:, 1, :], ln[:, 2, :], a2, None, ALU.mult, r=['lnT'], w=['A'])
        b.ts('dve', P['Bv'][:, 1, :], ln[:, 3, :], a2, None, ALU.mult, r=['lnT'], w=['Bv'])
        self.ga = mj(2)
        self.gaf = mj(5)
        self.sh = mj(0)

    def modulate_in(self):
        b = self.b
        P = self.P
        for c in range(KC):
            for (wh, t0, n, tis) in [(0, 0, 256, [0]), (1, 256, 2048, [1, 2, 3, 4])]:
                b.act(self.hb[:, c, t0:t0 + n], self.xa[:, c, t0:t0 + n], AF.Identity,
                      bias=self.sh[:, c, wh:wh + 1], scale=P['s1'][:, c, wh:wh + 1],
                      r=xa_keys(c, tis) + ['MOD', 's1'], w=h_keys(c, tis))

    def layernorm(self, idx, want_h):
        b = self.b
        P = self.P
        with b.scope() as es:
            sq = [b.sb(es, "ln_sq%d" % i, [128, 512], BF16) for i in range(4)]
            msb2 = [b.sb(es, "ln_msb%d" % i, [128, 512], F32) for i in range(2)]
            m22 = [b.sb(es, "ln_m2%d" % i, [128, 512], F32) for i in range(2)]
            rstd2 = [b.sb(es, "ln_rstd%d" % i, [128, 512], F32) for i in range(2)]
            tt_ = [b.sb(es, "ln_t%d" % i, [128, 512], F32) for i in range(4)]
            nsq = [0]

            def stats(ti):
                t0, n = TILES[ti]
                q_ = ti % 2
                msb, m2, rstd = msb2[q_], m22[q_], rstd2[q_]
                pm, pmk = b.ps(hold=True)
                pe2, pe2k = b.ps(hold=True)
                for c in range(KC):
                    k_ = nsq[0] % 4
                    nsq[0] += 1
                    b.act(sq[k_][:, :n], self.xa[:, c, t0:t0 + n], AF.Square, r=[('xa', c, ti)], w=[('lnsq', k_)])
                    b.mm(pm[:, :n], self.mean1k[:], self.xa[:, c, t0:t0 + n], start=(c == 0), stop=(c == KC - 1),
                         r=[('xa', c, ti), 'mean1k'], w=[pmk])
                    b.mm(pe2[:, :n], self.mean1kb[:], sq[k_][:, :n], start=(c == 0), stop=(c == KC - 1),
                         r=[('lnsq', k_), 'mean1k'], w=[pe2k])
                b.cp('act', msb[:, :n], pm[:, :n], r=[pmk], w=[('lnmsb', q_)])
                b.release(pmk)
                b.tt('dve', m2[:, :n], msb[:, :n], msb[:, :n], ALU.mult, r=[('lnmsb', q_)], w=[('lnm2', q_)])
                b.tt('dve', m2[:, :n], pe2[:, :n], m2[:, :n], ALU.subtract, r=[pe2k, ('lnm2', q_)], w=[('lnm2', q_)])
                b.release(pe2k)
                b.act(m2[:, :n], m2[:, :n], AF.Ln, bias=self.epsv[:, 0:1], r=[('lnm2', q_), 'cfa'], w=[('lnm2', q_)])
                b.act(rstd[:, :n], m2[:, :n], AF.Exp, scale=-0.5, r=[('lnm2', q_)], w=[('lnrstd', q_)])

            def norm(ti):
                t0, n = TILES[ti]
                wh = which_of(ti)
                q_ = ti % 2
                msb, rstd = msb2[q_], rstd2[q_]
                for c in range(KC):
                    t = tt_[c % 4]
                    tk = ('lnt', c % 4)
                    b.tt('dve', t[:, :n], self.xa[:, c, t0:t0 + n], msb[:, :n], ALU.subtract, r=[('xa', c, ti), ('lnmsb', q_)], w=[tk])
                    b.tt('dve', t[:, :n], t[:, :n], rstd[:, :n], ALU.mult, r=[tk, ('lnrstd', q_)], w=[tk])
                    b.act(self.xa[:, c, t0:t0 + n], t[:, :n], AF.Identity, bias=P['Bv'][:, idx, c:c + 1],
                          scale=P['A'][:, idx, c:c + 1], r=[tk, 'A', 'Bv'], w=[('xa', c, ti)])
                    if want_h:
                        b.ts('pool', self.hb[:, c, t0:t0 + n], t[:, :n], P['H1s'][:, c, wh:wh + 1], P['H1b'][:, c, wh:wh + 1],
                             ALU.mult, ALU.add, r=[tk, 'H1s', 'H1b'], w=[('h', c, ti)])
            stats(0)
            for ti in range(len(TILES)):
                if ti + 1 < len(TILES):
                    stats(ti + 1)
                norm(ti)

    def ffn(self, li):
        b = self.b
        w_in = self.din("w_ffn_in%d" % li, [D, 2 * DFF])
        w_out = self.din("w_ffn_out%d" % li, [DFF, D])
        with b.scope() as es:
            wis = [b.sb(es, "f_wis%d" % i, [128, KC, 512], F32) for i in range(2)]
            wos = [b.sb(es, "f_wos%d" % i, [128, 2, D], F32) for i in range(2)]
            wib = [b.sb(es, "f_wib%d" % i, [128, KC, 512], BF16) for i in range(2)]
            wob = [b.sb(es, "f_wob%d" % i, [128, 2, D], BF16) for i in range(2)]
            actb = b.sb(es, "f_act", [128, 2, T], BF16)
            sg = [b.sb(es, "f_sg%d" % i, [128, 512], F32) for i in range(2)]
            nsg = 0
            for fb in range(11):
                p = fb % 2
                f0 = fb * 256
                b.dma(wis[p][:, :, 0:256], w_in[:, f0:f0 + 256].rearrange("(k p) n -> p k n", p=128), w=[('wis', p, 0)])
                b.dma(wis[p][:, :, 256:512], w_in[:, DFF + f0:DFF + f0 + 256].rearrange("(k p) n -> p k n", p=128), w=[('wis', p, 1)])
                b.dma(wos[p][:], w_out[f0:f0 + 256, :].rearrange("(k p) n -> p k n", p=128), w=[('wos', p)])
                b.cp('pool', wib[p][:], wis[p][:], r=[('wis', p, 0), ('wis', p, 1)], w=[('wib', p)])
                b.cp('pool', wob[p][:], wos[p][:], r=[('wos', p)], w=[('wob', p)])
                for ti, (t0, n) in enumerate(TILES):
                    for j in range(2):
                        pg, pgk = b.ps()
                        pu, puk = b.ps()
                        for kc in range(KC):
                            b.mm(pg[:, :n], wib[p][:, kc, j * 128:(j + 1) * 128], self.hb[:, kc, t0:t0 + n],
                                 start=(kc == 0), stop=(kc == KC - 1), r=[('wib', p), ('h', kc, ti)], w=[pgk])
                        for kc in range(KC):
                            b.mm(pu[:, :n], wib[p][:, kc, 256 + j * 128:256 + (j + 1) * 128], self.hb[:, kc, t0:t0 + n],
                                 start=(kc == 0), stop=(kc == KC - 1), r=[('wib', p), ('h', kc, ti)], w=[puk])
                        s_ = sg[nsg % 2]
                        sk = ('fsg', nsg % 2)
                        nsg += 1
                        b.act(s_[:, :n], pg[:, :n], AF.Silu, r=[pgk], w=[sk])
                        b.tt('dve', actb[:, j, t0:t0 + n], pu[:, :n], s_[:, :n], ALU.mult, r=[puk, sk], w=[('fact', j, ti)])
                for ti, (t0, n) in enumerate(TILES):
                    wh = which_of(ti)
                    for oc in range(KC):
                        po, pok = b.ps()
                        for j in range(2):
                            b.mm(po[:, :n], wob[p][:, j, oc * 128:(oc + 1) * 128], actb[:, j, t0:t0 + n],
                                 start=(j == 0), stop=(j == 1), r=[('wob', p), ('fact', j, ti)], w=[pok])
                        b.stt('dve', self.xa[:, oc, t0:t0 + n], po[:, :n], self.gaf[:, oc, wh:wh + 1], self.xa[:, oc, t0:t0 + n],
                              ALU.mult, ALU.add, r=[pok, 'MOD', ('xa', oc, ti)], w=[('xa', oc, ti)])

    def out_proj(self, wb, wkey, nk, yfn, ykeys):
        b = self.b
        for ti, (t0, n) in enumerate(TILES):
            wh = which_of(ti)
            for oc in range(KC):
                po, pok = b.ps()
                for k in range(nk):
                    b.mm(po[:, :n], wb[:, k, oc * 128:(oc + 1) * 128], yfn(k, t0, n), start=(k == 0), stop=(k == nk - 1),
                         r=[wkey] + ykeys(k, ti), w=[pok])
                b.stt('dve', self.xa[:, oc, t0:t0 + n], po[:, :n], self.ga[:, oc, wh:wh + 1], self.xa[:, oc, t0:t0 + n],
                      ALU.mult, ALU.add, r=[pok, 'MOD', ('xa', oc, ti)], w=[('xa', oc, ti)])

    def attn(self, li):
        b = self.b
        j = li // 2
        w_qkv = self.din("attn_w_qkv%d" % j, [D, 1536])
        w_o = self.din("attn_w_out%d" % j, [D, D])
        gains = self.din("attn_gain%d" % j, [128, 2])
        ropeD = self.din("rope", [128, 4096])
        with b.scope() as es:
            QR = b.sb(es, "a_QR", [128, 8, T], BF16)
            KR = b.sb(es, "a_KR", [128, 2, T], BF16)
            VT = b.sb(es, "a_VT", [128, NCH, 256], BF16)
            gn = b.sb(es, "a_gn", [128, 2], F32)
            b.dma(gn[:], gains, w=['gn'])
            b.ts('dve', gn[:], gn[:], float(np.sqrt(128.0)), None, ALU.mult, r=['gn'], w=['gn'])
            with b.scope() as es2:
                rope = b.sb(es2, "a_rope", [128, 4096], F32)
                b.dma(rope[:], ropeD, w=['rope'])
                cosT = rope[:, 0:2048]
                sinT = rope[:, 2048:4096]
                wst_ = b.sb(es2, "a_wst", [128, KC, 128], F32)
                wst = [wst_, wst_]
                wbf = [b.sb(es2, "a_wbf%d" % i, [128, KC, 128], BF16) for i in range(2)]
                sqb = b.sb(es2, "a_sq", [128, 512], BF16)
                rs = b.sb(es2, "a_rs", [128, 512], F32)
                qn = b.sb(es2, "a_qn", [128, 512], F32)
                t1 = b.sb(es2, "a_t1", [128, 512], F32)
                t2 = b.sb(es2, "a_t2", [128, 512], F32)
                sqb2 = [sqb, b.sb(es2, "a_sq2", [128, 512], BF16)]
                qn2 = [qn, b.sb(es2, "a_qn2", [128, 512], F32)]

                def load_w(wbk):
                    p = wbk % 2
                    b.dma(wst[p][:], w_qkv[:, wbk * 128:(wbk + 1) * 128].rearrange("(k p) n -> p k n", p=128), w=[('awst', 0)])
                    b.cp('pool', wbf[p][:], wst[p][:], r=[('awst', 0)], w=[('awbf', p)])
                items = [(wbk, ti) for wbk in range(10) for ti in range(5)]
                st = {}

                def stA(i):
                    wbk, ti = items[i]
                    t0, n = TILES[ti]
                    p = wbk % 2
                    if ti == 0:
                        load_w(wbk)
                    pp, ppk = b.ps(hold=True)
                    for kc in range(KC):
                        b.mm(pp[:, :n], wbf[p][:, kc, :], self.hb[:, kc, t0:t0 + n],
                             start=(kc == 0), stop=(kc == KC - 1), r=[('awbf', p), ('h', kc, ti)], w=[ppk])
                    b.act(sqb2[i % 2][:, :n], pp[:, :n], AF.Square, r=[ppk], w=[('asq', i % 2)])
                    st[i] = (pp, ppk)

                def stB(i):
                    wbk, ti = items[i]
                    t0, n = TILES[ti]
                    pp, ppk = st.pop(i)
                    isq = wbk < 8
                    hidx = wbk if isq else wbk - 8
                    dst = QR if isq else KR
                    gcol = gn[:, 0:1] if isq else gn[:, 1:2]
                    p2, p2k = b.ps(hold=True)
                    b.mm(p2[:, :n], self.onesb[:], sqb2[i % 2][:, :n], r=[('asq', i % 2), 'onesb'], w=[p2k])
                    b.act(rs[:, :n], p2[:, :n], AF.Ln, bias=self.epsv[:, 1:2], r=[p2k, 'cfa'], w=['ars'])
                    b.release(p2k)
                    b.act(rs[:, :n], rs[:, :n], AF.Exp, scale=-0.5, r=['ars'], w=['ars'])
                    if ti == 0:
                        b.stt('dve', dst[:, hidx, t0:t0 + n], pp[:, :n], gcol, rs[:, :n], ALU.mult, ALU.mult,
                              r=[ppk, 'gn', 'ars'], w=[('aqk', isq, hidx, ti)])
                    else:
                        b.stt('dve', qn2[i % 2][:, :n], pp[:, :n], gcol, rs[:, :n], ALU.mult, ALU.mult,
                              r=[ppk, 'gn', 'ars'], w=[('aqn', i % 2)])
                    b.release(ppk)

                def stC(i):
                    wbk, ti = items[i]
                    if ti == 0:
                        return
                    t0, n = TILES[ti]
                    isq = wbk < 8
                    hidx = wbk if isq else wbk - 8
                    dst = QR if isq else KR
                    qn_ = qn2[i % 2]
                    p3, p3k = b.ps(hold=True)
                    b.mm(p3[:, :n], self.rot, qn_[:, :n], r=[('aqn', i % 2), 'cfa'], w=[p3k])
                    l0 = t0 - 256
                    b.tt('dve', t1[:, :n], qn_[:, :n], cosT[:, l0:l0 + n], ALU.mult, r=[('aqn', i % 2), 'rope'], w=['at1'])
                    b.tt('dve', t2[:, :n], p3[:, :n], sinT[:, l0:l0 + n], ALU.mult, r=[p3k, 'rope'], w=['at2'])
                    b.release(p3k)
                    b.tt('pool', dst[:, hidx, t0:t0 + n], t1[:, :n], t2[:, :n], ALU.add, r=['at1', 'at2'],
                         w=[('aqk', isq, hidx, ti)])
                nit = len(items)
                for i in range(nit + 2):
                    if i < nit:
                        stA(i)
                    if 0 <= i - 1 < nit:
                        stB(i - 1)
                    if 0 <= i - 2 < nit:
                        stC(i - 2)
                for wbk in (10, 11):
                    p = wbk % 2
                    load_w(wbk)
                    kvh = wbk - 10
                    for c in range(NCH):
                        pv, pvk = b.ps()
                        for kc in range(KC):
                            b.mm(pv[:, 0:128], self.hb[:, kc, c * 128:(c + 1) * 128], wbf[p][:, kc, :],
                                 start=(kc == 0), stop=(kc == KC - 1), r=[('awbf', p)] + h_keys(kc, ALLT), w=[pvk])
                        b.cp('act', VT[:, c, kvh * 128:(kvh + 1) * 128], pv[:, 0:128], r=[pvk], w=['VT'])
            with b.scope() as es3:
                pts = [b.sb(es3, "a_pt%d" % i, [128, 512], BF16) for i in range(5)]
                rden = b.sb(es3, "a_rden", [128, 512], F32)
                wos_ = b.sb(es3, "a_wos", [128, 2, D], F32)
                wos = [wos_, wos_]
                wob = b.sb(es3, "a_wob", [128, KC, D], BF16)
                for i in range(4):
                    b.dma(wos[i % 2][:], w_o[i * 256:(i + 1) * 256, :].rearrange("(k p) n -> p k n", p=128), w=[('aos', 0)])
                    b.cp('pool', wob[:, 2 * i:2 * i + 2, :], wos[i % 2][:], r=[('aos', 0)], w=['awob'])
                npt = 0
                scale = float(128.0 ** -0.5)
                for hq in range(8):
                    kv = hq // 4
                    for ti, (t0, n) in enumerate(TILES):
                        kts = [0, 1] if ti == 0 else list(range(NCH))
                        pden, pdk = b.ps(hold=True)
                        po, pok = b.ps(hold=True)
                        pend = []

                        def flush(pv_):
                            pt_, ptk_, ii_, kt_ = pv_
                            b.mm(pden[:, :n], self.onesb[:], pt_[:, :n], start=(ii_ == 0), stop=(ii_ == len(kts) - 1),
                                 r=[ptk_, 'onesb'], w=[pdk])
                            b.mm(po[:, :n], VT[:, kt_, kv * 128:(kv + 1) * 128], pt_[:, :n], start=(ii_ == 0), stop=(ii_ == len(kts) - 1),
                                 r=[ptk_, 'VT'], w=[pok])
                        for ii, kt in enumerate(kts):
                            psc, psk = b.ps()
                            b.mm(psc[:, :n], KR[:, kv, kt * 128:(kt + 1) * 128], QR[:, hq, t0:t0 + n],
                                 r=[('aqk', False, kv, tt_) for tt_ in ALLT] + [('aqk', True, hq, ti)], w=[psk])
                            pt = pts[npt % 5]
                            ptk = ('apt', npt % 5)
                            npt += 1
                            b.act(pt[:, :n], psc[:, :n], AF.Exp, scale=scale, r=[psk], w=[ptk])
                            pend.append((pt, ptk, ii, kt))
                            if len(pend) > 3:
                                flush(pend.pop(0))
                        while pend:
                            flush(pend.pop(0))
                        b.op('dve', lambda e, o=rden[:, :n], i=pden[:, :n]: e.reciprocal(out=o, in_=i), r=[pdk], w=['arden'])
                        b.tt('dve', self.hb[:, hq, t0:t0 + n], po[:, :n], rden[:, :n], ALU.mult, r=[pok, 'arden'], w=[('h', hq, ti)])
                        b.release(pdk)
                        b.release(pok)
                self.out_proj(wob, 'awob', KC, lambda k, t0, n: self.hb[:, k, t0:t0 + n], lambda k, ti: [('h', k, ti)])

    def gdn(self, li):
        b = self.b
        j = li // 2
        w_in = self.din("gdn_w_in%d" % j, [D, 6208])
        w_out = self.din("gdn_w_out%d" % j, [2048, D])
        convD = self.din("gdn_convT%d" % j, [128, 32, 5])
        gparD = self.din("gdn_gpar%d" % j, [128, 64])
        normgD = self.din("gdn_normg%d" % j, [128, 128])
        lvlD = self.din("lvlmask", [128, 7 * 4 * 128], U8)
        ident = self.ident
        one_col = self.epsv[:, 2:3]
        eps_col = self.epsv[:, 0:1]

        def bc(ap2, n=128):
            return ap2.unsqueeze(2).to_broadcast([128, ap2.shape[1], n])

        with b.scope() as es:
            G = {}
            for nm in ['NBETA', 'BETA', 'GCUM', 'EG', 'KD', 'EGL']:
                G[nm] = b.sb(es, "g_" + nm, [128, NCH, 32], F32)
            convw = b.sb(es, "g_convw", [128, 32, 5], F32)
            normg = b.sb(es, "g_normg", [128, 128], F32)
            lvl = b.sb(es, "g_lvl", [128, 7, 4, 128], U8)
            b.dma(convw[:], convD, w=['convw'])
            b.dma(normg[:], normgD, w=['normg'])
            b.dma(lvl[:].rearrange("p a b c -> p (a b c)"), lvlD, w=['lvl'])
            with b.scope() as ges:
                gpar = b.sb(ges, "g_gpar", [128, 64], F32)
                wgs = b.sb(ges, "g_wgs", [128, KC, 64], F32)
                wgb = b.sb(ges, "g_wgb", [128, KC, 64], BF16)
                GRAW = b.sb(ges, "g_graw", [128, NCH, 64], F32)
                T1 = b.sb(ges, "g_t1", [128, NCH, 32], F32)
                T2 = b.sb(ges, "g_t2", [128, NCH, 32], F32)
                GG = b.sb(ges, "g_g", [128, NCH, 32], F32)
                GL = b.sb(ges, "g_gl", [128, NCH, 32], F32)
                NA = b.sb(ges, "g_na", [128, 32], F32)
                b.dma(gpar[:], gparD, w=['gpar'])
                b.dma(wgs[:], w_in[:, 6144:6208].rearrange("(k p) n -> p k n", p=128), w=['wgs'])
                b.cp('pool', wgb[:], wgs[:], r=['wgs'], w=['wgb'])
                for c0 in range(0, NCH, 8):
                    nc_ = min(8, NCH - c0)
                    pg, pgk = b.ps()
                    for cc in range(nc_):
                        c = c0 + cc
                        for kc in range(KC):
                            b.mm(pg[:, cc * 64:(cc + 1) * 64], self.hb[:, kc, c * 128:(c + 1) * 128], wgb[:, kc, :],
                                 start=(kc == 0), stop=(kc == KC - 1), r=['wgb'] + h_keys(kc, ALLT), w=[pgk])
                    b.cp('act', GRAW[:, c0:c0 + nc_, :], pg[:, 0:nc_ * 64].rearrange("p (a b) -> p a b", b=64), r=[pgk], w=['graw'])
                braw = GRAW[:, :, 0:32]
                araw = GRAW[:, :, 32:64]
                b.act(T1[:], braw, AF.Exp, scale=-1.0, r=['graw'], w=['gt1'])
                b.act(T1[:], T1[:], AF.Ln, bias=one_col, r=['gt1', 'cfa'], w=['gt1'])
                b.act(G['BETA'][:], T1[:], AF.Exp, scale=-1.0, r=['gt1'], w=['BETA'])
                b.ts('pool', G['NBETA'][:], G['BETA'][:], -1.0, None, ALU.mult, r=['BETA'], w=['NBETA'])
                b.tt('dve', T2[:], araw, gpar[:, 32:64].unsqueeze(1).to_broadcast([128, NCH, 32]), ALU.add, r=['graw', 'gpar'], w=['gt2'])
                b.act(T2[:], T2[:], AF.Exp, r=['gt2'], w=['gt2'])
                b.act(T2[:], T2[:], AF.Ln, bias=one_col, r=['gt2', 'cfa'], w=['gt2'])
                b.act(NA[:], gpar[:, 0:32], AF.Exp, r=['gpar'], w=['gna'])
                b.ts('pool', NA[:], NA[:], -1.0, None, ALU.mult, r=['gna'], w=['gna'])
                b.tt('dve', GG[:], T2[:], NA[:].unsqueeze(1).to_broadcast([128, NCH, 32]), ALU.mult, r=['gt2', 'gna'], w=['gg'])
                pc, pck = b.ps()
                b.mm(pc[:, 0:288], self.trif, GG[:, :, 0:16], r=['gg', 'cfa'], w=[pck])
                pc2, pc2k = b.ps()
                b.mm(pc2[:, 0:288], self.trib, GG[:, :, 16:32], r=['gg', 'cfa'], w=[pc2k])
                b.cp('act', G['GCUM'][:, :, 0:16], pc[:, 0:288].rearrange("p (a b) -> p a b", b=16), r=[pck], w=['GCUM'])
                b.cp('act', G['GCUM'][:, :, 16:32], pc2[:, 0:288].rearrange("p (a b) -> p a b", b=16), r=[pc2k], w=['GCUM'])
                pl, plk = b.ps()
                b.mm(pl[:, 0:288], self.ones, GG[:, 0:9, :], r=['gg', 'cfa'], w=[plk])
                pl2, pl2k = b.ps()
                b.mm(pl2[:, 0:288], self.ones, GG[:, 9:18, :], r=['gg', 'cfa'], w=[pl2k])
                b.cp('act', GL[:, 0:9, :], pl[:, 0:288].rearrange("p (a b) -> p a b", b=32), r=[plk], w=['ggl'])
                b.cp('act', GL[:, 9:18, :], pl2[:, 0:288].rearrange("p (a b) -> p a b", b=32), r=[pl2k], w=['ggl'])
                b.act(G['EG'][:], G['GCUM'][:], AF.Exp, r=['GCUM'], w=['EG'])
                b.tt('dve', T1[:], GL[:], G['GCUM'][:], ALU.subtract, r=['ggl', 'GCUM', 'gt1'], w=['gt1'])
                b.act(G['KD'][:], T1[:], AF.Exp, r=['gt1'], w=['KD'])
                b.act(G['EGL'][:], GL[:], AF.Exp, r=['ggl'], w=['EGL'])
            for g in range(8):
                self.gdn_group(li, g, es, G, convw, normg, lvl, w_in, w_out, bc)

    def gdn_group(self, li, g, es_unused, G, convw, normg, lvl, w_in, w_out, bc):
        b = self.b
        ident = self.ident
        eps_col = self.epsv[:, 0:1]
        ps_bf = lambda ps: ps[:].bitcast(BF16)
        gk = lambda nm: nm + "_%d" % g

        def gcols(d):
            return slice(d * 16 + 2 * g, d * 16 + 2 * g + 2)

        with b.scope() as ges:
            O = b.sb(ges, "g_O", [128, NCH, 2, 128], BF16)
            with b.scope() as aes:
                KQ = b.sb(aes, "g_KQ", [128, NCH, 2, 128], BF16)
                KTM = b.sb(aes, "g_KTM", [128, NCH, 128], BF16)
                VTM = b.sb(aes, "g_VTM", [128, NCH, 2, 128], BF16)
                with b.scope() as pes:
                    CB = b.sb(pes, "g_CB", [128, 2310], F32)
                    ACC = b.sb(pes, "g_ACC", [128, T], F32)
                    TB = b.sb(pes, "g_TB", [128, T], BF16)
                    wst = [b.sb(pes, "g_wst%d" % i, [128, KC, 128], F32) for i in range(2)]
                    wbf = [b.sb(pes, "g_wbf%d" % i, [128, KC, 128], BF16) for i in range(2)]
                    rs = b.sb(pes, "g_rs", [128, 512], F32)
                    b.op('pool', lambda e: e.memset(CB[:, 0:2], 0.0), w=['CBp0'])
                    b.op('pool', lambda e: e.memset(CB[:, 258:260], 0.0), w=['CBp1'])
                    b.op('pool', lambda e: e.memset(CB[:, 2308:2310], 0.0), w=['CBp2'])
                    fcs = [('q', g * 128, g), ('k', 1024 + g * 128, 8 + g),
                           ('v0', 2048 + (2 * g) * 128, 16 + 2 * g), ('v1', 2048 + (2 * g + 1) * 128, 16 + 2 * g + 1)]
                    def emit_proj(fi):
                        kind, col0, cq = fcs[fi]
                        held = []
                        p = fi % 2
                        b.dma(wst[p][:], w_in[:, col0:col0 + 128].rearrange("(k p) n -> p k n", p=128), w=[('gwst', p)])
                        b.cp('pool', wbf[p][:], wst[p][:], r=[('gwst', p)], w=[('gwbf', p)])
                        for ti, (t0, n) in enumerate(TILES):
                            pp, ppk = b.ps(hold=True)
                            for kc in range(KC):
                                b.mm(pp[:, :n], wbf[p][:, kc, :], self.hb[:, kc, t0:t0 + n], start=(kc == 0), stop=(kc == KC - 1),
                                     r=[('gwbf', p), ('h', kc, ti)], w=[ppk])
                            o0 = 2 if ti == 0 else t0 + 4
                            b.cp('act', CB[:, o0:o0 + n], pp[:, :n], r=[ppk], w=[('CB', ti)])
                            held.append(ppk)
                        return held

                    def emit_conv(fi):
                        kind, col0, cq = fcs[fi]
                        acck = [('ACC', 0), ('ACC', 256), ('ACC', 1280)]
                        for (d0, L, s0, tis) in [(0, 256, 2, [0]), (256, 1024, 260, [1, 2]), (1280, 1024, 1284, [3, 4])]:
                            ak = ('ACC', d0)
                            rk = [('CB', tj) for tj in {0: [0], 256: [1, 2, 3], 1280: [2, 3, 4]}[d0]] + ['CBp0', 'CBp1', 'CBp2', 'convw']
                            b.ts('dve', ACC[:, d0:d0 + L], CB[:, s0 - 2:s0 - 2 + L], convw[:, cq, 0:1], None, ALU.mult, r=rk, w=[ak])
                            for jj in range(1, 5):
                                b.stt('dve', ACC[:, d0:d0 + L], CB[:, s0 - 2 + jj:s0 - 2 + jj + L], convw[:, cq, jj:jj + 1],
                                      ACC[:, d0:d0 + L], ALU.mult, ALU.add, r=rk + [ak], w=[ak])
                            if kind in ('q', 'k'):
                                b.act(ACC[:, d0:d0 + L], ACC[:, d0:d0 + L], AF.Silu, r=[ak], w=[ak])
                            else:
                                b.act(TB[:, d0:d0 + L], ACC[:, d0:d0 + L], AF.Silu, r=[ak], w=['TB'])

                    def emit_tail(fi):
                        kind, col0, cq = fcs[fi]
                        acck = [('ACC', 0), ('ACC', 256), ('ACC', 1280)]
                        if kind in ('q', 'k'):
                            b.act(TB[:], ACC[:], AF.Square, r=acck, w=['TB'])
                            kq = 0 if kind == 'k' else 1
                            sc_ = 1.0 if kind == 'k' else float(128.0 ** -0.5)
                            for ti, (t0, n) in enumerate(TILES):
                                p2, p2k = b.ps()
                                b.mm(p2[:, :n], self.onesb[:], TB[:, t0:t0 + n], r=['TB', 'onesb'], w=[p2k])
                                b.act(rs[:, :n], p2[:, :n], AF.Ln, bias=eps_col, r=[p2k, 'cfa'], w=['grs'])
                                b.act(rs[:, :n], rs[:, :n], AF.Exp, scale=-0.5, r=['grs'], w=['grs'])
                                c0 = t0 // 128
                                nc_ = n // 128
                                b.stt('dve', KQ[:, c0:c0 + nc_, kq, :], ACC[:, t0:t0 + n].rearrange("p (a b) -> p a b", b=128), sc_,
                                      rs[:, :n].rearrange("p (a b) -> p a b", b=128), ALU.mult, ALU.mult,
                                      r=acck + ['grs'], w=[gk('KQ')])
                            if kind == 'k':
                                for c0 in range(0, NCH, 4):
                                    nc_ = min(4, NCH - c0)
                                    pt, ptk = b.ps()
                                    ptb = ps_bf(pt)
                                    for cc in range(nc_):
                                        b.tr(ptb[:, cc * 128:(cc + 1) * 128], KQ[:, c0 + cc, 0, :], self.identb[:], r=[gk('KQ'), 'identb'], w=[ptk])
                                    b.cp('act', KTM[:, c0:c0 + nc_, :], ptb[:, 0:nc_ * 128].rearrange("p (a b) -> p a b", b=128), r=[ptk], w=[gk('KTM')])
                        else:
                            a_ = 0 if kind == 'v0' else 1
                            for c0 in range(0, NCH, 4):
                                nc_ = min(4, NCH - c0)
                                pt, ptk = b.ps()
                                ptb = ps_bf(pt)
                                for cc in range(nc_):
                                    b.tr(ptb[:, cc * 128:(cc + 1) * 128], TB[:, (c0 + cc) * 128:(c0 + cc + 1) * 128], self.identb[:], r=['TB', 'identb'], w=[ptk])
                                b.cp('act', VTM[:, c0:c0 + nc_, a_, :], ptb[:, 0:nc_ * 128].rearrange("p (a b) -> p a b", b=128), r=[ptk], w=[gk('VTM')])

                    for k_ in emit_proj(0):
                        b.release(k_)
                    for fi in range(4):
                        emit_conv(fi)
                        hk = emit_proj(fi + 1) if fi + 1 < 4 else []
                        emit_tail(fi)
                        for k_ in hk:
                            b.release(k_)
                with b.scope() as ses:
                    NL = GDN_LANES

                    def t4(nm, dt):
                        return b.sb(ses, "g_" + nm, [128, 4, 128], dt)
                    DG = b.sb(ses, "g_DG", [128, 4, 128], F32)
                    BGEg = b.sb(ses, "g_BGEg", [128, NCH, 4], F32)
                    C1 = t4("C1", F32)
                    TO = C1
                    DT = t4("DT", BF16)
                    Dm = t4("Dm", BF16)
                    KKn = t4("KKn", BF16)
                    QKs = b.sb(ses, "g_QKs", [128, 2, 128], BF16)
                    VN = t4("VN", BF16)
                    S = t4("S", F32)
                    Sbf = t4("Sbf", BF16)
                    VB = t4("VB", BF16)
                    KBG = t4("KBG", BF16)
                    KDEC = t4("KDEC", BF16)
                    NWT = t4("NWT", BF16)
                    lanes = []
                    for k in range(NL):
                        lanes.append({nm: t4("%s_l%d" % (nm, k), BF16) for nm in ['Ap', 'QKD', 'Y', 'W', 'X']})
                        lanes[-1]['k'] = k
                    b.op('pool', lambda e: e.memset(S[:], 0.0), w=['S'])
                    b.op('pool', lambda e: e.memset(Sbf[:], 0.0), w=['Sbf'])
                    for d in range(2):
                        b.tt('pool', BGEg[:, :, 2 * d:2 * d + 2], G['BETA'][:, :, gcols(d)], G['EG'][:, :, gcols(d)], ALU.mult, r=['BETA', 'EG'], w=['BGEg'])
                    owritten = set()
                    f2 = lambda ap: ap.rearrange("p a b -> p (a b)")
                    idb = ident.unsqueeze(1).to_broadcast([128, 2, 128])
                    nidb = self.nident.unsqueeze(1).to_broadcast([128, 2, 128])

                    def step_gen(s, Ln):
                        lk = lambda nm: (nm, Ln['k'])
                        Ap, QKD, Y, W, X = (Ln[nm] for nm in ['Ap', 'QKD', 'Y', 'W', 'X'])
                        cds = [FWD[s], BWD[s]]

                        def scale_ops(which):
                            for u in range(4):
                                d, a_ = u // 2, u % 2
                                cd = cds[d]
                                col = d * 16 + 2 * g + a_
                                if which == 'VB':
                                    b.act(VB[:, u, :], VTM[:, cd, a_, :], AF.Copy, scale=G['BETA'][:, cd, col:col + 1], r=[gk('VTM'), 'BETA'], w=['VB'])
                                elif which == 'KBG':
                                    b.act(KBG[:, u, :], KTM[:, cd, :], AF.Copy, scale=BGEg[:, cd, u:u + 1], r=[gk('KTM'), 'BGEg'], w=['KBG'])
                                else:
                                    b.act(KDEC[:, u, :], KTM[:, cd, :], AF.Copy, scale=G['KD'][:, cd, col:col + 1], r=[gk('KTM'), 'KD'], w=['KDEC'])
                        pkq, pkqk = b.ps(hold=True)
                        for d in range(2):
                            cd = cds[d]
                            b.mm(pkq[:, d * 256:(d + 1) * 256], KQ[:, cd, 0, :], KQ[:, cd, :, :].rearrange("p a b -> p (a b)"),
                                 r=[gk('KQ')], w=[pkqk])
                        for d in range(2):
                            cd = cds[d]
                            b.tt('pool', DG[:, 2 * d:2 * d + 2, :], idb, bc(G['GCUM'][:, cd, gcols(d)]), ALU.mult, r=['GCUM', 'cfa'], w=['DG'])
                        pe_, pek = b.ps(hold=True)
                        for u in range(4):
                            b.mm(pe_[:, u * 128:(u + 1) * 128], self.ones, DG[:, u, :], start=True, stop=False, r=['DG', 'cfa'], w=[pek])
                            b.mm(pe_[:, u * 128:(u + 1) * 128], DG[:, u, :], self.nones, start=False, stop=True, r=['DG', 'cfa'], w=[pek])
                        b.ts('dve', f2(C1[:]), pe_[:], 0.0, None, ALU.min, r=[pek], w=['C1'])
                        b.act(f2(DT[:]), f2(C1[:]), AF.Exp, r=['C1'], w=['DT'])
                        b.ts('dve', f2(C1[:]), pe_[:], 0.0, None, ALU.max, r=[pek], w=['C1'])
                        b.release(pek)
                        b.act(f2(Dm[:]), f2(C1[:]), AF.Exp, scale=-1.0, r=['C1'], w=['Dm'])
                        for d in range(2):
                            cd = cds[d]
                            kk = pkq[:, d * 256:d * 256 + 128].unsqueeze(1).to_broadcast([128, 2, 128])
                            b.tt('dve', KKn[:, 2 * d:2 * d + 2, :], kk, bc(G['NBETA'][:, cd, gcols(d)]), ALU.mult, r=[pkqk, 'NBETA'], w=['KKn'])
                        qkv_ = pkq[:].rearrange("p (d x) -> p d x", d=2)[:, :, 128:256]
                        b.tt('dve', QKs[:], qkv_, self.maskq.rearrange("p (d x) -> p d x", d=2), ALU.mult, r=[pkqk, 'cfa'], w=['QKs'])
                        b.release(pkqk)
                        yield
                        b.tt('pool', f2(Ap[:]), f2(KKn[:]), f2(Dm[:]), ALU.mult, r=['KKn', 'Dm'], w=[lk('Ap')])
                        b.tt('pool', QKD[:].rearrange("p (d a) x -> p d a x", d=2), QKs[:].unsqueeze(2).to_broadcast([128, 2, 2, 128]),
                             DT[:].rearrange("p (d a) x -> p d a x", d=2), ALU.mult, r=['QKs', 'DT'], w=[lk('QKD')])
                        ptt, pttk = b.ps(hold=True)
                        pttb = ps_bf(ptt)
                        for u in range(4):
                            b.tr(pttb[:, u * 128:(u + 1) * 128], Ap[:, u, :], self.identb[:], r=[lk('Ap'), 'identb'], w=[pttk])
                        b.cp('pool', Y[:], self.identb[:].unsqueeze(1).to_broadcast([128, 4, 128]), r=['identb'], w=[lk('Y')])
                        b.op('dve', lambda e, o=f2(Y[:]), m=f2(lvl[:, 0, :, :]), dd=pttb[:, 0:512]: e.copy_predicated(o, m, dd),
                             r=[pttk, 'lvl', lk('Y')], w=[lk('Y')])
                        b.release(pttk)
                        yield
                        for l in range(1, 7):
                            pw, pwk = b.ps(hold=True)
                            for u in range(4):
                                b.mm(pw[:, u * 128:(u + 1) * 128], Ap[:, u, :], Y[:, u, :], r=[lk('Ap'), lk('Y')], w=[pwk])
                            px, pxk = b.ps(hold=True)
                            pxb = ps_bf(px)
                            for u in range(4):
                                b.tr(pxb[:, u * 128:(u + 1) * 128], Y[:, u, :], self.identb[:], r=[lk('Y'), 'identb'], w=[pxk])
                            b.cp('act', f2(W[:]), pw[:], r=[pwk], w=[lk('W')])
                            b.cp('dve', f2(X[:]), pxb[:, 0:512], r=[pxk], w=[lk('X')])
                            b.release(pwk)
                            b.release(pxk)
                            yield
                            pz, pzk = b.ps(hold=True)
                            for u in range(4):
                                b.mm(pz[:, u * 128:(u + 1) * 128], X[:, u, :], W[:, u, :], r=[lk('X'), lk('W')], w=[pzk])
                            b.op('dve', lambda e, o=f2(Y[:]), m=f2(lvl[:, l, :, :]), dd=pz[:]: e.copy_predicated(o, m, dd),
                                 r=[pzk, 'lvl', lk('Y')], w=[lk('Y')])
                            b.release(pzk)
                            if l == 6:
                                scale_ops('KBG')
                            yield
                        pwt, pwtk = b.ps(hold=True)
                        for u in range(4):
                            b.mm(pwt[:, u * 128:(u + 1) * 128], KBG[:, u, :], Y[:, u, :], r=['KBG', lk('Y')], w=[pwtk])
                        b.act(f2(NWT[:]), pwt[:], AF.Copy, scale=-1.0, r=[pwtk], w=['NWT'])
                        b.release(pwtk)
                        scale_ops('VB')
                        yield
                        pvn, pvnk = b.ps(hold=True)
                        for u in range(4):
                            b.mm(pvn[:, u * 128:(u + 1) * 128], Y[:, u, :], VB[:, u, :], start=True, stop=False, r=[lk('Y'), 'VB'], w=[pvnk])
                            b.mm(pvn[:, u * 128:(u + 1) * 128], NWT[:, u, :], Sbf[:, u, :], start=False, stop=True, r=['NWT', 'Sbf'], w=[pvnk])
                        b.cp('act', f2(VN[:]), pvn[:], r=[pvnk], w=['VN'])
                        b.release(pvnk)
                        scale_ops('KDEC')
                        yield
                        pds, pdsk = b.ps(hold=True)
                        po1, po1k = b.ps(hold=True)
                        po2, po2k = b.ps(hold=True)
                        for u in range(4):
                            b.mm(pds[:, u * 128:(u + 1) * 128], KDEC[:, u, :], VN[:, u, :], r=['KDEC', 'VN'], w=[pdsk])
                        for d in range(2):
                            cd = cds[d]
                            b.mm(po1[:, d * 256:(d + 1) * 256], KQ[:, cd, 1, :], Sbf[:, 2 * d:2 * d + 2, :].rearrange("p a b -> p (a b)"),
                                 r=[gk('KQ'), 'Sbf'], w=[po1k])
                        for u in range(4):
                            b.mm(po2[:, u * 128:(u + 1) * 128], QKD[:, u, :], VN[:, u, :], r=[lk('QKD'), 'VN'], w=[po2k])
                        for d in range(2):
                            cd = cds[d]
                            b.tt('pool', S[:, 2 * d:2 * d + 2, :], S[:, 2 * d:2 * d + 2, :], bc(G['EGL'][:, cd, gcols(d)]), ALU.mult,
                                 r=['S', 'EGL'], w=['S'])
                        b.tt('dve', f2(S[:]), f2(S[:]), pds[:], ALU.add, r=['S', pdsk], w=['S'])
                        b.release(pdsk)
                        b.cp('act', f2(Sbf[:]), f2(S[:]), r=['S'], w=['Sbf'])
                        for d in range(2):
                            cd = cds[d]
                            b.tt('dve', TO[:, 2 * d:2 * d + 2, :], po1[:, d * 256:(d + 1) * 256].rearrange("p (a x) -> p a x", a=2),
                                 bc(G['EG'][:, cd, gcols(d)]), ALU.mult, r=[po1k, 'EG'], w=['C1'])
                        b.tt('dve', f2(TO[:]), f2(TO[:]), po2[:], ALU.add, r=['C1', po2k], w=['C1'])
                        b.release(po1k)
                        b.release(po2k)
                        for d in range(2):
                            cd = cds[d]
                            if cd not in owritten:
                                owritten.add(cd)
                                b.cp('act', O[:, cd, :, :], TO[:, 2 * d:2 * d + 2, :], r=['C1'], w=[gk('O')])
                            else:
                                b.tt('pool', O[:, cd, :, :], O[:, cd, :, :], TO[:, 2 * d:2 * d + 2, :], ALU.add, r=['C1', gk('O')], w=[gk('O')])
                        yield

                    gens = []
                    next_s = 0
                    turn = 0
                    stagger = GDN_STAGGER
                    while next_s < NCH or gens:
                        if next_s < NCH and turn % stagger == 0 and len(gens) < NL:
                            gens.append(step_gen(next_s, lanes[next_s % NL]))
                            next_s += 1
                        for g_ in list(gens):
                            try:
                                next(g_)
                            except StopIteration:
                                gens.remove(g_)
                        turn += 1
            with b.scope() as oes:
                ZS = b.sb(oes, "g_ZS", [128, NCH, 256], BF16)
                wzs = b.sb(oes, "g_wzs", [128, KC, 256], F32)
                wzb = b.sb(oes, "g_wzb", [128, KC, 256], BF16)
                wos = b.sb(oes, "g_wos", [128, 2, D], F32)
                wob = b.sb(oes, "g_wob", [128, 2, D], BF16)
                OSQ = b.sb(oes, "g_OSQ", [128, 6, 2, 128], F32)
                SS = b.sb(oes, "g_SS", [128, NCH, 2], F32)
                YT = b.sb(oes, "g_YT", [128, 6, 2, 128], F32)
                YTb = b.sb(oes, "g_YTb", [128, 6, 2, 128], BF16)
                YF = b.sb(oes, "g_YF", [128, 2, T], BF16)
                zc0 = 4096 + 2 * g * 128
                b.dma(wzs[:], w_in[:, zc0:zc0 + 256].rearrange("(k p) n -> p k n", p=128), w=['wzs'])
                b.cp('pool', wzb[:], wzs[:], r=['wzs'], w=['wzb'])
                b.dma(wos[:], w_out[2 * g * 128:(2 * g + 2) * 128, :].rearrange("(k p) n -> p k n", p=128), w=['gwos'])
                b.cp('pool', wob[:], wos[:], r=['gwos'], w=['gwob'])
                for c0 in range(0, NCH, 2):
                    pz_, pzk_ = b.ps()
                    for cc in range(2):
                        c = c0 + cc
                        for kc in range(KC):
                            b.mm(pz_[:, cc * 256:(cc + 1) * 256], self.hb[:, kc, c * 128:(c + 1) * 128], wzb[:, kc, :],
                                 start=(kc == 0), stop=(kc == KC - 1), r=['wzb'] + h_keys(kc, ALLT), w=[pzk_])
                    b.act(ZS[:, c0:c0 + 2, :], pz_[:].rearrange("p (a b) -> p a b", a=2), AF.Silu, r=[pzk_], w=['ZS'])
                for c0 in range(0, NCH, 6):
                    Oc = O[:, c0:c0 + 6, :, :]
                    b.tt('pool', OSQ[:], Oc, Oc, ALU.mult, r=[gk('O')], w=['OSQ'])
                    b.op('dve', lambda e, o=SS[:, c0:c0 + 6, :], i=OSQ[:]: e.tensor_reduce(out=o, in_=i, axis=AX.X, op=ALU.add), r=['OSQ'], w=['SS'])
                b.act(SS[:], SS[:], AF.Ln, bias=eps_col, scale=1.0 / 128.0, r=['SS', 'cfa'], w=['SS'])
                b.act(SS[:], SS[:], AF.Exp, scale=-0.5, r=['SS'], w=['SS'])
                for c0 in range(0, NCH, 6):
                    Oc = O[:, c0:c0 + 6, :, :]
                    b.tt('dve', YT[:], Oc, SS[:, c0:c0 + 6, :].unsqueeze(3).to_broadcast([128, 6, 2, 128]), ALU.mult, r=[gk('O'), 'SS'], w=['YT'])
                    b.tt('pool', YT[:].rearrange("p a b c -> p (a b) c"), YT[:].rearrange("p a b c -> p (a b) c"),
                         normg[:].unsqueeze(1).to_broadcast([128, 12, 128]), ALU.mult, r=['YT', 'normg'], w=['YT'])
                    b.tt('dve', YTb[:], YT[:], ZS[:, c0:c0 + 6, :].rearrange("p a (b c) -> p a b c", b=2), ALU.mult, r=['YT', 'ZS'], w=['YTb'])
                    for a_ in range(2):
                        for q4 in range(0, 6, 4):
                            nq = min(4, 6 - q4)
                            pt, ptk = b.ps()
                            ptb = ps_bf(pt)
                            for cc in range(nq):
                                b.tr(ptb[:, cc * 128:(cc + 1) * 128], YTb[:, q4 + cc, a_, :], self.identb[:], r=['YTb', 'identb'], w=[ptk])
                            t0_ = (c0 + q4) * 128
                            b.cp('act', YF[:, a_, t0_:t0_ + nq * 128], ptb[:, 0:nq * 128], r=[ptk], w=['YF'])
                self.out_proj(wob, 'gwob', 2, lambda k, t0, n: YF[:, k, t0:t0 + n], lambda k, ti: ['YF'])


def host_consts():
    cfa = np.zeros((128, NCFA), np.float32)
    idx = np.arange(128)
    cfa[:, 0:128] = np.eye(128, dtype=np.float32)
    cfa[:, 128:256] = (idx[:, None] <= idx[None, :]).astype(np.float32)
    cfa[:, 256:384] = (idx[:, None] >= idx[None, :]).astype(np.float32)
    cfa[:, 384:512] = 1.0
    cfa[:, 512:640] = (idx[None, :] >= idx[:, None]).astype(np.float32)
    cfa[:, 640:768] = (idx[None, :] <= idx[:, None]).astype(np.float32)
    rot = np.zeros((128, 128), np.float32)
    for m in range(128):
        half = (m % 64) // 32
        if half == 0:
            rot[m + 32, m] = -1.0
        else:
            rot[m - 32, m] = 1.0
    cfa[:, 768:896] = rot
    cfa[:, 1152] = EPS
    cfa[:, 1153] = 128.0 * EPS
    cfa[:, 1154] = 1.0
    cfa[:, 896:1024] = -np.eye(128, dtype=np.float32)
    cfa[:, 1024:1152] = -1.0
    return cfa


def rope_host():
    rows = 2048 // 64
    row = np.repeat(np.arange(rows), 64).astype(np.float32)
    col = np.tile(np.arange(64), rows).astype(np.float32)
    n_freq = 32
    freqs = (np.float32(10000.0) ** (-np.arange(n_freq, dtype=np.float32) / np.float32(n_freq))).astype(np.float32)
    ang_r = row[:, None] * freqs
    ang_c = col[:, None] * freqs
    ang = np.concatenate([ang_r, ang_r, ang_c, ang_c], axis=-1).astype(np.float32)
    out = np.zeros((128, 4096), np.float32)
    out[:, 0:2048] = np.cos(ang).T
    out[:, 2048:4096] = np.sin(ang).T
    return out


_CACHE = {}


def get_prog(layers, nseq):
    key = (tuple(layers), nseq)
    if key not in _CACHE:
        nc = bass.Bass("TRN2", target_bir_lowering=False)
        p = Prog(nc, list(layers), nseq)
        p.build()
        _CACHE[key] = (nc, p)
    return _CACHE[key]


def layer_inputs(inp, li):
    d = {}
    d["w_mod%d" % li] = np.ascontiguousarray(inp["w_mod"][li])
    d["bmodT%d" % li] = np.ascontiguousarray(inp["b_mod"][li].reshape(48, 128).T)
    ln = np.stack([inp["ln_g"][li, 0], inp["ln_b"][li, 0], inp["ln_g"][li, 1], inp["ln_b"][li, 1]], 0)
    d["lnT%d" % li] = np.ascontiguousarray(ln.reshape(4, 8, 128).transpose(2, 0, 1))
    d["w_ffn_in%d" % li] = np.ascontiguousarray(inp["w_ffn_in"][li])
    d["w_ffn_out%d" % li] = np.ascontiguousarray(inp["w_ffn_out"][li])
    j = li // 2
    if li % 2 == 1:
        d["attn_w_qkv%d" % j] = np.ascontiguousarray(inp["attn_w_qkv"][j])
        d["attn_w_out%d" % j] = np.ascontiguousarray(inp["attn_w_out"][j])
        d["attn_gain%d" % j] = np.ascontiguousarray(np.stack([inp["attn_q_norm"][j], inp["attn_k_norm"][j]], 1))
        d["rope"] = rope_host()
    else:
        d["gdn_w_in%d" % j] = np.ascontiguousarray(inp["gdn_w_in"][j])
        d["gdn_w_out%d" % j] = np.ascontiguousarray(inp["gdn_w_out"][j])
        d["gdn_convT%d" % j] = np.ascontiguousarray(inp["gdn_conv"][j].reshape(5, 32, 128).transpose(2, 1, 0))
        gp = np.concatenate([inp["gdn_a_log"][j].reshape(32), inp["gdn_dt_bias"][j].reshape(32)])
        d["gdn_gpar%d" % j] = np.ascontiguousarray(np.broadcast_to(gp[None, :], (128, 64)).astype(np.float32))
        d["gdn_normg%d" % j] = np.ascontiguousarray(np.broadcast_to(inp["gdn_norm_g"][j][None, :], (128, 128)).astype(np.float32))
        d["lvlmask"] = lvlmask_host()
    return d


def lvlmask_host():
    p = np.arange(128)[:, None]
    f = np.arange(128)[None, :]
    m = np.zeros((128, 7, 4, 128), np.uint8)
    for l in range(7):
        if l == 0:
            blk = (p >> 1) == (f >> 1)
        else:
            blk = ((p >> (l + 1)) == (f >> (l + 1))) & ((p >> l) != (f >> l))
        fw = (blk & (f > p)).astype(np.uint8)
        bw = (blk & (f < p)).astype(np.uint8)
        m[:, l, 0, :] = fw
        m[:, l, 1, :] = fw
        m[:, l, 2, :] = bw
        m[:, l, 3, :] = bw
    return np.ascontiguousarray(m.reshape(128, 7 * 4 * 128))


def seq_fm(inp, bidx):
    return np.ascontiguousarray(np.concatenate([inp["ctx"][bidx], inp["x"][bidx]], 0).T)


def cT_host(inp, bidxs):
    v = np.stack([inp["c_ctx"]] + [inp["c"][bi] for bi in bidxs], 1)
    return np.ascontiguousarray(v.reshape(8, 128, len(bidxs) + 1).transpose(1, 0, 2))


def run_layers(inp, layers, xs):
    nc, p = get_prog(layers, 1)
    outs = [None] * 16
    cfa = host_consts()
    for rnd in range(2):
        in_maps = []
        for core in range(8):
            bidx = rnd * 8 + core
            m = {"cfa": cfa, "xin0": xs[bidx], "cTall": cT_host(inp, [bidx])}
            for li in layers:
                m.update(layer_inputs(inp, li))
            in_maps.append({k: v for k, v in m.items() if k in p.dram})
        res = run_bass_kernel_spmd(nc, in_maps, core_ids=list(range(8)))
        for core in range(8):
            outs[rnd * 8 + core] = res.results[core]["xout0"]
    return outs


def kernel_unfused(**inp):
    inp = {k: np.asarray(v) for k, v in inp.items()}
    xs = [seq_fm(inp, bi) for bi in range(16)]
    for li in range(DEPTH):
        xs = run_layers(inp, [li], xs)
    out = np.stack([x.T[256:, :] for x in xs], 0)
    return np.ascontiguousarray(out.astype(np.float32))


def kernel(**inp):
    inp = {k: np.asarray(v) for k, v in inp.items()}
    layers = list(range(DEPTH))
    nc, p = get_prog(layers, 2)
    cfa = host_consts()
    shared = {"cfa": cfa}
    for li in layers:
        shared.update(layer_inputs(inp, li))
    shared = {k: v for k, v in shared.items() if k in p.dram}
    in_maps = []
    for core in range(8):
        m = dict(shared)
        for s in range(2):
            bidx = 2 * core + s
            m["xin%d" % s] = seq_fm(inp, bidx)
        m["cTall"] = cT_host(inp, [2 * core, 2 * core + 1])
        in_maps.append(m)
    res = run_bass_kernel_spmd(nc, in_maps, core_ids=list(range(8)))
    out = np.zeros((16, 2048, 1024), np.float32)
    for core in range(8):
        for s in range(2):
            out[2 * core + s] = res.results[core]["xout%d" % s].T[256:, :]
    return out
```
